# Optimizing a Trainium2 kernel written in Bass

```python
import math
import jax, jax.numpy as jnp
from jax import lax
import numpy as np

D_MODEL = 1024
BATCH = 8
SEQ = 2048
DEPTH = 2
DEC_BATCH = 128
DEC_SEQ = 8
PAST_LEN = 2048
PAGE_SIZE = 128

N_EVEN = (DEPTH + 1) // 2
N_ODD = DEPTH // 2
EPS = 1e-6
ROPE_THETA = 10000.0
Q_BLOCK = 128
NEG_INF = -1e30
FORCE = 1e4

RET_HEADS = 4
RET_DK = 64
RET_DV = 128
RET_CHUNK = 128
A_WIDTH = RET_HEADS * RET_DV

NSA_HEADS = 8
NSA_KV_HEADS = 2
NSA_DH = 64
NSA_HPG = NSA_HEADS // NSA_KV_HEADS
B_WIDTH = NSA_HEADS * NSA_DH
L_CMP = 32
CMP_STRIDE = 16
CMP_RATIO = L_CMP // CMP_STRIDE
L_SEL = 64
N_SEL = 8
WINDOW = 512
N_FULL_KV = 4
N_WIN_KV = 2

S5_WIDTH = D_MODEL
S5_GROUP = 16
S5_GROUPS = S5_WIDTH // S5_GROUP
S5_STATE = 64

EVEN_SPLITS = (RET_HEADS * RET_DK, RET_HEADS * RET_DK, A_WIDTH, A_WIDTH, B_WIDTH, 6 * NSA_KV_HEADS * NSA_DH, 3 * NSA_HEADS, B_WIDTH)
EVEN_IN = 2 * RET_HEADS * RET_DK + 2 * A_WIDTH + 2 * B_WIDTH + 6 * NSA_KV_HEADS * NSA_DH + 3 * NSA_HEADS
MIX_EVEN = A_WIDTH + B_WIDTH

kernel_name = 'hybrid_retention_nsa_s5_step'


def _rmsnorm(x, g):
    xf = x.astype(jnp.float32)
    y = xf * lax.rsqrt(jnp.mean(xf * xf, axis=-1, keepdims=True) + EPS)
    return (y * g.astype(jnp.float32)).astype(x.dtype)


def _rope(x, pos):
    half = x.shape[-1] // 2
    inv = ROPE_THETA ** (-jnp.arange(half, dtype=jnp.float32) / half)
    ang = pos.astype(jnp.float32)[:, None] * inv[None, :]
    cos = jnp.cos(ang)[:, None, :]
    sin = jnp.sin(ang)[:, None, :]
    xf = x.astype(jnp.float32)
    x1, x2 = xf[..., :half], xf[..., half:]
    return jnp.concatenate([x1 * cos - x2 * sin, x2 * cos + x1 * sin], axis=-1).astype(x.dtype)


def _split(x, sizes):
    outs, o = [], 0
    for s in sizes:
        outs.append(x[..., o:o + s])
        o += s
    return outs


def _head_groupnorm(o, g):
    mu = jnp.mean(o, axis=-1, keepdims=True)
    var = jnp.mean(jnp.square(o - mu), axis=-1, keepdims=True)
    y = (o - mu) * lax.rsqrt(var + EPS)
    B, L, H, dv = o.shape
    return y.reshape(B, L, H * dv) * g.astype(jnp.float32)


def _retention(q, k, v, s0):
    f32 = jnp.float32
    q, k, v, s0 = q.astype(f32), k.astype(f32), v.astype(f32), s0.astype(f32)
    B, L, H, dk = q.shape
    dv = v.shape[-1]
    C = min(RET_CHUNK, L)
    n = L // C
    log_g = jnp.log(1.0 - 2.0 ** (-5.0 - jnp.arange(H, dtype=f32)))
    idx = jnp.arange(C, dtype=f32)
    diff = idx[:, None] - idx[None, :]
    causal = diff >= 0
    decay_mat = jnp.exp(jnp.where(causal, diff, 0.0)[None] * log_g[:, None, None]) * causal[None]
    q_decay = jnp.exp((idx + 1.0)[None, :] * log_g[:, None]).T[None, :, :, None]
    k_decay = jnp.exp((C - 1.0 - idx)[None, :] * log_g[:, None]).T[None, :, :, None]
    chunk_decay = jnp.exp(C * log_g)[None, :, None, None]

    def step(S, xs):
        qc, kc, vc = xs
        scores = jnp.einsum('bihd,bjhd->bhij', qc, kc) * decay_mat[None]
        intra = jnp.einsum('bhij,bjhe->bihe', scores, vc)
        cross = jnp.einsum('bihd,bhde->bihe', qc * q_decay, S)
        S_new = S * chunk_decay + jnp.einsum('bjhd,bjhe->bhde', kc * k_decay, vc)
        return S_new, intra + cross

    to_chunks = lambda a: a.reshape(B, n, C, H, a.shape[-1]).transpose(1, 0, 2, 3, 4)
    S_fin, o = lax.scan(step, s0, (to_chunks(q), to_chunks(k), to_chunks(v)))
    o = o.transpose(1, 0, 2, 3, 4).reshape(B, L, H, dv)
    return o, S_fin


def _nsa(q, kv_full, kv_win, gates, cmp_pos, cmp_w):
    f32 = jnp.float32
    B, Tq, H, dh = q.shape
    Tk = kv_full.shape[1]
    Tw = kv_win.shape[1]
    G = NSA_KV_HEADS
    q_pos0 = Tk - Tq
    kw_start = Tk - Tw
    scale = dh ** -0.5

    nh = Tk // CMP_STRIDE
    n_c = nh - CMP_RATIO + 1
    halves = kv_full[:, :nh * CMP_STRIDE, :2].reshape(B, nh, CMP_STRIDE, 2, G, dh)
    comp = None
    for r in range(CMP_RATIO):
        w_r = cmp_w[:, r * CMP_STRIDE:(r + 1) * CMP_STRIDE]
        p_r = cmp_pos[:, r * CMP_STRIDE:(r + 1) * CMP_STRIDE]
        part = jnp.einsum('bnlcgd,clde->bncge', halves, w_r) + jnp.einsum('cld,clde->ce', p_r, w_r)[None, None, :, None, :]
        part = part[:, r:r + n_c]
        comp = part if comp is None else comp + part
    ck, cv = comp[:, :, 0], comp[:, :, 1]
    cmp_end = jnp.arange(n_c, dtype=jnp.int32) * CMP_STRIDE + (L_CMP - 1)

    n_s = -(-Tk // L_SEL)
    tk_pad = n_s * L_SEL
    sel_kv = jnp.pad(kv_full[:, :, 2:4], ((0, 0), (0, tk_pad - Tk), (0, 0), (0, 0), (0, 0)))
    sel_kv = sel_kv.reshape(B, n_s, L_SEL, 2, G, dh).transpose(0, 4, 1, 2, 3, 5)
    ks_blk, vs_blk = sel_kv[..., 0, :], sel_kv[..., 1, :]
    c_start = jnp.arange(n_c, dtype=jnp.int32) * CMP_STRIDE
    s_start = jnp.arange(n_s, dtype=jnp.int32) * L_SEL
    overlap = ((c_start[:, None] < s_start[None, :] + L_SEL) & (s_start[None, :] < c_start[:, None] + L_CMP)).astype(f32)
    n_top = min(N_SEL, n_s)
    blk_ids = jnp.arange(n_s, dtype=jnp.int32)
    bi = jnp.arange(B)[:, None, None, None]
    gi = jnp.arange(G)[None, :, None, None]

    kwp = jnp.pad(kv_win, ((0, 0), (WINDOW, 0), (0, 0), (0, 0), (0, 0)))

    qb = min(Q_BLOCK, Tq)
    nqb = Tq // qb
    qg = q.reshape(B, nqb, qb, G, NSA_HPG, dh).transpose(1, 0, 2, 3, 4, 5)
    gg = gates.reshape(B, nqb, qb, G, NSA_HPG, 3).transpose(1, 0, 2, 3, 4, 5)
    starts = q_pos0 + jnp.arange(nqb, dtype=jnp.int32) * qb

    def block(args):
        qblk, gblk, p0 = args
        tpos = p0 + jnp.arange(qb, dtype=jnp.int32)
        s1 = jnp.einsum('bqghd,bngd->bqghn', qblk, ck).astype(f32) * scale
        vmask = (cmp_end[None, :] <= tpos[:, None])[None, :, None, None, :]
        p1 = jax.nn.softmax(jnp.where(vmask, s1, NEG_INF), axis=-1) * vmask
        o_cmp = jnp.einsum('bqghn,bngd->bqghd', p1.astype(cv.dtype), cv)
        imp = jnp.einsum('bqghn,ns->bqgs', p1, overlap)
        cur = tpos // L_SEL
        forced = (blk_ids[None, :] == 0) | (blk_ids[None, :] == cur[:, None]) | (blk_ids[None, :] == cur[:, None] - 1)
        valid_s = s_start[None, :] <= tpos[:, None]
        score = jnp.where(forced[None, :, None, :], FORCE, imp)
        score = jnp.where(valid_s[None, :, None, :], score, -FORCE)
        _, sel = lax.top_k(score, n_top)
        sel = sel.transpose(0, 2, 1, 3)
        kg = ks_blk[bi, gi, sel].reshape(B, G, qb, n_top * L_SEL, dh)
        vg = vs_blk[bi, gi, sel].reshape(B, G, qb, n_top * L_SEL, dh)
        kpos = (sel[..., None] * L_SEL + jnp.arange(L_SEL, dtype=jnp.int32)).reshape(B, G, qb, n_top * L_SEL)
        m2 = (kpos <= tpos[None, None, :, None]).transpose(0, 2, 1, 3)[:, :, :, None, :]
        s2 = jnp.einsum('bqghd,bgqkd->bqghk', qblk, kg).astype(f32) * scale
        p2 = jax.nn.softmax(jnp.where(m2, s2, NEG_INF), axis=-1)
        o_sel = jnp.einsum('bqghk,bgqkd->bqghd', p2.astype(vg.dtype), vg)
        kwb = lax.dynamic_slice_in_dim(kwp, p0 - kw_start, WINDOW + qb, axis=1)
        kpos_w = p0 - WINDOW + jnp.arange(WINDOW + qb, dtype=jnp.int32)
        m3 = (kpos_w[None, :] <= tpos[:, None]) & (kpos_w[None, :] > tpos[:, None] - WINDOW) & (kpos_w[None, :] >= 0)
        s3 = jnp.einsum('bqghd,bkgd->bqghk', qblk, kwb[:, :, 0]).astype(f32) * scale
        p3 = jax.nn.softmax(jnp.where(m3[None, :, None, None, :], s3, NEG_INF), axis=-1)
        o_win = jnp.einsum('bqghk,bkgd->bqghd', p3.astype(kwb.dtype), kwb[:, :, 1])
        o = gblk[..., 0:1] * o_cmp + gblk[..., 1:2] * o_sel + gblk[..., 2:3] * o_win
        return o.astype(qblk.dtype)

    out = lax.map(block, (qg, gg, starts))
    return out.transpose(1, 0, 2, 3, 4, 5).reshape(B, Tq, H * dh)


def _even_layer(x, pos0, s_ret, kv_past, win_past, norm_g, w_in, w_out, gn_gain, q_norm, k_norm, cmp_pos, cmp_w):
    B, L, _ = x.shape
    pos = pos0 + jnp.arange(L, dtype=jnp.int32)
    h = _rmsnorm(x, norm_g)
    proj = jnp.einsum('bld,de->ble', h, w_in)
    qa, ka, va, za, qn, kvb, gl, zb = _split(proj, EVEN_SPLITS)
    qa = _rope(qa.reshape(B, L, RET_HEADS, RET_DK), pos)
    ka = _rope(ka.reshape(B, L, RET_HEADS, RET_DK), pos) * (RET_DK ** -0.5)
    va = va.reshape(B, L, RET_HEADS, RET_DV)
    oa, s_ret_new = _retention(qa, ka, va, s_ret)
    oa = _head_groupnorm(oa, gn_gain).astype(x.dtype) * jax.nn.silu(za)
    qn = _rope(_rmsnorm(qn.reshape(B, L, NSA_HEADS, NSA_DH), q_norm), pos)
    kvb = kvb.reshape(B, L, 6, NSA_KV_HEADS, NSA_DH)
    k_c = _rope(_rmsnorm(kvb[:, :, 0], k_norm[0]), pos)
    k_s = _rope(_rmsnorm(kvb[:, :, 2], k_norm[1]), pos)
    k_w = _rope(_rmsnorm(kvb[:, :, 4], k_norm[2]), pos)
    full_new = jnp.stack([k_c, kvb[:, :, 1], k_s, kvb[:, :, 3]], axis=2)
    win_new = jnp.stack([k_w, kvb[:, :, 5]], axis=2)
    kv_full = full_new if kv_past is None else jnp.concatenate([kv_past.astype(full_new.dtype), full_new], axis=1)
    kv_win = win_new if win_past is None else jnp.concatenate([win_past.astype(win_new.dtype), win_new], axis=1)
    gates = jax.nn.sigmoid(gl.reshape(B, L, NSA_HEADS, 3))
    ob = _nsa(qn, kv_full, kv_win, gates, cmp_pos, cmp_w) * jax.nn.silu(zb)
    y = x + jnp.einsum('ble,ed->bld', jnp.concatenate([oa, ob], axis=-1), w_out)
    keep = min(WINDOW, kv_win.shape[1])
    return y, s_ret_new, full_new, kv_win[:, kv_win.shape[1] - keep:]


def _complex_affine_combine(e1, e2):
    a1r, a1i, b1r, b1i = e1
    a2r, a2i, b2r, b2i = e2
    return (a1r * a2r - a1i * a2i, a1r * a2i + a1i * a2r,
            a2r * b1r - a2i * b1i + b2r, a2r * b1i + a2i * b1r + b2i)


def _s5(u, x0_re, x0_im, lam_re, lam_im, b_re, b_im, c_re, c_im, d, log_step):
    B, L, E = u.shape
    ug = u.reshape(B, L, S5_GROUPS, S5_GROUP)
    dt = jnp.exp(log_step)[:, None]
    mag = jnp.exp(lam_re * dt)
    ang = lam_im * dt
    ab_re, ab_im = mag * jnp.cos(ang), mag * jnp.sin(ang)
    den = lam_re * lam_re + lam_im * lam_im
    nr = ab_re - 1.0
    f_re = (nr * lam_re + ab_im * lam_im) / den
    f_im = (ab_im * lam_re - nr * lam_im) / den
    bb_re = f_re[..., None] * b_re - f_im[..., None] * b_im
    bb_im = f_re[..., None] * b_im + f_im[..., None] * b_re
    bu_re = jnp.einsum('blgc,gpc->blgp', ug, bb_re)
    bu_im = jnp.einsum('blgc,gpc->blgp', ug, bb_im)
    bu_re = bu_re.at[:, 0].add(ab_re * x0_re - ab_im * x0_im)
    bu_im = bu_im.at[:, 0].add(ab_re * x0_im + ab_im * x0_re)
    a_re = jnp.broadcast_to(ab_re, bu_re.shape)
    a_im = jnp.broadcast_to(ab_im, bu_im.shape)
    _, _, xr, xi = lax.associative_scan(_complex_affine_combine, (a_re, a_im, bu_re, bu_im), axis=1)
    y = jnp.einsum('blgp,gcp->blgc', xr, c_re) - jnp.einsum('blgp,gcp->blgc', xi, c_im)
    y = y.reshape(B, L, E) + d * u
    return y, xr[:, -1], xi[:, -1]


def _odd_layer(x, s_re, s_im, norm_g, w_in, lam_re, lam_im, b_re, b_im, c_re, c_im, d, log_step, glu_w1, glu_w2, w_out):
    f32 = jnp.float32
    h = _rmsnorm(x, norm_g)
    u, z = _split(jnp.einsum('bld,de->ble', h, w_in), (S5_WIDTH, S5_WIDTH))
    y, f_re, f_im = _s5(u.astype(f32), s_re.astype(f32), s_im.astype(f32), lam_re.astype(f32), lam_im.astype(f32),
                        b_re.astype(f32), b_im.astype(f32), c_re.astype(f32), c_im.astype(f32), d.astype(f32), log_step.astype(f32))
    y = jax.nn.gelu(y).astype(x.dtype)
    y = jnp.einsum('ble,ef->blf', y, glu_w1) * jax.nn.sigmoid(jnp.einsum('ble,ef->blf', y, glu_w2))
    y = x + jnp.einsum('ble,ed->bld', y * jax.nn.silu(z), w_out)
    return y, f_re, f_im


def setup_inputs(seed: int = 0) -> dict:
    key = jax.random.key(seed)
    ks = jax.random.split(key, 32)
    f32 = jnp.float32
    n_pages = PAST_LEN // PAGE_SIZE
    n_used = DEC_BATCH * n_pages
    n_phys = n_used + n_used // 4
    w_buf = min(WINDOW, PAST_LEN)
    nrm = lambda k, shape, s: jax.random.normal(k, shape, f32) * s
    gain = lambda k, shape: 1.0 + 0.02 * jax.random.normal(k, shape, f32)
    page_table = jax.random.permutation(ks[0], n_phys)[:n_used].reshape(DEC_BATCH, n_pages).astype(jnp.int32)
    lam_re = -0.5 + 0.01 * jax.random.normal(ks[1], (N_ODD, S5_GROUPS, S5_STATE), f32)
    lam_im = jnp.broadcast_to(math.pi * jnp.arange(S5_STATE, dtype=f32), (N_ODD, S5_GROUPS, S5_STATE))
    log_step = jax.random.uniform(ks[2], (N_ODD, S5_GROUPS), f32, math.log(1e-3), math.log(1e-1))
    return {
        'x_prompt': nrm(ks[3], (BATCH, SEQ, D_MODEL), 1.0),
        'x_sample': nrm(ks[4], (DEC_BATCH, DEC_SEQ, D_MODEL), 1.0),
        'cache_nsa_kv': nrm(ks[5], (N_EVEN, n_phys, PAGE_SIZE, N_FULL_KV, NSA_KV_HEADS, NSA_DH), 1.0),
        'cache_nsa_win': nrm(ks[6], (N_EVEN, DEC_BATCH, w_buf, N_WIN_KV, NSA_KV_HEADS, NSA_DH), 1.0),
        'state_ret': nrm(ks[7], (N_EVEN, DEC_BATCH, RET_HEADS, RET_DK, RET_DV), 0.5),
        'state_ssm_re': nrm(ks[8], (N_ODD, DEC_BATCH, S5_GROUPS, S5_STATE), 0.5),
        'state_ssm_im': nrm(ks[9], (N_ODD, DEC_BATCH, S5_GROUPS, S5_STATE), 0.5),
        'page_table': page_table,
        'norm_even': gain(ks[10], (N_EVEN, D_MODEL)),
        'w_in_even': nrm(ks[11], (N_EVEN, D_MODEL, EVEN_IN), D_MODEL ** -0.5),
        'w_out_even': nrm(ks[12], (N_EVEN, MIX_EVEN, D_MODEL), MIX_EVEN ** -0.5),
        'ret_gn_gain': gain(ks[13], (N_EVEN, A_WIDTH)),
        'nsa_q_norm': gain(ks[14], (N_EVEN, NSA_DH)),
        'nsa_k_norm': gain(ks[15], (N_EVEN, 3, NSA_DH)),
        'nsa_cmp_pos': nrm(ks[16], (N_EVEN, 2, L_CMP, NSA_DH), 0.02),
        'nsa_cmp_w': nrm(ks[17], (N_EVEN, 2, L_CMP, NSA_DH, NSA_DH), (L_CMP * NSA_DH) ** -0.5),
        'norm_odd': gain(ks[18], (N_ODD, D_MODEL)),
        'w_in_odd': nrm(ks[19], (N_ODD, D_MODEL, 2 * S5_WIDTH), D_MODEL ** -0.5),
        'ssm_lambda_re': lam_re,
        'ssm_lambda_im': lam_im,
        'ssm_b_re': nrm(ks[20], (N_ODD, S5_GROUPS, S5_STATE, S5_GROUP), (2 * S5_GROUP) ** -0.5),
        'ssm_b_im': nrm(ks[21], (N_ODD, S5_GROUPS, S5_STATE, S5_GROUP), (2 * S5_GROUP) ** -0.5),
        'ssm_c_re': nrm(ks[22], (N_ODD, S5_GROUPS, S5_GROUP, S5_STATE), S5_STATE ** -0.5),
        'ssm_c_im': nrm(ks[23], (N_ODD, S5_GROUPS, S5_GROUP, S5_STATE), S5_STATE ** -0.5),
        'ssm_d': nrm(ks[24], (N_ODD, S5_WIDTH), 1.0),
        'ssm_log_step': log_step,
        'glu_w1': nrm(ks[25], (N_ODD, S5_WIDTH, S5_WIDTH), S5_WIDTH ** -0.5),
        'glu_w2': nrm(ks[26], (N_ODD, S5_WIDTH, S5_WIDTH), S5_WIDTH ** -0.5),
        'w_out_odd': nrm(ks[27], (N_ODD, S5_WIDTH, D_MODEL), S5_WIDTH ** -0.5),
    }


def reference(x_prompt, x_sample, cache_nsa_kv, cache_nsa_win, state_ret, state_ssm_re, state_ssm_im, page_table,
              norm_even, w_in_even, w_out_even, ret_gn_gain, nsa_q_norm, nsa_k_norm, nsa_cmp_pos, nsa_cmp_w,
              norm_odd, w_in_odd, ssm_lambda_re, ssm_lambda_im, ssm_b_re, ssm_b_im, ssm_c_re, ssm_c_im, ssm_d,
              ssm_log_step, glu_w1, glu_w2, w_out_odd):
    bp = x_prompt.shape[0]
    db = x_sample.shape[0]
    n_pages = page_table.shape[1]
    past_len = n_pages * PAGE_SIZE
    yp, ys = x_prompt, x_sample
    ret_p, ret_s, kv_p, kv_s, win_p, win_s = [], [], [], [], [], []
    sre_p, sim_p, sre_s, sim_s = [], [], [], []
    for layer in range(DEPTH):
        li = layer // 2
        if layer % 2 == 0:
            ew = (norm_even[li], w_in_even[li], w_out_even[li], ret_gn_gain[li], nsa_q_norm[li], nsa_k_norm[li],
                  nsa_cmp_pos[li], nsa_cmp_w[li])
            s0 = jnp.zeros((bp, RET_HEADS, RET_DK, RET_DV), jnp.float32)
            yp, sr, kvr, wr = _even_layer(yp, 0, s0, None, None, *ew)
            ret_p.append(sr); kv_p.append(kvr); win_p.append(wr)
            past = cache_nsa_kv[li][page_table].reshape(db, past_len, N_FULL_KV, NSA_KV_HEADS, NSA_DH)
            ys, sr2, kvr2, wr2 = _even_layer(ys, past_len, state_ret[li], past, cache_nsa_win[li], *ew)
            ret_s.append(sr2); kv_s.append(kvr2); win_s.append(wr2)
        else:
            ow = (norm_odd[li], w_in_odd[li], ssm_lambda_re[li], ssm_lambda_im[li], ssm_b_re[li], ssm_b_im[li],
                  ssm_c_re[li], ssm_c_im[li], ssm_d[li], ssm_log_step[li], glu_w1[li], glu_w2[li], w_out_odd[li])
            z0 = jnp.zeros((bp, S5_GROUPS, S5_STATE), jnp.float32)
            yp, fr, fi = _odd_layer(yp, z0, z0, *ow)
            sre_p.append(fr); sim_p.append(fi)
            ys, fr2, fi2 = _odd_layer(ys, state_ssm_re[li], state_ssm_im[li], *ow)
            sre_s.append(fr2); sim_s.append(fi2)
    return (yp, ys, jnp.stack(ret_p), jnp.stack(ret_s), jnp.stack(kv_p), jnp.stack(kv_s), jnp.stack(win_p),
            jnp.stack(win_s), jnp.stack(sre_p), jnp.stack(sim_p), jnp.stack(sre_s), jnp.stack(sim_s))
```

```python
import numpy as np
import concourse.bass as bass
import concourse.mybir as mybir
from concourse.bass_utils import run_bass_kernel_spmd

F32 = mybir.dt.float32
BF16 = mybir.dt.bfloat16
I32 = mybir.dt.int32
AF = mybir.ActivationFunctionType
ALU = mybir.AluOpType
AX = mybir.AxisListType

ENGS = ("pe", "act", "dve", "pool", "sp")
NDMASEM = 24


class Prog:
    def __init__(self, nc):
        self.nc = nc
        self.ops = {e: [] for e in ENGS}
        self.cnt = {e: 0 for e in ENGS}
        self.sem = {}
        self.seen = {e: {} for e in ENGS}
        self.last_w = {}
        self.readers = {}
        self.dma_i = 0
        self.dma_uses = [0] * NDMASEM
        self.dma_tok = [None] * NDMASEM
        self.stack = None
        self.n_ops = 0

    def setup(self, stack):
        self.stack = stack
        for e in ENGS:
            self.sem[e] = stack.enter_context(self.nc.semaphore("s_" + e))
        for i in range(NDMASEM):
            self.sem["d%d" % i] = stack.enter_context(self.nc.semaphore("s_d%d" % i))

    def _key(self, a):
        if isinstance(a, str):
            return a
        if isinstance(a, tuple):
            return a
        t = getattr(a, 'tensor', None)
        return t.name if t is not None else a.name

    def _deps(self, eng, reads, writes):
        toks = []
        for k in reads:
            k = self._key(k)
            t = self.last_w.get(k)
            if t is not None:
                toks.append(t)
        for k in writes:
            k = self._key(k)
            t = self.last_w.get(k)
            if t is not None:
                toks.append(t)
            toks.extend(self.readers.get(k, ()))
        need = {}
        for (s, v) in toks:
            if eng == "pe" and s == "pe":
                continue
            if v > need.get(s, 0):
                need[s] = v
        waits = []
        seen = self.seen[eng]
        for s, v in need.items():
            if seen.get(s, 0) >= v:
                continue
            seen[s] = v
            waits.append((s, v))
        return waits

    def _commit(self, tok, reads, writes):
        for k in writes:
            k = self._key(k)
            self.last_w[k] = tok
            self.readers[k] = []
        for k in reads:
            k = self._key(k)
            self.readers.setdefault(k, []).append(tok)

    def op(self, eng, fn, reads=(), writes=()):
        waits = self._deps(eng, reads, writes)
        self.cnt[eng] += 1
        tok = (eng, self.cnt[eng])
        self.ops[eng].append((waits, fn, (eng, 1)))
        self._commit(tok, reads, writes)
        self.n_ops += 1

    def dma(self, out, in_, reads=None, writes=None, q="sp", fn=None):
        if reads is None:
            reads = [in_]
        if writes is None:
            writes = [out]
        i = self.dma_i % NDMASEM
        self.dma_i += 1
        sname = "d%d" % i
        waits = self._deps(q, reads, writes)
        prev = self.dma_tok[i]
        if prev is not None and self.seen[q].get(sname, 0) < prev[1]:
            self.seen[q][sname] = prev[1]
            waits.append(prev)
        self.dma_uses[i] += 1
        tok = (sname, 16 * self.dma_uses[i])
        self.dma_tok[i] = tok
        if fn is None:
            fn = lambda e, o=out, a=in_: e.dma_start(out=o, in_=a)
        self.ops[q].append((waits, fn, (sname, 16)))
        self._commit(tok, reads, writes)
        self.n_ops += 1

    def barrier(self):
        for e in ENGS:
            waits = []
            for e2 in ENGS:
                if e2 == e:
                    continue
                v = self.cnt[e2]
                if v > self.seen[e].get(e2, 0):
                    self.seen[e][e2] = v
                    waits.append((e2, v))
            for i in range(NDMASEM):
                t = self.dma_tok[i]
                if t is not None and self.seen[e].get(t[0], 0) < t[1]:
                    self.seen[e][t[0]] = t[1]
                    waits.append(t)
            if waits:
                self.ops[e].append((waits, None, None))

    def flush(self):
        nc = self.nc
        ops = self.ops
        sem = self.sem

        def replay(engh, lst):
            for (waits, fn, inc) in lst:
                for (s, v) in waits:
                    engh.wait_ge(sem[s], v)
                if fn is not None:
                    ins = fn(engh)
                    ins.then_inc(sem[inc[0]], inc[1])

        with nc.Block() as block:
            @block.tensor
            def _(e):
                replay(e, ops["pe"])

            @block.scalar
            def _(e):
                replay(e, ops["act"])

            @block.vector
            def _(e):
                replay(e, ops["dve"])

            @block.gpsimd
            def _(e):
                replay(e, ops["pool"])

            @block.sync
            def _(e):
                replay(e, ops["sp"])
        self.ops = {e: [] for e in ENGS}

    def mm(self, out, lhsT, rhs, start=True, stop=True, reads=None, writes=None):
        if reads is None:
            reads = [lhsT, rhs]
        if writes is None:
            writes = [out]
        self.op("pe", lambda e: e.matmul(out, lhsT, rhs, start=start, stop=stop), reads, writes)

    def tr(self, out, in_, ident, reads=None, writes=None):
        if reads is None:
            reads = [in_, ident]
        if writes is None:
            writes = [out]
        self.op("pe", lambda e: e.transpose(out, in_, ident), reads, writes)

    def actv(self, out, in_, func, bias=None, scale=None, accum_out=None, reads=None, writes=None, eng="act"):
        kw = {}
        if bias is not None:
            kw["bias"] = bias
        if scale is not None:
            kw["scale"] = scale
        if accum_out is not None:
            kw["accum_out"] = accum_out
        if reads is None:
            reads = [in_]
            if bias is not None and not isinstance(bias, (int, float)):
                reads.append(bias)
            if scale is not None and not isinstance(scale, (int, float)):
                reads.append(scale)
        if writes is None:
            writes = [out]
            if accum_out is not None:
                writes.append(accum_out)
        self.op("act", lambda e: e.activation(out, in_, func, **kw), reads, writes)

    def ts(self, eng, out, in0, s1, s2, op0, op1=None, accum_out=None, reads=None, writes=None):
        kw = {}
        if op1 is not None:
            kw["op1"] = op1
        if accum_out is not None:
            kw["accum_out"] = accum_out
        if reads is None:
            reads = [in0]
            for s in (s1, s2):
                if s is not None and not isinstance(s, (int, float)):
                    reads.append(s)
        if writes is None:
            writes = [out]
            if accum_out is not None:
                writes.append(accum_out)
        self.op(eng, lambda e: e.tensor_scalar(out, in0, s1, s2, op0, **kw), reads, writes)

    def tt(self, eng, out, in0, in1, op, reads=None, writes=None):
        if reads is None:
            reads = [in0, in1]
        if writes is None:
            writes = [out]
        self.op(eng, lambda e: e.tensor_tensor(out, in0, in1, op), reads, writes)

    def stt(self, eng, out, in0, scalar, in1, op0, op1, accum_out=None, reads=None, writes=None):
        kw = {}
        if accum_out is not None:
            kw["accum_out"] = accum_out
        if reads is None:
            reads = [in0, in1]
            if not isinstance(scalar, (int, float)):
                reads.append(scalar)
        if writes is None:
            writes = [out]
            if accum_out is not None:
                writes.append(accum_out)
        self.op(eng, lambda e: e.scalar_tensor_tensor(out, in0, scalar, in1, op0, op1, **kw), reads, writes)

    def cp(self, eng, out, in_, reads=None, writes=None):
        if reads is None:
            reads = [in_]
        if writes is None:
            writes = [out]
        if eng == "act":
            self.op(eng, lambda e: e.copy(out, in_), reads, writes)
        else:
            self.op(eng, lambda e: e.tensor_copy(out, in_), reads, writes)

    def memset(self, eng, ap, val, writes=None):
        if writes is None:
            writes = [ap]
        self.op(eng, lambda e: e.memset(ap, val), (), writes)

import math
from contextlib import ExitStack
import ml_dtypes

BF = ml_dtypes.bfloat16
NCORES = 8
BIG = 1.0e4
FORCE = 1.0e4
EPS = 1e-6
SCALE = 0.125
QA, KA, QN, KC, KS, KW, VC, VS, VW, GL, VA, ZA, ZB = 0, 256, 512, 1024, 1152, 1280, 1408, 1536, 1664, 1792, 1816, 2328, 2840
EIN = 3352
GROUPS = [(0, 512), (512, 1024), (1024, 1408), (1408, 1816), (1816, 2328), (2328, 2840), (2840, 3352)]


def _perm_even():
    perm = np.zeros(EIN, np.int64)
    perm[QA:QA + 256] = np.arange(0, 256)
    perm[KA:KA + 256] = np.arange(256, 512)
    o_va, o_za, o_qn, o_kvb, o_gl, o_zb = 512, 1024, 1536, 2048, 2816, 2840
    for j in range(4):
        for g in range(2):
            for dd in range(64):
                perm[QN + (j * 2 + g) * 64 + dd] = o_qn + (4 * g + j) * 64 + dd
    for dst, kind in ((KC, 0), (KS, 2), (KW, 4), (VC, 1), (VS, 3), (VW, 5)):
        perm[dst:dst + 128] = o_kvb + kind * 128 + np.arange(128)
    perm[GL:GL + 24] = o_gl + np.arange(24)
    perm[VA:VA + 512] = o_va + np.arange(512)
    perm[ZA:ZA + 512] = o_za + np.arange(512)
    perm[ZB:ZB + 512] = o_zb + np.arange(512)
    assert len(set(perm.tolist())) == EIN
    return perm


def make_consts():
    c = {}
    f32 = np.float32
    c["ident"] = np.eye(128, dtype=f32).astype(BF)
    half = 32
    inv = (np.float32(10000.0) ** (-np.arange(half, dtype=f32) / f32(half))).astype(f32)
    pos_p = (np.arange(16)[None, :] * 128 + np.arange(128)[:, None]).astype(f32)
    ang = (pos_p[:, :, None] * inv[None, None, :]).astype(f32)
    c["cos_p"] = np.cos(ang).astype(f32)
    c["sin_p"] = np.sin(ang).astype(f32)
    pos_s = (2048 + (np.arange(128) % 8)).astype(f32)
    ang = (pos_s[:, None] * inv[None, :]).astype(f32)
    c["cos_s"] = np.cos(ang).astype(f32)
    c["sin_s"] = np.sin(ang).astype(f32)
    log_g = np.log(1.0 - 2.0 ** (-5.0 - np.arange(4, dtype=np.float64)))
    i = np.arange(128)
    diff = i[None, :] - i[:, None]
    caus = diff >= 0
    DT = np.zeros((128, 4, 128), f32)
    for h in range(4):
        DT[:, h, :] = 0.125 * np.exp(np.where(caus, diff, 0) * log_g[h]) * caus
    c["DTp"] = DT
    c["qdec_p"] = np.exp((i[:, None] + 1.0) * log_g[None, :]).astype(f32)
    c["kdec_p"] = (0.125 * np.exp((127.0 - i[:, None]) * log_g[None, :])).astype(f32)
    cd = np.zeros((128, 2), f32)
    cds = np.zeros((128, 2), f32)
    for p in range(128):
        for pr in range(2):
            h = 2 * pr + p // 64
            cd[p, pr] = np.exp(128.0 * log_g[h])
            cds[p, pr] = np.exp(8.0 * log_g[h])
    c["cdec_p"] = cd
    c["cdec_s"] = cds
    i8 = i % 8
    same = (i[:, None] // 8) == (i[None, :] // 8)
    diff8 = i8[None, :] - i8[:, None]
    caus8 = same & (diff8 >= 0)
    DTs = np.zeros((128, 4, 128), f32)
    for h in range(4):
        DTs[:, h, :] = 0.125 * np.exp(np.where(caus8, diff8, 0) * log_g[h]) * caus8
    c["DTs"] = DTs
    c["qdec_s"] = np.exp((i8[:, None] + 1.0) * log_g[None, :]).astype(f32)
    c["kdec_s"] = (0.125 * np.exp((7.0 - i8[:, None]) * log_g[None, :])).astype(f32)
    c["blkmask"] = ((i[:, None] // 8) == np.arange(16)[None, :]).astype(f32)
    cm = np.zeros((128, 16, 128), f32)
    cm[:, :, :] = ((i[None, :] // 8) == np.arange(16)[:, None])[None, :, :]
    c["colmask"] = cm.astype(BF)
    keys = np.arange(2176)
    E = (keys[None, :] // 64 == np.arange(33)[:, None]).astype(f32)
    c["E"] = E.astype(BF)
    c["CB"] = np.where(i[:, None] <= i[None, :], 0.0, -BIG).astype(f32).astype(BF)
    c["AB"] = np.where(i[:, None] > i[None, :], 0.0, -BIG).astype(f32).astype(BF)
    n = np.arange(128)
    cmpb = np.zeros((128, 16, 128), f32)
    for t in range(16):
        tpos = 128 * t + i
        cmpb[:, t, :] = np.where((16 * n[:, None] + 31) <= tpos[None, :], 0.0, -BIG)
    c["CMPB"] = cmpb.astype(BF)

    def overlap(n_s):
        c_start = np.arange(127) * 16
        s_start = np.arange(n_s) * 64
        return ((c_start[:, None] < s_start[None, :] + 64) & (s_start[None, :] < c_start[:, None] + 32)).astype(f32)
    c["ovl_p"] = overlap(32)
    c["ovl_s"] = overlap(33)
    mulc = np.zeros((128, 16, 32), f32)
    addc = np.zeros((128, 16, 32), f32)
    s_ids = np.arange(32)
    for t in range(16):
        tpos = 128 * t + i
        cur = tpos // 64
        forced = (s_ids[None, :] == 0) | (s_ids[None, :] == cur[:, None]) | (s_ids[None, :] == cur[:, None] - 1)
        valid = (s_ids[None, :] * 64) <= tpos[:, None]
        mulc[:, t, :] = (valid & ~forced)
        addc[:, t, :] = np.where(valid, np.where(forced, FORCE, 0.0), -FORCE)
    c["mulc_p"] = mulc
    c["addc_p"] = addc
    s33 = np.arange(33)
    forced = (s33 == 0) | (s33 == 32) | (s33 == 31)
    c["mulc_s"] = np.tile((~forced).astype(f32)[None, :], (8, 1))
    c["addc_s"] = np.tile(np.where(forced, FORCE, 0.0).astype(f32)[None, :], (8, 1))
    q8 = np.arange(8)
    SB = np.where(((i[:, None, None] // 8) == np.arange(16)[None, :, None]) & ((i[:, None, None] % 8) <= q8[None, None, :]), 0.0, -BIG)
    c["SB"] = SB.astype(f32).astype(BF)
    c["ABs"] = np.where(i[:, None] > q8[None, :], 0.0, -BIG).astype(f32).astype(BF)
    c["iota_p"] = i.astype(f32)[:, None].copy()
    c["ones_row"] = np.ones((1, 128), f32).astype(BF)
    c["zeros_row"] = np.zeros((1, 512), f32).astype(BF)
    jv = np.zeros((128, 9, 32), f32)
    jv[:, :, :] = np.arange(9, dtype=f32)[None, :, None]
    c["JV"] = jv
    c["KI"] = np.tile(np.arange(256, dtype=f32)[None, :], (128, 1))
    c["PM4"] = (np.arange(128)[:, None] // 32 == np.arange(4)[None, :]).astype(f32)
    c["BM16"] = (np.arange(128)[:, None] // 16 == np.arange(128)[None, :] // 16).astype(f32)
    return c


def _dt_of(a):
    if a.dtype == np.float32:
        return F32
    if a.dtype == np.int32:
        return I32
    if a.dtype == BF:
        return BF16
    raise ValueError(a.dtype)


def v3(ap, h):
    return ap.rearrange("p (h d) -> p h d", h=h)


def bc_mid(ap, n):
    return ap.unsqueeze(1).to_broadcast([ap.shape[0], n, ap.shape[1]])


def bc_last(ap, n):
    return ap.unsqueeze(2).to_broadcast([ap.shape[0], ap.shape[1], n])


def build_program(consts, phase_b=True, do_sample=True, n_ptiles=16, stage=99):
    nc = bass.Bass("TRN2", target_bir_lowering=False)

    in_names = []

    def din(name, shape, dt=F32):
        in_names.append(name)
        return nc.dram_tensor(name, list(shape), dt, kind="ExternalInput").ap()

    def dout(name, shape, dt=F32):
        return nc.dram_tensor(name, list(shape), dt, kind="ExternalOutput").ap()

    xp = din("xp", [2048, 1024])
    xs = din("xs", [128, 1024])
    if do_sample:
        cache = din("cache", [2560 * 128, 512])
        cwin = din("cwin", [16, 512, 256])
        sret = din("sret", [16, 4, 64, 128])
        ptab = din("ptab", [1, 256], I32)
    w_in_e = din("w_in_e", [1024, EIN])
    w_out_e = din("w_out_e", [1024, 1024])
    norm_e = din("norm_e", [128, 8])
    gn_gain = din("gn_gain", [1, 512])
    qk_gain = din("qk_gain", [1, 896])
    cmp_posT = din("cmp_posT", [64, 2, 32])
    cmp_w = din("cmp_w", [2, 32, 64, 64])
    S_ONLY = ("cos_s", "sin_s", "DTs", "qdec_s", "kdec_s", "cdec_s", "blkmask", "colmask", "SB", "ABs", "mulc_s", "addc_s", "ovl_s", "iota_p")
    if phase_b:
        w_in_o = din("w_in_o", [1024, 2048])
        glu1 = din("glu1", [1024, 1024])
        glu2 = din("glu2", [1024, 1024])
        w_out_o = din("w_out_o", [1024, 1024])
        norm_o = din("norm_o", [128, 8])
        ssmd = din("ssmd", [128, 8])
        lamre_A = din("lamre_A", [128, 32])
        lamim_A = din("lamim_A", [128, 32])
        lstep_A = din("lstep_A", [128, 32])
        bA_re = din("bA_re", [128, 512])
        bA_im = din("bA_im", [128, 512])
        cA_re = din("cA_re", [128, 512])
        cA_im = din("cA_im", [128, 512])
        x0A_re = din("x0A_re", [128, 512])
        x0A_im = din("x0A_im", [128, 512])
    cd = {k: din("c_" + k, v.shape, _dt_of(v)) for k, v in consts.items() if ((do_sample or k not in S_ONLY) and (phase_b or k not in ("JV", "KI", "PM4", "BM16")))}
    yp = dout("yp", [2048, 1024])
    ys = dout("ys", [128, 1024])
    o_retp = dout("o_retp", [4, 64, 128])
    o_rets = dout("o_rets", [16, 4, 64, 128])
    o_kvp = dout("o_kvp", [2048, 512])
    o_kvs = dout("o_kvs", [128, 512])
    o_winp = dout("o_winp", [512, 256])
    o_wins = dout("o_wins", [16, 512, 256])
    if phase_b:
        o_ssp = dout("o_ssp", [2, 128, 32])
        o_sss = dout("o_sss", [2, 128, 512])

    P = Prog(nc)
    with ExitStack() as st0:
        P.setup(st0)

        def alloc(st, name, shape, dt=F32):
            return st.enter_context(nc.sbuf_tensor(name, list(shape), dt))

        def palloc(st, name, shape, dt=F32):
            return st.enter_context(nc.psum_tensor(name, list(shape), dt))

        psT = palloc(st0, "psT", [128, 1024], BF16)
        psA = [palloc(st0, "psA%d" % i, [128, 512]) for i in range(2)]
        psS = [palloc(st0, "psS%d" % i, [128, 512]) for i in range(2)]
        psV = palloc(st0, "psV", [128, 512])
        psC = palloc(st0, "psC", [128, 512])
        psR = palloc(st0, "psR", [128, 512])

        with ExitStack() as stA:
            A = lambda name, shape, dt=F32: alloc(stA, name, shape, dt)
            cs = {}
            P_ONLY = ("cos_p", "sin_p", "DTp", "qdec_p", "kdec_p", "cdec_p", "CMPB", "mulc_p", "addc_p", "ovl_p", "CB", "AB")

            def load_consts(stx, names, pre="k_"):
                for k in names:
                    v = consts[k]
                    shp = list(v.shape)
                    if len(shp) == 3:
                        tl = alloc(stx, pre + k, [shp[0], shp[1] * shp[2]], _dt_of(v))
                        P.dma(tl[:], cd[k].rearrange("p a b -> p (a b)"))
                    else:
                        tl = alloc(stx, pre + k, shp, _dt_of(v))
                        P.dma(tl[:], cd[k])
                    cs[k] = tl
            B_ONLY = ("JV", "KI", "PM4", "BM16")
            load_consts(stA, [k for k in consts if k not in P_ONLY and k not in S_ONLY and k not in B_ONLY])
            ident = cs["ident"]
            Wob = A("Wob", [128, 8 * 1024], BF16)
            ng = A("ng", [128, 8])
            BDW = A("BDW", [128, 2 * 32 * 128], BF16)
            peb = A("peb", [128, 64], BF16)
            posk = A("posk", [128, 1])
            posv = A("posv", [1, 128], BF16)
            gnb = A("gnb", [128, 512])
            P.dma(gnb[:], gn_gain.partition_broadcast(128))
            qkg = A("qkg", [128, 896])
            P.dma(qkg[:], qk_gain.partition_broadcast(128))

            xt = [A("xt%d" % i, [128, 1024]) for i in range(2)]
            xb = A("xb", [128, 1024], BF16)
            xT = A("xT", [128, 1024], BF16)
            stt_ = A("stats", [128, 64])
            proj = A("proj", [128, EIN])
            rp = A("rp", [128, 1408])
            tmpa = A("tmpa", [128, 1408])
            r16 = A("r16", [128, 1024], BF16)
            vb16 = A("vb16", [128, 512], BF16)
            rT = A("rT", [128, 256], BF16)
            qz = A("qz", [128, 1024], BF16)
            qTz = A("qTz", [128, 1024], BF16)
            scm = A("scm", [128, 512], BF16)
            S32 = A("S32", [128, 256])
            Sb = A("Sb", [128, 256], BF16)
            osb = A("osb", [128, 512])
            ocb = A("ocb", [128, 512])
            sz = A("sz", [128, 512])
            mix = A("mix", [128, 1024], BF16)
            mixT = A("mixT", [128, 1024], BF16)
            n16 = A("n16", [128, 1024], BF16)
            gts = A("gts", [128, 24])
            on = A("on", [128, 3 * 512])
            tmpb = on
            obt = A("obt", [128, 512])
            pt_ = [A("pt%d" % i, [128, 512], BF16) for i in range(3)]
            selT = A("selT", [33, 128], BF16)
            sc_ = A("sc", [128, 64])
            sc2 = A("sc2", [128, 64])
            sc16 = A("sc16", [128, 64], BF16)
            rden = A("rden", [128, 16])
            top8 = A("top8", [128, 8])
            P.memset("dve", S32[:], 0.0)
            P.memset("dve", Sb[:], 0.0)
            P.memset("pool", qz[:], 0.0)
            P.memset("pool", qTz[:], 0.0)

            stWb = ExitStack()
            Wb = alloc(stWb, "Wb", [128, 8 * EIN], BF16)
            P.dma(ng[:], norm_e)
            with ExitStack() as stW:
                stg = [alloc(stW, "stg%d" % i, [128, EIN]) for i in range(2)]
                for kc in range(8):
                    s_ = stg[kc % 2]
                    P.dma(s_[:], w_in_e[kc * 128:(kc + 1) * 128, :])
                    if kc % 2 == 0:
                        P.ts("dve", Wb[:, kc * EIN:(kc + 1) * EIN], s_[:], ng[:, kc:kc + 1], None, ALU.mult)
                    else:
                        P.op("act", lambda e, kc=kc, s_=s_: e.mul(Wb[:, kc * EIN:(kc + 1) * EIN], s_[:], ng[:, kc:kc + 1]), [s_, ng], [Wb])
                for kc in range(8):
                    s_ = stg[kc % 2]
                    P.dma(s_[:, 0:1024], w_out_e[kc * 128:(kc + 1) * 128, :])
                    if kc % 2 == 0:
                        P.cp("dve", Wob[:, kc * 1024:(kc + 1) * 1024], s_[:, 0:1024])
                    else:
                        P.cp("pool", Wob[:, kc * 1024:(kc + 1) * 1024], s_[:, 0:1024])
                P.barrier()
            with ExitStack() as stW:
                P.memset("pool", BDW[:], 0.0)
                wst = alloc(stW, "wst", [128, 2 * 32 * 64])
                srcw = cmp_w.rearrange("c l d e -> d (c l) e")
                P.dma(wst[0:64, :].rearrange("p (a e) -> p a e", e=64), srcw)
                P.dma(wst[64:128, :].rearrange("p (a e) -> p a e", e=64), srcw)
                bdv = BDW[:].rearrange("p (a e) -> p a e", e=128)
                P.cp("dve", bdv[0:64, :, 0:64], wst[0:64, :].rearrange("p (a e) -> p a e", e=64))
                P.cp("dve", bdv[64:128, :, 64:128], wst[64:128, :].rearrange("p (a e) -> p a e", e=64))
                pe32 = alloc(stW, "pe32", [128, 64])
                P.dma(pe32[0:64, :], cmp_posT.rearrange("d c l -> d (c l)"))
                P.dma(pe32[64:128, :], cmp_posT.rearrange("d c l -> d (c l)"))
                P.cp("dve", peb[:], pe32[:])
                for l in range(32):
                    P.mm(psC[:, 0:1], BDW[:, (0 * 32 + l) * 128:(0 * 32 + l + 1) * 128], peb[:, l:l + 1], start=(l == 0), stop=(l == 31))
                P.cp("dve", posk[:], psC[:, 0:1])
                for l in range(32):
                    P.mm(psV[0:1, 0:128], peb[:, 32 + l:32 + l + 1], BDW[:, (32 + l) * 128:(32 + l + 1) * 128], start=(l == 0), stop=(l == 31))
                P.cp("dve", posv[:], psV[0:1, 0:128])
                P.barrier()
            cnt = {"x": 0, "pt": 0, "ps": 0, "pa": 0}

            def rsqrt_small(out, in_, mult, add):
                P.ts("dve", out, in_, mult, add, ALU.mult, ALU.add)
                P.actv(out, out, AF.Sqrt)
                P.op("dve", lambda e: e.reciprocal(out, out), [out], [out])

            def nsa_group(nq, g, qTg, cmp_k, cmp_v, cmp_bias, sel_chunks, win_chunks, mulc, addc, ns, on_dst):
                W = 4 * nq
                wv = 65 + ns
                ps_s = psS[cnt["ps"] % 2]; cnt["ps"] += 1
                P.mm(ps_s[0:127, 0:W].rearrange("p (j q) -> p j q", j=4), cmp_k, qTg, start=True, stop=(cmp_bias is None))
                if cmp_bias is not None:
                    P.mm(ps_s[0:127, 0:W].rearrange("p (j q) -> p j q", j=4), ident[0:127, 0:127], cmp_bias, start=False, stop=True)
                pt = pt_[cnt["pt"] % 3]; cnt["pt"] += 1
                P.actv(pt[0:127, 0:W], ps_s[0:127, 0:W], AF.Exp, scale=SCALE)
                for j in range(4):
                    P.mm(psC[0:nq, j * wv:(j + 1) * wv], pt[0:127, j * nq:(j + 1) * nq], cmp_v, start=True, stop=True)
                pcv = psC[0:nq, 0:4 * wv].rearrange("p (j w) -> p j w", j=4)
                P.ts("dve", rden[0:nq, 0:4], pcv[:, :, 64], 1e-30, None, ALU.add)
                P.op("dve", lambda e: e.reciprocal(rden[0:nq, 0:4], rden[0:nq, 0:4]), [rden], [rden])
                P.tt("dve", on_dst(0), pcv[:, :, 0:64], bc_last(rden[0:nq, 0:4], 64), ALU.mult)
                P.ts("dve", sc_[0:nq, 0:ns], pcv[:, 0, 65:65 + ns], rden[0:nq, 0:1], None, ALU.mult)
                for j in range(1, 4):
                    P.stt("dve", sc_[0:nq, 0:ns], pcv[:, j, 65:65 + ns], rden[0:nq, j:j + 1], sc_[0:nq, 0:ns], ALU.mult, ALU.add)
                P.tt("dve", sc_[0:nq, 0:ns], sc_[0:nq, 0:ns], mulc, ALU.mult)
                P.tt("dve", sc_[0:nq, 0:ns], sc_[0:nq, 0:ns], addc, ALU.add)
                P.op("dve", lambda e: e.max(top8[0:nq, :], sc_[0:nq, 0:ns]), [sc_], [top8])
                P.ts("dve", sc2[0:nq, 0:ns], sc_[0:nq, 0:ns], top8[0:nq, 7:8], BIG, ALU.is_ge, ALU.mult)
                P.ts("dve", sc16[0:nq, 0:ns], sc2[0:nq, 0:ns], -BIG, None, ALU.add)
                P.tr(psT[0:ns, 0:nq], sc16[0:nq, 0:ns], ident[0:nq, 0:nq])
                P.cp("dve", selT[0:ns, 0:nq], psT[0:ns, 0:nq])
                selrhs = bc_mid(selT[0:ns, 0:nq], 4)
                for br, chunks in ((1, sel_chunks), (2, win_chunks)):
                    nch = len(chunks)
                    P.mm(psV[0:nq, 0:260], cs["zeros_row"][0:1, 0:nq], cs["zeros_row"][0:1, 0:260], start=True, stop=False)
                    for ci, (kT, v1, nk, ecols, bias) in enumerate(chunks):
                        ps_s = psS[cnt["ps"] % 2]; cnt["ps"] += 1
                        extra = (1 if (br == 1 and ecols is not None) else 0) + (1 if bias is not None else 0)
                        P.mm(ps_s[0:nk, 0:W].rearrange("p (j q) -> p j q", j=4), kT, qTg, start=True, stop=(extra == 0))
                        if br == 1 and ecols is not None:
                            extra -= 1
                            P.mm(ps_s[0:nk, 0:W].rearrange("p (j q) -> p j q", j=4), ecols, selrhs, start=False, stop=(extra == 0))
                        if bias is not None:
                            extra -= 1
                            P.mm(ps_s[0:nk, 0:W].rearrange("p (j q) -> p j q", j=4), ident[0:nk, 0:nk], bias, start=False, stop=True)
                        pt = pt_[cnt["pt"] % 3]; cnt["pt"] += 1
                        P.actv(pt[0:nk, 0:W], ps_s[0:nk, 0:W], AF.Exp, scale=SCALE)
                        for j in range(4):
                            P.mm(psV[0:nq, j * 65:(j + 1) * 65], pt[0:nk, j * nq:(j + 1) * nq], v1, start=False, stop=(ci == nch - 1))
                    pvv = psV[0:nq, 0:260].rearrange("p (j w) -> p j w", j=4)
                    P.ts("dve", rden[0:nq, 4 * br:4 * br + 4], pvv[:, :, 64], 1e-30, None, ALU.add)
                    P.op("dve", lambda e, br=br: e.reciprocal(rden[0:nq, 4 * br:4 * br + 4], rden[0:nq, 4 * br:4 * br + 4]), [rden], [rden])
                    P.tt("dve", on_dst(br), pvv[:, :, 0:64], bc_last(rden[0:nq, 4 * br:4 * br + 4], 64), ALU.mult)

            def even_tile(mode, t, caches):
                isp = (mode == "p")
                xsrc = xp[t * 128:(t + 1) * 128, :] if isp else xs
                xtile = xt[cnt["x"] % 2]; cnt["x"] += 1
                P.dma(xtile[:], xsrc)
                P.memset("dve", stt_[:, 0:1], 0.0)
                P.actv(mixT[:], xtile[:], AF.Square, accum_out=stt_[:, 0:1])
                rsqrt_small(stt_[:, 1:2], stt_[:, 0:1], 1.0 / 1024, EPS)
                P.cp("pool", xb[:], xtile[:])
                for kc in range(8):
                    P.tr(psT[:, kc * 128:(kc + 1) * 128], xb[:, kc * 128:(kc + 1) * 128], ident[:])
                P.cp("act", xT[:], psT[:])
                for gi, (c0, c1) in enumerate(GROUPS):
                    pa = psA[cnt["pa"] % 2]; cnt["pa"] += 1
                    w = c1 - c0
                    for kc in range(8):
                        P.mm(pa[:, 0:w], xT[:, kc * 128:(kc + 1) * 128], Wb[:, kc * EIN + c0:kc * EIN + c1], start=(kc == 0), stop=(kc == 7))
                    if gi % 2 == 0:
                        P.ts("dve", proj[:, c0:c1], pa[:, 0:w], stt_[:, 1:2], None, ALU.mult)
                    else:
                        P.op("act", lambda e, c0=c0, c1=c1, pa=pa, w=w: e.mul(proj[:, c0:c1], pa[:, 0:w], stt_[:, 1:2]), [pa, stt_], [proj])
                if not isp:
                    caches["hook"]()
                if stage <= 1:
                    return
                nv = v3(proj[:, QN:QN + 896], 14)
                P.tt("dve", tmpa[:, 0:896], proj[:, QN:QN + 896], proj[:, QN:QN + 896], ALU.mult)
                P.op("dve", lambda e: e.reduce_sum(stt_[:, 8:22], v3(tmpa[:, 0:896], 14), AX.X), [tmpa], [stt_])
                rsqrt_small(stt_[:, 8:22], stt_[:, 8:22], 1.0 / 64, EPS)
                P.tt("dve", nv, nv, bc_last(stt_[:, 8:22], 64), ALU.mult)
                P.tt("pool", proj[:, QN:QN + 896], proj[:, QN:QN + 896], qkg[:], ALU.mult)
                if isp:
                    cos = cs["cos_p"][:, t * 32:(t + 1) * 32]
                    sin = cs["sin_p"][:, t * 32:(t + 1) * 32]
                else:
                    cos = cs["cos_s"][:, :]
                    sin = cs["sin_s"][:, :]
                pv = v3(proj[:, 0:1408], 22)
                rv = v3(rp[:, 0:1408], 22)
                ta = tmpa[:, 0:704].rearrange("p (h d) -> p h d", h=22)
                tb = tmpa[:, 704:1408].rearrange("p (h d) -> p h d", h=22)
                tc_ = tmpb[:, 0:704].rearrange("p (h d) -> p h d", h=22)
                td_ = tmpb[:, 704:1408].rearrange("p (h d) -> p h d", h=22)
                cosb = bc_mid(cos, 22)
                sinb = bc_mid(sin, 22)
                P.tt("dve", ta, pv[:, :, 0:32], cosb, ALU.mult)
                P.tt("dve", tb, pv[:, :, 32:64], sinb, ALU.mult)
                P.tt("dve", rv[:, :, 0:32], ta, tb, ALU.subtract)
                P.tt("pool", tc_, pv[:, :, 32:64], cosb, ALU.mult)
                P.tt("pool", td_, pv[:, :, 0:32], sinb, ALU.mult)
                P.tt("pool", rv[:, :, 32:64], tc_, td_, ALU.add)
                if stage <= 2:
                    return
                if isp:
                    okv = o_kvp[t * 128:(t + 1) * 128, :]
                else:
                    okv = o_kvs
                P.dma(okv[:, 0:128], rp[:, KC:KC + 128])
                P.dma(okv[:, 128:256], proj[:, VC:VC + 128])
                P.dma(okv[:, 256:384], rp[:, KS:KS + 128])
                P.dma(okv[:, 384:512], proj[:, VS:VS + 128])
                if isp and t >= 12:
                    ow = o_winp[(t - 12) * 128:(t - 11) * 128, :]
                    P.dma(ow[:, 0:128], rp[:, KW:KW + 128])
                    P.dma(ow[:, 128:256], proj[:, VW:VW + 128])
                if not isp:
                    for b in range(16):
                        P.dma(o_wins[b, 504:512, 0:128], rp[b * 8:(b + 1) * 8, KW:KW + 128])
                        P.dma(o_wins[b, 504:512, 128:256], proj[b * 8:(b + 1) * 8, VW:VW + 128])
                        P.dma(o_wins[b, 0:504, :], cwin[b, 8:512, :])
                if stage <= 3:
                    return
                qdec = cs["qdec_p"] if isp else cs["qdec_s"]
                kdec = cs["kdec_p"] if isp else cs["kdec_s"]
                DTm = cs["DTp"] if isp else cs["DTs"]
                P.cp("pool", r16[:, 0:256], rp[:, QA:QA + 256])
                P.tt("dve", v3(r16[:, 256:512], 4), v3(rp[:, QA:QA + 256], 4), bc_last(qdec[:, 0:4], 64), ALU.mult)
                P.cp("pool", r16[:, 512:768], rp[:, KA:KA + 256])
                P.tt("dve", v3(r16[:, 768:1024], 4), v3(rp[:, KA:KA + 256], 4), bc_last(kdec[:, 0:4], 64), ALU.mult)
                P.cp("pool", vb16[:], proj[:, VA:VA + 512])
                for i6 in range(6):
                    P.tr(psT[:, i6 * 128:(i6 + 1) * 128], r16[:, i6 * 128:(i6 + 1) * 128], ident[:])
                P.cp("act", rT[:, 0:256], psT[:, 512:768])
                qzv = qz[:].rearrange("p (a h q) -> p a h q", a=4, h=2)
                P.cp("act", qzv[0:64, :, 0, :], psT[0:64, 0:512].rearrange("p (a q) -> p a q", a=4))
                P.cp("dve", qzv[64:128, :, 1, :], psT[64:128, 0:512].rearrange("p (a q) -> p a q", a=4))
                if stage <= 3.1:
                    return
                for h in range(4):
                    hp, pr = h % 2, h // 2
                    P.mm(psR[:, h * 128:(h + 1) * 128], rT[:, pr * 128:(pr + 1) * 128],
                         qz[:, ((0 * 2 + pr) * 2 + hp) * 128:((0 * 2 + pr) * 2 + hp + 1) * 128])
                if stage <= 3.2:
                    return
                P.tt("dve", scm[:], psR[:], DTm[:], ALU.mult)
                if isp:
                    for h in range(4):
                        hp, pr = h % 2, h // 2
                        P.mm(psC[:, h * 128:(h + 1) * 128], scm[:, h * 128:(h + 1) * 128], vb16[:, h * 128:(h + 1) * 128], start=True, stop=False)
                        P.mm(psC[:, h * 128:(h + 1) * 128], qz[:, ((1 * 2 + pr) * 2 + hp) * 128:((1 * 2 + pr) * 2 + hp + 1) * 128],
                             Sb[:, pr * 128:(pr + 1) * 128], start=False, stop=True)
                else:
                    S0b = caches["S0b"]
                    qdTm = caches["qdTm"]
                    for h in range(4):
                        hp, pr = h % 2, h // 2
                        if hp == 0:
                            for b in range(16):
                                P.tt("dve" if b % 2 == 0 else "pool", qdTm[:, b * 256:(b + 1) * 256].rearrange("p (a q) -> p a q", a=2),
                                     qz[:, 512 + pr * 256:512 + (pr + 1) * 256].rearrange("p (a q) -> p a q", a=2),
                                     bc_mid(cs["colmask"][:, b * 128:(b + 1) * 128], 2), ALU.mult)
                        P.mm(psC[:, h * 128:(h + 1) * 128], scm[:, h * 128:(h + 1) * 128], vb16[:, h * 128:(h + 1) * 128], start=True, stop=False)
                        for b in range(16):
                            P.mm(psC[:, h * 128:(h + 1) * 128], qdTm[:, b * 256 + hp * 128:b * 256 + (hp + 1) * 128],
                                 S0b[:, (b * 2 + pr) * 128:(b * 2 + pr + 1) * 128], start=False, stop=(b == 15))
                if stage <= 3.4:
                    return
                P.cp("act", osb[:], psC[:])
                if stage <= 3.5:
                    return
                if isp:
                    for h in range(4):
                        hp, pr = h % 2, h // 2
                        P.mm(psR[:, h * 128:(h + 1) * 128], r16[:, 768 + pr * 128:768 + (pr + 1) * 128], vb16[:, h * 128:(h + 1) * 128])
                    for h in range(4):
                        hp, pr = h % 2, h // 2
                        rows = slice(hp * 64, (hp + 1) * 64)
                        P.stt("dve", S32[rows, pr * 128:(pr + 1) * 128], S32[rows, pr * 128:(pr + 1) * 128], cs["cdec_p"][rows, pr:pr + 1],
                              psR[rows, h * 128:(h + 1) * 128], ALU.mult, ALU.add)
                    P.cp("pool", Sb[:], S32[:])
                    if t == n_ptiles - 1:
                        for h in range(4):
                            hp, pr = h % 2, h // 2
                            P.dma(o_retp[h, :, :], S32[hp * 64:(hp + 1) * 64, pr * 128:(pr + 1) * 128])
                else:
                    S0 = caches["S0"]
                    vblk = caches["vblk"]
                    Sn = caches["Sn"]
                    for h in range(4):
                        hp, pr = h % 2, h // 2
                        rows = slice(hp * 64, (hp + 1) * 64)
                        P.tt("dve", vblk[:].rearrange("p (b e) -> p b e", b=16), bc_mid(vb16[:, h * 128:(h + 1) * 128], 16),
                             bc_last(cs["blkmask"][:, 0:16], 128), ALU.mult)
                        for q4 in range(4):
                            pa = psA[cnt["pa"] % 2]; cnt["pa"] += 1
                            P.mm(pa[:, :], r16[:, 768 + pr * 128:768 + (pr + 1) * 128], vblk[:, q4 * 512:(q4 + 1) * 512])
                            s0v = S0[rows, :].rearrange("p (b a e) -> p b a e", b=16, a=2)[:, q4 * 4:(q4 + 1) * 4, pr, :]
                            P.stt("dve", Sn[rows, q4 * 512:(q4 + 1) * 512].rearrange("p (b e) -> p b e", b=4), s0v, cs["cdec_s"][rows, pr:pr + 1],
                                  pa[rows, :].rearrange("p (b e) -> p b e", b=4), ALU.mult, ALU.add)
                        P.dma(o_rets[:, h, :, :].rearrange("b d e -> d b e"), Sn[rows, :].rearrange("p (b e) -> p b e", b=16))
                if stage <= 4:
                    return
                P.op("dve", lambda e: e.reduce_sum(stt_[:, 24:28], v3(osb[:], 4), AX.X), [osb], [stt_])
                P.ts("dve", stt_[:, 24:28], stt_[:, 24:28], -1.0 / 128, None, ALU.mult)
                P.tt("dve", v3(ocb[:], 4), v3(osb[:], 4), bc_last(stt_[:, 24:28], 128), ALU.add)
                P.tt("pool", osb[:], ocb[:], ocb[:], ALU.mult)
                P.op("dve", lambda e: e.reduce_sum(stt_[:, 28:32], v3(osb[:], 4), AX.X), [osb], [stt_])
                rsqrt_small(stt_[:, 28:32], stt_[:, 28:32], 1.0 / 128, EPS)
                P.tt("dve", v3(ocb[:], 4), v3(ocb[:], 4), bc_last(stt_[:, 28:32], 128), ALU.mult)
                P.tt("pool", ocb[:], ocb[:], gnb[:], ALU.mult)
                P.actv(sz[:], proj[:, ZA:ZA + 512], AF.Silu)
                P.tt("dve", mix[:, 0:512], ocb[:], sz[:], ALU.mult)
                if stage <= 5:
                    return
                P.actv(gts[:], proj[:, GL:GL + 24], AF.Sigmoid)
                P.cp("pool", n16[:, 0:896], rp[:, QN:QN + 896])
                P.cp("pool", n16[:, 896:1024], proj[:, VC:VC + 128])
                for i4 in range(4):
                    P.tr(psT[:, i4 * 128:(i4 + 1) * 128], n16[:, i4 * 128:(i4 + 1) * 128], ident[:])
                P.cp("act", qTz[0:64, 0:512], psT[0:64, 0:512])
                P.cp("dve", qTz[64:128, 512:1024], psT[64:128, 0:512])
                for i4 in range(4):
                    P.tr(psT[:, i4 * 128:(i4 + 1) * 128], n16[:, 512 + i4 * 128:512 + (i4 + 1) * 128], ident[:])
                if isp:
                    cT = caches["cT"]; vs1 = caches["vs1"]; vw1 = caches["vw1"]
                    P.cp("act", cT[:].rearrange("p (k n) -> p k n", k=4)[:, :, t * 128:(t + 1) * 128], psT[:, 0:512].rearrange("p (k n) -> p k n", k=4))
                    P.cp("dve", vs1[:].rearrange("p (c g w) -> p c g w", c=16, g=2)[:, t, :, 0:64], v3(proj[:, VS:VS + 128], 2))
                    P.cp("dve", vw1[:].rearrange("p (c g w) -> p c g w", c=16, g=2)[:, t, :, 0:64], v3(proj[:, VW:VW + 128], 2))
                    ckT = caches["ckT"]; cvx = caches["cvx"]
                    compress(cT, 0, cT, 3 * 2048, ckT, cvx)
                    for g in range(2):
                        qTg = qTz[:, g * 512:(g + 1) * 512].rearrange("p (j q) -> p j q", j=4)
                        rows = slice(0, 128)
                        selc = []
                        for c in range(t + 1):
                            bias = bc_mid(cs["CB"][:, :], 4) if c == t else None
                            selc.append((cT[rows, 2048 + c * 128:2048 + (c + 1) * 128], vs1[:, (c * 2 + g) * 65:(c * 2 + g + 1) * 65], 128,
                                         cs["E"][0:32, c * 128:(c + 1) * 128], bias))
                        winc = []
                        for c in range(max(0, t - 4), t + 1):
                            bias = bc_mid(cs["CB"][:, :], 4) if c == t else (bc_mid(cs["AB"][:, :], 4) if c == t - 4 else None)
                            winc.append((cT[rows, 4096 + c * 128:4096 + (c + 1) * 128], vw1[:, (c * 2 + g) * 65:(c * 2 + g + 1) * 65], 128, None, bias))
                        cmpb = bc_mid(cs["CMPB"][0:127, t * 128:(t + 1) * 128], 4)
                        nsa_group(128, g, qTg, ckT[rows, 0:127], cvx[0:127, g * 97:(g + 1) * 97], cmpb, selc, winc,
                                  cs["mulc_p"][:, t * 32:(t + 1) * 32], cs["addc_p"][:, t * 32:(t + 1) * 32], 32,
                                  lambda x, g=g: on[:, x * 512 + g * 256:x * 512 + (g + 1) * 256].rearrange("p (j d) -> p j d", j=4))
                else:
                    cTs = caches["cTs"]; vs1s = caches["vs1s"]
                    P.cp("act", cTs[:], psT[:, 0:512])
                    P.cp("dve", vs1s[:].rearrange("p (k g w) -> p k g w", k=2, g=2)[:, 0, :, 0:64], v3(proj[:, VS:VS + 128], 2))
                    P.cp("dve", vs1s[:].rearrange("p (k g w) -> p k g w", k=2, g=2)[:, 1, :, 0:64], v3(proj[:, VW:VW + 128], 2))
                    sample_nsa(caches)
                if stage <= 6:
                    return
                gv = gts[:].rearrange("p (h x) -> p h x", h=8)
                for x in range(3):
                    P.tt("dve" if x != 1 else "pool", v3(on[:, x * 512:(x + 1) * 512], 8), v3(on[:, x * 512:(x + 1) * 512], 8), bc_last(gv[:, :, x], 64), ALU.mult)
                P.tt("dve", obt[:], on[:, 0:512], on[:, 512:1024], ALU.add)
                P.tt("dve", obt[:], obt[:], on[:, 1024:1536], ALU.add)
                P.actv(sz[:], proj[:, ZB:ZB + 512], AF.Silu)
                P.tt("dve", mix[:, 512:1024], obt[:], sz[:], ALU.mult)
                if stage <= 7:
                    return
                for kc in range(8):
                    P.tr(psT[:, kc * 128:(kc + 1) * 128], mix[:, kc * 128:(kc + 1) * 128], ident[:])
                P.cp("act", mixT[:], psT[:])
                for hf in range(2):
                    pa = psA[cnt["pa"] % 2]; cnt["pa"] += 1
                    for kc in range(8):
                        P.mm(pa[:, :], mixT[:, kc * 128:(kc + 1) * 128], Wob[:, kc * 1024 + hf * 512:kc * 1024 + (hf + 1) * 512], start=(kc == 0), stop=(kc == 7))
                    P.tt("dve", xtile[:, hf * 512:(hf + 1) * 512], xtile[:, hf * 512:(hf + 1) * 512], pa[:, :], ALU.add)
                ydst = yp[t * 128:(t + 1) * 128, :] if isp else ys
                P.dma(ydst, xtile[:], writes=[("y0", mode, t)])

            def compress(kc_t, kc_off, vc_t, vc_off, ckT, cvx):
                for l in range(32):
                    P.mm(psC[:, 0:127], BDW[:, l * 128:(l + 1) * 128], kc_t[:, kc_off + l:kc_off + l + 16 * 126 + 1:16], start=(l == 0), stop=(l == 31))
                P.ts("dve", ckT[:, 0:127], psC[:, 0:127], posk[:, 0:1], None, ALU.add)
                for l in range(32):
                    P.mm(psV[0:127, 0:128], vc_t[:, vc_off + l:vc_off + l + 16 * 126 + 1:16], BDW[:, (32 + l) * 128:(33 + l) * 128], start=(l == 0), stop=False)
                P.mm(psV[0:127, 0:128], cs["ones_row"][0:1, 0:127], posv[0:1, 0:128], start=False, stop=True)
                P.cp("act", cvx[0:127, :].rearrange("p (g w) -> p g w", g=2)[:, :, 0:64], psV[0:127, 0:128].rearrange("p (g d) -> p g d", g=2))

            def sample_nsa(caches):
                C = caches
                cTq, kwTq, vs1q, vw1q, ckTq, cvxq = C["cTq"], C["kwTq"], C["vs1q"], C["vw1q"], C["ckTq"], C["cvxq"]
                cTs, vs1s, idx = C["cTs"], C["vs1s"], C["idx"]
                on8 = [osb, ocb, sz]
                pg, pg16, wst32, w16 = C["pg"], C["pg16"], C["wst32"], C["w16"]
                for b in range(16):
                    for i in range(16):
                        pgt = pg[(b * 16 + i) % 3]
                        col = b * 16 + i
                        P.dma(pgt[:], cache, reads=[cache, idx], q="pool",
                              fn=lambda e, pgt=pgt, col=col: e.indirect_dma_start(
                                  out=pgt[:, :], out_offset=None, in_=cache[:, :],
                                  in_offset=bass.IndirectOffsetOnAxis(ap=idx[:, col:col + 1], axis=0)))
                        p16 = pg16[(b * 16 + i) % 2]
                        P.cp("dve" if i % 2 == 0 else "pool", p16[:], pgt[:])
                        for k in range(3):
                            P.tr(psT[:, k * 128:(k + 1) * 128], p16[:, k * 128:(k + 1) * 128], ident[:])
                        P.cp("act", cTq[:].rearrange("p (k n) -> p k n", k=3)[:, :, i * 128:(i + 1) * 128], psT[:, 0:384].rearrange("p (k n) -> p k n", k=3))
                        P.cp("pool" if i % 2 == 0 else "dve", vs1q[:].rearrange("p (c g w) -> p c g w", c=16, g=2)[:, i, :, 0:64], v3(p16[:, 384:512], 2))
                    P.dma(wst32[:].rearrange("p (c w) -> p c w", c=4), cwin[b].rearrange("(c r) w -> r c w", r=128))
                    P.cp("dve", w16[:], wst32[:])
                    for c in range(4):
                        P.tr(psT[:, 512 + c * 128:512 + (c + 1) * 128], w16[:, c * 256:c * 256 + 128], ident[:])
                    P.cp("act", kwTq[:], psT[:, 512:1024])
                    for c in range(4):
                        P.cp("pool", vw1q[:, c * 130:(c + 1) * 130].rearrange("p (g w) -> p g w", g=2)[:, :, 0:64], v3(w16[:, c * 256 + 128:(c + 1) * 256], 2))
                    compress(cTq, 0, cTq, 2048, ckTq, cvxq)
                    for g in range(2):
                        qTg = qTz[:, g * 512:(g + 1) * 512].rearrange("p (j q) -> p j q", j=4)[:, :, 8 * b:8 * b + 8]
                        sbias = bc_mid(cs["SB"][:, b * 8:(b + 1) * 8], 4)
                        selc = []
                        for c in range(16):
                            selc.append((cTq[:, 4096 + c * 128:4096 + (c + 1) * 128], vs1q[:, (c * 2 + g) * 65:(c * 2 + g + 1) * 65], 128,
                                         cs["E"][0:33, c * 128:(c + 1) * 128], None))
                        selc.append((cTs[:, 128:256], vs1s[:, (0 * 2 + g) * 65:(0 * 2 + g + 1) * 65], 128, None, sbias))
                        winc = []
                        for c in range(4):
                            bias = bc_mid(cs["ABs"][:, :], 4) if c == 0 else None
                            winc.append((kwTq[:, c * 128:(c + 1) * 128], vw1q[:, (c * 2 + g) * 65:(c * 2 + g + 1) * 65], 128, None, bias))
                        winc.append((cTs[:, 256:384], vs1s[:, (1 * 2 + g) * 65:(1 * 2 + g + 1) * 65], 128, None, sbias))
                        nsa_group(8, g, qTg, ckTq[:, 0:127], cvxq[0:127, g * 98:(g + 1) * 98], None, selc, winc,
                                  cs["mulc_s"][:, :], cs["addc_s"][:, :], 33,
                                  lambda x, g=g: on8[x][0:8, g * 256:(g + 1) * 256].rearrange("p (j d) -> p j d", j=4))
                    for x in range(3):
                        P.dma(on[b * 8:(b + 1) * 8, x * 512:(x + 1) * 512], on8[x][0:8, :])

            with ExitStack() as stP:
                Ap = lambda name, shape, dt=F32: alloc(stP, name, shape, dt)
                load_consts(stP, P_ONLY)
                cT = Ap("cT", [128, 4 * 2048], BF16)
                vs1 = Ap("vs1", [128, 16 * 2 * 65], BF16)
                vw1 = Ap("vw1", [128, 16 * 2 * 65], BF16)
                ckT = Ap("ckT", [128, 128], BF16)
                cvx = Ap("cvx", [128, 2 * 97], BF16)
                ov32 = Ap("ov32", [128, 32])
                P.memset("pool", cT[:], 0.0)
                P.memset("pool", vs1[:], 1.0)
                P.memset("pool", vw1[:], 1.0)
                P.memset("pool", cvx[:], 1.0)
                P.memset("pool", ckT[:], 0.0)
                for g in range(2):
                    P.cp("dve", cvx[0:127, g * 97 + 65:(g + 1) * 97], cs["ovl_p"][0:127, :])
                caches = {"cT": cT, "vs1": vs1, "vw1": vw1, "ckT": ckT, "cvx": cvx}
                for t in range(n_ptiles):
                    even_tile("p", t, caches)
                P.barrier()

            if do_sample:
                stS = ExitStack()
                scaches = {}

                def sample_hook():
                    P.barrier()
                    stWb.close()
                    As = lambda name, shape, dt=F32: alloc(stS, name, shape, dt)
                    load_consts(stS, S_ONLY)
                    C = scaches
                    C["S0"] = As("S0", [128, 4096])
                    C["S0b"] = As("S0b", [128, 4096], BF16)
                    C["qdTm"] = As("qdTm", [128, 16 * 256], BF16)
                    C["vblk"] = As("vblk", [128, 2048], BF16)
                    C["Sn"] = As("Sn", [128, 2048])
                    C["cTq"] = As("cTq", [128, 3 * 2048], BF16)
                    C["kwTq"] = As("kwTq", [128, 512], BF16)
                    C["vs1q"] = As("vs1q", [128, 16 * 130], BF16)
                    C["vw1q"] = As("vw1q", [128, 4 * 130], BF16)
                    C["ckTq"] = As("ckTq", [128, 128], BF16)
                    C["cvxq"] = As("cvxq", [128, 2 * 98], BF16)
                    C["cTs"] = As("cTs", [128, 512], BF16)
                    C["vs1s"] = As("vs1s", [128, 4 * 65], BF16)
                    C["pg"] = [As("pg%d" % i, [128, 512]) for i in range(3)]
                    C["pg16"] = [As("pg16_%d" % i, [128, 512], BF16) for i in range(2)]
                    C["wst32"] = As("wst32", [128, 1024])
                    C["w16"] = As("w16", [128, 1024], BF16)
                    C["idx"] = As("idx", [128, 256], I32)
                    pti = As("pti", [128, 256], I32)
                    ptf = tmpa[:, 0:256]
                    P.memset("pool", C["vs1q"][:], 1.0)
                    P.memset("pool", C["vw1q"][:], 1.0)
                    P.memset("pool", C["vs1s"][:], 1.0)
                    P.memset("pool", C["cvxq"][:], 1.0)
                    for g in range(2):
                        P.cp("dve", C["cvxq"][0:127, g * 98 + 65:(g + 1) * 98], cs["ovl_s"][0:127, :])
                    for hp in range(2):
                        P.dma(C["S0"][hp * 64:(hp + 1) * 64, :].rearrange("p (b a e) -> p b a e", b=16, a=2),
                              sret[:, hp::2, :, :].rearrange("b a d e -> d b a e"))
                    P.cp("dve", C["S0b"][:], C["S0"][:])
                    P.dma(pti[:], ptab.partition_broadcast(128))
                    P.cp("dve", ptf, pti[:])
                    P.ts("dve", ptf, ptf, 128.0, cs["iota_p"][:, 0:1], ALU.mult, ALU.add)
                    P.cp("dve", C["idx"][:], ptf)

                scaches["hook"] = sample_hook
                even_tile("s", 0, scaches)
                P.barrier()
                stS.close()
            else:
                stWb.close()
            P.barrier()
        if phase_b:
            NT = 2176
            NCH = 272
            TWO_PI = 2.0 * math.pi
            with ExitStack() as stB:
                Bf = lambda name, shape, dt=F32: alloc(stB, name, shape, dt)
                load_consts(stB, ["ident", "JV", "KI", "PM4", "BM16"], pre="kb_")
                ident = cs["ident"]
                uT = Bf("uT", [128, 8 * NT], BF16)
                szT = Bf("szT", [128, 8 * NT], BF16)
                ngo = Bf("ngo", [128, 8])
                dsk = Bf("dsk", [128, 8])
                statb = Bf("statb", [128, 8])
                FSp = Bf("FSp", [128, 64])
                FSs = Bf("FSs", [128, 1024])
                P.dma(ngo[:], norm_o)
                P.dma(dsk[:], ssmd)
                cntb = {"x": 0, "pa": 0}

                def load_tile(dst, ti):
                    if ti < 16:
                        P.dma(dst, yp[ti * 128:(ti + 1) * 128, :], reads=[("y0", "p", ti)])
                    else:
                        P.dma(dst, ys, reads=[("y0", "s", 0)])

                with ExitStack() as st1:
                    B1 = lambda name, shape, dt=F32: alloc(st1, name, shape, dt)
                    Wodd = B1("Wodd", [128, 8 * 2048], BF16)
                    yt = [B1("yt%d" % i, [128, 1024]) for i in range(2)]
                    hb = B1("hb", [128, 1024], BF16)
                    hT = B1("hT", [128, 8 * 512], BF16)
                    with ExitStack() as stg_:
                        stg = [alloc(stg_, "stgb%d" % i, [128, 2048]) for i in range(2)]
                        for kc in range(8):
                            s_ = stg[kc % 2]
                            P.dma(s_[:], w_in_o[kc * 128:(kc + 1) * 128, :])
                            if kc % 2 == 0:
                                P.ts("dve", Wodd[:, kc * 2048:(kc + 1) * 2048], s_[:], ngo[:, kc:kc + 1], None, ALU.mult)
                            else:
                                P.op("act", lambda e, kc=kc, s_=s_: e.mul(Wodd[:, kc * 2048:(kc + 1) * 2048], s_[:], ngo[:, kc:kc + 1]), [s_, ngo], [Wodd])
                        P.barrier()
                    blocks = [(0, 4), (4, 8), (8, 12), (12, 16), (16, 17)]
                    for (t0, t1) in blocks:
                        nb = (t1 - t0) * 128
                        col0 = t0 * 128
                        for ti in range(t0, t1):
                            ytile = yt[cntb["x"] % 2]; cntb["x"] += 1
                            load_tile(ytile[:], ti)
                            P.memset("dve", statb[:, 0:1], 0.0)
                            P.actv(hb[:], ytile[:], AF.Square, accum_out=statb[:, 0:1])
                            P.ts("dve", statb[:, 1:2], statb[:, 0:1], 1.0 / 1024, EPS, ALU.mult, ALU.add)
                            P.actv(statb[:, 1:2], statb[:, 1:2], AF.Sqrt)
                            P.op("dve", lambda e: e.reciprocal(statb[:, 1:2], statb[:, 1:2]), [statb], [statb])
                            P.ts("dve", hb[:], ytile[:], statb[:, 1:2], None, ALU.mult)
                            for kc in range(8):
                                P.tr(psT[:, kc * 128:(kc + 1) * 128], hb[:, kc * 128:(kc + 1) * 128], ident[:])
                            lt = ti - t0
                            P.cp("act", hT[:].rearrange("p (k n) -> p k n", k=8)[:, :, lt * 128:(lt + 1) * 128], psT[:].rearrange("p (k n) -> p k n", k=8))
                        for oc in range(16):
                            pa = psA[cntb["pa"] % 2]; cntb["pa"] += 1
                            for kc in range(8):
                                P.mm(pa[:, 0:nb], Wodd[:, kc * 2048 + oc * 128:kc * 2048 + (oc + 1) * 128], hT[:, kc * 512:kc * 512 + nb], start=(kc == 0), stop=(kc == 7))
                            if oc < 8:
                                P.cp("dve", uT[:, oc * NT + col0:oc * NT + col0 + nb], pa[:, 0:nb])
                            else:
                                P.actv(szT[:, (oc - 8) * NT + col0:(oc - 8) * NT + col0 + nb], pa[:, 0:nb], AF.Silu)
                    P.barrier()

                with ExitStack() as st2:
                    B2 = lambda name, shape, dt=F32: alloc(st2, name, shape, dt)
                    lr = B2("lr", [128, 32]); li = B2("li", [128, 32]); ls = B2("ls", [128, 32])
                    bre = B2("bre", [128, 512]); bim = B2("bim", [128, 512])
                    cre = B2("cre", [128, 512]); cim = B2("cim", [128, 512])
                    x0r = B2("x0r", [128, 512]); x0i = B2("x0i", [128, 512])
                    for tl, src in ((lr, lamre_A), (li, lamim_A), (ls, lstep_A), (bre, bA_re), (bim, bA_im), (cre, cA_re), (cim, cA_im), (x0r, x0A_re), (x0i, x0A_im)):
                        P.dma(tl[:], src)
                    aa = B2("aa", [128, 32]); th = B2("th", [128, 32])
                    A9 = B2("A9", [128, 288]); T9 = B2("T9", [128, 288]); T9c = B2("T9c", [128, 288])
                    PR = B2("PR", [128, 288]); PI = B2("PI", [128, 288])
                    rrt = B2("rrt", [128, 1024]); rri = B2("rri", [128, 1024], I32); rrm = B2("rrm", [128, 1024])
                    npi = B2("npi", [128, 1])

                    def range_reduce(x, n):
                        P.ts("dve", rrt[:, 0:n], x, 1.0 / TWO_PI, None, ALU.mult)
                        P.cp("dve", rri[:, 0:n], rrt[:, 0:n])
                        P.cp("dve", rrt[:, 0:n], rri[:, 0:n])
                        P.stt("dve", x, rrt[:, 0:n], -TWO_PI, x, ALU.mult, ALU.add)
                        P.ts("dve", rrm[:, 0:n], x, math.pi, -TWO_PI, ALU.is_gt, ALU.mult)
                        P.tt("dve", x, x, rrm[:, 0:n], ALU.add)
                        P.ts("dve", rrm[:, 0:n], x, -math.pi, TWO_PI, ALU.is_lt, ALU.mult)
                        P.tt("dve", x, x, rrm[:, 0:n], ALU.add)

                    P.actv(ls[:], ls[:], AF.Exp)
                    P.tt("dve", aa[:], lr[:], ls[:], ALU.mult)
                    P.tt("dve", th[:], li[:], ls[:], ALU.mult)
                    JV = cs["JV"]
                    P.tt("dve", A9[:].rearrange("p (j m) -> p j m", j=9), JV[:].rearrange("p (j m) -> p j m", j=9), bc_mid(aa[:, :], 9), ALU.mult)
                    P.actv(A9[:], A9[:], AF.Exp)
                    range_reduce(th[:, :], 32)
                    P.tt("dve", T9[:].rearrange("p (j m) -> p j m", j=9), JV[:].rearrange("p (j m) -> p j m", j=9), bc_mid(th[:, :], 9), ALU.mult)
                    P.ts("dve", T9c[:], T9[:], math.pi / 2, None, ALU.add)
                    range_reduce(T9[:, :], 288)
                    range_reduce(T9c[:, :], 288)
                    P.actv(T9[:], T9[:], AF.Sin)
                    P.actv(T9c[:], T9c[:], AF.Sin)
                    P.tt("dve", PR[:], A9[:], T9c[:], ALU.mult)
                    P.tt("dve", PI[:], A9[:], T9[:], ALU.mult)
                    PRv = PR[:].rearrange("p (j m) -> p j m", j=9)
                    PIv = PI[:].rearrange("p (j m) -> p j m", j=9)
                    den = B2("den", [128, 32]); nr = B2("nr", [128, 32]); fre = B2("fre", [128, 32]); fim = B2("fim", [128, 32]); t32 = B2("t32", [128, 32])
                    P.tt("dve", den[:], lr[:], lr[:], ALU.mult)
                    P.tt("dve", t32[:], li[:], li[:], ALU.mult)
                    P.tt("dve", den[:], den[:], t32[:], ALU.add)
                    P.op("dve", lambda e: e.reciprocal(den[:], den[:]), [den], [den])
                    P.ts("dve", nr[:], PR[:, 32:64], -1.0, None, ALU.add)
                    P.tt("dve", fre[:], nr[:], lr[:], ALU.mult)
                    P.tt("dve", t32[:], PI[:, 32:64], li[:], ALU.mult)
                    P.tt("dve", fre[:], fre[:], t32[:], ALU.add)
                    P.tt("dve", fre[:], fre[:], den[:], ALU.mult)
                    P.tt("dve", fim[:], PI[:, 32:64], lr[:], ALU.mult)
                    P.tt("dve", t32[:], nr[:], li[:], ALU.mult)
                    P.tt("dve", fim[:], fim[:], t32[:], ALU.subtract)
                    P.tt("dve", fim[:], fim[:], den[:], ALU.mult)
                    Bre = B2("Bre", [128, 512]); Bim = B2("Bim", [128, 512]); t512 = B2("t512", [128, 512])
                    v16 = lambda ap: ap.rearrange("p (m c) -> p m c", c=16)
                    P.tt("dve", v16(Bre[:]), v16(bre[:]), bc_last(fre[:, :], 16), ALU.mult)
                    P.tt("dve", v16(t512[:]), v16(bim[:]), bc_last(fim[:, :], 16), ALU.mult)
                    P.tt("dve", Bre[:], Bre[:], t512[:], ALU.subtract)
                    P.tt("dve", v16(Bim[:]), v16(bim[:]), bc_last(fre[:, :], 16), ALU.mult)
                    P.tt("dve", v16(t512[:]), v16(bre[:]), bc_last(fim[:, :], 16), ALU.mult)
                    P.tt("dve", Bim[:], Bim[:], t512[:], ALU.add)
                    Cbd_re = B2("Cbd_re", [128, 32 * 32], BF16); Cbd_nim = B2("Cbd_nim", [128, 32 * 32], BF16)
                    P.memset("pool", Cbd_re[:], 0.0)
                    P.memset("pool", Cbd_nim[:], 0.0)
                    cbv = lambda t_: t_[:].rearrange("p (m c) -> p m c", c=32)
                    for hp in range(2):
                        rows = slice(hp * 64, (hp + 1) * 64)
                        P.cp("dve", cbv(Cbd_re)[rows, :, hp * 16:(hp + 1) * 16], v16(cre[:])[rows, :, :])
                        P.ts("dve", cbv(Cbd_nim)[rows, :, hp * 16:(hp + 1) * 16], v16(cim[:])[rows, :, :], -1.0, None, ALU.mult)
                    th8 = B2("th8", [128, 32])
                    P.ts("dve", th8[:], th[:], 8.0, None, ALU.mult)
                    range_reduce(th8[:, :], 32)
                    Xre = B2("Xre", [128, 512]); Xim = B2("Xim", [128, 512]); tX = B2("tX", [128, 512])
                    XBD = B2("XBD", [128, 2 * 8 * 4 * 32], BF16)
                    VZ = B2("VZ", [128, 8 * 4 * 2 * 128], BF16)
                    WS = B2("WS", [128, 8 * 2 * 128], BF16)
                    uzb = [B2("uz%d" % i, [128, NT], BF16) for i in range(2)]
                    BD = B2("BD", [128, 8 * 128], BF16)
                    Sin_r = B2("Sin_r", [128, 4 * NCH]); Sin_i = B2("Sin_i", [128, 4 * NCH])
                    cosT = B2("cosT", [128, 1024]); sinT = B2("sinT", [128, 1024])
                    c_r = B2("c_r", [128, 1024]); c_i = B2("c_i", [128, 1024]); t1k = B2("t1k", [128, 1024])
                    w_r = B2("w_r", [128, 1024]); w_i = B2("w_i", [128, 1024])
                    Sp_r = B2("Sp_r", [128, 4 * NCH], BF16); Sp_i = B2("Sp_i", [128, 4 * NCH], BF16)
                    yv = B2("yv", [128, NCH]); y2 = B2("y2", [128, NCH]); y3 = B2("y3", [128, NCH])
                    ygc = B2("ygc", [128, NT], BF16)
                    P.memset("pool", XBD[:], 0.0)
                    P.memset("pool", VZ[:], 0.0)
                    for c in range(8):
                        msl = slice(4 * c, 4 * c + 4)
                        x4 = lambda t_: t_[:].rearrange("p (t m c) -> p t m c", t=8, m=4)
                        prb = PRv[:, 0:8, msl].unsqueeze(3).to_broadcast([128, 8, 4, 16])
                        pib = PIv[:, 0:8, msl].unsqueeze(3).to_broadcast([128, 8, 4, 16])
                        brb = v16(Bre[:])[:, msl, :].unsqueeze(1).to_broadcast([128, 8, 4, 16])
                        bib = v16(Bim[:])[:, msl, :].unsqueeze(1).to_broadcast([128, 8, 4, 16])
                        P.tt("dve", x4(Xre), prb, brb, ALU.mult)
                        P.tt("dve", x4(tX), pib, bib, ALU.mult)
                        P.tt("dve", Xre[:], Xre[:], tX[:], ALU.subtract)
                        P.tt("dve", x4(Xim), prb, bib, ALU.mult)
                        P.tt("dve", x4(tX), pib, brb, ALU.mult)
                        P.tt("dve", Xim[:], Xim[:], tX[:], ALU.add)
                        xbv = XBD[:].rearrange("p (r t m c) -> p r t m c", r=2, t=8, m=4)
                        for hp in range(2):
                            rows = slice(hp * 64, (hp + 1) * 64)
                            P.cp("dve", xbv[rows, 0, :, :, hp * 16:(hp + 1) * 16], x4(Xre)[rows])
                            P.cp("pool", xbv[rows, 1, :, :, hp * 16:(hp + 1) * 16], x4(Xim)[rows])
                        prb = PRv[:, 1:9, msl].unsqueeze(3).to_broadcast([128, 8, 4, 16])
                        pib = PIv[:, 1:9, msl].unsqueeze(3).to_broadcast([128, 8, 4, 16])
                        crb = v16(cre[:])[:, msl, :].unsqueeze(1).to_broadcast([128, 8, 4, 16])
                        cib = v16(cim[:])[:, msl, :].unsqueeze(1).to_broadcast([128, 8, 4, 16])
                        P.tt("dve", x4(Xre), prb, crb, ALU.mult)
                        P.tt("dve", x4(tX), pib, cib, ALU.mult)
                        P.tt("dve", Xre[:], Xre[:], tX[:], ALU.subtract)
                        P.tt("dve", x4(Xim), pib, crb, ALU.mult)
                        P.tt("dve", x4(tX), prb, cib, ALU.mult)
                        P.stt("dve", Xim[:], Xim[:], -1.0, tX[:], ALU.mult, ALU.subtract)
                        vzv = VZ[:].rearrange("p (t m r n) -> p t m r n", t=8, m=4, r=2)
                        for hp in range(2):
                            rows = slice(hp * 64, (hp + 1) * 64)
                            for m4 in range(4):
                                c0_ = 32 * m4 + 16 * hp
                                P.cp("dve", vzv[rows, :, m4, 0, c0_:c0_ + 16], x4(Xre)[rows, :, m4, :])
                                P.cp("pool", vzv[rows, :, m4, 1, c0_:c0_ + 16], x4(Xim)[rows, :, m4, :])
                        for th2 in range(2):
                            for tl_ in range(4):
                                tau = th2 * 4 + tl_
                                for ri in range(2):
                                    P.tr(psT[:, (tl_ * 2 + ri) * 128:(tl_ * 2 + ri + 1) * 128], XBD[:, (ri * 8 + tau) * 128:(ri * 8 + tau + 1) * 128], ident[:])
                            P.cp("act", WS[:, th2 * 1024:(th2 + 1) * 1024], psT[:])
                        for th2 in range(2):
                            for tl_ in range(4):
                                tau = th2 * 4 + tl_
                                outp = psC[:, tl_ * 128:(tl_ + 1) * 128]
                                P.mm(outp, XBD[:, (0 * 8 + tau) * 128:(0 * 8 + tau + 1) * 128], Cbd_re[:, c * 128:(c + 1) * 128], start=True, stop=False)
                                P.mm(outp, XBD[:, (1 * 8 + tau) * 128:(1 * 8 + tau + 1) * 128], Cbd_nim[:, c * 128:(c + 1) * 128], start=False, stop=True)
                            P.tt("dve", BD[:, th2 * 512:(th2 + 1) * 512].rearrange("p (t n) -> p t n", t=4), psC[:, 0:512].rearrange("p (t n) -> p t n", t=4),
                                 bc_mid(cs["BM16"][:, :], 4), ALU.mult)
                        uc = uT[:, c * NT:(c + 1) * NT]
                        for m4 in range(4):
                            uz = uzb[m4 % 2]
                            P.ts("dve" if m4 % 2 == 0 else "dve", uz[:], uc, cs["PM4"][:, m4:m4 + 1], None, ALU.mult)
                            for ri, dst in ((0, Sin_r), (1, Sin_i)):
                                pa = psA[ri]
                                for s_ in range(8):
                                    tau = 7 - s_
                                    P.mm(pa[:, 0:NCH], WS[:, (tau * 2 + ri) * 128:(tau * 2 + ri + 1) * 128], uz[:, s_:NT:8], start=(s_ == 0), stop=(s_ == 7))
                                if ri == 0:
                                    P.cp("act", dst[:, m4 * NCH:(m4 + 1) * NCH], pa[:, 0:NCH])
                                else:
                                    P.cp("dve", dst[:, m4 * NCH:(m4 + 1) * NCH], pa[:, 0:NCH])
                        a3 = lambda t_: t_[:].rearrange("p (m k) -> p m k", m=4)
                        P.tt("dve", a3(sinT), bc_last(th8[:, msl], 256), bc_mid(cs["KI"][:, :], 4), ALU.mult)
                        P.ts("dve", cosT[:], sinT[:], math.pi / 2, None, ALU.add)
                        range_reduce(sinT[:, :], 1024)
                        range_reduce(cosT[:, :], 1024)
                        P.actv(sinT[:], sinT[:], AF.Sin)
                        P.actv(cosT[:], cosT[:], AF.Sin)
                        s3 = lambda t_: t_[:].rearrange("p (m k) -> p m k", m=4)[:, :, 0:256]
                        P.tt("dve", a3(c_r), a3(cosT), s3(Sin_r), ALU.mult)
                        P.tt("dve", a3(t1k), a3(sinT), s3(Sin_i), ALU.mult)
                        P.tt("dve", c_r[:], c_r[:], t1k[:], ALU.add)
                        P.tt("dve", a3(c_i), a3(cosT), s3(Sin_i), ALU.mult)
                        P.tt("dve", a3(t1k), a3(sinT), s3(Sin_r), ALU.mult)
                        P.tt("dve", c_i[:], c_i[:], t1k[:], ALU.subtract)
                        for m4 in range(4):
                            m = 4 * c + m4
                            r8b = A9[:, 8 * 32 + m:8 * 32 + m + 1].to_broadcast([128, 256])
                            for tl_, wo_ in ((c_r, w_r), (c_i, w_i)):
                                seg = tl_[:, m4 * 256:(m4 + 1) * 256]
                                oseg = wo_[:, m4 * 256:(m4 + 1) * 256]
                                P.op("dve", lambda e, seg=seg, oseg=oseg, r8b=r8b: e.tensor_tensor_scan(oseg, r8b, seg, 0.0, ALU.mult, ALU.add), [tl_, A9], [wo_])
                        P.tt("dve", t1k[:], cosT[:], w_r[:], ALU.mult)
                        P.tt("pool", rrm[:], sinT[:], w_i[:], ALU.mult)
                        P.tt("dve", t1k[:], t1k[:], rrm[:], ALU.subtract)
                        P.tt("pool", rrt[:], cosT[:], w_i[:], ALU.mult)
                        P.tt("dve", rrm[:], sinT[:], w_r[:], ALU.mult)
                        P.tt("dve", rrt[:], rrt[:], rrm[:], ALU.add)
                        spr = Sp_r[:].rearrange("p (m k) -> p m k", m=4)
                        spi = Sp_i[:].rearrange("p (m k) -> p m k", m=4)
                        P.memset("pool", spr[:, :, 0:1], 0.0)
                        P.memset("pool", spi[:, :, 0:1], 0.0)
                        P.cp("dve", spr[:, :, 1:256], a3(t1k)[:, :, 0:255])
                        P.cp("pool", spi[:, :, 1:256], a3(rrt)[:, :, 0:255])
                        x0rv = x0r[:].rearrange("p (m b) -> p m b", b=16)[:, msl, :]
                        x0iv = x0i[:].rearrange("p (m b) -> p m b", b=16)[:, msl, :]
                        P.cp("dve", spr[:, :, 256:272], x0rv)
                        P.cp("pool", spi[:, :, 256:272], x0iv)
                        P.cp("dve", FSp[:, 4 * c:4 * c + 4], a3(t1k)[:, :, 255])
                        P.cp("dve", FSp[:, 32 + 4 * c:32 + 4 * c + 4], a3(rrt)[:, :, 255])
                        fsr = FSs[:, 0:512].rearrange("p (m b) -> p m b", b=16)[:, msl, :]
                        fsi = FSs[:, 512:1024].rearrange("p (m b) -> p m b", b=16)[:, msl, :]
                        p8 = bc_last(PR[:, 8 * 32 + 4 * c:8 * 32 + 4 * c + 4], 16)
                        i8_ = bc_last(PI[:, 8 * 32 + 4 * c:8 * 32 + 4 * c + 4], 16)
                        sir = Sin_r[:].rearrange("p (m k) -> p m k", m=4)[:, :, 256:272]
                        sii = Sin_i[:].rearrange("p (m k) -> p m k", m=4)[:, :, 256:272]
                        tq = tX[:, 0:64].rearrange("p (m b) -> p m b", b=16)
                        P.tt("dve", fsr, p8, x0rv, ALU.mult)
                        P.tt("dve", tq, i8_, x0iv, ALU.mult)
                        P.tt("dve", fsr, fsr, tq, ALU.subtract)
                        P.tt("dve", fsr, fsr, sir, ALU.add)
                        P.tt("dve", fsi, p8, x0iv, ALU.mult)
                        P.tt("dve", tq, i8_, x0rv, ALU.mult)
                        P.tt("dve", fsi, fsi, tq, ALU.add)
                        P.tt("dve", fsi, fsi, sii, ALU.add)
                        for j in range(8):
                            acc = psS[j % 2]
                            for s_ in range(j + 1):
                                P.mm(acc[:, 0:NCH], BD[:, (j - s_) * 128:(j - s_ + 1) * 128], uc[:, s_:NT:8], start=(s_ == 0), stop=False)
                            for m4 in range(4):
                                for ri, spt in ((0, Sp_r), (1, Sp_i)):
                                    last = (m4 == 3 and ri == 1)
                                    P.mm(acc[:, 0:NCH], VZ[:, ((j * 4 + m4) * 2 + ri) * 128:((j * 4 + m4) * 2 + ri + 1) * 128],
                                         spt[:, m4 * NCH:(m4 + 1) * NCH], start=False, stop=last)
                            P.stt("dve", yv[:], uc[:, j:NT:8], dsk[:, c:c + 1], acc[:, 0:NCH], ALU.mult, ALU.add)
                            P.tt("pool", y2[:], yv[:], yv[:], ALU.mult)
                            P.ts("dve", y2[:], y2[:], 0.044715, 1.0, ALU.mult, ALU.add)
                            P.tt("pool", y2[:], y2[:], yv[:], ALU.mult)
                            P.actv(y3[:], y2[:], AF.Sigmoid, scale=1.5957691216057308)
                            P.tt("dve", ygc[:, j:NT:8], yv[:], y3[:], ALU.mult)
                        P.cp("pool", uT[:, c * NT:(c + 1) * NT], ygc[:])
                    P.dma(o_ssp.rearrange("r p m -> p r m"), FSp[:].rearrange("p (r m) -> p r m", r=2))
                    P.dma(o_sss.rearrange("r p x -> p r x"), FSs[:].rearrange("p (r x) -> p r x", r=2))
                    P.barrier()

                with ExitStack() as st3:
                    B3 = lambda name, shape, dt=F32: alloc(st3, name, shape, dt)
                    W1 = B3("W1", [128, 8 * 1024], BF16)
                    W2 = B3("W2", [128, 8 * 1024], BF16)
                    Wo2 = B3("Wo2", [128, 8 * 1024], BF16)
                    yt = [B3("ytc%d" % i, [128, 1024]) for i in range(2)]
                    oT = B3("oT", [128, 8 * 512], BF16)
                    sg = B3("sg", [128, 512]); tg = B3("tg", [128, 512])
                    with ExitStack() as stg_:
                        stg = [alloc(stg_, "stgc%d" % i, [128, 1024]) for i in range(2)]
                        k_ = 0
                        for (Wd, src) in ((W1, glu1), (W2, glu2), (Wo2, w_out_o)):
                            for kc in range(8):
                                s_ = stg[k_ % 2]; k_ += 1
                                P.dma(s_[:], src[kc * 128:(kc + 1) * 128, :])
                                P.cp("dve" if kc % 2 == 0 else "pool", Wd[:, kc * 1024:(kc + 1) * 1024], s_[:])
                        P.barrier()
                    for (t0, t1) in blocks:
                        nb = (t1 - t0) * 128
                        col0 = t0 * 128
                        for fc in range(8):
                            p1 = psA[0]; p2 = psA[1]
                            for kc in range(8):
                                P.mm(p1[:, 0:nb], W1[:, kc * 1024 + fc * 128:kc * 1024 + (fc + 1) * 128], uT[:, kc * NT + col0:kc * NT + col0 + nb], start=(kc == 0), stop=(kc == 7))
                            for kc in range(8):
                                P.mm(p2[:, 0:nb], W2[:, kc * 1024 + fc * 128:kc * 1024 + (fc + 1) * 128], uT[:, kc * NT + col0:kc * NT + col0 + nb], start=(kc == 0), stop=(kc == 7))
                            P.actv(sg[:, 0:nb], p2[:, 0:nb], AF.Sigmoid)
                            P.tt("dve", tg[:, 0:nb], p1[:, 0:nb], sg[:, 0:nb], ALU.mult)
                            P.tt("pool", oT[:, fc * 512:fc * 512 + nb], tg[:, 0:nb], szT[:, fc * NT + col0:fc * NT + col0 + nb], ALU.mult)
                        for ti in range(t0, t1):
                            lt = ti - t0
                            ytile = yt[cntb["x"] % 2]; cntb["x"] += 1
                            load_tile(ytile[:], ti)
                            for hf in range(2):
                                pa = psS[hf]
                                for fc in range(8):
                                    P.mm(pa[:, :], oT[:, fc * 512 + lt * 128:fc * 512 + (lt + 1) * 128], Wo2[:, fc * 1024 + hf * 512:fc * 1024 + (hf + 1) * 512], start=(fc == 0), stop=(fc == 7))
                                P.tt("dve", ytile[:, hf * 512:(hf + 1) * 512], ytile[:, hf * 512:(hf + 1) * 512], pa[:, :], ALU.add)
                            if ti < 16:
                                P.dma(yp[ti * 128:(ti + 1) * 128, :], ytile[:], reads=[ytile, ("y0", "p", ti)], writes=[("y0", "p", ti)])
                            else:
                                P.dma(ys, ytile[:], reads=[ytile, ("y0", "s", 0)], writes=[("y0", "s", 0)])
                    P.barrier()
        P.barrier()
        P.flush()
    return nc, in_names


_PROG_CACHE = {}


def _shared_inputs(inputs, consts):
    perm = _perm_even()
    sh = {}
    sh["w_in_e"] = np.ascontiguousarray(inputs["w_in_even"][0][:, perm])
    sh["w_out_e"] = np.ascontiguousarray(inputs["w_out_even"][0])
    sh["norm_e"] = np.ascontiguousarray(inputs["norm_even"][0].reshape(8, 128).T)
    sh["gn_gain"] = np.ascontiguousarray(inputs["ret_gn_gain"][0].reshape(1, 512))
    qn = inputs["nsa_q_norm"][0]
    kn = inputs["nsa_k_norm"][0]
    sh["qk_gain"] = np.concatenate([np.tile(qn, 8), np.tile(kn[0], 2), np.tile(kn[1], 2), np.tile(kn[2], 2)]).reshape(1, 896).astype(np.float32)
    sh["cmp_posT"] = np.ascontiguousarray(inputs["nsa_cmp_pos"][0].transpose(2, 0, 1))
    sh["cmp_w"] = np.ascontiguousarray(inputs["nsa_cmp_w"][0])
    for k, v in consts.items():
        sh["c_" + k] = v
    ca = np.ascontiguousarray
    sh["w_in_o"] = ca(inputs["w_in_odd"][0])
    sh["glu1"] = ca(inputs["glu_w1"][0])
    sh["glu2"] = ca(inputs["glu_w2"][0])
    sh["w_out_o"] = ca(inputs["w_out_odd"][0])
    sh["norm_o"] = ca(inputs["norm_odd"][0].reshape(8, 128).T)
    sh["ssmd"] = ca(inputs["ssm_d"][0].reshape(8, 128).T)
    sh["lamre_A"] = ca(inputs["ssm_lambda_re"][0].reshape(32, 2, 64).transpose(1, 2, 0).reshape(128, 32))
    sh["lamim_A"] = ca(inputs["ssm_lambda_im"][0].reshape(32, 2, 64).transpose(1, 2, 0).reshape(128, 32))
    ls = np.broadcast_to(inputs["ssm_log_step"][0].reshape(32, 2, 1), (32, 2, 64))
    sh["lstep_A"] = ca(ls.transpose(1, 2, 0).reshape(128, 32)).astype(np.float32)
    sh["bA_re"] = ca(inputs["ssm_b_re"][0].reshape(32, 2, 64, 16).transpose(1, 2, 0, 3).reshape(128, 512))
    sh["bA_im"] = ca(inputs["ssm_b_im"][0].reshape(32, 2, 64, 16).transpose(1, 2, 0, 3).reshape(128, 512))
    sh["cA_re"] = ca(inputs["ssm_c_re"][0].reshape(32, 2, 16, 64).transpose(1, 3, 0, 2).reshape(128, 512))
    sh["cA_im"] = ca(inputs["ssm_c_im"][0].reshape(32, 2, 16, 64).transpose(1, 3, 0, 2).reshape(128, 512))
    return sh


def _core_inputs(inputs, c, sh, cache_rows):
    im = dict(sh)
    im["xp"] = np.ascontiguousarray(inputs["x_prompt"][c])
    im["xs"] = np.ascontiguousarray(inputs["x_sample"][16 * c:16 * c + 16].reshape(128, 1024))
    im["cache"] = inputs["cache_nsa_kv"][0].reshape(2560 * 128, 512)[0:cache_rows]
    im["cwin"] = np.ascontiguousarray(inputs["cache_nsa_win"][0, 16 * c:16 * c + 16].reshape(16, 512, 256))
    im["sret"] = np.ascontiguousarray(inputs["state_ret"][0, 16 * c:16 * c + 16])
    im["x0A_re"] = np.ascontiguousarray(inputs["state_ssm_re"][0, 16 * c:16 * c + 16].reshape(16, 32, 2, 64).transpose(2, 3, 1, 0).reshape(128, 512))
    im["x0A_im"] = np.ascontiguousarray(inputs["state_ssm_im"][0, 16 * c:16 * c + 16].reshape(16, 32, 2, 64).transpose(2, 3, 1, 0).reshape(128, 512))
    im["ptab"] = np.ascontiguousarray(inputs["page_table"][16 * c:16 * c + 16].reshape(1, 256)).astype(np.int32)
    return im


def run_cores(inputs, cores, **opts):
    consts = make_consts()
    key = tuple(sorted(opts.items()))
    nc, in_names = build_program(consts, **opts)
    sh = _shared_inputs(inputs, consts)
    cache_rows = 2560 * 128 if opts.get("do_sample", True) else 128
    in_maps = [_core_inputs(inputs, c, sh, cache_rows) for c in cores]
    in_maps = [{k: m[k] for k in in_names} for m in in_maps]
    res = run_bass_kernel_spmd(nc, in_maps, core_ids=list(range(len(cores))))
    return res.results


def kernel(**inputs):
    inputs = {k: np.asarray(v) for k, v in inputs.items()}
    res = run_cores(inputs, list(range(NCORES)))
    f32 = np.float32
    y_p = np.zeros((8, 2048, 1024), f32)
    y_s = np.zeros((128, 8, 1024), f32)
    ret_p = np.zeros((1, 8, 4, 64, 128), f32)
    ret_s = np.zeros((1, 128, 4, 64, 128), f32)
    kv_p = np.zeros((1, 8, 2048, 4, 2, 64), f32)
    kv_s = np.zeros((1, 128, 8, 4, 2, 64), f32)
    win_p = np.zeros((1, 8, 512, 2, 2, 64), f32)
    win_s = np.zeros((1, 128, 512, 2, 2, 64), f32)
    sre_p = np.zeros((1, 8, 64, 64), f32)
    sim_p = np.zeros((1, 8, 64, 64), f32)
    sre_s = np.zeros((1, 128, 64, 64), f32)
    sim_s = np.zeros((1, 128, 64, 64), f32)
    for c in range(NCORES):
        r = res[c]
        sl = slice(16 * c, 16 * c + 16)
        y_p[c] = r["yp"]
        y_s[sl] = r["ys"].reshape(16, 8, 1024)
        ret_p[0, c] = r["o_retp"]
        ret_s[0, sl] = r["o_rets"]
        kv_p[0, c] = r["o_kvp"].reshape(2048, 4, 2, 64)
        kv_s[0, sl] = r["o_kvs"].reshape(16, 8, 4, 2, 64)
        win_p[0, c] = r["o_winp"].reshape(512, 2, 2, 64)
        win_s[0, sl] = r["o_wins"].reshape(16, 512, 2, 2, 64)
        if "o_ssp" in r:
            sp = r["o_ssp"].reshape(2, 2, 64, 32).transpose(0, 3, 1, 2).reshape(2, 64, 64)
            sre_p[0, c] = sp[0]
            sim_p[0, c] = sp[1]
            ss = r["o_sss"].reshape(2, 2, 64, 32, 16).transpose(0, 4, 3, 1, 2).reshape(2, 16, 64, 64)
            sre_s[0, sl] = ss[0]
            sim_s[0, sl] = ss[1]
    return (y_p, y_s, ret_p, ret_s, kv_p, kv_s, win_p, win_s, sre_p, sim_p, sre_s, sim_s)
```

```python
import numpy as np
import concourse.bass as bass
import concourse.mybir as mybir
from concourse.bass_utils import run_bass_kernel_spmd

F32 = mybir.dt.float32
BF16 = mybir.dt.bfloat16
I32 = mybir.dt.int32
AF = mybir.ActivationFunctionType
ALU = mybir.AluOpType
AX = mybir.AxisListType

ENGS = ("pe", "act", "dve", "pool", "sp")
NDMASEM = 24


class Prog:
    def __init__(self, nc):
        self.nc = nc
        self.ops = {e: [] for e in ENGS}
        self.cnt = {e: 0 for e in ENGS}
        self.sem = {}
        self.seen = {e: {} for e in ENGS}
        self.last_w = {}
        self.readers = {}
        self.dma_i = 0
        self.dma_uses = [0] * NDMASEM
        self.dma_tok = [None] * NDMASEM
        self.stack = None
        self.n_ops = 0

    def setup(self, stack):
        self.stack = stack
        for e in ENGS:
            self.sem[e] = stack.enter_context(self.nc.semaphore("s_" + e))
        for i in range(NDMASEM):
            self.sem["d%d" % i] = stack.enter_context(self.nc.semaphore("s_d%d" % i))

    def _key(self, a):
        if isinstance(a, str):
            return a
        if isinstance(a, tuple):
            return a
        t = getattr(a, 'tensor', None)
        return t.name if t is not None else a.name

    def _deps(self, eng, reads, writes):
        toks = []
        for k in reads:
            k = self._key(k)
            t = self.last_w.get(k)
            if t is not None:
                toks.append(t)
        for k in writes:
            k = self._key(k)
            t = self.last_w.get(k)
            if t is not None:
                toks.append(t)
            toks.extend(self.readers.get(k, ()))
        need = {}
        for (s, v) in toks:
            if eng == "pe" and s == "pe":
                continue
            if v > need.get(s, 0):
                need[s] = v
        waits = []
        seen = self.seen[eng]
        for s, v in need.items():
            if seen.get(s, 0) >= v:
                continue
            seen[s] = v
            waits.append((s, v))
        return waits

    def _commit(self, tok, reads, writes):
        for k in writes:
            k = self._key(k)
            self.last_w[k] = tok
            self.readers[k] = []
        for k in reads:
            k = self._key(k)
            self.readers.setdefault(k, []).append(tok)

    def op(self, eng, fn, reads=(), writes=()):
        waits = self._deps(eng, reads, writes)
        self.cnt[eng] += 1
        tok = (eng, self.cnt[eng])
        self.ops[eng].append((waits, fn, (eng, 1)))
        self._commit(tok, reads, writes)
        self.n_ops += 1

    def dma(self, out, in_, reads=None, writes=None, q="sp", fn=None):
        if reads is None:
            reads = [in_]
        if writes is None:
            writes = [out]
        i = self.dma_i % NDMASEM
        self.dma_i += 1
        sname = "d%d" % i
        waits = self._deps(q, reads, writes)
        prev = self.dma_tok[i]
        if prev is not None and self.seen[q].get(sname, 0) < prev[1]:
            self.seen[q][sname] = prev[1]
            waits.append(prev)
        self.dma_uses[i] += 1
        tok = (sname, 16 * self.dma_uses[i])
        self.dma_tok[i] = tok
        if fn is None:
            fn = lambda e, o=out, a=in_: e.dma_start(out=o, in_=a)
        self.ops[q].append((waits, fn, (sname, 16)))
        self._commit(tok, reads, writes)
        self.n_ops += 1

    def barrier(self):
        for e in ENGS:
            waits = []
            for e2 in ENGS:
                if e2 == e:
                    continue
                v = self.cnt[e2]
                if v > self.seen[e].get(e2, 0):
                    self.seen[e][e2] = v
                    waits.append((e2, v))
            for i in range(NDMASEM):
                t = self.dma_tok[i]
                if t is not None and self.seen[e].get(t[0], 0) < t[1]:
                    self.seen[e][t[0]] = t[1]
                    waits.append(t)
            if waits:
                self.ops[e].append((waits, None, None))

    def flush(self):
        nc = self.nc
        ops = self.ops
        sem = self.sem

        def replay(engh, lst):
            for (waits, fn, inc) in lst:
                for (s, v) in waits:
                    engh.wait_ge(sem[s], v)
                if fn is not None:
                    ins = fn(engh)
                    ins.then_inc(sem[inc[0]], inc[1])

        with nc.Block() as block:
            @block.tensor
            def _(e):
                replay(e, ops["pe"])

            @block.scalar
            def _(e):
                replay(e, ops["act"])

            @block.vector
            def _(e):
                replay(e, ops["dve"])

            @block.gpsimd
            def _(e):
                replay(e, ops["pool"])

            @block.sync
            def _(e):
                replay(e, ops["sp"])
        self.ops = {e: [] for e in ENGS}

    def mm(self, out, lhsT, rhs, start=True, stop=True, reads=None, writes=None):
        if reads is None:
            reads = [lhsT, rhs]
        if writes is None:
            writes = [out]
        self.op("pe", lambda e: e.matmul(out, lhsT, rhs, start=start, stop=stop), reads, writes)

    def tr(self, out, in_, ident, reads=None, writes=None):
        if reads is None:
            reads = [in_, ident]
        if writes is None:
            writes = [out]
        self.op("pe", lambda e: e.transpose(out, in_, ident), reads, writes)

    def actv(self, out, in_, func, bias=None, scale=None, accum_out=None, reads=None, writes=None, eng="act"):
        kw = {}
        if bias is not None:
            kw["bias"] = bias
        if scale is not None:
            kw["scale"] = scale
        if accum_out is not None:
            kw["accum_out"] = accum_out
        if reads is None:
            reads = [in_]
            if bias is not None and not isinstance(bias, (int, float)):
                reads.append(bias)
            if scale is not None and not isinstance(scale, (int, float)):
                reads.append(scale)
        if writes is None:
            writes = [out]
            if accum_out is not None:
                writes.append(accum_out)
        self.op("act", lambda e: e.activation(out, in_, func, **kw), reads, writes)

    def ts(self, eng, out, in0, s1, s2, op0, op1=None, accum_out=None, reads=None, writes=None):
        kw = {}
        if op1 is not None:
            kw["op1"] = op1
        if accum_out is not None:
            kw["accum_out"] = accum_out
        if reads is None:
            reads = [in0]
            for s in (s1, s2):
                if s is not None and not isinstance(s, (int, float)):
                    reads.append(s)
        if writes is None:
            writes = [out]
            if accum_out is not None:
                writes.append(accum_out)
        self.op(eng, lambda e: e.tensor_scalar(out, in0, s1, s2, op0, **kw), reads, writes)

    def tt(self, eng, out, in0, in1, op, reads=None, writes=None):
        if reads is None:
            reads = [in0, in1]
        if writes is None:
            writes = [out]
        self.op(eng, lambda e: e.tensor_tensor(out, in0, in1, op), reads, writes)

    def stt(self, eng, out, in0, scalar, in1, op0, op1, accum_out=None, reads=None, writes=None):
        kw = {}
        if accum_out is not None:
            kw["accum_out"] = accum_out
        if reads is None:
            reads = [in0, in1]
            if not isinstance(scalar, (int, float)):
                reads.append(scalar)
        if writes is None:
            writes = [out]
            if accum_out is not None:
                writes.append(accum_out)
        self.op(eng, lambda e: e.scalar_tensor_tensor(out, in0, scalar, in1, op0, op1, **kw), reads, writes)

    def cp(self, eng, out, in_, reads=None, writes=None):
        if reads is None:
            reads = [in_]
        if writes is None:
            writes = [out]
        if eng == "act":
            self.op(eng, lambda e: e.copy(out, in_), reads, writes)
        else:
            self.op(eng, lambda e: e.tensor_copy(out, in_), reads, writes)

    def memset(self, eng, ap, val, writes=None):
        if writes is None:
            writes = [ap]
        self.op(eng, lambda e: e.memset(ap, val), (), writes)

import math
from contextlib import ExitStack
import ml_dtypes

BF = ml_dtypes.bfloat16
NCORES = 8
BIG = 1.0e4
FORCE = 1.0e4
EPS = 1e-6
SCALE = 0.125
QA, KA, QN, KC, KS, KW, VC, VS, VW, GL, VA, ZA, ZB = 0, 256, 512, 1024, 1152, 1280, 1408, 1536, 1664, 1792, 1816, 2328, 2840
EIN = 3352
GROUPS = [(0, 512), (512, 1024), (1024, 1408), (1408, 1816), (1816, 2328), (2328, 2840), (2840, 3352)]


def _perm_even():
    perm = np.zeros(EIN, np.int64)
    perm[QA:QA + 256] = np.arange(0, 256)
    perm[KA:KA + 256] = np.arange(256, 512)
    o_va, o_za, o_qn, o_kvb, o_gl, o_zb = 512, 1024, 1536, 2048, 2816, 2840
    for j in range(4):
        for g in range(2):
            for dd in range(64):
                perm[QN + (j * 2 + g) * 64 + dd] = o_qn + (4 * g + j) * 64 + dd
    for dst, kind in ((KC, 0), (KS, 2), (KW, 4), (VC, 1), (VS, 3), (VW, 5)):
        perm[dst:dst + 128] = o_kvb + kind * 128 + np.arange(128)
    perm[GL:GL + 24] = o_gl + np.arange(24)
    perm[VA:VA + 512] = o_va + np.arange(512)
    perm[ZA:ZA + 512] = o_za + np.arange(512)
    perm[ZB:ZB + 512] = o_zb + np.arange(512)
    assert len(set(perm.tolist())) == EIN
    return perm


def make_consts():
    c = {}
    f32 = np.float32
    c["ident"] = np.eye(128, dtype=f32).astype(BF)
    half = 32
    inv = (np.float32(10000.0) ** (-np.arange(half, dtype=f32) / f32(half))).astype(f32)
    pos_p = (np.arange(16)[None, :] * 128 + np.arange(128)[:, None]).astype(f32)
    ang = (pos_p[:, :, None] * inv[None, None, :]).astype(f32)
    c["cos_p"] = np.cos(ang).astype(f32)
    c["sin_p"] = np.sin(ang).astype(f32)
    pos_s = (2048 + (np.arange(128) % 8)).astype(f32)
    ang = (pos_s[:, None] * inv[None, :]).astype(f32)
    c["cos_s"] = np.cos(ang).astype(f32)
    c["sin_s"] = np.sin(ang).astype(f32)
    log_g = np.log(1.0 - 2.0 ** (-5.0 - np.arange(4, dtype=np.float64)))
    i = np.arange(128)
    diff = i[None, :] - i[:, None]
    caus = diff >= 0
    DT = np.zeros((128, 4, 128), f32)
    for h in range(4):
        DT[:, h, :] = 0.125 * np.exp(np.where(caus, diff, 0) * log_g[h]) * caus
    c["DTp"] = DT
    c["qdec_p"] = np.exp((i[:, None] + 1.0) * log_g[None, :]).astype(f32)
    c["kdec_p"] = (0.125 * np.exp((127.0 - i[:, None]) * log_g[None, :])).astype(f32)
    cd = np.zeros((128, 2), f32)
    cds = np.zeros((128, 2), f32)
    for p in range(128):
        for pr in range(2):
            h = 2 * pr + p // 64
            cd[p, pr] = np.exp(128.0 * log_g[h])
            cds[p, pr] = np.exp(8.0 * log_g[h])
    c["cdec_p"] = cd
    c["cdec_s"] = cds
    i8 = i % 8
    same = (i[:, None] // 8) == (i[None, :] // 8)
    diff8 = i8[None, :] - i8[:, None]
    caus8 = same & (diff8 >= 0)
    DTs = np.zeros((128, 4, 128), f32)
    for h in range(4):
        DTs[:, h, :] = 0.125 * np.exp(np.where(caus8, diff8, 0) * log_g[h]) * caus8
    c["DTs"] = DTs
    c["qdec_s"] = np.exp((i8[:, None] + 1.0) * log_g[None, :]).astype(f32)
    c["kdec_s"] = (0.125 * np.exp((7.0 - i8[:, None]) * log_g[None, :])).astype(f32)
    c["blkmask"] = ((i[:, None] // 8) == np.arange(16)[None, :]).astype(f32)
    cm = np.zeros((128, 16, 128), f32)
    cm[:, :, :] = ((i[None, :] // 8) == np.arange(16)[:, None])[None, :, :]
    c["colmask"] = cm.astype(BF)
    keys = np.arange(2176)
    E = (keys[None, :] // 64 == np.arange(33)[:, None]).astype(f32)
    c["E"] = E.astype(BF)
    c["CB"] = np.where(i[:, None] <= i[None, :], 0.0, -BIG).astype(f32).astype(BF)
    c["AB"] = np.where(i[:, None] > i[None, :], 0.0, -BIG).astype(f32).astype(BF)
    n = np.arange(128)
    cmpb = np.zeros((128, 16, 128), f32)
    for t in range(16):
        tpos = 128 * t + i
        cmpb[:, t, :] = np.where((16 * n[:, None] + 31) <= tpos[None, :], 0.0, -BIG)
    c["CMPB"] = cmpb.astype(BF)

    def overlap(n_s):
        c_start = np.arange(127) * 16
        s_start = np.arange(n_s) * 64
        return ((c_start[:, None] < s_start[None, :] + 64) & (s_start[None, :] < c_start[:, None] + 32)).astype(f32)
    c["ovl_p"] = overlap(32)
    c["ovl_s"] = overlap(33)
    mulc = np.zeros((128, 16, 32), f32)
    addc = np.zeros((128, 16, 32), f32)
    s_ids = np.arange(32)
    for t in range(16):
        tpos = 128 * t + i
        cur = tpos // 64
        forced = (s_ids[None, :] == 0) | (s_ids[None, :] == cur[:, None]) | (s_ids[None, :] == cur[:, None] - 1)
        valid = (s_ids[None, :] * 64) <= tpos[:, None]
        mulc[:, t, :] = (valid & ~forced)
        addc[:, t, :] = np.where(valid, np.where(forced, FORCE, 0.0), -FORCE)
    c["mulc_p"] = mulc
    c["addc_p"] = addc
    s33 = np.arange(33)
    forced = (s33 == 0) | (s33 == 32) | (s33 == 31)
    c["mulc_s"] = np.tile((~forced).astype(f32)[None, :], (8, 1))
    c["addc_s"] = np.tile(np.where(forced, FORCE, 0.0).astype(f32)[None, :], (8, 1))
    q8 = np.arange(8)
    SB = np.where(((i[:, None, None] // 8) == np.arange(16)[None, :, None]) & ((i[:, None, None] % 8) <= q8[None, None, :]), 0.0, -BIG)
    c["SB"] = SB.astype(f32).astype(BF)
    c["ABs"] = np.where(i[:, None] > q8[None, :], 0.0, -BIG).astype(f32).astype(BF)
    c["iota_p"] = i.astype(f32)[:, None].copy()
    c["ones_row"] = np.ones((1, 128), f32).astype(BF)
    c["zeros_row"] = np.zeros((1, 512), f32).astype(BF)
    jv = np.zeros((128, 9, 32), f32)
    jv[:, :, :] = np.arange(9, dtype=f32)[None, :, None]
    c["JV"] = jv
    c["KI"] = np.tile(np.arange(256, dtype=f32)[None, :], (128, 1))
    c["PM4"] = (np.arange(128)[:, None] // 32 == np.arange(4)[None, :]).astype(f32)
    c["BM16"] = (np.arange(128)[:, None] // 16 == np.arange(128)[None, :] // 16).astype(f32)
    return c


def _dt_of(a):
    if a.dtype == np.float32:
        return F32
    if a.dtype == np.int32:
        return I32
    if a.dtype == BF:
        return BF16
    raise ValueError(a.dtype)


def v3(ap, h):
    return ap.rearrange("p (h d) -> p h d", h=h)


def bc_mid(ap, n):
    return ap.unsqueeze(1).to_broadcast([ap.shape[0], n, ap.shape[1]])


def bc_last(ap, n):
    return ap.unsqueeze(2).to_broadcast([ap.shape[0], ap.shape[1], n])


def build_program(consts, phase_b=True, do_sample=True, n_ptiles=16, stage=99):
    nc = bass.Bass("TRN2", target_bir_lowering=False)

    in_names = []

    def din(name, shape, dt=F32):
        in_names.append(name)
        return nc.dram_tensor(name, list(shape), dt, kind="ExternalInput").ap()

    def dout(name, shape, dt=F32):
        return nc.dram_tensor(name, list(shape), dt, kind="ExternalOutput").ap()

    xp = din("xp", [2048, 1024])
    xs = din("xs", [128, 1024])
    if do_sample:
        cache = din("cache", [2560 * 128, 512])
        cwin = din("cwin", [16, 512, 256])
        sret = din("sret", [16, 4, 64, 128])
        ptab = din("ptab", [1, 256], I32)
    w_in_e = din("w_in_e", [1024, EIN])
    w_out_e = din("w_out_e", [1024, 1024])
    norm_e = din("norm_e", [128, 8])
    gn_gain = din("gn_gain", [1, 512])
    qk_gain = din("qk_gain", [1, 896])
    cmp_posT = din("cmp_posT", [64, 2, 32])
    cmp_w = din("cmp_w", [2, 32, 64, 64])
    S_ONLY = ("cos_s", "sin_s", "DTs", "qdec_s", "kdec_s", "cdec_s", "blkmask", "colmask", "SB", "ABs", "mulc_s", "addc_s", "ovl_s", "iota_p")
    if phase_b:
        w_in_o = din("w_in_o", [1024, 2048])
        glu1 = din("glu1", [1024, 1024])
        glu2 = din("glu2", [1024, 1024])
        w_out_o = din("w_out_o", [1024, 1024])
        norm_o = din("norm_o", [128, 8])
        ssmd = din("ssmd", [128, 8])
        lamre_A = din("lamre_A", [128, 32])
        lamim_A = din("lamim_A", [128, 32])
        lstep_A = din("lstep_A", [128, 32])
        bA_re = din("bA_re", [128, 512])
        bA_im = din("bA_im", [128, 512])
        cA_re = din("cA_re", [128, 512])
        cA_im = din("cA_im", [128, 512])
        x0A_re = din("x0A_re", [128, 512])
        x0A_im = din("x0A_im", [128, 512])
    cd = {k: din("c_" + k, v.shape, _dt_of(v)) for k, v in consts.items() if ((do_sample or k not in S_ONLY) and (phase_b or k not in ("JV", "KI", "PM4", "BM16")))}
    yp = dout("yp", [2048, 1024])
    ys = dout("ys", [128, 1024])
    o_retp = dout("o_retp", [4, 64, 128])
    o_rets = dout("o_rets", [16, 4, 64, 128])
    o_kvp = dout("o_kvp", [2048, 512])
    o_kvs = dout("o_kvs", [128, 512])
    o_winp = dout("o_winp", [512, 256])
    o_wins = dout("o_wins", [16, 512, 256])
    if phase_b:
        o_ssp = dout("o_ssp", [2, 128, 32])
        o_sss = dout("o_sss", [2, 128, 512])

    P = Prog(nc)
    with ExitStack() as st0:
        P.setup(st0)

        def alloc(st, name, shape, dt=F32):
            return st.enter_context(nc.sbuf_tensor(name, list(shape), dt))

        def palloc(st, name, shape, dt=F32):
            return st.enter_context(nc.psum_tensor(name, list(shape), dt))

        psT = palloc(st0, "psT", [128, 1024], BF16)
        psA = [palloc(st0, "psA%d" % i, [128, 512]) for i in range(2)]
        psS = [palloc(st0, "psS%d" % i, [128, 512]) for i in range(2)]
        psV = palloc(st0, "psV", [128, 512])
        psC = palloc(st0, "psC", [128, 512])
        psR = palloc(st0, "psR", [128, 512])

        with ExitStack() as stA:
            A = lambda name, shape, dt=F32: alloc(stA, name, shape, dt)
            cs = {}
            P_ONLY = ("cos_p", "sin_p", "DTp", "qdec_p", "kdec_p", "cdec_p", "CMPB", "mulc_p", "addc_p", "ovl_p", "CB", "AB")

            def load_consts(stx, names, pre="k_"):
                for k in names:
                    v = consts[k]
                    shp = list(v.shape)
                    if len(shp) == 3:
                        tl = alloc(stx, pre + k, [shp[0], shp[1] * shp[2]], _dt_of(v))
                        P.dma(tl[:], cd[k].rearrange("p a b -> p (a b)"))
                    else:
                        tl = alloc(stx, pre + k, shp, _dt_of(v))
                        P.dma(tl[:], cd[k])
                    cs[k] = tl
            B_ONLY = ("JV", "KI", "PM4", "BM16")
            load_consts(stA, [k for k in consts if k not in P_ONLY and k not in S_ONLY and k not in B_ONLY])
            ident = cs["ident"]
            Wob = A("Wob", [128, 8 * 1024], BF16)
            ng = A("ng", [128, 8])
            BDW = A("BDW", [128, 2 * 32 * 128], BF16)
            peb = A("peb", [128, 64], BF16)
            posk = A("posk", [128, 1])
            posv = A("posv", [1, 128], BF16)
            gnb = A("gnb", [128, 512])
            P.dma(gnb[:], gn_gain.partition_broadcast(128))
            qkg = A("qkg", [128, 896])
            P.dma(qkg[:], qk_gain.partition_broadcast(128))

            xt = [A("xt%d" % i, [128, 1024]) for i in range(2)]
            xb = A("xb", [128, 1024], BF16)
            xT = A("xT", [128, 1024], BF16)
            stt_ = A("stats", [128, 64])
            proj = A("proj", [128, EIN])
            rp = A("rp", [128, 1408])
            tmpa = A("tmpa", [128, 1408])
            r16 = A("r16", [128, 1024], BF16)
            vb16 = A("vb16", [128, 512], BF16)
            rT = A("rT", [128, 256], BF16)
            qz = A("qz", [128, 1024], BF16)
            qTz = A("qTz", [128, 1024], BF16)
            scm = A("scm", [128, 512], BF16)
            S32 = A("S32", [128, 256])
            Sb = A("Sb", [128, 256], BF16)
            osb = A("osb", [128, 512])
            ocb = A("ocb", [128, 512])
            sz = A("sz", [128, 512])
            mix = A("mix", [128, 1024], BF16)
            mixT = A("mixT", [128, 1024], BF16)
            n16 = A("n16", [128, 1024], BF16)
            gts = A("gts", [128, 24])
            on = A("on", [128, 3 * 512])
            tmpb = on
            obt = A("obt", [128, 512])
            pt_ = [A("pt%d" % i, [128, 512], BF16) for i in range(3)]
            selT = A("selT", [33, 256], BF16)
            sc_ = A("sc", [128, 80])
            sc2 = A("sc2", [128, 80])
            sc16 = A("sc16", [128, 80], BF16)
            rden = A("rden", [128, 32])
            top8 = A("top8", [128, 16])
            P.memset("dve", S32[:], 0.0)
            P.memset("dve", Sb[:], 0.0)
            P.memset("pool", qz[:], 0.0)
            P.memset("pool", qTz[:], 0.0)

            stWb = ExitStack()
            Wb = alloc(stWb, "Wb", [128, 8 * EIN], BF16)
            P.dma(ng[:], norm_e)
            with ExitStack() as stW:
                stg = [alloc(stW, "stg%d" % i, [128, EIN]) for i in range(2)]
                for kc in range(8):
                    s_ = stg[kc % 2]
                    P.dma(s_[:], w_in_e[kc * 128:(kc + 1) * 128, :])
                    if kc % 2 == 0:
                        P.ts("dve", Wb[:, kc * EIN:(kc + 1) * EIN], s_[:], ng[:, kc:kc + 1], None, ALU.mult)
                    else:
                        P.op("act", lambda e, kc=kc, s_=s_: e.mul(Wb[:, kc * EIN:(kc + 1) * EIN], s_[:], ng[:, kc:kc + 1]), [s_, ng], [Wb])
                for kc in range(8):
                    s_ = stg[kc % 2]
                    P.dma(s_[:, 0:1024], w_out_e[kc * 128:(kc + 1) * 128, :])
                    if kc % 2 == 0:
                        P.cp("dve", Wob[:, kc * 1024:(kc + 1) * 1024], s_[:, 0:1024])
                    else:
                        P.cp("pool", Wob[:, kc * 1024:(kc + 1) * 1024], s_[:, 0:1024])
                P.barrier()
            with ExitStack() as stW:
                P.memset("pool", BDW[:], 0.0)
                wst = alloc(stW, "wst", [128, 2 * 32 * 64])
                srcw = cmp_w.rearrange("c l d e -> d (c l) e")
                P.dma(wst[0:64, :].rearrange("p (a e) -> p a e", e=64), srcw)
                P.dma(wst[64:128, :].rearrange("p (a e) -> p a e", e=64), srcw)
                bdv = BDW[:].rearrange("p (a e) -> p a e", e=128)
                P.cp("dve", bdv[0:64, :, 0:64], wst[0:64, :].rearrange("p (a e) -> p a e", e=64))
                P.cp("dve", bdv[64:128, :, 64:128], wst[64:128, :].rearrange("p (a e) -> p a e", e=64))
                pe32 = alloc(stW, "pe32", [128, 64])
                P.dma(pe32[0:64, :], cmp_posT.rearrange("d c l -> d (c l)"))
                P.dma(pe32[64:128, :], cmp_posT.rearrange("d c l -> d (c l)"))
                P.cp("dve", peb[:], pe32[:])
                for l in range(32):
                    P.mm(psC[:, 0:1], BDW[:, (0 * 32 + l) * 128:(0 * 32 + l + 1) * 128], peb[:, l:l + 1], start=(l == 0), stop=(l == 31))
                P.cp("dve", posk[:], psC[:, 0:1])
                for l in range(32):
                    P.mm(psV[0:1, 0:128], peb[:, 32 + l:32 + l + 1], BDW[:, (32 + l) * 128:(32 + l + 1) * 128], start=(l == 0), stop=(l == 31))
                P.cp("dve", posv[:], psV[0:1, 0:128])
                P.barrier()
            cnt = {"x": 0, "pt": 0, "ps": 0, "pa": 0}
            xpref = {}

            def rsqrt_small(out, in_, mult, add):
                P.ts("dve", out, in_, mult, add, ALU.mult, ALU.add)
                P.actv(out, out, AF.Sqrt)
                P.op("dve", lambda e: e.reciprocal(out, out), [out], [out])

            def nsa_tile(nq, groups, ns, mulc, addc, cmp_bias):
                W = 4 * nq
                wv = 65 + ns
                R = [dict(cmp=psA[0], win=psR, sel=psV, o=0), dict(cmp=psA[1], win=psC, sel=psA[0], o=40)]
                v4 = lambda ap: ap.rearrange("p (j q) -> p j q", j=4)
                for g, G in enumerate(groups):
                    r = R[g]; o = r["o"]
                    ps_s = psS[cnt["ps"] % 2]; cnt["ps"] += 1
                    P.mm(v4(ps_s[0:127, 0:W]), G["cmp_k"], G["qTg"], start=True, stop=(cmp_bias is None))
                    if cmp_bias is not None:
                        P.mm(v4(ps_s[0:127, 0:W]), ident[0:127, 0:127], cmp_bias, start=False, stop=True)
                    pt = pt_[cnt["pt"] % 3]; cnt["pt"] += 1
                    P.actv(pt[0:127, 0:W], ps_s[0:127, 0:W], AF.Exp, scale=SCALE)
                    pc = r["cmp"]
                    for j in range(4):
                        P.mm(pc[0:nq, j * wv:(j + 1) * wv], pt[0:127, j * nq:(j + 1) * nq], G["cmp_v"], start=True, stop=True)
                    pcv = pc[0:nq, 0:4 * wv].rearrange("p (j w) -> p j w", j=4)
                    rd = rden[0:nq, 16 * g:16 * g + 4]
                    P.ts("dve", rd, pcv[:, :, 64], 1e-30, None, ALU.add)
                    P.op("dve", lambda e, rd=rd: e.reciprocal(rd, rd), [rden], [rden])
                    P.tt("dve", G["on_dst"](0), pcv[:, :, 0:64], bc_last(rd, 64), ALU.mult)
                    sc = sc_[0:nq, o:o + ns]
                    P.ts("dve", sc, pcv[:, 0, 65:65 + ns], rden[0:nq, 16 * g:16 * g + 1], None, ALU.mult)
                    for j in range(1, 4):
                        P.stt("dve", sc, pcv[:, j, 65:65 + ns], rden[0:nq, 16 * g + j:16 * g + j + 1], sc, ALU.mult, ALU.add)
                    P.tt("dve", sc, sc, mulc, ALU.mult)
                    P.tt("dve", sc, sc, addc, ALU.add)
                    t8 = top8[0:nq, 8 * g:8 * g + 8]
                    P.op("dve", lambda e, t8=t8, sc=sc: e.max(t8, sc), [sc_], [top8])
                    P.ts("dve", sc2[0:nq, o:o + ns], sc, top8[0:nq, 8 * g + 7:8 * g + 8], BIG, ALU.is_ge, ALU.mult)
                    P.ts("dve", sc16[0:nq, o:o + ns], sc2[0:nq, o:o + ns], -BIG, None, ALU.add)

                def scores(stp):
                    (g, br, ci, nch, chk) = stp["d"]
                    (kT, v1, nk, ecols, bias) = chk[0:5]
                    kkey = chk[5] if len(chk) > 5 and chk[5] is not None else kT
                    ps_s = psS[cnt["ps"] % 2]; cnt["ps"] += 1
                    stp["ps"] = ps_s
                    use_e = (br == 1 and ecols is not None)
                    extra = (1 if use_e else 0) + (1 if bias is not None else 0)
                    P.mm(v4(ps_s[0:nk, 0:W]), kT, groups[g]["qTg"], start=True, stop=(extra == 0), reads=[kkey, groups[g]["qTg"]])
                    if bias is not None:
                        extra -= 1
                        P.mm(v4(ps_s[0:nk, 0:W]), ident[0:nk, 0:nk], bias, start=False, stop=(extra == 0))
                    if use_e:
                        P.mm(v4(ps_s[0:nk, 0:W]), ecols, bc_mid(selT[0:ns, 128 * g:128 * g + nq], 4), start=False, stop=True)

                def exp_pv(stp):
                    (g, br, ci, nch, chk) = stp["d"]
                    (kT, v1, nk, ecols, bias) = chk[0:5]
                    vkey = chk[6] if len(chk) > 6 and chk[6] is not None else v1
                    ps_s = stp["ps"]
                    pt = pt_[cnt["pt"] % 3]; cnt["pt"] += 1
                    P.actv(pt[0:nk, 0:W], ps_s[0:nk, 0:W], AF.Exp, scale=SCALE)
                    acc = R[g]["sel" if br == 1 else "win"]
                    for j in range(4):
                        P.mm(acc[0:nq, j * 65:(j + 1) * 65], pt[0:nk, j * nq:(j + 1) * nq], v1, start=False, stop=(ci == nch - 1), reads=[pt, vkey])

                def run_steps(steps):
                    n = len(steps)
                    if n == 0:
                        return
                    scores(steps[0])
                    for i in range(n):
                        if i + 1 < n:
                            scores(steps[i + 1])
                        exp_pv(steps[i])

                def normalize(g, br):
                    acc = R[g]["sel" if br == 1 else "win"]
                    pvv = acc[0:nq, 0:260].rearrange("p (j w) -> p j w", j=4)
                    rd = rden[0:nq, 16 * g + 4 * br:16 * g + 4 * br + 4]
                    P.ts("dve", rd, pvv[:, :, 64], 1e-30, None, ALU.add)
                    P.op("dve", lambda e, rd=rd: e.reciprocal(rd, rd), [rden], [rden])
                    P.tt("dve", groups[g]["on_dst"](br), pvv[:, :, 0:64], bc_last(rd, 64), ALU.mult)

                for br, key in ((2, "win_chunks"), (1, "sel_chunks")):
                    if br == 1:
                        for g in range(len(groups)):
                            o = R[g]["o"]
                            pc0 = 384 + 512 * g
                            P.tr(psT[0:ns, pc0:pc0 + nq], sc16[0:nq, o:o + ns], ident[0:nq, 0:nq])
                            P.cp("dve", selT[0:ns, 128 * g:128 * g + nq], psT[0:ns, pc0:pc0 + nq])
                    steps = []
                    for g, G in enumerate(groups):
                        acc = R[g]["sel" if br == 1 else "win"]
                        P.mm(acc[0:nq, 0:260], cs["zeros_row"][0:1, 0:nq], cs["zeros_row"][0:1, 0:260], start=True, stop=False)
                        chunks = G[key]
                        for ci, ch in enumerate(chunks):
                            steps.append({"d": (g, br, ci, len(chunks), ch)})
                    run_steps(steps)
                    for g in range(len(groups)):
                        normalize(g, br)

            def even_tile(mode, t, caches):
                isp = (mode == "p")
                xsrc = xp[t * 128:(t + 1) * 128, :] if isp else xs
                if xpref.get("cur") == (mode, t):
                    xtile = xpref["tile"]
                else:
                    xtile = xt[cnt["x"] % 2]; cnt["x"] += 1
                    P.dma(xtile[:], xsrc)
                nxt = caches.get("next")
                if nxt is not None:
                    ntile = xt[cnt["x"] % 2]; cnt["x"] += 1
                    nsrc = xp[nxt[1] * 128:(nxt[1] + 1) * 128, :] if nxt[0] == "p" else xs
                    P.dma(ntile[:], nsrc)
                    xpref["cur"] = nxt
                    xpref["tile"] = ntile
                P.memset("dve", stt_[:, 0:1], 0.0)
                P.actv(mixT[:], xtile[:], AF.Square, accum_out=stt_[:, 0:1])
                rsqrt_small(stt_[:, 1:2], stt_[:, 0:1], 1.0 / 1024, EPS)
                P.cp("act", xb[:], xtile[:])
                for kc in range(8):
                    P.tr(psT[:, kc * 128:(kc + 1) * 128], xb[:, kc * 128:(kc + 1) * 128], ident[:])
                P.cp("act", xT[:], psT[:])
                for gi, (c0, c1) in enumerate(GROUPS):
                    pa = psA[cnt["pa"] % 2]; cnt["pa"] += 1
                    w = c1 - c0
                    for kc in range(8):
                        P.mm(pa[:, 0:w], xT[:, kc * 128:(kc + 1) * 128], Wb[:, kc * EIN + c0:kc * EIN + c1], start=(kc == 0), stop=(kc == 7))
                    if gi % 2 == 0:
                        P.ts("dve", proj[:, c0:c1], pa[:, 0:w], stt_[:, 1:2], None, ALU.mult)
                    else:
                        P.op("act", lambda e, c0=c0, c1=c1, pa=pa, w=w: e.mul(proj[:, c0:c1], pa[:, 0:w], stt_[:, 1:2]), [pa, stt_], [proj])
                if not isp:
                    caches["hook"]()
                if stage <= 1:
                    return
                nv = v3(proj[:, QN:QN + 896], 14)
                P.tt("dve", tmpa[:, 0:896], proj[:, QN:QN + 896], proj[:, QN:QN + 896], ALU.mult)
                P.op("dve", lambda e: e.reduce_sum(stt_[:, 8:22], v3(tmpa[:, 0:896], 14), AX.X), [tmpa], [stt_])
                rsqrt_small(stt_[:, 8:22], stt_[:, 8:22], 1.0 / 64, EPS)
                P.tt("dve", nv, nv, bc_last(stt_[:, 8:22], 64), ALU.mult)
                P.tt("dve", proj[:, QN:QN + 896], proj[:, QN:QN + 896], qkg[:], ALU.mult)
                if isp:
                    cos = cs["cos_p"][:, t * 32:(t + 1) * 32]
                    sin = cs["sin_p"][:, t * 32:(t + 1) * 32]
                else:
                    cos = cs["cos_s"][:, :]
                    sin = cs["sin_s"][:, :]
                pv = v3(proj[:, 0:1408], 22)
                rv = v3(rp[:, 0:1408], 22)
                ta = tmpa[:, 0:704].rearrange("p (h d) -> p h d", h=22)
                tb = tmpa[:, 704:1408].rearrange("p (h d) -> p h d", h=22)
                tc_ = tmpb[:, 0:704].rearrange("p (h d) -> p h d", h=22)
                td_ = tmpb[:, 704:1408].rearrange("p (h d) -> p h d", h=22)
                cosb = bc_mid(cos, 22)
                sinb = bc_mid(sin, 22)
                P.tt("dve", ta, pv[:, :, 0:32], cosb, ALU.mult)
                P.tt("dve", tb, pv[:, :, 32:64], sinb, ALU.mult)
                P.tt("dve", rv[:, :, 0:32], ta, tb, ALU.subtract)
                P.tt("pool", tc_, pv[:, :, 32:64], cosb, ALU.mult)
                P.tt("pool", td_, pv[:, :, 0:32], sinb, ALU.mult)
                P.tt("pool", rv[:, :, 32:64], tc_, td_, ALU.add)
                if stage <= 2:
                    return
                if isp:
                    okv = o_kvp[t * 128:(t + 1) * 128, :]
                else:
                    okv = o_kvs
                P.dma(okv[:, 0:128], rp[:, KC:KC + 128])
                P.dma(okv[:, 128:256], proj[:, VC:VC + 128])
                P.dma(okv[:, 256:384], rp[:, KS:KS + 128])
                P.dma(okv[:, 384:512], proj[:, VS:VS + 128])
                if isp and t >= 12:
                    ow = o_winp[(t - 12) * 128:(t - 11) * 128, :]
                    P.dma(ow[:, 0:128], rp[:, KW:KW + 128])
                    P.dma(ow[:, 128:256], proj[:, VW:VW + 128])
                if not isp:
                    for b in range(16):
                        P.dma(o_wins[b, 504:512, 0:128], rp[b * 8:(b + 1) * 8, KW:KW + 128])
                        P.dma(o_wins[b, 504:512, 128:256], proj[b * 8:(b + 1) * 8, VW:VW + 128])
                        P.dma(o_wins[b, 0:504, :], cwin[b, 8:512, :])
                if stage <= 3:
                    return
                qdec = cs["qdec_p"] if isp else cs["qdec_s"]
                kdec = cs["kdec_p"] if isp else cs["kdec_s"]
                DTm = cs["DTp"] if isp else cs["DTs"]
                P.cp("act", r16[:, 0:256], rp[:, QA:QA + 256])
                P.tt("dve", v3(r16[:, 256:512], 4), v3(rp[:, QA:QA + 256], 4), bc_last(qdec[:, 0:4], 64), ALU.mult)
                P.cp("act", r16[:, 512:768], rp[:, KA:KA + 256])
                P.tt("dve", v3(r16[:, 768:1024], 4), v3(rp[:, KA:KA + 256], 4), bc_last(kdec[:, 0:4], 64), ALU.mult)
                P.cp("act", vb16[:], proj[:, VA:VA + 512])
                for i6 in range(6):
                    P.tr(psT[:, i6 * 128:(i6 + 1) * 128], r16[:, i6 * 128:(i6 + 1) * 128], ident[:])
                P.cp("act", rT[:, 0:256], psT[:, 512:768])
                qzv = qz[:].rearrange("p (a h q) -> p a h q", a=4, h=2)
                P.cp("act", qzv[0:64, :, 0, :], psT[0:64, 0:512].rearrange("p (a q) -> p a q", a=4))
                P.cp("dve", qzv[64:128, :, 1, :], psT[64:128, 0:512].rearrange("p (a q) -> p a q", a=4))
                if stage <= 3.1:
                    return
                for h in range(4):
                    hp, pr = h % 2, h // 2
                    P.mm(psR[:, h * 128:(h + 1) * 128], rT[:, pr * 128:(pr + 1) * 128],
                         qz[:, ((0 * 2 + pr) * 2 + hp) * 128:((0 * 2 + pr) * 2 + hp + 1) * 128])
                if stage <= 3.2:
                    return
                P.tt("dve", scm[:], psR[:], DTm[:], ALU.mult)
                if isp:
                    for h in range(4):
                        hp, pr = h % 2, h // 2
                        P.mm(psC[:, h * 128:(h + 1) * 128], scm[:, h * 128:(h + 1) * 128], vb16[:, h * 128:(h + 1) * 128], start=True, stop=False)
                        P.mm(psC[:, h * 128:(h + 1) * 128], qz[:, ((1 * 2 + pr) * 2 + hp) * 128:((1 * 2 + pr) * 2 + hp + 1) * 128],
                             Sb[:, pr * 128:(pr + 1) * 128], start=False, stop=True)
                else:
                    S0b = caches["S0b"]
                    qdTm = caches["qdTm"]
                    for h in range(4):
                        hp, pr = h % 2, h // 2
                        if hp == 0:
                            for b in range(16):
                                P.tt("dve" if b % 2 == 0 else "pool", qdTm[:, b * 256:(b + 1) * 256].rearrange("p (a q) -> p a q", a=2),
                                     qz[:, 512 + pr * 256:512 + (pr + 1) * 256].rearrange("p (a q) -> p a q", a=2),
                                     bc_mid(cs["colmask"][:, b * 128:(b + 1) * 128], 2), ALU.mult)
                        P.mm(psC[:, h * 128:(h + 1) * 128], scm[:, h * 128:(h + 1) * 128], vb16[:, h * 128:(h + 1) * 128], start=True, stop=False)
                        for b in range(16):
                            P.mm(psC[:, h * 128:(h + 1) * 128], qdTm[:, b * 256 + hp * 128:b * 256 + (hp + 1) * 128],
                                 S0b[:, (b * 2 + pr) * 128:(b * 2 + pr + 1) * 128], start=False, stop=(b == 15))
                if stage <= 3.4:
                    return
                P.cp("act", osb[:], psC[:])
                if stage <= 3.5:
                    return
                if isp:
                    for h in range(4):
                        hp, pr = h % 2, h // 2
                        P.mm(psR[:, h * 128:(h + 1) * 128], r16[:, 768 + pr * 128:768 + (pr + 1) * 128], vb16[:, h * 128:(h + 1) * 128])
                    for h in range(4):
                        hp, pr = h % 2, h // 2
                        rows = slice(hp * 64, (hp + 1) * 64)
                        P.stt("dve", S32[rows, pr * 128:(pr + 1) * 128], S32[rows, pr * 128:(pr + 1) * 128], cs["cdec_p"][rows, pr:pr + 1],
                              psR[rows, h * 128:(h + 1) * 128], ALU.mult, ALU.add)
                    P.cp("act", Sb[:], S32[:])
                    if t == n_ptiles - 1:
                        for h in range(4):
                            hp, pr = h % 2, h // 2
                            P.dma(o_retp[h, :, :], S32[hp * 64:(hp + 1) * 64, pr * 128:(pr + 1) * 128])
                else:
                    S0 = caches["S0"]
                    vblk = caches["vblk"]
                    Sn = caches["Sn"]
                    for h in range(4):
                        hp, pr = h % 2, h // 2
                        rows = slice(hp * 64, (hp + 1) * 64)
                        P.tt("dve", vblk[:].rearrange("p (b e) -> p b e", b=16), bc_mid(vb16[:, h * 128:(h + 1) * 128], 16),
                             bc_last(cs["blkmask"][:, 0:16], 128), ALU.mult)
                        for q4 in range(4):
                            pa = psA[cnt["pa"] % 2]; cnt["pa"] += 1
                            P.mm(pa[:, :], r16[:, 768 + pr * 128:768 + (pr + 1) * 128], vblk[:, q4 * 512:(q4 + 1) * 512])
                            s0v = S0[rows, :].rearrange("p (b a e) -> p b a e", b=16, a=2)[:, q4 * 4:(q4 + 1) * 4, pr, :]
                            P.stt("dve", Sn[rows, q4 * 512:(q4 + 1) * 512].rearrange("p (b e) -> p b e", b=4), s0v, cs["cdec_s"][rows, pr:pr + 1],
                                  pa[rows, :].rearrange("p (b e) -> p b e", b=4), ALU.mult, ALU.add)
                        P.dma(o_rets[:, h, :, :].rearrange("b d e -> d b e"), Sn[rows, :].rearrange("p (b e) -> p b e", b=16))
                if stage <= 4:
                    return
                P.op("dve", lambda e: e.reduce_sum(stt_[:, 24:28], v3(osb[:], 4), AX.X), [osb], [stt_])
                P.ts("dve", stt_[:, 24:28], stt_[:, 24:28], -1.0 / 128, None, ALU.mult)
                P.tt("dve", v3(ocb[:], 4), v3(osb[:], 4), bc_last(stt_[:, 24:28], 128), ALU.add)
                P.tt("dve", osb[:], ocb[:], ocb[:], ALU.mult)
                P.op("dve", lambda e: e.reduce_sum(stt_[:, 28:32], v3(osb[:], 4), AX.X), [osb], [stt_])
                rsqrt_small(stt_[:, 28:32], stt_[:, 28:32], 1.0 / 128, EPS)
                P.tt("dve", v3(ocb[:], 4), v3(ocb[:], 4), bc_last(stt_[:, 28:32], 128), ALU.mult)
                P.tt("dve", ocb[:], ocb[:], gnb[:], ALU.mult)
                P.actv(sz[:], proj[:, ZA:ZA + 512], AF.Silu)
                P.tt("dve", mix[:, 0:512], ocb[:], sz[:], ALU.mult)
                if stage <= 5:
                    return
                P.actv(gts[:], proj[:, GL:GL + 24], AF.Sigmoid)
                P.cp("act", n16[:, 0:896], rp[:, QN:QN + 896])
                P.cp("dve", n16[:, 896:1024], proj[:, VC:VC + 128])
                for i4 in range(4):
                    P.tr(psT[:, i4 * 128:(i4 + 1) * 128], n16[:, i4 * 128:(i4 + 1) * 128], ident[:])
                P.cp("act", qTz[0:64, 0:512], psT[0:64, 0:512])
                P.cp("dve", qTz[64:128, 512:1024], psT[64:128, 0:512])
                for i4 in range(4):
                    P.tr(psT[:, i4 * 128:(i4 + 1) * 128], n16[:, 512 + i4 * 128:512 + (i4 + 1) * 128], ident[:])
                if isp:
                    cT = caches["cT"]; vs1 = caches["vs1"]; vw1 = caches["vw1"]
                    P.cp("act", cT[:].rearrange("p (k n) -> p k n", k=4)[:, :, t * 128:(t + 1) * 128], psT[:, 0:512].rearrange("p (k n) -> p k n", k=4))
                    P.cp("dve", vs1[:].rearrange("p (c g w) -> p c g w", c=16, g=2)[:, t, :, 0:64], v3(proj[:, VS:VS + 128], 2))
                    P.cp("dve", vw1[:].rearrange("p (c g w) -> p c g w", c=16, g=2)[:, t, :, 0:64], v3(proj[:, VW:VW + 128], 2))
                    ckT = caches["ckT"]; cvx = caches["cvx"]
                    compress(cT, 0, cT, 3 * 2048, ckT, cvx)
                    groups = []
                    for g in range(2):
                        qTg = qTz[:, g * 512:(g + 1) * 512].rearrange("p (j q) -> p j q", j=4)
                        selc = []
                        for c in range(t + 1):
                            bias = bc_mid(cs["CB"][:, :], 4) if c == t else None
                            selc.append((cT[:, 2048 + c * 128:2048 + (c + 1) * 128], vs1[:, (c * 2 + g) * 65:(c * 2 + g + 1) * 65], 128,
                                         cs["E"][0:32, c * 128:(c + 1) * 128], bias))
                        winc = []
                        for c in range(max(0, t - 4), t + 1):
                            bias = bc_mid(cs["CB"][:, :], 4) if c == t else (bc_mid(cs["AB"][:, :], 4) if c == t - 4 else None)
                            winc.append((cT[:, 4096 + c * 128:4096 + (c + 1) * 128], vw1[:, (c * 2 + g) * 65:(c * 2 + g + 1) * 65], 128, None, bias))
                        groups.append(dict(qTg=qTg, cmp_k=ckT[:, 0:127], cmp_v=cvx[0:127, g * 97:(g + 1) * 97], sel_chunks=selc, win_chunks=winc,
                                           on_dst=lambda x, g=g: on[:, x * 512 + g * 256:x * 512 + (g + 1) * 256].rearrange("p (j d) -> p j d", j=4)))
                    cmpb = bc_mid(cs["CMPB"][0:127, t * 128:(t + 1) * 128], 4)
                    nsa_tile(128, groups, 32, cs["mulc_p"][:, t * 32:(t + 1) * 32], cs["addc_p"][:, t * 32:(t + 1) * 32], cmpb)
                else:
                    cTs = caches["cTs"]; vs1s = caches["vs1s"]
                    P.cp("act", cTs[:], psT[:, 0:512])
                    P.cp("dve", vs1s[:].rearrange("p (k g w) -> p k g w", k=2, g=2)[:, 0, :, 0:64], v3(proj[:, VS:VS + 128], 2))
                    P.cp("dve", vs1s[:].rearrange("p (k g w) -> p k g w", k=2, g=2)[:, 1, :, 0:64], v3(proj[:, VW:VW + 128], 2))
                    sample_nsa(caches)
                if stage <= 6:
                    return
                gv = gts[:].rearrange("p (h x) -> p h x", h=8)
                for x in range(3):
                    P.tt("dve", v3(on[:, x * 512:(x + 1) * 512], 8), v3(on[:, x * 512:(x + 1) * 512], 8), bc_last(gv[:, :, x], 64), ALU.mult)
                P.tt("dve", obt[:], on[:, 0:512], on[:, 512:1024], ALU.add)
                P.tt("dve", obt[:], obt[:], on[:, 1024:1536], ALU.add)
                P.actv(sz[:], proj[:, ZB:ZB + 512], AF.Silu)
                P.tt("dve", mix[:, 512:1024], obt[:], sz[:], ALU.mult)
                if stage <= 7:
                    return
                for kc in range(8):
                    P.tr(psT[:, kc * 128:(kc + 1) * 128], mix[:, kc * 128:(kc + 1) * 128], ident[:])
                P.cp("act", mixT[:], psT[:])
                for hf in range(2):
                    pa = psA[cnt["pa"] % 2]; cnt["pa"] += 1
                    for kc in range(8):
                        P.mm(pa[:, :], mixT[:, kc * 128:(kc + 1) * 128], Wob[:, kc * 1024 + hf * 512:kc * 1024 + (hf + 1) * 512], start=(kc == 0), stop=(kc == 7))
                    P.tt("dve", xtile[:, hf * 512:(hf + 1) * 512], xtile[:, hf * 512:(hf + 1) * 512], pa[:, :], ALU.add)
                ydst = yp[t * 128:(t + 1) * 128, :] if isp else ys
                P.dma(ydst, xtile[:], writes=[("y0", mode, t)])

            def compress(kc_t, kc_off, vc_t, vc_off, ckT, cvx, rkeys=None):
                for l in range(32):
                    P.mm(psC[:, 0:127], BDW[:, l * 128:(l + 1) * 128], kc_t[:, kc_off + l:kc_off + l + 16 * 126 + 1:16], start=(l == 0), stop=(l == 31),
                         reads=([BDW] + rkeys) if rkeys else None)
                P.ts("dve", ckT[:, 0:127], psC[:, 0:127], posk[:, 0:1], None, ALU.add)
                for l in range(32):
                    P.mm(psV[0:127, 0:128], vc_t[:, vc_off + l:vc_off + l + 16 * 126 + 1:16], BDW[:, (32 + l) * 128:(33 + l) * 128], start=(l == 0), stop=False,
                         reads=([BDW] + rkeys) if rkeys else None)
                P.mm(psV[0:127, 0:128], cs["ones_row"][0:1, 0:127], posv[0:1, 0:128], start=False, stop=True)
                P.cp("act", cvx[0:127, :].rearrange("p (g w) -> p g w", g=2)[:, :, 0:64], psV[0:127, 0:128].rearrange("p (g d) -> p g d", g=2))

            def sample_nsa(caches):
                C = caches
                cTq, kwTq, vs1q, vw1q, ckTq, cvxq = C["cTq"], C["kwTq"], C["vs1q"], C["vw1q"], C["ckTq"], C["cvxq"]
                cTs, vs1s, idx = C["cTs"], C["vs1s"], C["idx"]
                on8 = [osb, ocb, sz]
                wst32, w16 = C["wst32"], C["w16"]
                P.barrier()
                C["stR"].close()
                stPG = C["stPG"]
                pgb = [alloc(stPG, "pgb%d" % i, [128, 16 * 384], BF16) for i in range(2)]
                pg = [alloc(stPG, "pgf%d" % i, [128, 512]) for i in range(6)]
                for b in range(16):
                    pb = pgb[b % 2]
                    for i in range(16):
                        pgt = pg[(b * 16 + i) % 6]
                        col = b * 16 + i
                        P.dma(pgt[:], cache, reads=[cache, idx], q="pool",
                              fn=lambda e, pgt=pgt, col=col: e.indirect_dma_start(
                                  out=pgt[:, :], out_offset=None, in_=cache[:, :],
                                  in_offset=bass.IndirectOffsetOnAxis(ap=idx[:, col:col + 1], axis=0)))
                        P.cp("act", pb[:, i * 384:(i + 1) * 384], pgt[:, 0:384], writes=[("pgb", b % 2, i)])
                        P.cp("dve", vs1q[:].rearrange("p (c g w) -> p c g w", c=16, g=2)[:, i, :, 0:64], v3(pgt[:, 384:512], 2), writes=[("vs1q", i)])
                    psRb = psR[:].bitcast(BF16)
                    for i in range(16):
                        pdst = psT[:, 0:384] if i % 2 == 0 else psRb[:, 0:384]
                        pkey = psT if i % 2 == 0 else psR
                        for k in range(3):
                            P.tr(pdst[:, k * 128:(k + 1) * 128], pb[:, i * 384 + k * 128:i * 384 + (k + 1) * 128], ident[:],
                                 reads=[("pgb", b % 2, i), ident], writes=[pkey])
                        P.cp("act" if i % 2 == 0 else "dve", cTq[:].rearrange("p (k n) -> p k n", k=3)[:, :, i * 128:(i + 1) * 128],
                             pdst.rearrange("p (k n) -> p k n", k=3), reads=[pkey], writes=[("cTq", i)])
                    P.dma(wst32[:].rearrange("p (c w) -> p c w", c=4), cwin[b].rearrange("(c r) w -> r c w", r=128))
                    P.cp("dve", w16[:], wst32[:])
                    for c in range(4):
                        P.tr(psT[:, 512 + c * 128:512 + (c + 1) * 128], w16[:, c * 256:c * 256 + 128], ident[:])
                    P.cp("act", kwTq[:], psT[:, 512:1024])
                    for c in range(4):
                        P.cp("dve", vw1q[:, c * 130:(c + 1) * 130].rearrange("p (g w) -> p g w", g=2)[:, :, 0:64], v3(w16[:, c * 256 + 128:(c + 1) * 256], 2))
                    compress(cTq, 0, cTq, 2048, ckTq, cvxq, rkeys=[("cTq", i) for i in range(16)])
                    groups = []
                    for g in range(2):
                        qTg = qTz[:, g * 512:(g + 1) * 512].rearrange("p (j q) -> p j q", j=4)[:, :, 8 * b:8 * b + 8]
                        sbias = bc_mid(cs["SB"][:, b * 8:(b + 1) * 8], 4)
                        selc = []
                        for c in range(16):
                            selc.append((cTq[:, 4096 + c * 128:4096 + (c + 1) * 128], vs1q[:, (c * 2 + g) * 65:(c * 2 + g + 1) * 65], 128,
                                         cs["E"][0:33, c * 128:(c + 1) * 128], None, ("cTq", c), ("vs1q", c)))
                        selc.append((cTs[:, 128:256], vs1s[:, (0 * 2 + g) * 65:(0 * 2 + g + 1) * 65], 128, None, sbias))
                        winc = []
                        for c in range(4):
                            bias = bc_mid(cs["ABs"][:, :], 4) if c == 0 else None
                            winc.append((kwTq[:, c * 128:(c + 1) * 128], vw1q[:, (c * 2 + g) * 65:(c * 2 + g + 1) * 65], 128, None, bias))
                        winc.append((cTs[:, 256:384], vs1s[:, (1 * 2 + g) * 65:(1 * 2 + g + 1) * 65], 128, None, sbias))
                        groups.append(dict(qTg=qTg, cmp_k=ckTq[:, 0:127], cmp_v=cvxq[0:127, g * 98:(g + 1) * 98], sel_chunks=selc, win_chunks=winc,
                                           on_dst=lambda x, g=g: on8[x][0:8, g * 256:(g + 1) * 256].rearrange("p (j d) -> p j d", j=4)))
                    nsa_tile(8, groups, 33, cs["mulc_s"][:, :], cs["addc_s"][:, :], None)
                    for x in range(3):
                        P.dma(on[b * 8:(b + 1) * 8, x * 512:(x + 1) * 512], on8[x][0:8, :])
                P.barrier()

            with ExitStack() as stP:
                Ap = lambda name, shape, dt=F32: alloc(stP, name, shape, dt)
                load_consts(stP, P_ONLY)
                cT = Ap("cT", [128, 4 * 2048], BF16)
                vs1 = Ap("vs1", [128, 16 * 2 * 65], BF16)
                vw1 = Ap("vw1", [128, 16 * 2 * 65], BF16)
                ckT = Ap("ckT", [128, 128], BF16)
                cvx = Ap("cvx", [128, 2 * 97], BF16)
                ov32 = Ap("ov32", [128, 32])
                P.memset("pool", cT[:], 0.0)
                P.memset("pool", vs1[:], 1.0)
                P.memset("pool", vw1[:], 1.0)
                P.memset("pool", cvx[:], 1.0)
                P.memset("pool", ckT[:], 0.0)
                for g in range(2):
                    P.cp("dve", cvx[0:127, g * 97 + 65:(g + 1) * 97], cs["ovl_p"][0:127, :])
                caches = {"cT": cT, "vs1": vs1, "vw1": vw1, "ckT": ckT, "cvx": cvx}
                for t in range(n_ptiles):
                    caches["next"] = ("p", t + 1) if t + 1 < n_ptiles else (("s", 0) if do_sample else None)
                    even_tile("p", t, caches)
                P.barrier()

            if do_sample:
                stS = ExitStack()
                stPG = ExitStack()
                scaches = {}

                def sample_hook():
                    P.barrier()
                    stWb.close()
                    As = lambda name, shape, dt=F32: alloc(stS, name, shape, dt)
                    load_consts(stS, S_ONLY)
                    C = scaches
                    C["cTq"] = As("cTq", [128, 3 * 2048], BF16)
                    C["kwTq"] = As("kwTq", [128, 512], BF16)
                    C["vs1q"] = As("vs1q", [128, 16 * 130], BF16)
                    C["vw1q"] = As("vw1q", [128, 4 * 130], BF16)
                    C["ckTq"] = As("ckTq", [128, 128], BF16)
                    C["cvxq"] = As("cvxq", [128, 2 * 98], BF16)
                    C["cTs"] = As("cTs", [128, 512], BF16)
                    C["vs1s"] = As("vs1s", [128, 4 * 65], BF16)
                    C["wst32"] = As("wst32", [128, 1024])
                    C["w16"] = As("w16", [128, 1024], BF16)
                    C["idx"] = As("idx", [128, 256], I32)
                    pti = As("pti", [128, 256], I32)
                    stR = ExitStack()
                    C["stR"] = stR
                    C["stPG"] = stPG
                    Ar = lambda name, shape, dt=F32: alloc(stR, name, shape, dt)
                    C["S0"] = Ar("S0", [128, 4096])
                    C["S0b"] = Ar("S0b", [128, 4096], BF16)
                    C["qdTm"] = Ar("qdTm", [128, 16 * 256], BF16)
                    C["vblk"] = Ar("vblk", [128, 2048], BF16)
                    C["Sn"] = Ar("Sn", [128, 2048])
                    ptf = tmpa[:, 0:256]
                    P.memset("pool", C["vs1q"][:], 1.0)
                    P.memset("pool", C["vw1q"][:], 1.0)
                    P.memset("pool", C["vs1s"][:], 1.0)
                    P.memset("pool", C["cvxq"][:], 1.0)
                    for g in range(2):
                        P.cp("dve", C["cvxq"][0:127, g * 98 + 65:(g + 1) * 98], cs["ovl_s"][0:127, :])
                    for hp in range(2):
                        P.dma(C["S0"][hp * 64:(hp + 1) * 64, :].rearrange("p (b a e) -> p b a e", b=16, a=2),
                              sret[:, hp::2, :, :].rearrange("b a d e -> d b a e"))
                    P.cp("dve", C["S0b"][:], C["S0"][:])
                    P.dma(pti[:], ptab.partition_broadcast(128))
                    P.cp("dve", ptf, pti[:])
                    P.ts("dve", ptf, ptf, 128.0, cs["iota_p"][:, 0:1], ALU.mult, ALU.add)
                    P.cp("dve", C["idx"][:], ptf)

                scaches["hook"] = sample_hook
                even_tile("s", 0, scaches)
                P.barrier()
                stPG.close()
                stS.close()
            else:
                stWb.close()
            P.barrier()
        if phase_b:
            NT = 2176
            NCH = 272
            TWO_PI = 2.0 * math.pi
            with ExitStack() as stB:
                Bf = lambda name, shape, dt=F32: alloc(stB, name, shape, dt)
                load_consts(stB, ["ident", "JV", "KI", "PM4", "BM16"], pre="kb_")
                ident = cs["ident"]
                uT = Bf("uT", [128, 8 * NT], BF16)
                szT = Bf("szT", [128, 8 * NT], BF16)
                ngo = Bf("ngo", [128, 8])
                dsk = Bf("dsk", [128, 8])
                statb = Bf("statb", [128, 8])
                FSp = Bf("FSp", [128, 64])
                FSs = Bf("FSs", [128, 1024])
                P.dma(ngo[:], norm_o)
                P.dma(dsk[:], ssmd)
                cntb = {"x": 0, "pa": 0}

                def load_tile(dst, ti):
                    if ti < 16:
                        P.dma(dst, yp[ti * 128:(ti + 1) * 128, :], reads=[("y0", "p", ti)])
                    else:
                        P.dma(dst, ys, reads=[("y0", "s", 0)])

                with ExitStack() as st1:
                    B1 = lambda name, shape, dt=F32: alloc(st1, name, shape, dt)
                    Wodd = B1("Wodd", [128, 8 * 2048], BF16)
                    yt = [B1("yt%d" % i, [128, 1024]) for i in range(2)]
                    hb = B1("hb", [128, 1024], BF16)
                    hT = B1("hT", [128, 8 * 512], BF16)
                    with ExitStack() as stg_:
                        stg = [alloc(stg_, "stgb%d" % i, [128, 2048]) for i in range(2)]
                        for kc in range(8):
                            s_ = stg[kc % 2]
                            P.dma(s_[:], w_in_o[kc * 128:(kc + 1) * 128, :])
                            if kc % 2 == 0:
                                P.ts("dve", Wodd[:, kc * 2048:(kc + 1) * 2048], s_[:], ngo[:, kc:kc + 1], None, ALU.mult)
                            else:
                                P.op("act", lambda e, kc=kc, s_=s_: e.mul(Wodd[:, kc * 2048:(kc + 1) * 2048], s_[:], ngo[:, kc:kc + 1]), [s_, ngo], [Wodd])
                        P.barrier()
                    blocks = [(0, 4), (4, 8), (8, 12), (12, 16), (16, 17)]
                    for (t0, t1) in blocks:
                        nb = (t1 - t0) * 128
                        col0 = t0 * 128
                        for ti in range(t0, t1):
                            ytile = yt[cntb["x"] % 2]; cntb["x"] += 1
                            load_tile(ytile[:], ti)
                            P.memset("dve", statb[:, 0:1], 0.0)
                            P.actv(hb[:], ytile[:], AF.Square, accum_out=statb[:, 0:1])
                            P.ts("dve", statb[:, 1:2], statb[:, 0:1], 1.0 / 1024, EPS, ALU.mult, ALU.add)
                            P.actv(statb[:, 1:2], statb[:, 1:2], AF.Sqrt)
                            P.op("dve", lambda e: e.reciprocal(statb[:, 1:2], statb[:, 1:2]), [statb], [statb])
                            P.ts("dve", hb[:], ytile[:], statb[:, 1:2], None, ALU.mult)
                            for kc in range(8):
                                P.tr(psT[:, kc * 128:(kc + 1) * 128], hb[:, kc * 128:(kc + 1) * 128], ident[:])
                            lt = ti - t0
                            P.cp("act", hT[:].rearrange("p (k n) -> p k n", k=8)[:, :, lt * 128:(lt + 1) * 128], psT[:].rearrange("p (k n) -> p k n", k=8))
                        for oc in range(16):
                            pa = psA[cntb["pa"] % 2]; cntb["pa"] += 1
                            for kc in range(8):
                                P.mm(pa[:, 0:nb], Wodd[:, kc * 2048 + oc * 128:kc * 2048 + (oc + 1) * 128], hT[:, kc * 512:kc * 512 + nb], start=(kc == 0), stop=(kc == 7))
                            if oc < 8:
                                P.cp("dve", uT[:, oc * NT + col0:oc * NT + col0 + nb], pa[:, 0:nb])
                            else:
                                P.actv(szT[:, (oc - 8) * NT + col0:(oc - 8) * NT + col0 + nb], pa[:, 0:nb], AF.Silu)
                    P.barrier()

                with ExitStack() as st2:
                    B2 = lambda name, shape, dt=F32: alloc(st2, name, shape, dt)
                    lr = B2("lr", [128, 32]); li = B2("li", [128, 32]); ls = B2("ls", [128, 32])
                    bre = B2("bre", [128, 512]); bim = B2("bim", [128, 512])
                    cre = B2("cre", [128, 512]); cim = B2("cim", [128, 512])
                    x0r = B2("x0r", [128, 512]); x0i = B2("x0i", [128, 512])
                    for tl, src in ((lr, lamre_A), (li, lamim_A), (ls, lstep_A), (bre, bA_re), (bim, bA_im), (cre, cA_re), (cim, cA_im), (x0r, x0A_re), (x0i, x0A_im)):
                        P.dma(tl[:], src)
                    aa = B2("aa", [128, 32]); th = B2("th", [128, 32])
                    A9 = B2("A9", [128, 288]); T9 = B2("T9", [128, 288]); T9c = B2("T9c", [128, 288])
                    PR = B2("PR", [128, 288]); PI = B2("PI", [128, 288])
                    rrt = B2("rrt", [128, 1024]); rri = B2("rri", [128, 1024], I32); rrm = B2("rrm", [128, 1024])
                    npi = B2("npi", [128, 1])

                    def range_reduce(x, n):
                        P.ts("dve", rrt[:, 0:n], x, 1.0 / TWO_PI, None, ALU.mult)
                        P.cp("dve", rri[:, 0:n], rrt[:, 0:n])
                        P.cp("dve", rrt[:, 0:n], rri[:, 0:n])
                        P.stt("dve", x, rrt[:, 0:n], -TWO_PI, x, ALU.mult, ALU.add)
                        P.ts("dve", rrm[:, 0:n], x, math.pi, -TWO_PI, ALU.is_gt, ALU.mult)
                        P.tt("dve", x, x, rrm[:, 0:n], ALU.add)
                        P.ts("dve", rrm[:, 0:n], x, -math.pi, TWO_PI, ALU.is_lt, ALU.mult)
                        P.tt("dve", x, x, rrm[:, 0:n], ALU.add)

                    P.actv(ls[:], ls[:], AF.Exp)
                    P.tt("dve", aa[:], lr[:], ls[:], ALU.mult)
                    P.tt("dve", th[:], li[:], ls[:], ALU.mult)
                    JV = cs["JV"]
                    P.tt("dve", A9[:].rearrange("p (j m) -> p j m", j=9), JV[:].rearrange("p (j m) -> p j m", j=9), bc_mid(aa[:, :], 9), ALU.mult)
                    P.actv(A9[:], A9[:], AF.Exp)
                    range_reduce(th[:, :], 32)
                    P.tt("dve", T9[:].rearrange("p (j m) -> p j m", j=9), JV[:].rearrange("p (j m) -> p j m", j=9), bc_mid(th[:, :], 9), ALU.mult)
                    P.ts("dve", T9c[:], T9[:], math.pi / 2, None, ALU.add)
                    range_reduce(T9[:, :], 288)
                    range_reduce(T9c[:, :], 288)
                    P.actv(T9[:], T9[:], AF.Sin)
                    P.actv(T9c[:], T9c[:], AF.Sin)
                    P.tt("dve", PR[:], A9[:], T9c[:], ALU.mult)
                    P.tt("dve", PI[:], A9[:], T9[:], ALU.mult)
                    PRv = PR[:].rearrange("p (j m) -> p j m", j=9)
                    PIv = PI[:].rearrange("p (j m) -> p j m", j=9)
                    den = B2("den", [128, 32]); nr = B2("nr", [128, 32]); fre = B2("fre", [128, 32]); fim = B2("fim", [128, 32]); t32 = B2("t32", [128, 32])
                    P.tt("dve", den[:], lr[:], lr[:], ALU.mult)
                    P.tt("dve", t32[:], li[:], li[:], ALU.mult)
                    P.tt("dve", den[:], den[:], t32[:], ALU.add)
                    P.op("dve", lambda e: e.reciprocal(den[:], den[:]), [den], [den])
                    P.ts("dve", nr[:], PR[:, 32:64], -1.0, None, ALU.add)
                    P.tt("dve", fre[:], nr[:], lr[:], ALU.mult)
                    P.tt("dve", t32[:], PI[:, 32:64], li[:], ALU.mult)
                    P.tt("dve", fre[:], fre[:], t32[:], ALU.add)
                    P.tt("dve", fre[:], fre[:], den[:], ALU.mult)
                    P.tt("dve", fim[:], PI[:, 32:64], lr[:], ALU.mult)
                    P.tt("dve", t32[:], nr[:], li[:], ALU.mult)
                    P.tt("dve", fim[:], fim[:], t32[:], ALU.subtract)
                    P.tt("dve", fim[:], fim[:], den[:], ALU.mult)
                    Bre = B2("Bre", [128, 512]); Bim = B2("Bim", [128, 512]); t512 = B2("t512", [128, 512])
                    v16 = lambda ap: ap.rearrange("p (m c) -> p m c", c=16)
                    P.tt("dve", v16(Bre[:]), v16(bre[:]), bc_last(fre[:, :], 16), ALU.mult)
                    P.tt("dve", v16(t512[:]), v16(bim[:]), bc_last(fim[:, :], 16), ALU.mult)
                    P.tt("dve", Bre[:], Bre[:], t512[:], ALU.subtract)
                    P.tt("dve", v16(Bim[:]), v16(bim[:]), bc_last(fre[:, :], 16), ALU.mult)
                    P.tt("dve", v16(t512[:]), v16(bre[:]), bc_last(fim[:, :], 16), ALU.mult)
                    P.tt("dve", Bim[:], Bim[:], t512[:], ALU.add)
                    Cbd_re = B2("Cbd_re", [128, 32 * 32], BF16); Cbd_nim = B2("Cbd_nim", [128, 32 * 32], BF16)
                    P.memset("pool", Cbd_re[:], 0.0)
                    P.memset("pool", Cbd_nim[:], 0.0)
                    cbv = lambda t_: t_[:].rearrange("p (m c) -> p m c", c=32)
                    for hp in range(2):
                        rows = slice(hp * 64, (hp + 1) * 64)
                        P.cp("dve", cbv(Cbd_re)[rows, :, hp * 16:(hp + 1) * 16], v16(cre[:])[rows, :, :])
                        P.ts("dve", cbv(Cbd_nim)[rows, :, hp * 16:(hp + 1) * 16], v16(cim[:])[rows, :, :], -1.0, None, ALU.mult)
                    th8 = B2("th8", [128, 32])
                    P.ts("dve", th8[:], th[:], 8.0, None, ALU.mult)
                    range_reduce(th8[:, :], 32)
                    Xre = B2("Xre", [128, 512]); Xim = B2("Xim", [128, 512]); tX = B2("tX", [128, 512])
                    XBD = B2("XBD", [128, 2 * 8 * 4 * 32], BF16)
                    VZ = B2("VZ", [128, 8 * 4 * 2 * 128], BF16)
                    WS = B2("WS", [128, 8 * 2 * 128], BF16)
                    uzb = [B2("uz%d" % i, [128, NT], BF16) for i in range(2)]
                    BD = B2("BD", [128, 8 * 128], BF16)
                    Sin_r = B2("Sin_r", [128, 4 * NCH]); Sin_i = B2("Sin_i", [128, 4 * NCH])
                    cosT = B2("cosT", [128, 1024]); sinT = B2("sinT", [128, 1024])
                    c_r = B2("c_r", [128, 1024]); c_i = B2("c_i", [128, 1024]); t1k = B2("t1k", [128, 1024])
                    w_r = B2("w_r", [128, 1024]); w_i = B2("w_i", [128, 1024])
                    Sp_r = B2("Sp_r", [128, 4 * NCH], BF16); Sp_i = B2("Sp_i", [128, 4 * NCH], BF16)
                    yv = B2("yv", [128, NCH]); y2 = B2("y2", [128, NCH]); y3 = B2("y3", [128, NCH])
                    ygc = B2("ygc", [128, NT], BF16)
                    P.memset("pool", XBD[:], 0.0)
                    P.memset("pool", VZ[:], 0.0)
                    for c in range(8):
                        msl = slice(4 * c, 4 * c + 4)
                        x4 = lambda t_: t_[:].rearrange("p (t m c) -> p t m c", t=8, m=4)
                        prb = PRv[:, 0:8, msl].unsqueeze(3).to_broadcast([128, 8, 4, 16])
                        pib = PIv[:, 0:8, msl].unsqueeze(3).to_broadcast([128, 8, 4, 16])
                        brb = v16(Bre[:])[:, msl, :].unsqueeze(1).to_broadcast([128, 8, 4, 16])
                        bib = v16(Bim[:])[:, msl, :].unsqueeze(1).to_broadcast([128, 8, 4, 16])
                        P.tt("dve", x4(Xre), prb, brb, ALU.mult)
                        P.tt("dve", x4(tX), pib, bib, ALU.mult)
                        P.tt("dve", Xre[:], Xre[:], tX[:], ALU.subtract)
                        P.tt("dve", x4(Xim), prb, bib, ALU.mult)
                        P.tt("dve", x4(tX), pib, brb, ALU.mult)
                        P.tt("dve", Xim[:], Xim[:], tX[:], ALU.add)
                        xbv = XBD[:].rearrange("p (r t m c) -> p r t m c", r=2, t=8, m=4)
                        for hp in range(2):
                            rows = slice(hp * 64, (hp + 1) * 64)
                            P.cp("dve", xbv[rows, 0, :, :, hp * 16:(hp + 1) * 16], x4(Xre)[rows])
                            P.cp("pool", xbv[rows, 1, :, :, hp * 16:(hp + 1) * 16], x4(Xim)[rows])
                        prb = PRv[:, 1:9, msl].unsqueeze(3).to_broadcast([128, 8, 4, 16])
                        pib = PIv[:, 1:9, msl].unsqueeze(3).to_broadcast([128, 8, 4, 16])
                        crb = v16(cre[:])[:, msl, :].unsqueeze(1).to_broadcast([128, 8, 4, 16])
                        cib = v16(cim[:])[:, msl, :].unsqueeze(1).to_broadcast([128, 8, 4, 16])
                        P.tt("dve", x4(Xre), prb, crb, ALU.mult)
                        P.tt("dve", x4(tX), pib, cib, ALU.mult)
                        P.tt("dve", Xre[:], Xre[:], tX[:], ALU.subtract)
                        P.tt("dve", x4(Xim), pib, crb, ALU.mult)
                        P.tt("dve", x4(tX), prb, cib, ALU.mult)
                        P.stt("dve", Xim[:], Xim[:], -1.0, tX[:], ALU.mult, ALU.subtract)
                        vzv = VZ[:].rearrange("p (t m r n) -> p t m r n", t=8, m=4, r=2)
                        for hp in range(2):
                            rows = slice(hp * 64, (hp + 1) * 64)
                            for m4 in range(4):
                                c0_ = 32 * m4 + 16 * hp
                                P.cp("dve", vzv[rows, :, m4, 0, c0_:c0_ + 16], x4(Xre)[rows, :, m4, :])
                                P.cp("pool", vzv[rows, :, m4, 1, c0_:c0_ + 16], x4(Xim)[rows, :, m4, :])
                        for th2 in range(2):
                            for tl_ in range(4):
                                tau = th2 * 4 + tl_
                                for ri in range(2):
                                    P.tr(psT[:, (tl_ * 2 + ri) * 128:(tl_ * 2 + ri + 1) * 128], XBD[:, (ri * 8 + tau) * 128:(ri * 8 + tau + 1) * 128], ident[:])
                            P.cp("act", WS[:, th2 * 1024:(th2 + 1) * 1024], psT[:])
                        for th2 in range(2):
                            for tl_ in range(4):
                                tau = th2 * 4 + tl_
                                outp = psC[:, tl_ * 128:(tl_ + 1) * 128]
                                P.mm(outp, XBD[:, (0 * 8 + tau) * 128:(0 * 8 + tau + 1) * 128], Cbd_re[:, c * 128:(c + 1) * 128], start=True, stop=False)
                                P.mm(outp, XBD[:, (1 * 8 + tau) * 128:(1 * 8 + tau + 1) * 128], Cbd_nim[:, c * 128:(c + 1) * 128], start=False, stop=True)
                            P.tt("dve", BD[:, th2 * 512:(th2 + 1) * 512].rearrange("p (t n) -> p t n", t=4), psC[:, 0:512].rearrange("p (t n) -> p t n", t=4),
                                 bc_mid(cs["BM16"][:, :], 4), ALU.mult)
                        uc = uT[:, c * NT:(c + 1) * NT]
                        for m4 in range(4):
                            uz = uzb[m4 % 2]
                            P.ts("dve" if m4 % 2 == 0 else "dve", uz[:], uc, cs["PM4"][:, m4:m4 + 1], None, ALU.mult)
                            for ri, dst in ((0, Sin_r), (1, Sin_i)):
                                pa = psA[ri]
                                for s_ in range(8):
                                    tau = 7 - s_
                                    P.mm(pa[:, 0:NCH], WS[:, (tau * 2 + ri) * 128:(tau * 2 + ri + 1) * 128], uz[:, s_:NT:8], start=(s_ == 0), stop=(s_ == 7))
                                if ri == 0:
                                    P.cp("act", dst[:, m4 * NCH:(m4 + 1) * NCH], pa[:, 0:NCH])
                                else:
                                    P.cp("dve", dst[:, m4 * NCH:(m4 + 1) * NCH], pa[:, 0:NCH])
                        a3 = lambda t_: t_[:].rearrange("p (m k) -> p m k", m=4)
                        P.tt("dve", a3(sinT), bc_last(th8[:, msl], 256), bc_mid(cs["KI"][:, :], 4), ALU.mult)
                        P.ts("dve", cosT[:], sinT[:], math.pi / 2, None, ALU.add)
                        range_reduce(sinT[:, :], 1024)
                        range_reduce(cosT[:, :], 1024)
                        P.actv(sinT[:], sinT[:], AF.Sin)
                        P.actv(cosT[:], cosT[:], AF.Sin)
                        s3 = lambda t_: t_[:].rearrange("p (m k) -> p m k", m=4)[:, :, 0:256]
                        P.tt("dve", a3(c_r), a3(cosT), s3(Sin_r), ALU.mult)
                        P.tt("dve", a3(t1k), a3(sinT), s3(Sin_i), ALU.mult)
                        P.tt("dve", c_r[:], c_r[:], t1k[:], ALU.add)
                        P.tt("dve", a3(c_i), a3(cosT), s3(Sin_i), ALU.mult)
                        P.tt("dve", a3(t1k), a3(sinT), s3(Sin_r), ALU.mult)
                        P.tt("dve", c_i[:], c_i[:], t1k[:], ALU.subtract)
                        for m4 in range(4):
                            m = 4 * c + m4
                            r8b = A9[:, 8 * 32 + m:8 * 32 + m + 1].to_broadcast([128, 256])
                            for tl_, wo_ in ((c_r, w_r), (c_i, w_i)):
                                seg = tl_[:, m4 * 256:(m4 + 1) * 256]
                                oseg = wo_[:, m4 * 256:(m4 + 1) * 256]
                                P.op("dve", lambda e, seg=seg, oseg=oseg, r8b=r8b: e.tensor_tensor_scan(oseg, r8b, seg, 0.0, ALU.mult, ALU.add), [tl_, A9], [wo_])
                        P.tt("dve", t1k[:], cosT[:], w_r[:], ALU.mult)
                        P.tt("pool", rrm[:], sinT[:], w_i[:], ALU.mult)
                        P.tt("dve", t1k[:], t1k[:], rrm[:], ALU.subtract)
                        P.tt("pool", rrt[:], cosT[:], w_i[:], ALU.mult)
                        P.tt("dve", rrm[:], sinT[:], w_r[:], ALU.mult)
                        P.tt("dve", rrt[:], rrt[:], rrm[:], ALU.add)
                        spr = Sp_r[:].rearrange("p (m k) -> p m k", m=4)
                        spi = Sp_i[:].rearrange("p (m k) -> p m k", m=4)
                        P.memset("pool", spr[:, :, 0:1], 0.0)
                        P.memset("pool", spi[:, :, 0:1], 0.0)
                        P.cp("dve", spr[:, :, 1:256], a3(t1k)[:, :, 0:255])
                        P.cp("pool", spi[:, :, 1:256], a3(rrt)[:, :, 0:255])
                        x0rv = x0r[:].rearrange("p (m b) -> p m b", b=16)[:, msl, :]
                        x0iv = x0i[:].rearrange("p (m b) -> p m b", b=16)[:, msl, :]
                        P.cp("dve", spr[:, :, 256:272], x0rv)
                        P.cp("pool", spi[:, :, 256:272], x0iv)
                        P.cp("dve", FSp[:, 4 * c:4 * c + 4], a3(t1k)[:, :, 255])
                        P.cp("dve", FSp[:, 32 + 4 * c:32 + 4 * c + 4], a3(rrt)[:, :, 255])
                        fsr = FSs[:, 0:512].rearrange("p (m b) -> p m b", b=16)[:, msl, :]
                        fsi = FSs[:, 512:1024].rearrange("p (m b) -> p m b", b=16)[:, msl, :]
                        p8 = bc_last(PR[:, 8 * 32 + 4 * c:8 * 32 + 4 * c + 4], 16)
                        i8_ = bc_last(PI[:, 8 * 32 + 4 * c:8 * 32 + 4 * c + 4], 16)
                        sir = Sin_r[:].rearrange("p (m k) -> p m k", m=4)[:, :, 256:272]
                        sii = Sin_i[:].rearrange("p (m k) -> p m k", m=4)[:, :, 256:272]
                        tq = tX[:, 0:64].rearrange("p (m b) -> p m b", b=16)
                        P.tt("dve", fsr, p8, x0rv, ALU.mult)
                        P.tt("dve", tq, i8_, x0iv, ALU.mult)
                        P.tt("dve", fsr, fsr, tq, ALU.subtract)
                        P.tt("dve", fsr, fsr, sir, ALU.add)
                        P.tt("dve", fsi, p8, x0iv, ALU.mult)
                        P.tt("dve", tq, i8_, x0rv, ALU.mult)
                        P.tt("dve", fsi, fsi, tq, ALU.add)
                        P.tt("dve", fsi, fsi, sii, ALU.add)
                        for j in range(8):
                            acc = psS[j % 2]
                            for s_ in range(j + 1):
                                P.mm(acc[:, 0:NCH], BD[:, (j - s_) * 128:(j - s_ + 1) * 128], uc[:, s_:NT:8], start=(s_ == 0), stop=False)
                            for m4 in range(4):
                                for ri, spt in ((0, Sp_r), (1, Sp_i)):
                                    last = (m4 == 3 and ri == 1)
                                    P.mm(acc[:, 0:NCH], VZ[:, ((j * 4 + m4) * 2 + ri) * 128:((j * 4 + m4) * 2 + ri + 1) * 128],
                                         spt[:, m4 * NCH:(m4 + 1) * NCH], start=False, stop=last)
                            P.stt("dve", yv[:], uc[:, j:NT:8], dsk[:, c:c + 1], acc[:, 0:NCH], ALU.mult, ALU.add)
                            P.tt("pool", y2[:], yv[:], yv[:], ALU.mult)
                            P.ts("dve", y2[:], y2[:], 0.044715, 1.0, ALU.mult, ALU.add)
                            P.tt("pool", y2[:], y2[:], yv[:], ALU.mult)
                            P.actv(y3[:], y2[:], AF.Sigmoid, scale=1.5957691216057308)
                            P.tt("dve", ygc[:, j:NT:8], yv[:], y3[:], ALU.mult)
                        P.cp("pool", uT[:, c * NT:(c + 1) * NT], ygc[:])
                    P.dma(o_ssp.rearrange("r p m -> p r m"), FSp[:].rearrange("p (r m) -> p r m", r=2))
                    P.dma(o_sss.rearrange("r p x -> p r x"), FSs[:].rearrange("p (r x) -> p r x", r=2))
                    P.barrier()

                with ExitStack() as st3:
                    B3 = lambda name, shape, dt=F32: alloc(st3, name, shape, dt)
                    W1 = B3("W1", [128, 8 * 1024], BF16)
                    W2 = B3("W2", [128, 8 * 1024], BF16)
                    Wo2 = B3("Wo2", [128, 8 * 1024], BF16)
                    yt = [B3("ytc%d" % i, [128, 1024]) for i in range(2)]
                    oT = B3("oT", [128, 8 * 512], BF16)
                    sg = B3("sg", [128, 512]); tg = B3("tg", [128, 512])
                    with ExitStack() as stg_:
                        stg = [alloc(stg_, "stgc%d" % i, [128, 1024]) for i in range(2)]
                        k_ = 0
                        for (Wd, src) in ((W1, glu1), (W2, glu2), (Wo2, w_out_o)):
                            for kc in range(8):
                                s_ = stg[k_ % 2]; k_ += 1
                                P.dma(s_[:], src[kc * 128:(kc + 1) * 128, :])
                                P.cp("dve" if kc % 2 == 0 else "pool", Wd[:, kc * 1024:(kc + 1) * 1024], s_[:])
                        P.barrier()
                    for (t0, t1) in blocks:
                        nb = (t1 - t0) * 128
                        col0 = t0 * 128
                        for fc in range(8):
                            p1 = psA[0]; p2 = psA[1]
                            for kc in range(8):
                                P.mm(p1[:, 0:nb], W1[:, kc * 1024 + fc * 128:kc * 1024 + (fc + 1) * 128], uT[:, kc * NT + col0:kc * NT + col0 + nb], start=(kc == 0), stop=(kc == 7))
                            for kc in range(8):
                                P.mm(p2[:, 0:nb], W2[:, kc * 1024 + fc * 128:kc * 1024 + (fc + 1) * 128], uT[:, kc * NT + col0:kc * NT + col0 + nb], start=(kc == 0), stop=(kc == 7))
                            P.actv(sg[:, 0:nb], p2[:, 0:nb], AF.Sigmoid)
                            P.tt("dve", tg[:, 0:nb], p1[:, 0:nb], sg[:, 0:nb], ALU.mult)
                            P.tt("pool", oT[:, fc * 512:fc * 512 + nb], tg[:, 0:nb], szT[:, fc * NT + col0:fc * NT + col0 + nb], ALU.mult)
                        for ti in range(t0, t1):
                            lt = ti - t0
                            ytile = yt[cntb["x"] % 2]; cntb["x"] += 1
                            load_tile(ytile[:], ti)
                            for hf in range(2):
                                pa = psS[hf]
                                for fc in range(8):
                                    P.mm(pa[:, :], oT[:, fc * 512 + lt * 128:fc * 512 + (lt + 1) * 128], Wo2[:, fc * 1024 + hf * 512:fc * 1024 + (hf + 1) * 512], start=(fc == 0), stop=(fc == 7))
                                P.tt("dve", ytile[:, hf * 512:(hf + 1) * 512], ytile[:, hf * 512:(hf + 1) * 512], pa[:, :], ALU.add)
                            if ti < 16:
                                P.dma(yp[ti * 128:(ti + 1) * 128, :], ytile[:], reads=[ytile, ("y0", "p", ti)], writes=[("y0", "p", ti)])
                            else:
                                P.dma(ys, ytile[:], reads=[ytile, ("y0", "s", 0)], writes=[("y0", "s", 0)])
                    P.barrier()
        P.barrier()
        P.flush()
    return nc, in_names


_PROG_CACHE = {}


def _shared_inputs(inputs, consts):
    perm = _perm_even()
    sh = {}
    sh["w_in_e"] = np.ascontiguousarray(inputs["w_in_even"][0][:, perm])
    sh["w_out_e"] = np.ascontiguousarray(inputs["w_out_even"][0])
    sh["norm_e"] = np.ascontiguousarray(inputs["norm_even"][0].reshape(8, 128).T)
    sh["gn_gain"] = np.ascontiguousarray(inputs["ret_gn_gain"][0].reshape(1, 512))
    qn = inputs["nsa_q_norm"][0]
    kn = inputs["nsa_k_norm"][0]
    sh["qk_gain"] = np.concatenate([np.tile(qn, 8), np.tile(kn[0], 2), np.tile(kn[1], 2), np.tile(kn[2], 2)]).reshape(1, 896).astype(np.float32)
    sh["cmp_posT"] = np.ascontiguousarray(inputs["nsa_cmp_pos"][0].transpose(2, 0, 1))
    sh["cmp_w"] = np.ascontiguousarray(inputs["nsa_cmp_w"][0])
    for k, v in consts.items():
        sh["c_" + k] = v
    ca = np.ascontiguousarray
    sh["w_in_o"] = ca(inputs["w_in_odd"][0])
    sh["glu1"] = ca(inputs["glu_w1"][0])
    sh["glu2"] = ca(inputs["glu_w2"][0])
    sh["w_out_o"] = ca(inputs["w_out_odd"][0])
    sh["norm_o"] = ca(inputs["norm_odd"][0].reshape(8, 128).T)
    sh["ssmd"] = ca(inputs["ssm_d"][0].reshape(8, 128).T)
    sh["lamre_A"] = ca(inputs["ssm_lambda_re"][0].reshape(32, 2, 64).transpose(1, 2, 0).reshape(128, 32))
    sh["lamim_A"] = ca(inputs["ssm_lambda_im"][0].reshape(32, 2, 64).transpose(1, 2, 0).reshape(128, 32))
    ls = np.broadcast_to(inputs["ssm_log_step"][0].reshape(32, 2, 1), (32, 2, 64))
    sh["lstep_A"] = ca(ls.transpose(1, 2, 0).reshape(128, 32)).astype(np.float32)
    sh["bA_re"] = ca(inputs["ssm_b_re"][0].reshape(32, 2, 64, 16).transpose(1, 2, 0, 3).reshape(128, 512))
    sh["bA_im"] = ca(inputs["ssm_b_im"][0].reshape(32, 2, 64, 16).transpose(1, 2, 0, 3).reshape(128, 512))
    sh["cA_re"] = ca(inputs["ssm_c_re"][0].reshape(32, 2, 16, 64).transpose(1, 3, 0, 2).reshape(128, 512))
    sh["cA_im"] = ca(inputs["ssm_c_im"][0].reshape(32, 2, 16, 64).transpose(1, 3, 0, 2).reshape(128, 512))
    return sh


def _core_inputs(inputs, c, sh, cache_rows):
    im = dict(sh)
    im["xp"] = np.ascontiguousarray(inputs["x_prompt"][c])
    im["xs"] = np.ascontiguousarray(inputs["x_sample"][16 * c:16 * c + 16].reshape(128, 1024))
    im["cache"] = inputs["cache_nsa_kv"][0].reshape(2560 * 128, 512)[0:cache_rows]
    im["cwin"] = np.ascontiguousarray(inputs["cache_nsa_win"][0, 16 * c:16 * c + 16].reshape(16, 512, 256))
    im["sret"] = np.ascontiguousarray(inputs["state_ret"][0, 16 * c:16 * c + 16])
    im["x0A_re"] = np.ascontiguousarray(inputs["state_ssm_re"][0, 16 * c:16 * c + 16].reshape(16, 32, 2, 64).transpose(2, 3, 1, 0).reshape(128, 512))
    im["x0A_im"] = np.ascontiguousarray(inputs["state_ssm_im"][0, 16 * c:16 * c + 16].reshape(16, 32, 2, 64).transpose(2, 3, 1, 0).reshape(128, 512))
    im["ptab"] = np.ascontiguousarray(inputs["page_table"][16 * c:16 * c + 16].reshape(1, 256)).astype(np.int32)
    return im


def run_cores(inputs, cores, trace=False, **opts):
    consts = make_consts()
    key = tuple(sorted(opts.items()))
    nc, in_names = build_program(consts, **opts)
    sh = _shared_inputs(inputs, consts)
    cache_rows = 2560 * 128 if opts.get("do_sample", True) else 128
    in_maps = [_core_inputs(inputs, c, sh, cache_rows) for c in cores]
    in_maps = [{k: m[k] for k in in_names} for m in in_maps]
    if trace:
        res = run_bass_kernel_spmd(nc, in_maps, core_ids=list(range(len(cores))), trace=True)
        print("EXEC_TIME_NS", res.exec_time_ns)
        return res.results
    res = run_bass_kernel_spmd(nc, in_maps, core_ids=list(range(len(cores))))
    return res.results


def kernel(**inputs):
    inputs = {k: np.asarray(v) for k, v in inputs.items()}
    res = run_cores(inputs, list(range(NCORES)))
    f32 = np.float32
    y_p = np.zeros((8, 2048, 1024), f32)
    y_s = np.zeros((128, 8, 1024), f32)
    ret_p = np.zeros((1, 8, 4, 64, 128), f32)
    ret_s = np.zeros((1, 128, 4, 64, 128), f32)
    kv_p = np.zeros((1, 8, 2048, 4, 2, 64), f32)
    kv_s = np.zeros((1, 128, 8, 4, 2, 64), f32)
    win_p = np.zeros((1, 8, 512, 2, 2, 64), f32)
    win_s = np.zeros((1, 128, 512, 2, 2, 64), f32)
    sre_p = np.zeros((1, 8, 64, 64), f32)
    sim_p = np.zeros((1, 8, 64, 64), f32)
    sre_s = np.zeros((1, 128, 64, 64), f32)
    sim_s = np.zeros((1, 128, 64, 64), f32)
    for c in range(NCORES):
        r = res[c]
        sl = slice(16 * c, 16 * c + 16)
        y_p[c] = r["yp"]
        y_s[sl] = r["ys"].reshape(16, 8, 1024)
        ret_p[0, c] = r["o_retp"]
        ret_s[0, sl] = r["o_rets"]
        kv_p[0, c] = r["o_kvp"].reshape(2048, 4, 2, 64)
        kv_s[0, sl] = r["o_kvs"].reshape(16, 8, 4, 2, 64)
        win_p[0, c] = r["o_winp"].reshape(512, 2, 2, 64)
        win_s[0, sl] = r["o_wins"].reshape(16, 512, 2, 2, 64)
        if "o_ssp" in r:
            sp = r["o_ssp"].reshape(2, 2, 64, 32).transpose(0, 3, 1, 2).reshape(2, 64, 64)
            sre_p[0, c] = sp[0]
            sim_p[0, c] = sp[1]
            ss = r["o_sss"].reshape(2, 2, 64, 32, 16).transpose(0, 4, 3, 1, 2).reshape(2, 16, 64, 64)
            sre_s[0, sl] = ss[0]
            sim_s[0, sl] = ss[1]
    return (y_p, y_s, ret_p, ret_s, kv_p, kv_s, win_p, win_s, sre_p, sim_p, sre_s, sim_s)
```

```python
import numpy as np
import concourse.bass as bass
import concourse.mybir as mybir
from concourse.bass_utils import run_bass_kernel_spmd

F32 = mybir.dt.float32
BF16 = mybir.dt.bfloat16
I32 = mybir.dt.int32
AF = mybir.ActivationFunctionType
ALU = mybir.AluOpType
AX = mybir.AxisListType

ENGS = ("pe", "act", "dve", "pool", "sp")
NDMASEM = 24


class Prog:
    def __init__(self, nc):
        self.nc = nc
        self.ops = {e: [] for e in ENGS}
        self.cnt = {e: 0 for e in ENGS}
        self.sem = {}
        self.seen = {e: {} for e in ENGS}
        self.last_w = {}
        self.readers = {}
        self.dma_i = 0
        self.dma_uses = [0] * NDMASEM
        self.dma_tok = [None] * NDMASEM
        self.stack = None
        self.n_ops = 0

    def setup(self, stack):
        self.stack = stack
        for e in ENGS:
            self.sem[e] = stack.enter_context(self.nc.semaphore("s_" + e))
        for i in range(NDMASEM):
            self.sem["d%d" % i] = stack.enter_context(self.nc.semaphore("s_d%d" % i))

    def _key(self, a):
        if isinstance(a, str):
            return a
        if isinstance(a, tuple):
            return a
        t = getattr(a, 'tensor', None)
        return t.name if t is not None else a.name

    def _deps(self, eng, reads, writes):
        toks = []
        for k in reads:
            k = self._key(k)
            t = self.last_w.get(k)
            if t is not None:
                toks.append(t)
        for k in writes:
            k = self._key(k)
            t = self.last_w.get(k)
            if t is not None:
                toks.append(t)
            toks.extend(self.readers.get(k, ()))
        need = {}
        for (s, v) in toks:
            if eng == "pe" and s == "pe":
                continue
            if v > need.get(s, 0):
                need[s] = v
        waits = []
        seen = self.seen[eng]
        for s, v in need.items():
            if seen.get(s, 0) >= v:
                continue
            seen[s] = v
            waits.append((s, v))
        return waits

    def _commit(self, tok, reads, writes):
        for k in writes:
            k = self._key(k)
            self.last_w[k] = tok
            self.readers[k] = []
        for k in reads:
            k = self._key(k)
            self.readers.setdefault(k, []).append(tok)

    def op(self, eng, fn, reads=(), writes=()):
        waits = self._deps(eng, reads, writes)
        self.cnt[eng] += 1
        tok = (eng, self.cnt[eng])
        self.ops[eng].append((waits, fn, (eng, 1)))
        self._commit(tok, reads, writes)
        self.n_ops += 1

    def dma(self, out, in_, reads=None, writes=None, q="sp", fn=None):
        if reads is None:
            reads = [in_]
        if writes is None:
            writes = [out]
        i = self.dma_i % NDMASEM
        self.dma_i += 1
        sname = "d%d" % i
        waits = self._deps(q, reads, writes)
        prev = self.dma_tok[i]
        if prev is not None and self.seen[q].get(sname, 0) < prev[1]:
            self.seen[q][sname] = prev[1]
            waits.append(prev)
        self.dma_uses[i] += 1
        tok = (sname, 16 * self.dma_uses[i])
        self.dma_tok[i] = tok
        if fn is None:
            fn = lambda e, o=out, a=in_: e.dma_start(out=o, in_=a)
        self.ops[q].append((waits, fn, (sname, 16)))
        self._commit(tok, reads, writes)
        self.n_ops += 1

    def barrier(self):
        for e in ENGS:
            waits = []
            for e2 in ENGS:
                if e2 == e:
                    continue
                v = self.cnt[e2]
                if v > self.seen[e].get(e2, 0):
                    self.seen[e][e2] = v
                    waits.append((e2, v))
            for i in range(NDMASEM):
                t = self.dma_tok[i]
                if t is not None and self.seen[e].get(t[0], 0) < t[1]:
                    self.seen[e][t[0]] = t[1]
                    waits.append(t)
            if waits:
                self.ops[e].append((waits, None, None))

    def flush(self):
        nc = self.nc
        ops = self.ops
        sem = self.sem

        def replay(engh, lst):
            for (waits, fn, inc) in lst:
                for (s, v) in waits:
                    engh.wait_ge(sem[s], v)
                if fn is not None:
                    ins = fn(engh)
                    ins.then_inc(sem[inc[0]], inc[1])

        with nc.Block() as block:
            @block.tensor
            def _(e):
                replay(e, ops["pe"])

            @block.scalar
            def _(e):
                replay(e, ops["act"])

            @block.vector
            def _(e):
                replay(e, ops["dve"])

            @block.gpsimd
            def _(e):
                replay(e, ops["pool"])

            @block.sync
            def _(e):
                replay(e, ops["sp"])
        self.ops = {e: [] for e in ENGS}

    def mm(self, out, lhsT, rhs, start=True, stop=True, reads=None, writes=None):
        if reads is None:
            reads = [lhsT, rhs]
        if writes is None:
            writes = [out]
        self.op("pe", lambda e: e.matmul(out, lhsT, rhs, start=start, stop=stop), reads, writes)

    def tr(self, out, in_, ident, reads=None, writes=None):
        if reads is None:
            reads = [in_, ident]
        if writes is None:
            writes = [out]
        self.op("pe", lambda e: e.transpose(out, in_, ident), reads, writes)

    def actv(self, out, in_, func, bias=None, scale=None, accum_out=None, reads=None, writes=None, eng="act"):
        kw = {}
        if bias is not None:
            kw["bias"] = bias
        if scale is not None:
            kw["scale"] = scale
        if accum_out is not None:
            kw["accum_out"] = accum_out
        if reads is None:
            reads = [in_]
            if bias is not None and not isinstance(bias, (int, float)):
                reads.append(bias)
            if scale is not None and not isinstance(scale, (int, float)):
                reads.append(scale)
        if writes is None:
            writes = [out]
            if accum_out is not None:
                writes.append(accum_out)
        self.op("act", lambda e: e.activation(out, in_, func, **kw), reads, writes)

    def ts(self, eng, out, in0, s1, s2, op0, op1=None, accum_out=None, reads=None, writes=None):
        kw = {}
        if op1 is not None:
            kw["op1"] = op1
        if accum_out is not None:
            kw["accum_out"] = accum_out
        if reads is None:
            reads = [in0]
            for s in (s1, s2):
                if s is not None and not isinstance(s, (int, float)):
                    reads.append(s)
        if writes is None:
            writes = [out]
            if accum_out is not None:
                writes.append(accum_out)
        self.op(eng, lambda e: e.tensor_scalar(out, in0, s1, s2, op0, **kw), reads, writes)

    def tt(self, eng, out, in0, in1, op, reads=None, writes=None):
        if reads is None:
            reads = [in0, in1]
        if writes is None:
            writes = [out]
        self.op(eng, lambda e: e.tensor_tensor(out, in0, in1, op), reads, writes)

    def stt(self, eng, out, in0, scalar, in1, op0, op1, accum_out=None, reads=None, writes=None):
        kw = {}
        if accum_out is not None:
            kw["accum_out"] = accum_out
        if reads is None:
            reads = [in0, in1]
            if not isinstance(scalar, (int, float)):
                reads.append(scalar)
        if writes is None:
            writes = [out]
            if accum_out is not None:
                writes.append(accum_out)
        self.op(eng, lambda e: e.scalar_tensor_tensor(out, in0, scalar, in1, op0, op1, **kw), reads, writes)

    def cp(self, eng, out, in_, reads=None, writes=None):
        if reads is None:
            reads = [in_]
        if writes is None:
            writes = [out]
        if eng == "act":
            self.op(eng, lambda e: e.copy(out, in_), reads, writes)
        else:
            self.op(eng, lambda e: e.tensor_copy(out, in_), reads, writes)

    def memset(self, eng, ap, val, writes=None):
        if writes is None:
            writes = [ap]
        self.op(eng, lambda e: e.memset(ap, val), (), writes)

import math
from contextlib import ExitStack
import ml_dtypes

BF = ml_dtypes.bfloat16
NCORES = 8
BIG = 1.0e4
FORCE = 1.0e4
EPS = 1e-6
SCALE = 0.125
QA, KA, QN, KC, KS, KW, VC, VS, VW, GL, VA, ZA, ZB = 0, 256, 512, 1024, 1152, 1280, 1408, 1536, 1664, 1792, 1816, 2328, 2840
EIN = 3352
GROUPS = [(0, 512), (512, 1024), (1024, 1408), (1408, 1816), (1816, 2328), (2328, 2840), (2840, 3352)]


def _perm_even():
    perm = np.zeros(EIN, np.int64)
    perm[QA:QA + 256] = np.arange(0, 256)
    perm[KA:KA + 256] = np.arange(256, 512)
    o_va, o_za, o_qn, o_kvb, o_gl, o_zb = 512, 1024, 1536, 2048, 2816, 2840
    for j in range(4):
        for g in range(2):
            for dd in range(64):
                perm[QN + (j * 2 + g) * 64 + dd] = o_qn + (4 * g + j) * 64 + dd
    for dst, kind in ((KC, 0), (KS, 2), (KW, 4), (VC, 1), (VS, 3), (VW, 5)):
        perm[dst:dst + 128] = o_kvb + kind * 128 + np.arange(128)
    perm[GL:GL + 24] = o_gl + np.arange(24)
    perm[VA:VA + 512] = o_va + np.arange(512)
    perm[ZA:ZA + 512] = o_za + np.arange(512)
    perm[ZB:ZB + 512] = o_zb + np.arange(512)
    assert len(set(perm.tolist())) == EIN
    return perm


def make_consts():
    c = {}
    f32 = np.float32
    c["ident"] = np.eye(128, dtype=f32).astype(BF)
    c["ident32"] = np.eye(128, dtype=f32)
    half = 32
    inv = (np.float32(10000.0) ** (-np.arange(half, dtype=f32) / f32(half))).astype(f32)
    pos_p = (np.arange(16)[None, :] * 128 + np.arange(128)[:, None]).astype(f32)
    ang = (pos_p[:, :, None] * inv[None, None, :]).astype(f32)
    c["cos_p"] = np.cos(ang).astype(f32)
    c["sin_p"] = np.sin(ang).astype(f32)
    pos_s = (2048 + (np.arange(128) % 8)).astype(f32)
    ang = (pos_s[:, None] * inv[None, :]).astype(f32)
    c["cos_s"] = np.cos(ang).astype(f32)
    c["sin_s"] = np.sin(ang).astype(f32)
    log_g = np.log(1.0 - 2.0 ** (-5.0 - np.arange(4, dtype=np.float64)))
    i = np.arange(128)
    diff = i[None, :] - i[:, None]
    caus = diff >= 0
    DT = np.zeros((128, 4, 128), f32)
    for h in range(4):
        DT[:, h, :] = 0.125 * np.exp(np.where(caus, diff, 0) * log_g[h]) * caus
    c["DTp"] = DT
    c["qdec_p"] = np.exp((i[:, None] + 1.0) * log_g[None, :]).astype(f32)
    c["kdec_p"] = (0.125 * np.exp((127.0 - i[:, None]) * log_g[None, :])).astype(f32)
    cd = np.zeros((128, 2), f32)
    cds = np.zeros((128, 2), f32)
    for p in range(128):
        for pr in range(2):
            h = 2 * pr + p // 64
            cd[p, pr] = np.exp(128.0 * log_g[h])
            cds[p, pr] = np.exp(8.0 * log_g[h])
    c["cdec_p"] = cd
    c["cdec_s"] = cds
    i8 = i % 8
    same = (i[:, None] // 8) == (i[None, :] // 8)
    diff8 = i8[None, :] - i8[:, None]
    caus8 = same & (diff8 >= 0)
    DTs = np.zeros((128, 4, 128), f32)
    for h in range(4):
        DTs[:, h, :] = 0.125 * np.exp(np.where(caus8, diff8, 0) * log_g[h]) * caus8
    c["DTs"] = DTs
    c["qdec_s"] = np.exp((i8[:, None] + 1.0) * log_g[None, :]).astype(f32)
    c["kdec_s"] = (0.125 * np.exp((7.0 - i8[:, None]) * log_g[None, :])).astype(f32)
    c["blkmask"] = ((i[:, None] // 8) == np.arange(16)[None, :]).astype(f32)
    cm = np.zeros((128, 16, 128), f32)
    cm[:, :, :] = ((i[None, :] // 8) == np.arange(16)[:, None])[None, :, :]
    c["colmask"] = cm.astype(BF)
    keys = np.arange(2176)
    E = (keys[None, :] // 64 == np.arange(33)[:, None]).astype(f32)
    c["E"] = E.astype(BF)
    c["CB"] = np.where(i[:, None] <= i[None, :], 0.0, -BIG).astype(f32).astype(BF)
    c["AB"] = np.where(i[:, None] > i[None, :], 0.0, -BIG).astype(f32).astype(BF)
    n = np.arange(128)
    cmpb = np.zeros((128, 16, 128), f32)
    for t in range(16):
        tpos = 128 * t + i
        cmpb[:, t, :] = np.where((16 * n[:, None] + 31) <= tpos[None, :], 0.0, -BIG)
    c["CMPB"] = cmpb.astype(BF)

    def overlap(n_s):
        c_start = np.arange(127) * 16
        s_start = np.arange(n_s) * 64
        return ((c_start[:, None] < s_start[None, :] + 64) & (s_start[None, :] < c_start[:, None] + 32)).astype(f32)
    c["ovl_p"] = overlap(32)
    c["ovl_s"] = overlap(33)
    mulc = np.zeros((128, 16, 32), f32)
    addc = np.zeros((128, 16, 32), f32)
    s_ids = np.arange(32)
    for t in range(16):
        tpos = 128 * t + i
        cur = tpos // 64
        forced = (s_ids[None, :] == 0) | (s_ids[None, :] == cur[:, None]) | (s_ids[None, :] == cur[:, None] - 1)
        valid = (s_ids[None, :] * 64) <= tpos[:, None]
        mulc[:, t, :] = (valid & ~forced)
        addc[:, t, :] = np.where(valid, np.where(forced, FORCE, 0.0), -FORCE)
    c["mulc_p"] = mulc
    c["addc_p"] = addc
    s33 = np.arange(33)
    forced = (s33 == 0) | (s33 == 32) | (s33 == 31)
    c["mulc_s"] = np.tile((~forced).astype(f32)[None, :], (8, 1))
    c["addc_s"] = np.tile(np.where(forced, FORCE, 0.0).astype(f32)[None, :], (8, 1))
    q8 = np.arange(8)
    SB = np.where(((i[:, None, None] // 8) == np.arange(16)[None, :, None]) & ((i[:, None, None] % 8) <= q8[None, None, :]), 0.0, -BIG)
    c["SB"] = SB.astype(f32).astype(BF)
    c["ABs"] = np.where(i[:, None] > q8[None, :], 0.0, -BIG).astype(f32).astype(BF)
    c["iota_p"] = i.astype(f32)[:, None].copy()
    c["ones_row"] = np.ones((1, 128), f32).astype(BF)
    c["zeros_row"] = np.zeros((1, 512), f32).astype(BF)
    jv = np.zeros((128, 9, 32), f32)
    jv[:, :, :] = np.arange(9, dtype=f32)[None, :, None]
    c["JV"] = jv
    c["KI"] = np.tile(np.arange(256, dtype=f32)[None, :], (128, 1))
    c["PM4"] = (np.arange(128)[:, None] // 32 == np.arange(4)[None, :]).astype(f32)
    c["BM16"] = (np.arange(128)[:, None] // 16 == np.arange(128)[None, :] // 16).astype(f32)
    return c


def _dt_of(a):
    if a.dtype == np.float32:
        return F32
    if a.dtype == np.int32:
        return I32
    if a.dtype == BF:
        return BF16
    raise ValueError(a.dtype)


def v3(ap, h):
    return ap.rearrange("p (h d) -> p h d", h=h)


def bc_mid(ap, n):
    return ap.unsqueeze(1).to_broadcast([ap.shape[0], n, ap.shape[1]])


def bc_last(ap, n):
    return ap.unsqueeze(2).to_broadcast([ap.shape[0], ap.shape[1], n])


def build_program(consts, phase_b=True, do_sample=True, n_ptiles=16, stage=99):
    nc = bass.Bass("TRN2", target_bir_lowering=False)

    in_names = []

    def din(name, shape, dt=F32):
        in_names.append(name)
        return nc.dram_tensor(name, list(shape), dt, kind="ExternalInput").ap()

    def dout(name, shape, dt=F32):
        return nc.dram_tensor(name, list(shape), dt, kind="ExternalOutput").ap()

    xp = din("xp", [2048, 1024])
    xs = din("xs", [128, 1024])
    if do_sample:
        cache = din("cache", [2560 * 128, 512])
        cwin = din("cwin", [16, 512, 256])
        sret = din("sret", [16, 4, 64, 128])
        ptab = din("ptab", [1, 256], I32)
    w_in_e = din("w_in_e", [1024, EIN])
    w_out_e = din("w_out_e", [1024, 1024])
    norm_e = din("norm_e", [128, 8])
    gn_gain = din("gn_gain", [1, 512])
    qk_gain = din("qk_gain", [1, 896])
    cmp_posT = din("cmp_posT", [64, 2, 32])
    cmp_w = din("cmp_w", [2, 32, 64, 64])
    S_ONLY = ("cos_s", "sin_s", "DTs", "qdec_s", "kdec_s", "cdec_s", "blkmask", "colmask", "SB", "ABs", "mulc_s", "addc_s", "ovl_s", "iota_p")
    if phase_b:
        w_in_o = din("w_in_o", [1024, 2048])
        glu1 = din("glu1", [1024, 1024])
        glu2 = din("glu2", [1024, 1024])
        w_out_o = din("w_out_o", [1024, 1024])
        norm_o = din("norm_o", [128, 8])
        ssmd = din("ssmd", [128, 8])
        lamre_A = din("lamre_A", [128, 32])
        lamim_A = din("lamim_A", [128, 32])
        lstep_A = din("lstep_A", [128, 32])
        bA_re = din("bA_re", [128, 512])
        bA_im = din("bA_im", [128, 512])
        cA_re = din("cA_re", [128, 512])
        cA_im = din("cA_im", [128, 512])
        x0A_re = din("x0A_re", [128, 512])
        x0A_im = din("x0A_im", [128, 512])
    cd = {k: din("c_" + k, v.shape, _dt_of(v)) for k, v in consts.items() if ((do_sample or k not in S_ONLY) and (phase_b or k not in ("JV", "KI", "PM4", "BM16")))}
    yp = dout("yp", [2048, 1024])
    ys = dout("ys", [128, 1024])
    o_retp = dout("o_retp", [4, 64, 128])
    o_rets = dout("o_rets", [16, 4, 64, 128])
    o_kvp = dout("o_kvp", [2048, 512])
    o_kvs = dout("o_kvs", [128, 512])
    o_winp = dout("o_winp", [512, 256])
    o_wins = dout("o_wins", [16, 512, 256])
    if phase_b:
        o_ssp = dout("o_ssp", [2, 128, 32])
        o_sss = dout("o_sss", [2, 128, 512])

    P = Prog(nc)
    with ExitStack() as st0:
        P.setup(st0)

        def alloc(st, name, shape, dt=F32):
            return st.enter_context(nc.sbuf_tensor(name, list(shape), dt))

        def palloc(st, name, shape, dt=F32):
            return st.enter_context(nc.psum_tensor(name, list(shape), dt))

        psT = palloc(st0, "psT", [128, 1024], BF16)
        psA = [palloc(st0, "psA%d" % i, [128, 512]) for i in range(2)]
        psS = [palloc(st0, "psS%d" % i, [128, 512]) for i in range(2)]
        psV = palloc(st0, "psV", [128, 512])
        psC = palloc(st0, "psC", [128, 512])
        psR = palloc(st0, "psR", [128, 512])

        with ExitStack() as stA:
            A = lambda name, shape, dt=F32: alloc(stA, name, shape, dt)
            cs = {}
            P_ONLY = ("cos_p", "sin_p", "DTp", "qdec_p", "kdec_p", "cdec_p", "CMPB", "mulc_p", "addc_p", "ovl_p", "CB", "AB")

            def load_consts(stx, names, pre="k_"):
                for k in names:
                    v = consts[k]
                    shp = list(v.shape)
                    if len(shp) == 3:
                        tl = alloc(stx, pre + k, [shp[0], shp[1] * shp[2]], _dt_of(v))
                        P.dma(tl[:], cd[k].rearrange("p a b -> p (a b)"))
                    else:
                        tl = alloc(stx, pre + k, shp, _dt_of(v))
                        P.dma(tl[:], cd[k])
                    cs[k] = tl
            B_ONLY = ("JV", "KI", "PM4", "BM16")
            load_consts(stA, [k for k in consts if k not in P_ONLY and k not in S_ONLY and k not in B_ONLY])
            ident = cs["ident"]
            Wob = A("Wob", [128, 8 * 1024], BF16)
            ng = A("ng", [128, 8])
            BDW = A("BDW", [128, 2 * 32 * 128], BF16)
            peb = A("peb", [128, 64], BF16)
            posk = A("posk", [128, 2])
            posv = A("posv", [1, 128], BF16)
            gnb = A("gnb", [128, 512])
            P.dma(gnb[:], gn_gain.partition_broadcast(128))
            qkg = A("qkg", [128, 896])
            P.dma(qkg[:], qk_gain.partition_broadcast(128))

            xt = [A("xt%d" % i, [128, 1024]) for i in range(2)]
            xb = A("xb", [128, 1024], BF16)
            xT = A("xT", [128, 1024], BF16)
            stt_ = A("stats", [128, 64])
            proj = A("proj", [128, EIN])
            rp = A("rp", [128, 1408])
            tmpa = A("tmpa", [128, 1408])
            r16 = A("r16", [128, 1024], BF16)
            vb16 = A("vb16", [128, 512], BF16)
            rT = A("rT", [128, 256], BF16)
            qz = A("qz", [128, 1024], BF16)
            qTz = A("qTz", [128, 1024], BF16)
            scm = A("scm", [128, 512], BF16)
            S32 = A("S32", [128, 256])
            Sb = A("Sb", [128, 256], BF16)
            osb = A("osb", [128, 512])
            ocb = A("ocb", [128, 512])
            sz = A("sz", [128, 512])
            mix = A("mix", [128, 1024], BF16)
            mixT = A("mixT", [128, 1024], BF16)
            n16 = A("n16", [128, 1024], BF16)
            gts = A("gts", [128, 24])
            on = A("on", [128, 3 * 512])
            tmpb = on
            obt = A("obt", [128, 512])
            pt_ = [A("pt%d" % i, [128, 512], BF16) for i in range(3)]
            selT = A("selT", [33, 256], BF16)
            sc_ = A("sc", [128, 80])
            sc2 = A("sc2", [128, 80])
            sc16 = A("sc16", [128, 80], BF16)
            rden = A("rden", [128, 32])
            top8 = A("top8", [128, 16])
            P.memset("dve", S32[:], 0.0)
            P.memset("dve", Sb[:], 0.0)
            P.memset("pool", qz[:], 0.0)
            P.memset("pool", qTz[:], 0.0)

            stWb = ExitStack()
            Wb = alloc(stWb, "Wb", [128, 8 * EIN], BF16)
            P.dma(ng[:], norm_e)
            with ExitStack() as stW:
                stg = [alloc(stW, "stg%d" % i, [128, EIN]) for i in range(2)]
                for kc in range(8):
                    s_ = stg[kc % 2]
                    P.dma(s_[:], w_in_e[kc * 128:(kc + 1) * 128, :])
                    if kc % 2 == 0:
                        P.ts("dve", Wb[:, kc * EIN:(kc + 1) * EIN], s_[:], ng[:, kc:kc + 1], None, ALU.mult)
                    else:
                        P.op("act", lambda e, kc=kc, s_=s_: e.mul(Wb[:, kc * EIN:(kc + 1) * EIN], s_[:], ng[:, kc:kc + 1]), [s_, ng], [Wb])
                for kc in range(8):
                    s_ = stg[kc % 2]
                    P.dma(s_[:, 0:1024], w_out_e[kc * 128:(kc + 1) * 128, :])
                    if kc % 2 == 0:
                        P.cp("dve", Wob[:, kc * 1024:(kc + 1) * 1024], s_[:, 0:1024])
                    else:
                        P.cp("pool", Wob[:, kc * 1024:(kc + 1) * 1024], s_[:, 0:1024])
                P.barrier()
            with ExitStack() as stW:
                P.memset("pool", BDW[:], 0.0)
                wst = alloc(stW, "wst", [128, 2 * 32 * 64])
                srcw = cmp_w.rearrange("c l d e -> d (c l) e")
                P.dma(wst[0:64, :].rearrange("p (a e) -> p a e", e=64), srcw)
                P.dma(wst[64:128, :].rearrange("p (a e) -> p a e", e=64), srcw)
                bdv = BDW[:].rearrange("p (a e) -> p a e", e=128)
                P.cp("dve", bdv[0:64, :, 0:64], wst[0:64, :].rearrange("p (a e) -> p a e", e=64))
                P.cp("dve", bdv[64:128, :, 64:128], wst[64:128, :].rearrange("p (a e) -> p a e", e=64))
                pe32 = alloc(stW, "pe32", [128, 64])
                P.dma(pe32[0:64, :], cmp_posT.rearrange("d c l -> d (c l)"))
                P.dma(pe32[64:128, :], cmp_posT.rearrange("d c l -> d (c l)"))
                P.cp("dve", peb[:], pe32[:])
                for l in range(32):
                    P.mm(psC[:, 0:1], BDW[:, (0 * 32 + l) * 128:(0 * 32 + l + 1) * 128], peb[:, l:l + 1], start=(l == 0), stop=(l == 31))
                for l in range(32):
                    P.mm(psC[:, 1:2], BDW[:, (32 + l) * 128:(32 + l + 1) * 128], peb[:, 32 + l:32 + l + 1], start=(l == 0), stop=(l == 31))
                P.cp("dve", posk[:, 0:2], psC[:, 0:2])
                P.barrier()
            cnt = {"x": 0, "pt": 0, "ps": 0, "pa": 0, "ps3": 0}
            xpref = {}

            def rsqrt_small(out, in_, mult, add):
                P.ts("dve", out, in_, mult, add, ALU.mult, ALU.add)
                P.actv(out, out, AF.Sqrt)
                P.op("dve", lambda e: e.reciprocal(out, out), [out], [out])

            def nsa_tile(nq, groups, ns, mulc, addc, cmp_bias, merge, qall=None):
                W = 4 * nq
                wv = 65 + ns
                R = [dict(cmp=psA[0], o=0), dict(cmp=psA[1], o=40)]
                if merge:
                    accT = {2: [psR, psR], 1: [psV, psV]}
                    aoff = [0, W]
                else:
                    accT = {2: [psR, psC], 1: [psV, psA[0]]}
                    aoff = [0, 0]
                back = {2: [psR, psC], 1: [psV, psA[0]]}
                if merge:
                    oTb = [obt[:, 0:256], obt[:, 256:512]]
                else:
                    oTb = [obt[:, :], sz[:, :]]
                v4 = lambda ap: ap.rearrange("p (j q) -> p j q", j=4)
                v24 = lambda ap: ap.rearrange("p (g j q) -> p g j q", g=2, j=4)
                for g, G in enumerate(groups):
                    r = R[g]; o = r["o"]
                    ps_s = psS[cnt["ps"] % 2]; cnt["ps"] += 1
                    P.mm(v4(ps_s[0:127, 0:W]), G["cmp_k"], G["qTg"], start=True, stop=(cmp_bias is None))
                    if cmp_bias is not None:
                        P.mm(v4(ps_s[0:127, 0:W]), ident[0:127, 0:127], cmp_bias, start=False, stop=True)
                    pt = pt_[cnt["pt"] % 3]; cnt["pt"] += 1
                    P.actv(pt[0:127, 0:W], ps_s[0:127, 0:W], AF.Exp, scale=SCALE)
                    pc = r["cmp"]
                    for j in range(4):
                        P.mm(pc[0:nq, j * wv:(j + 1) * wv], pt[0:127, j * nq:(j + 1) * nq], G["cmp_v"], start=True, stop=True)
                    pcv = pc[0:nq, 0:4 * wv].rearrange("p (j w) -> p j w", j=4)
                    rd = rden[0:nq, 16 * g:16 * g + 4]
                    P.ts("dve", rd, pcv[:, :, 64], 1e-30, None, ALU.add)
                    P.op("dve", lambda e, rd=rd: e.reciprocal(rd, rd), [rden], [rden])
                    P.tt("dve", G["on_dst"](0), pcv[:, :, 0:64], bc_last(rd, 64), ALU.mult)
                    sc = sc_[0:nq, o:o + ns]
                    P.ts("dve", sc, pcv[:, 0, 65:65 + ns], rden[0:nq, 16 * g:16 * g + 1], None, ALU.mult)
                    for j in range(1, 4):
                        P.stt("dve", sc, pcv[:, j, 65:65 + ns], rden[0:nq, 16 * g + j:16 * g + j + 1], sc, ALU.mult, ALU.add)
                    P.tt("dve", sc, sc, mulc, ALU.mult)
                    P.tt("dve", sc, sc, addc, ALU.add)
                    t8 = top8[0:nq, 8 * g:8 * g + 8]
                    P.op("dve", lambda e, t8=t8, sc=sc: e.max(t8, sc), [sc_], [top8])
                    P.ts("dve", sc2[0:nq, o:o + ns], sc, top8[0:nq, 8 * g + 7:8 * g + 8], BIG, ALU.is_ge, ALU.mult)
                    P.ts("dve", sc16[0:nq, o:o + ns], sc2[0:nq, o:o + ns], -BIG, None, ALU.add)

                def scores(stp):
                    gs, br, ci, nch, chs = stp["d"]
                    chk = chs[0]
                    (kT, v1, nk, ecols, bias2d) = chk[0:5]
                    kkey = chk[5] if len(chk) > 5 and chk[5] is not None else kT
                    ps_s = (psS[0], psS[1], psA[1])[cnt["ps3"] % 3]; cnt["ps3"] += 1
                    stp["ps"] = ps_s
                    use_e = (br == 1 and ecols is not None)
                    extra = (1 if use_e else 0) + (1 if bias2d is not None else 0)
                    if len(gs) == 2:
                        outv = v24(ps_s[0:nk, 0:2 * W])
                        qr = qall
                        br_ = None if bias2d is None else bias2d.unsqueeze(1).unsqueeze(1).to_broadcast([nk, 2, 4, nq])
                        er_ = selT[0:ns, 0:256].rearrange("p (g q) -> p g q", g=2)[:, :, 0:nq].unsqueeze(2).to_broadcast([ns, 2, 4, nq])
                    else:
                        g = gs[0]
                        outv = v4(ps_s[0:nk, 0:W])
                        qr = groups[g]["qTg"]
                        br_ = None if bias2d is None else bc_mid(bias2d, 4)
                        er_ = bc_mid(selT[0:ns, 128 * g:128 * g + nq], 4)
                    P.mm(outv, kT, qr, start=True, stop=(extra == 0), reads=[kkey, qTz])
                    if bias2d is not None:
                        extra -= 1
                        P.mm(outv, ident[0:nk, 0:nk], br_, start=False, stop=(extra == 0))
                    if use_e:
                        P.mm(outv, ecols, er_, start=False, stop=True)

                def exp_pv(stp):
                    gs, br, ci, nch, chs = stp["d"]
                    nk = chs[0][2]
                    ps_s = stp["ps"]
                    Wt = W * len(gs)
                    pt = pt_[cnt["pt"] % 3]; cnt["pt"] += 1
                    P.actv(pt[0:nk, 0:Wt], ps_s[0:nk, 0:Wt], AF.Exp, scale=SCALE)
                    for k, g in enumerate(gs):
                        chk = chs[k]
                        v1 = chk[1]
                        vkey = chk[6] if len(chk) > 6 and chk[6] is not None else v1
                        acc = accT[br][g]
                        P.mm(acc[0:65, aoff[g]:aoff[g] + W], v1, pt[0:nk, k * W:(k + 1) * W], start=False, stop=(ci == nch - 1), reads=[pt, vkey])

                def run_steps(steps):
                    n = len(steps)
                    D = 2
                    for i in range(min(D, n)):
                        scores(steps[i])
                    for i in range(n):
                        if i + D < n:
                            scores(steps[i + D])
                        exp_pv(steps[i])

                for br, key in ((2, "win_chunks"), (1, "sel_chunks")):
                    if br == 1:
                        for g in range(2):
                            o = R[g]["o"]
                            pc0 = 384 + 512 * g
                            P.tr(psT[0:ns, pc0:pc0 + nq], sc16[0:nq, o:o + ns], ident[0:nq, 0:nq])
                            P.cp("dve", selT[0:ns, 128 * g:128 * g + nq], psT[0:ns, pc0:pc0 + nq])
                    steps = []
                    if merge:
                        acc = accT[br][0]
                        P.mm(acc[0:65, 0:2 * W], cs["zeros_row"][0:1, 0:65], cs["zeros_row"][0:1, 0:2 * W], start=True, stop=False)
                        c0, c1 = groups[0][key], groups[1][key]
                        for ci in range(len(c0)):
                            steps.append({"d": ([0, 1], br, ci, len(c0), [c0[ci], c1[ci]])})
                    else:
                        for g in range(2):
                            acc = accT[br][g]
                            P.mm(acc[0:65, 0:W], cs["zeros_row"][0:1, 0:65], cs["zeros_row"][0:1, 0:W], start=True, stop=False)
                            chunks = groups[g][key]
                            for ci, ch in enumerate(chunks):
                                steps.append({"d": ([g], br, ci, len(chunks), [ch])})
                    run_steps(steps)
                    for g in range(2):
                        acc = accT[br][g]
                        if g == 0:
                            P.cp("act", oTb[g][0:65, 0:W], acc[0:65, aoff[g]:aoff[g] + W])
                        else:
                            P.cp("dve", oTb[g][0:65, 0:W], acc[0:65, aoff[g]:aoff[g] + W])
                    for g in range(2):
                        bk = back[br][g]
                        for j in range(4):
                            P.tr(bk[0:nq, j * 65:(j + 1) * 65], oTb[g][0:65, j * nq:(j + 1) * nq], cs["ident32"][0:65, 0:65])
                        pvv = bk[0:nq, 0:260].rearrange("p (j w) -> p j w", j=4)
                        rd = rden[0:nq, 16 * g + 4 * br:16 * g + 4 * br + 4]
                        P.ts("dve", rd, pvv[:, :, 64], 1e-30, None, ALU.add)
                        P.op("dve", lambda e, rd=rd: e.reciprocal(rd, rd), [rden], [rden])
                        P.tt("dve", groups[g]["on_dst"](br), pvv[:, :, 0:64], bc_last(rd, 64), ALU.mult)

            def even_tile(mode, t, caches):
                isp = (mode == "p")
                xsrc = xp[t * 128:(t + 1) * 128, :] if isp else xs
                if xpref.get("cur") == (mode, t):
                    xtile = xpref["tile"]
                else:
                    xtile = xt[cnt["x"] % 2]; cnt["x"] += 1
                    P.dma(xtile[:], xsrc)
                nxt = caches.get("next")
                if nxt is not None:
                    ntile = xt[cnt["x"] % 2]; cnt["x"] += 1
                    nsrc = xp[nxt[1] * 128:(nxt[1] + 1) * 128, :] if nxt[0] == "p" else xs
                    P.dma(ntile[:], nsrc)
                    xpref["cur"] = nxt
                    xpref["tile"] = ntile
                P.memset("dve", stt_[:, 0:1], 0.0)
                P.actv(mixT[:], xtile[:], AF.Square, accum_out=stt_[:, 0:1])
                rsqrt_small(stt_[:, 1:2], stt_[:, 0:1], 1.0 / 1024, EPS)
                P.cp("act", xb[:], xtile[:])
                for kc in range(8):
                    P.tr(psT[:, kc * 128:(kc + 1) * 128], xb[:, kc * 128:(kc + 1) * 128], ident[:])
                P.cp("act", xT[:], psT[:])
                for gi, (c0, c1) in enumerate(GROUPS):
                    pa = psA[cnt["pa"] % 2]; cnt["pa"] += 1
                    w = c1 - c0
                    for kc in range(8):
                        P.mm(pa[:, 0:w], xT[:, kc * 128:(kc + 1) * 128], Wb[:, kc * EIN + c0:kc * EIN + c1], start=(kc == 0), stop=(kc == 7))
                    if gi % 2 == 0:
                        P.ts("dve", proj[:, c0:c1], pa[:, 0:w], stt_[:, 1:2], None, ALU.mult)
                    else:
                        P.op("act", lambda e, c0=c0, c1=c1, pa=pa, w=w: e.mul(proj[:, c0:c1], pa[:, 0:w], stt_[:, 1:2]), [pa, stt_], [proj])
                if not isp:
                    caches["hook"]()
                if stage <= 1:
                    return
                nv = v3(proj[:, QN:QN + 896], 14)
                P.tt("dve", tmpa[:, 0:896], proj[:, QN:QN + 896], proj[:, QN:QN + 896], ALU.mult)
                P.op("dve", lambda e: e.reduce_sum(stt_[:, 8:22], v3(tmpa[:, 0:896], 14), AX.X), [tmpa], [stt_])
                rsqrt_small(stt_[:, 8:22], stt_[:, 8:22], 1.0 / 64, EPS)
                P.tt("dve", nv, nv, bc_last(stt_[:, 8:22], 64), ALU.mult)
                P.tt("dve", proj[:, QN:QN + 896], proj[:, QN:QN + 896], qkg[:], ALU.mult)
                if isp:
                    cos = cs["cos_p"][:, t * 32:(t + 1) * 32]
                    sin = cs["sin_p"][:, t * 32:(t + 1) * 32]
                else:
                    cos = cs["cos_s"][:, :]
                    sin = cs["sin_s"][:, :]
                pv = v3(proj[:, 0:1408], 22)
                rv = v3(rp[:, 0:1408], 22)
                ta = tmpa[:, 0:704].rearrange("p (h d) -> p h d", h=22)
                tb = tmpa[:, 704:1408].rearrange("p (h d) -> p h d", h=22)
                tc_ = tmpb[:, 0:704].rearrange("p (h d) -> p h d", h=22)
                td_ = tmpb[:, 704:1408].rearrange("p (h d) -> p h d", h=22)
                cosb = bc_mid(cos, 22)
                sinb = bc_mid(sin, 22)
                P.tt("dve", ta, pv[:, :, 0:32], cosb, ALU.mult)
                P.tt("dve", tb, pv[:, :, 32:64], sinb, ALU.mult)
                P.tt("dve", rv[:, :, 0:32], ta, tb, ALU.subtract)
                P.tt("pool", tc_, pv[:, :, 32:64], cosb, ALU.mult)
                P.tt("pool", td_, pv[:, :, 0:32], sinb, ALU.mult)
                P.tt("pool", rv[:, :, 32:64], tc_, td_, ALU.add)
                if stage <= 2:
                    return
                if isp:
                    okv = o_kvp[t * 128:(t + 1) * 128, :]
                else:
                    okv = o_kvs
                P.dma(okv[:, 0:128], rp[:, KC:KC + 128])
                P.dma(okv[:, 128:256], proj[:, VC:VC + 128])
                P.dma(okv[:, 256:384], rp[:, KS:KS + 128])
                P.dma(okv[:, 384:512], proj[:, VS:VS + 128])
                if isp and t >= 12:
                    ow = o_winp[(t - 12) * 128:(t - 11) * 128, :]
                    P.dma(ow[:, 0:128], rp[:, KW:KW + 128])
                    P.dma(ow[:, 128:256], proj[:, VW:VW + 128])
                if not isp:
                    for b in range(16):
                        P.dma(o_wins[b, 504:512, 0:128], rp[b * 8:(b + 1) * 8, KW:KW + 128])
                        P.dma(o_wins[b, 504:512, 128:256], proj[b * 8:(b + 1) * 8, VW:VW + 128])
                        P.dma(o_wins[b, 0:504, :], cwin[b, 8:512, :])
                if stage <= 3:
                    return
                qdec = cs["qdec_p"] if isp else cs["qdec_s"]
                kdec = cs["kdec_p"] if isp else cs["kdec_s"]
                DTm = cs["DTp"] if isp else cs["DTs"]
                P.cp("act", r16[:, 0:256], rp[:, QA:QA + 256])
                P.tt("dve", v3(r16[:, 256:512], 4), v3(rp[:, QA:QA + 256], 4), bc_last(qdec[:, 0:4], 64), ALU.mult)
                P.cp("act", r16[:, 512:768], rp[:, KA:KA + 256])
                P.tt("dve", v3(r16[:, 768:1024], 4), v3(rp[:, KA:KA + 256], 4), bc_last(kdec[:, 0:4], 64), ALU.mult)
                P.cp("act", vb16[:], proj[:, VA:VA + 512])
                for i6 in range(6):
                    P.tr(psT[:, i6 * 128:(i6 + 1) * 128], r16[:, i6 * 128:(i6 + 1) * 128], ident[:])
                P.cp("act", rT[:, 0:256], psT[:, 512:768])
                qzv = qz[:].rearrange("p (a h q) -> p a h q", a=4, h=2)
                P.cp("act", qzv[0:64, :, 0, :], psT[0:64, 0:512].rearrange("p (a q) -> p a q", a=4))
                P.cp("dve", qzv[64:128, :, 1, :], psT[64:128, 0:512].rearrange("p (a q) -> p a q", a=4))
                if stage <= 3.1:
                    return
                for h in range(4):
                    hp, pr = h % 2, h // 2
                    P.mm(psR[:, h * 128:(h + 1) * 128], rT[:, pr * 128:(pr + 1) * 128],
                         qz[:, ((0 * 2 + pr) * 2 + hp) * 128:((0 * 2 + pr) * 2 + hp + 1) * 128])
                if stage <= 3.2:
                    return
                P.tt("dve", scm[:], psR[:], DTm[:], ALU.mult)
                if isp:
                    for h in range(4):
                        hp, pr = h % 2, h // 2
                        P.mm(psC[:, h * 128:(h + 1) * 128], scm[:, h * 128:(h + 1) * 128], vb16[:, h * 128:(h + 1) * 128], start=True, stop=False)
                        P.mm(psC[:, h * 128:(h + 1) * 128], qz[:, ((1 * 2 + pr) * 2 + hp) * 128:((1 * 2 + pr) * 2 + hp + 1) * 128],
                             Sb[:, pr * 128:(pr + 1) * 128], start=False, stop=True)
                else:
                    S0b = caches["S0b"]
                    qdTm = caches["qdTm"]
                    for h in range(4):
                        hp, pr = h % 2, h // 2
                        if hp == 0:
                            for b in range(16):
                                P.tt("dve" if b % 2 == 0 else "pool", qdTm[:, b * 256:(b + 1) * 256].rearrange("p (a q) -> p a q", a=2),
                                     qz[:, 512 + pr * 256:512 + (pr + 1) * 256].rearrange("p (a q) -> p a q", a=2),
                                     bc_mid(cs["colmask"][:, b * 128:(b + 1) * 128], 2), ALU.mult)
                        P.mm(psC[:, h * 128:(h + 1) * 128], scm[:, h * 128:(h + 1) * 128], vb16[:, h * 128:(h + 1) * 128], start=True, stop=False)
                        for b in range(16):
                            P.mm(psC[:, h * 128:(h + 1) * 128], qdTm[:, b * 256 + hp * 128:b * 256 + (hp + 1) * 128],
                                 S0b[:, (b * 2 + pr) * 128:(b * 2 + pr + 1) * 128], start=False, stop=(b == 15))
                if stage <= 3.4:
                    return
                P.cp("act", osb[:], psC[:])
                if stage <= 3.5:
                    return
                if isp:
                    for h in range(4):
                        hp, pr = h % 2, h // 2
                        P.mm(psR[:, h * 128:(h + 1) * 128], r16[:, 768 + pr * 128:768 + (pr + 1) * 128], vb16[:, h * 128:(h + 1) * 128])
                    for h in range(4):
                        hp, pr = h % 2, h // 2
                        rows = slice(hp * 64, (hp + 1) * 64)
                        P.stt("dve", S32[rows, pr * 128:(pr + 1) * 128], S32[rows, pr * 128:(pr + 1) * 128], cs["cdec_p"][rows, pr:pr + 1],
                              psR[rows, h * 128:(h + 1) * 128], ALU.mult, ALU.add)
                    P.cp("act", Sb[:], S32[:])
                    if t == n_ptiles - 1:
                        for h in range(4):
                            hp, pr = h % 2, h // 2
                            P.dma(o_retp[h, :, :], S32[hp * 64:(hp + 1) * 64, pr * 128:(pr + 1) * 128])
                else:
                    S0 = caches["S0"]
                    vblk = caches["vblk"]
                    Sn = caches["Sn"]
                    for h in range(4):
                        hp, pr = h % 2, h // 2
                        rows = slice(hp * 64, (hp + 1) * 64)
                        P.tt("dve", vblk[:].rearrange("p (b e) -> p b e", b=16), bc_mid(vb16[:, h * 128:(h + 1) * 128], 16),
                             bc_last(cs["blkmask"][:, 0:16], 128), ALU.mult)
                        for q4 in range(4):
                            pa = psA[cnt["pa"] % 2]; cnt["pa"] += 1
                            P.mm(pa[:, :], r16[:, 768 + pr * 128:768 + (pr + 1) * 128], vblk[:, q4 * 512:(q4 + 1) * 512])
                            s0v = S0[rows, :].rearrange("p (b a e) -> p b a e", b=16, a=2)[:, q4 * 4:(q4 + 1) * 4, pr, :]
                            P.stt("dve", Sn[rows, q4 * 512:(q4 + 1) * 512].rearrange("p (b e) -> p b e", b=4), s0v, cs["cdec_s"][rows, pr:pr + 1],
                                  pa[rows, :].rearrange("p (b e) -> p b e", b=4), ALU.mult, ALU.add)
                        P.dma(o_rets[:, h, :, :].rearrange("b d e -> d b e"), Sn[rows, :].rearrange("p (b e) -> p b e", b=16))
                if stage <= 4:
                    return
                P.op("dve", lambda e: e.reduce_sum(stt_[:, 24:28], v3(osb[:], 4), AX.X), [osb], [stt_])
                P.ts("dve", stt_[:, 24:28], stt_[:, 24:28], -1.0 / 128, None, ALU.mult)
                P.tt("dve", v3(ocb[:], 4), v3(osb[:], 4), bc_last(stt_[:, 24:28], 128), ALU.add)
                P.tt("dve", osb[:], ocb[:], ocb[:], ALU.mult)
                P.op("dve", lambda e: e.reduce_sum(stt_[:, 28:32], v3(osb[:], 4), AX.X), [osb], [stt_])
                rsqrt_small(stt_[:, 28:32], stt_[:, 28:32], 1.0 / 128, EPS)
                P.tt("dve", v3(ocb[:], 4), v3(ocb[:], 4), bc_last(stt_[:, 28:32], 128), ALU.mult)
                P.tt("dve", ocb[:], ocb[:], gnb[:], ALU.mult)
                P.actv(sz[:], proj[:, ZA:ZA + 512], AF.Silu)
                P.tt("dve", mix[:, 0:512], ocb[:], sz[:], ALU.mult)
                if stage <= 5:
                    return
                P.actv(gts[:], proj[:, GL:GL + 24], AF.Sigmoid)
                P.cp("act", n16[:, 0:896], rp[:, QN:QN + 896])
                P.cp("dve", n16[:, 896:1024], proj[:, VC:VC + 128])
                for i4 in range(4):
                    P.tr(psT[:, i4 * 128:(i4 + 1) * 128], n16[:, i4 * 128:(i4 + 1) * 128], ident[:])
                P.cp("act", qTz[0:64, 0:512], psT[0:64, 0:512])
                P.cp("dve", qTz[64:128, 512:1024], psT[64:128, 0:512])
                for i4 in range(4):
                    P.tr(psT[:, i4 * 128:(i4 + 1) * 128], n16[:, 512 + i4 * 128:512 + (i4 + 1) * 128], ident[:])
                if isp:
                    cT = caches["cT"]; vs1 = caches["vs1"]; vw1 = caches["vw1"]
                    P.cp("act", cT[:].rearrange("p (k n) -> p k n", k=4)[:, :, t * 128:(t + 1) * 128], psT[:, 0:512].rearrange("p (k n) -> p k n", k=4))
                    P.cp("dve", vs1[:].rearrange("p (c g w) -> p c g w", c=16, g=2)[:, t, :, 0:64], v3(proj[:, VS:VS + 128], 2))
                    P.cp("dve", vw1[:].rearrange("p (c g w) -> p c g w", c=16, g=2)[:, t, :, 0:64], v3(proj[:, VW:VW + 128], 2))
                    ckT = caches["ckT"]; cvx = caches["cvx"]
                    compress(cT, 0, cT, 3 * 2048, ckT, caches["cvT"], cvx, 97, max(0, 8 * t - 1), 8 * t + 6)
                    groups = []
                    for g in range(2):
                        qTg = qTz[:, g * 512:(g + 1) * 512].rearrange("p (j q) -> p j q", j=4)
                        selc = []
                        for c in range(t + 1):
                            bias = cs["CB"][:, :] if c == t else None
                            selc.append((cT[:, 2048 + c * 128:2048 + (c + 1) * 128], vs1[:, (c * 2 + g) * 65:(c * 2 + g + 1) * 65], 128,
                                         cs["E"][0:32, c * 128:(c + 1) * 128], bias))
                        winc = []
                        for c in range(max(0, t - 4), t + 1):
                            bias = cs["CB"][:, :] if c == t else (cs["AB"][:, :] if c == t - 4 else None)
                            winc.append((cT[:, 4096 + c * 128:4096 + (c + 1) * 128], vw1[:, (c * 2 + g) * 65:(c * 2 + g + 1) * 65], 128, None, bias))
                        groups.append(dict(qTg=qTg, cmp_k=ckT[:, 0:127], cmp_v=cvx[0:127, g * 97:(g + 1) * 97], sel_chunks=selc, win_chunks=winc,
                                           on_dst=lambda x, g=g: on[:, x * 512 + g * 256:x * 512 + (g + 1) * 256].rearrange("p (j d) -> p j d", j=4)))
                    cmpb = bc_mid(cs["CMPB"][0:127, t * 128:(t + 1) * 128], 4)
                    nsa_tile(128, groups, 32, cs["mulc_p"][:, t * 32:(t + 1) * 32], cs["addc_p"][:, t * 32:(t + 1) * 32], cmpb, False)
                else:
                    cTs = caches["cTs"]; vs1s = caches["vs1s"]
                    P.cp("act", cTs[:], psT[:, 0:512])
                    P.cp("dve", vs1s[:].rearrange("p (k g w) -> p k g w", k=2, g=2)[:, 0, :, 0:64], v3(proj[:, VS:VS + 128], 2))
                    P.cp("dve", vs1s[:].rearrange("p (k g w) -> p k g w", k=2, g=2)[:, 1, :, 0:64], v3(proj[:, VW:VW + 128], 2))
                    sample_nsa(caches)
                if stage <= 6:
                    return
                gv = gts[:].rearrange("p (h x) -> p h x", h=8)
                for x in range(3):
                    P.tt("dve", v3(on[:, x * 512:(x + 1) * 512], 8), v3(on[:, x * 512:(x + 1) * 512], 8), bc_last(gv[:, :, x], 64), ALU.mult)
                P.tt("dve", obt[:], on[:, 0:512], on[:, 512:1024], ALU.add)
                P.tt("dve", obt[:], obt[:], on[:, 1024:1536], ALU.add)
                P.actv(sz[:], proj[:, ZB:ZB + 512], AF.Silu)
                P.tt("dve", mix[:, 512:1024], obt[:], sz[:], ALU.mult)
                if stage <= 7:
                    return
                for kc in range(8):
                    P.tr(psT[:, kc * 128:(kc + 1) * 128], mix[:, kc * 128:(kc + 1) * 128], ident[:])
                P.cp("act", mixT[:], psT[:])
                for hf in range(2):
                    pa = psA[cnt["pa"] % 2]; cnt["pa"] += 1
                    for kc in range(8):
                        P.mm(pa[:, :], mixT[:, kc * 128:(kc + 1) * 128], Wob[:, kc * 1024 + hf * 512:kc * 1024 + (hf + 1) * 512], start=(kc == 0), stop=(kc == 7))
                    P.tt("dve", xtile[:, hf * 512:(hf + 1) * 512], xtile[:, hf * 512:(hf + 1) * 512], pa[:, :], ALU.add)
                ydst = yp[t * 128:(t + 1) * 128, :] if isp else ys
                P.dma(ydst, xtile[:], writes=[("y0", mode, t)])

            def compress(kc_t, kc_off, vc_t, vc_off, ckT, cvT, cvx, wv, n0, n1, rkeys=None):
                nn = n1 - n0 + 1
                rd = ([BDW] + rkeys) if rkeys else None
                for l in range(32):
                    a0 = kc_off + 16 * n0 + l
                    P.mm(psC[:, 0:nn], BDW[:, l * 128:(l + 1) * 128], kc_t[:, a0:a0 + 16 * (nn - 1) + 1:16], start=(l == 0), stop=(l == 31), reads=rd)
                for l in range(32):
                    a0 = vc_off + 16 * n0 + l
                    P.mm(psC[:, 128:128 + nn], BDW[:, (32 + l) * 128:(33 + l) * 128], vc_t[:, a0:a0 + 16 * (nn - 1) + 1:16], start=(l == 0), stop=(l == 31), reads=rd)
                P.ts("dve", ckT[:, n0:n1 + 1], psC[:, 0:nn], posk[:, 0:1], None, ALU.add)
                P.ts("dve", cvT[:, n0:n1 + 1], psC[:, 128:128 + nn], posk[:, 1:2], None, ALU.add)
                P.tr(psT[0:127, 0:128], cvT[:, 0:127], ident[:])
                P.cp("act", cvx[0:127, :].rearrange("p (g w) -> p g w", g=2)[:, :, 0:64], psT[0:127, 0:128].rearrange("p (g d) -> p g d", g=2))

            def sample_nsa(caches):
                C = caches
                cTq, kwTq, vs1q, vw1q, ckTq, cvxq = C["cTq"], C["kwTq"], C["vs1q"], C["vw1q"], C["ckTq"], C["cvxq"]
                cTs, vs1s, idx = C["cTs"], C["vs1s"], C["idx"]
                on8 = [osb, ocb, sz]
                wst32, w16 = C["wst32"], C["w16"]
                P.barrier()
                C["stR"].close()
                stPG = C["stPG"]
                pgb = [alloc(stPG, "pgb%d" % i, [128, 16 * 384], BF16) for i in range(2)]
                pg = [alloc(stPG, "pgf%d" % i, [128, 512]) for i in range(6)]
                for b in range(16):
                    pb = pgb[b % 2]
                    for i in range(16):
                        pgt = pg[(b * 16 + i) % 6]
                        col = b * 16 + i
                        P.dma(pgt[:], cache, reads=[cache, idx], q="pool",
                              fn=lambda e, pgt=pgt, col=col: e.indirect_dma_start(
                                  out=pgt[:, :], out_offset=None, in_=cache[:, :],
                                  in_offset=bass.IndirectOffsetOnAxis(ap=idx[:, col:col + 1], axis=0)))
                        P.cp("act", pb[:, i * 384:(i + 1) * 384], pgt[:, 0:384], writes=[("pgb", b % 2, i)])
                        P.cp("dve", vs1q[:].rearrange("p (c g w) -> p c g w", c=16, g=2)[:, i, :, 0:64], v3(pgt[:, 384:512], 2), writes=[("vs1q", i)])
                    psRb = psR[:].bitcast(BF16)
                    for i in range(16):
                        pdst = psT[:, 0:384] if i % 2 == 0 else psRb[:, 0:384]
                        pkey = psT if i % 2 == 0 else psR
                        for k in range(3):
                            P.tr(pdst[:, k * 128:(k + 1) * 128], pb[:, i * 384 + k * 128:i * 384 + (k + 1) * 128], ident[:],
                                 reads=[("pgb", b % 2, i), ident], writes=[pkey])
                        P.cp("act" if i % 2 == 0 else "dve", cTq[:].rearrange("p (k n) -> p k n", k=3)[:, :, i * 128:(i + 1) * 128],
                             pdst.rearrange("p (k n) -> p k n", k=3), reads=[pkey], writes=[("cTq", i)])
                    P.dma(wst32[:].rearrange("p (c w) -> p c w", c=4), cwin[b].rearrange("(c r) w -> r c w", r=128))
                    P.cp("dve", w16[:], wst32[:])
                    for c in range(4):
                        P.tr(psT[:, 512 + c * 128:512 + (c + 1) * 128], w16[:, c * 256:c * 256 + 128], ident[:])
                    P.cp("act", kwTq[:], psT[:, 512:1024])
                    for c in range(4):
                        P.cp("dve", vw1q[:, c * 130:(c + 1) * 130].rearrange("p (g w) -> p g w", g=2)[:, :, 0:64], v3(w16[:, c * 256 + 128:(c + 1) * 256], 2))
                    compress(cTq, 0, cTq, 2048, ckTq, C["cvTq"], cvxq, 98, 0, 126, rkeys=[("cTq", i) for i in range(16)])
                    groups = []
                    for g in range(2):
                        qTg = qTz[:, g * 512:(g + 1) * 512].rearrange("p (j q) -> p j q", j=4)[:, :, 8 * b:8 * b + 8]
                        sbias = cs["SB"][:, b * 8:(b + 1) * 8]
                        selc = []
                        for c in range(16):
                            selc.append((cTq[:, 4096 + c * 128:4096 + (c + 1) * 128], vs1q[:, (c * 2 + g) * 65:(c * 2 + g + 1) * 65], 128,
                                         cs["E"][0:33, c * 128:(c + 1) * 128], None, ("cTq", c), ("vs1q", c)))
                        selc.append((cTs[:, 128:256], vs1s[:, (0 * 2 + g) * 65:(0 * 2 + g + 1) * 65], 128, None, sbias))
                        winc = []
                        for c in range(4):
                            bias = cs["ABs"][:, :] if c == 0 else None
                            winc.append((kwTq[:, c * 128:(c + 1) * 128], vw1q[:, (c * 2 + g) * 65:(c * 2 + g + 1) * 65], 128, None, bias))
                        winc.append((cTs[:, 256:384], vs1s[:, (1 * 2 + g) * 65:(1 * 2 + g + 1) * 65], 128, None, sbias))
                        groups.append(dict(qTg=qTg, cmp_k=ckTq[:, 0:127], cmp_v=cvxq[0:127, g * 98:(g + 1) * 98], sel_chunks=selc, win_chunks=winc,
                                           on_dst=lambda x, g=g: on8[x][0:8, g * 256:(g + 1) * 256].rearrange("p (j d) -> p j d", j=4)))
                    qall = qTz[:, :].rearrange("p (g j q) -> p g j q", g=2, j=4)[:, :, :, 8 * b:8 * b + 8]
                    nsa_tile(8, groups, 33, cs["mulc_s"][:, :], cs["addc_s"][:, :], None, True, qall)
                    for x in range(3):
                        P.dma(on[b * 8:(b + 1) * 8, x * 512:(x + 1) * 512], on8[x][0:8, :])
                P.barrier()

            with ExitStack() as stP:
                Ap = lambda name, shape, dt=F32: alloc(stP, name, shape, dt)
                load_consts(stP, P_ONLY)
                cT = Ap("cT", [128, 4 * 2048], BF16)
                vs1 = Ap("vs1", [128, 16 * 2 * 65], BF16)
                vw1 = Ap("vw1", [128, 16 * 2 * 65], BF16)
                ckT = Ap("ckT", [128, 128], BF16)
                cvT = Ap("cvT", [128, 128], BF16)
                cvx = Ap("cvx", [128, 2 * 97], BF16)
                ov32 = Ap("ov32", [128, 32])
                P.memset("pool", cT[:], 0.0)
                P.memset("pool", vs1[:], 1.0)
                P.memset("pool", vw1[:], 1.0)
                P.memset("pool", cvx[:], 1.0)
                P.memset("pool", ckT[:], 0.0)
                P.memset("pool", cvT[:], 0.0)
                for g in range(2):
                    P.cp("dve", cvx[0:127, g * 97 + 65:(g + 1) * 97], cs["ovl_p"][0:127, :])
                caches = {"cT": cT, "vs1": vs1, "vw1": vw1, "ckT": ckT, "cvx": cvx, "cvT": cvT}
                for t in range(n_ptiles):
                    caches["next"] = ("p", t + 1) if t + 1 < n_ptiles else (("s", 0) if do_sample else None)
                    even_tile("p", t, caches)
                P.barrier()

            if do_sample:
                stS = ExitStack()
                stPG = ExitStack()
                scaches = {}

                def sample_hook():
                    P.barrier()
                    stWb.close()
                    As = lambda name, shape, dt=F32: alloc(stS, name, shape, dt)
                    load_consts(stS, S_ONLY)
                    C = scaches
                    C["cTq"] = As("cTq", [128, 3 * 2048], BF16)
                    C["kwTq"] = As("kwTq", [128, 512], BF16)
                    C["vs1q"] = As("vs1q", [128, 16 * 130], BF16)
                    C["vw1q"] = As("vw1q", [128, 4 * 130], BF16)
                    C["ckTq"] = As("ckTq", [128, 128], BF16)
                    C["cvTq"] = As("cvTq", [128, 128], BF16)
                    C["cvxq"] = As("cvxq", [128, 2 * 98], BF16)
                    C["cTs"] = As("cTs", [128, 512], BF16)
                    C["vs1s"] = As("vs1s", [128, 4 * 65], BF16)
                    C["wst32"] = As("wst32", [128, 1024])
                    C["w16"] = As("w16", [128, 1024], BF16)
                    C["idx"] = As("idx", [128, 256], I32)
                    pti = As("pti", [128, 256], I32)
                    stR = ExitStack()
                    C["stR"] = stR
                    C["stPG"] = stPG
                    Ar = lambda name, shape, dt=F32: alloc(stR, name, shape, dt)
                    C["S0"] = Ar("S0", [128, 4096])
                    C["S0b"] = Ar("S0b", [128, 4096], BF16)
                    C["qdTm"] = Ar("qdTm", [128, 16 * 256], BF16)
                    C["vblk"] = Ar("vblk", [128, 2048], BF16)
                    C["Sn"] = Ar("Sn", [128, 2048])
                    ptf = tmpa[:, 0:256]
                    P.memset("pool", C["vs1q"][:], 1.0)
                    P.memset("pool", C["vw1q"][:], 1.0)
                    P.memset("pool", C["vs1s"][:], 1.0)
                    P.memset("pool", C["cvxq"][:], 1.0)
                    for g in range(2):
                        P.cp("dve", C["cvxq"][0:127, g * 98 + 65:(g + 1) * 98], cs["ovl_s"][0:127, :])
                    for hp in range(2):
                        P.dma(C["S0"][hp * 64:(hp + 1) * 64, :].rearrange("p (b a e) -> p b a e", b=16, a=2),
                              sret[:, hp::2, :, :].rearrange("b a d e -> d b a e"))
                    P.cp("dve", C["S0b"][:], C["S0"][:])
                    P.dma(pti[:], ptab.partition_broadcast(128))
                    P.cp("dve", ptf, pti[:])
                    P.ts("dve", ptf, ptf, 128.0, cs["iota_p"][:, 0:1], ALU.mult, ALU.add)
                    P.cp("dve", C["idx"][:], ptf)

                scaches["hook"] = sample_hook
                even_tile("s", 0, scaches)
                P.barrier()
                stPG.close()
                stS.close()
            else:
                stWb.close()
            P.barrier()
        if phase_b:
            NT = 2176
            NCH = 272
            TWO_PI = 2.0 * math.pi
            with ExitStack() as stB:
                Bf = lambda name, shape, dt=F32: alloc(stB, name, shape, dt)
                load_consts(stB, ["ident", "JV", "KI", "PM4", "BM16"], pre="kb_")
                ident = cs["ident"]
                uT = Bf("uT", [128, 8 * NT], BF16)
                szT = Bf("szT", [128, 8 * NT], BF16)
                ngo = Bf("ngo", [128, 8])
                dsk = Bf("dsk", [128, 8])
                statb = Bf("statb", [128, 8])
                FSp = Bf("FSp", [128, 64])
                FSs = Bf("FSs", [128, 1024])
                P.dma(ngo[:], norm_o)
                P.dma(dsk[:], ssmd)
                cntb = {"x": 0, "pa": 0}

                def load_tile(dst, ti):
                    if ti < 16:
                        P.dma(dst, yp[ti * 128:(ti + 1) * 128, :], reads=[("y0", "p", ti)])
                    else:
                        P.dma(dst, ys, reads=[("y0", "s", 0)])

                with ExitStack() as st1:
                    B1 = lambda name, shape, dt=F32: alloc(st1, name, shape, dt)
                    Wodd = B1("Wodd", [128, 8 * 2048], BF16)
                    yt = [B1("yt%d" % i, [128, 1024]) for i in range(2)]
                    hb = B1("hb", [128, 1024], BF16)
                    hT = B1("hT", [128, 8 * 512], BF16)
                    with ExitStack() as stg_:
                        stg = [alloc(stg_, "stgb%d" % i, [128, 2048]) for i in range(2)]
                        for kc in range(8):
                            s_ = stg[kc % 2]
                            P.dma(s_[:], w_in_o[kc * 128:(kc + 1) * 128, :])
                            if kc % 2 == 0:
                                P.ts("dve", Wodd[:, kc * 2048:(kc + 1) * 2048], s_[:], ngo[:, kc:kc + 1], None, ALU.mult)
                            else:
                                P.op("act", lambda e, kc=kc, s_=s_: e.mul(Wodd[:, kc * 2048:(kc + 1) * 2048], s_[:], ngo[:, kc:kc + 1]), [s_, ngo], [Wodd])
                        P.barrier()
                    blocks = [(0, 4), (4, 8), (8, 12), (12, 16), (16, 17)]
                    for (t0, t1) in blocks:
                        nb = (t1 - t0) * 128
                        col0 = t0 * 128
                        for ti in range(t0, t1):
                            ytile = yt[cntb["x"] % 2]; cntb["x"] += 1
                            load_tile(ytile[:], ti)
                            P.memset("dve", statb[:, 0:1], 0.0)
                            P.actv(hb[:], ytile[:], AF.Square, accum_out=statb[:, 0:1])
                            P.ts("dve", statb[:, 1:2], statb[:, 0:1], 1.0 / 1024, EPS, ALU.mult, ALU.add)
                            P.actv(statb[:, 1:2], statb[:, 1:2], AF.Sqrt)
                            P.op("dve", lambda e: e.reciprocal(statb[:, 1:2], statb[:, 1:2]), [statb], [statb])
                            P.ts("dve", hb[:], ytile[:], statb[:, 1:2], None, ALU.mult)
                            for kc in range(8):
                                P.tr(psT[:, kc * 128:(kc + 1) * 128], hb[:, kc * 128:(kc + 1) * 128], ident[:])
                            lt = ti - t0
                            P.cp("act", hT[:].rearrange("p (k n) -> p k n", k=8)[:, :, lt * 128:(lt + 1) * 128], psT[:].rearrange("p (k n) -> p k n", k=8))
                        for oc in range(16):
                            pa = psA[cntb["pa"] % 2]; cntb["pa"] += 1
                            for kc in range(8):
                                P.mm(pa[:, 0:nb], Wodd[:, kc * 2048 + oc * 128:kc * 2048 + (oc + 1) * 128], hT[:, kc * 512:kc * 512 + nb], start=(kc == 0), stop=(kc == 7))
                            if oc < 8:
                                P.cp("dve", uT[:, oc * NT + col0:oc * NT + col0 + nb], pa[:, 0:nb])
                            else:
                                P.actv(szT[:, (oc - 8) * NT + col0:(oc - 8) * NT + col0 + nb], pa[:, 0:nb], AF.Silu)
                    P.barrier()

                with ExitStack() as st2:
                    B2 = lambda name, shape, dt=F32: alloc(st2, name, shape, dt)
                    lr = B2("lr", [128, 32]); li = B2("li", [128, 32]); ls = B2("ls", [128, 32])
                    bre = B2("bre", [128, 512]); bim = B2("bim", [128, 512])
                    cre = B2("cre", [128, 512]); cim = B2("cim", [128, 512])
                    x0r = B2("x0r", [128, 512]); x0i = B2("x0i", [128, 512])
                    for tl, src in ((lr, lamre_A), (li, lamim_A), (ls, lstep_A), (bre, bA_re), (bim, bA_im), (cre, cA_re), (cim, cA_im), (x0r, x0A_re), (x0i, x0A_im)):
                        P.dma(tl[:], src)
                    aa = B2("aa", [128, 32]); th = B2("th", [128, 32])
                    A9 = B2("A9", [128, 288]); T9 = B2("T9", [128, 288]); T9c = B2("T9c", [128, 288])
                    PR = B2("PR", [128, 288]); PI = B2("PI", [128, 288])
                    rri = B2("rri", [128, 1024], I32); rri2 = B2("rri2", [128, 1024], I32)
                    npi = B2("npi", [128, 1])

                    hpi = B2("hpi", [128, 1])
                    P.memset("dve", hpi[:], math.pi / 2)

                    def range_reduce(x, n):
                        P.ts("dve", rri[:, 0:n], x, 1.0 / TWO_PI, None, ALU.mult)
                        P.stt("dve", x, rri[:, 0:n], -TWO_PI, x, ALU.mult, ALU.add)

                    def sincos(x, n, sin_out, cos_out):
                        P.ts("dve", rri[:, 0:n], x, 1.0 / TWO_PI, None, ALU.mult)
                        P.stt("dve", sin_out, rri[:, 0:n], -TWO_PI, x, ALU.mult, ALU.add)
                        P.actv(sin_out, sin_out, AF.Sin)
                        P.ts("dve", rri2[:, 0:n], x, 1.0 / TWO_PI, 0.25, ALU.mult, ALU.add)
                        P.stt("dve", cos_out, rri2[:, 0:n], -TWO_PI, x, ALU.mult, ALU.add)
                        P.actv(cos_out, cos_out, AF.Sin, bias=hpi[:, 0:1])

                    P.actv(ls[:], ls[:], AF.Exp)
                    P.tt("dve", aa[:], lr[:], ls[:], ALU.mult)
                    P.tt("dve", th[:], li[:], ls[:], ALU.mult)
                    JV = cs["JV"]
                    P.tt("dve", A9[:].rearrange("p (j m) -> p j m", j=9), JV[:].rearrange("p (j m) -> p j m", j=9), bc_mid(aa[:, :], 9), ALU.mult)
                    P.actv(A9[:], A9[:], AF.Exp)
                    range_reduce(th[:, :], 32)
                    P.tt("dve", T9[:].rearrange("p (j m) -> p j m", j=9), JV[:].rearrange("p (j m) -> p j m", j=9), bc_mid(th[:, :], 9), ALU.mult)
                    sincos(T9[:, :], 288, PI[:, :], T9c[:, :])
                    P.tt("dve", PR[:], A9[:], T9c[:], ALU.mult)
                    P.tt("dve", PI[:], A9[:], PI[:], ALU.mult)
                    PRv = PR[:].rearrange("p (j m) -> p j m", j=9)
                    PIv = PI[:].rearrange("p (j m) -> p j m", j=9)
                    den = B2("den", [128, 32]); nr = B2("nr", [128, 32]); fre = B2("fre", [128, 32]); fim = B2("fim", [128, 32]); t32 = B2("t32", [128, 32])
                    P.tt("dve", den[:], lr[:], lr[:], ALU.mult)
                    P.tt("dve", t32[:], li[:], li[:], ALU.mult)
                    P.tt("dve", den[:], den[:], t32[:], ALU.add)
                    P.op("dve", lambda e: e.reciprocal(den[:], den[:]), [den], [den])
                    P.ts("dve", nr[:], PR[:, 32:64], -1.0, None, ALU.add)
                    P.tt("dve", fre[:], nr[:], lr[:], ALU.mult)
                    P.tt("dve", t32[:], PI[:, 32:64], li[:], ALU.mult)
                    P.tt("dve", fre[:], fre[:], t32[:], ALU.add)
                    P.tt("dve", fre[:], fre[:], den[:], ALU.mult)
                    P.tt("dve", fim[:], PI[:, 32:64], lr[:], ALU.mult)
                    P.tt("dve", t32[:], nr[:], li[:], ALU.mult)
                    P.tt("dve", fim[:], fim[:], t32[:], ALU.subtract)
                    P.tt("dve", fim[:], fim[:], den[:], ALU.mult)
                    Bre = B2("Bre", [128, 512]); Bim = B2("Bim", [128, 512]); t512 = B2("t512", [128, 512])
                    v16 = lambda ap: ap.rearrange("p (m c) -> p m c", c=16)
                    P.tt("dve", v16(Bre[:]), v16(bre[:]), bc_last(fre[:, :], 16), ALU.mult)
                    P.tt("dve", v16(t512[:]), v16(bim[:]), bc_last(fim[:, :], 16), ALU.mult)
                    P.tt("dve", Bre[:], Bre[:], t512[:], ALU.subtract)
                    P.tt("dve", v16(Bim[:]), v16(bim[:]), bc_last(fre[:, :], 16), ALU.mult)
                    P.tt("dve", v16(t512[:]), v16(bre[:]), bc_last(fim[:, :], 16), ALU.mult)
                    P.tt("dve", Bim[:], Bim[:], t512[:], ALU.add)
                    Cbd_re = B2("Cbd_re", [128, 32 * 32], BF16); Cbd_nim = B2("Cbd_nim", [128, 32 * 32], BF16)
                    P.memset("pool", Cbd_re[:], 0.0)
                    P.memset("pool", Cbd_nim[:], 0.0)
                    cbv = lambda t_: t_[:].rearrange("p (m c) -> p m c", c=32)
                    for hp in range(2):
                        rows = slice(hp * 64, (hp + 1) * 64)
                        P.cp("dve", cbv(Cbd_re)[rows, :, hp * 16:(hp + 1) * 16], v16(cre[:])[rows, :, :])
                        P.ts("dve", cbv(Cbd_nim)[rows, :, hp * 16:(hp + 1) * 16], v16(cim[:])[rows, :, :], -1.0, None, ALU.mult)
                    th8 = B2("th8", [128, 32])
                    P.ts("dve", th8[:], th[:], 8.0, None, ALU.mult)
                    range_reduce(th8[:, :], 32)
                    Xre = B2("Xre", [128, 512]); Xim = B2("Xim", [128, 512]); tX = B2("tX", [128, 512])
                    XBD = B2("XBD", [128, 2 * 8 * 4 * 32], BF16)
                    VZ = B2("VZ", [128, 8 * 4 * 2 * 128], BF16)
                    WS = B2("WS", [128, 8 * 2 * 128], BF16)
                    uzb = [B2("uz%d" % i, [128, NT], BF16) for i in range(2)]
                    BD = B2("BD", [128, 8 * 128], BF16)
                    Sin_r = B2("Sin_r", [128, 4 * NCH]); Sin_i = B2("Sin_i", [128, 4 * NCH])
                    cosT = B2("cosT", [128, 1024]); sinT = B2("sinT", [128, 1024])
                    c_r = B2("c_r", [128, 1024]); c_i = B2("c_i", [128, 1024]); t1k = B2("t1k", [128, 1024])
                    w_r = B2("w_r", [128, 1024]); w_i = B2("w_i", [128, 1024])
                    Sp_r = B2("Sp_r", [128, 4 * NCH], BF16); Sp_i = B2("Sp_i", [128, 4 * NCH], BF16)
                    yv = B2("yv", [128, NCH]); y2 = B2("y2", [128, NCH]); y3 = B2("y3", [128, NCH])
                    ygc = B2("ygc", [128, NT], BF16)
                    P.memset("pool", XBD[:], 0.0)
                    P.memset("pool", VZ[:], 0.0)
                    for c in range(8):
                        msl = slice(4 * c, 4 * c + 4)
                        x4 = lambda t_: t_[:].rearrange("p (t m c) -> p t m c", t=8, m=4)
                        prb = PRv[:, 0:8, msl].unsqueeze(3).to_broadcast([128, 8, 4, 16])
                        pib = PIv[:, 0:8, msl].unsqueeze(3).to_broadcast([128, 8, 4, 16])
                        brb = v16(Bre[:])[:, msl, :].unsqueeze(1).to_broadcast([128, 8, 4, 16])
                        bib = v16(Bim[:])[:, msl, :].unsqueeze(1).to_broadcast([128, 8, 4, 16])
                        P.tt("dve", x4(Xre), prb, brb, ALU.mult)
                        P.tt("dve", x4(tX), pib, bib, ALU.mult)
                        P.tt("dve", Xre[:], Xre[:], tX[:], ALU.subtract)
                        P.tt("dve", x4(Xim), prb, bib, ALU.mult)
                        P.tt("dve", x4(tX), pib, brb, ALU.mult)
                        P.tt("dve", Xim[:], Xim[:], tX[:], ALU.add)
                        xbv = XBD[:].rearrange("p (r t m c) -> p r t m c", r=2, t=8, m=4)
                        for hp in range(2):
                            rows = slice(hp * 64, (hp + 1) * 64)
                            P.cp("dve", xbv[rows, 0, :, :, hp * 16:(hp + 1) * 16], x4(Xre)[rows])
                            P.cp("pool", xbv[rows, 1, :, :, hp * 16:(hp + 1) * 16], x4(Xim)[rows])
                        prb = PRv[:, 1:9, msl].unsqueeze(3).to_broadcast([128, 8, 4, 16])
                        pib = PIv[:, 1:9, msl].unsqueeze(3).to_broadcast([128, 8, 4, 16])
                        crb = v16(cre[:])[:, msl, :].unsqueeze(1).to_broadcast([128, 8, 4, 16])
                        cib = v16(cim[:])[:, msl, :].unsqueeze(1).to_broadcast([128, 8, 4, 16])
                        P.tt("dve", x4(Xre), prb, crb, ALU.mult)
                        P.tt("dve", x4(tX), pib, cib, ALU.mult)
                        P.tt("dve", Xre[:], Xre[:], tX[:], ALU.subtract)
                        P.tt("dve", x4(Xim), pib, crb, ALU.mult)
                        P.tt("dve", x4(tX), prb, cib, ALU.mult)
                        P.stt("dve", Xim[:], Xim[:], -1.0, tX[:], ALU.mult, ALU.subtract)
                        vzv = VZ[:].rearrange("p (t m r n) -> p t m r n", t=8, m=4, r=2)
                        for hp in range(2):
                            rows = slice(hp * 64, (hp + 1) * 64)
                            for m4 in range(4):
                                c0_ = 32 * m4 + 16 * hp
                                P.cp("dve", vzv[rows, :, m4, 0, c0_:c0_ + 16], x4(Xre)[rows, :, m4, :])
                                P.cp("pool", vzv[rows, :, m4, 1, c0_:c0_ + 16], x4(Xim)[rows, :, m4, :])
                        for th2 in range(2):
                            for tl_ in range(4):
                                tau = th2 * 4 + tl_
                                for ri in range(2):
                                    P.tr(psT[:, (tl_ * 2 + ri) * 128:(tl_ * 2 + ri + 1) * 128], XBD[:, (ri * 8 + tau) * 128:(ri * 8 + tau + 1) * 128], ident[:])
                            P.cp("act", WS[:, th2 * 1024:(th2 + 1) * 1024], psT[:])
                        for th2 in range(2):
                            for tl_ in range(4):
                                tau = th2 * 4 + tl_
                                outp = psC[:, tl_ * 128:(tl_ + 1) * 128]
                                P.mm(outp, XBD[:, (0 * 8 + tau) * 128:(0 * 8 + tau + 1) * 128], Cbd_re[:, c * 128:(c + 1) * 128], start=True, stop=False)
                                P.mm(outp, XBD[:, (1 * 8 + tau) * 128:(1 * 8 + tau + 1) * 128], Cbd_nim[:, c * 128:(c + 1) * 128], start=False, stop=True)
                            P.tt("dve", BD[:, th2 * 512:(th2 + 1) * 512].rearrange("p (t n) -> p t n", t=4), psC[:, 0:512].rearrange("p (t n) -> p t n", t=4),
                                 bc_mid(cs["BM16"][:, :], 4), ALU.mult)
                        uc = uT[:, c * NT:(c + 1) * NT]
                        for m4 in range(4):
                            uz = uzb[m4 % 2]
                            P.op("act", lambda e, uz=uz, uc=uc, m4=m4: e.mul(uz[:], uc, cs["PM4"][:, m4:m4 + 1]), [uT, cs["PM4"]], [uz])
                            for ri, dst in ((0, Sin_r), (1, Sin_i)):
                                pa = psA[ri]
                                for s_ in range(8):
                                    tau = 7 - s_
                                    P.mm(pa[:, 0:NCH], WS[:, (tau * 2 + ri) * 128:(tau * 2 + ri + 1) * 128], uz[:, s_:NT:8], start=(s_ == 0), stop=(s_ == 7))
                                if ri == 0:
                                    P.cp("act", dst[:, m4 * NCH:(m4 + 1) * NCH], pa[:, 0:NCH])
                                else:
                                    P.cp("dve", dst[:, m4 * NCH:(m4 + 1) * NCH], pa[:, 0:NCH])
                        a3 = lambda t_: t_[:].rearrange("p (m k) -> p m k", m=4)
                        P.tt("dve", a3(t1k), bc_last(th8[:, msl], 256), bc_mid(cs["KI"][:, :], 4), ALU.mult)
                        sincos(t1k[:, :], 1024, sinT[:, :], cosT[:, :])
                        s3 = lambda t_: t_[:].rearrange("p (m k) -> p m k", m=4)[:, :, 0:256]
                        P.tt("dve", a3(c_r), a3(cosT), s3(Sin_r), ALU.mult)
                        P.tt("dve", a3(t1k), a3(sinT), s3(Sin_i), ALU.mult)
                        P.tt("dve", c_r[:], c_r[:], t1k[:], ALU.add)
                        P.tt("dve", a3(c_i), a3(cosT), s3(Sin_i), ALU.mult)
                        P.tt("dve", a3(t1k), a3(sinT), s3(Sin_r), ALU.mult)
                        P.tt("dve", c_i[:], c_i[:], t1k[:], ALU.subtract)
                        for m4 in range(4):
                            m = 4 * c + m4
                            r8b = A9[:, 8 * 32 + m:8 * 32 + m + 1].to_broadcast([128, 256])
                            for tl_, wo_ in ((c_r, w_r), (c_i, w_i)):
                                seg = tl_[:, m4 * 256:(m4 + 1) * 256]
                                oseg = wo_[:, m4 * 256:(m4 + 1) * 256]
                                P.op("dve", lambda e, seg=seg, oseg=oseg, r8b=r8b: e.tensor_tensor_scan(oseg, r8b, seg, 0.0, ALU.mult, ALU.add), [tl_, A9], [wo_])
                        P.tt("dve", t1k[:], cosT[:], w_r[:], ALU.mult)
                        P.tt("pool", c_r[:], sinT[:], w_i[:], ALU.mult)
                        P.tt("dve", t1k[:], t1k[:], c_r[:], ALU.subtract)
                        P.tt("pool", c_i[:], cosT[:], w_i[:], ALU.mult)
                        P.tt("dve", c_r[:], sinT[:], w_r[:], ALU.mult)
                        P.tt("dve", c_i[:], c_i[:], c_r[:], ALU.add)
                        spr = Sp_r[:].rearrange("p (m k) -> p m k", m=4)
                        spi = Sp_i[:].rearrange("p (m k) -> p m k", m=4)
                        P.memset("pool", spr[:, :, 0:1], 0.0)
                        P.memset("pool", spi[:, :, 0:1], 0.0)
                        P.cp("dve", spr[:, :, 1:256], a3(t1k)[:, :, 0:255])
                        P.cp("pool", spi[:, :, 1:256], a3(c_i)[:, :, 0:255])
                        x0rv = x0r[:].rearrange("p (m b) -> p m b", b=16)[:, msl, :]
                        x0iv = x0i[:].rearrange("p (m b) -> p m b", b=16)[:, msl, :]
                        P.cp("dve", spr[:, :, 256:272], x0rv)
                        P.cp("pool", spi[:, :, 256:272], x0iv)
                        P.cp("dve", FSp[:, 4 * c:4 * c + 4], a3(t1k)[:, :, 255])
                        P.cp("dve", FSp[:, 32 + 4 * c:32 + 4 * c + 4], a3(c_i)[:, :, 255])
                        fsr = FSs[:, 0:512].rearrange("p (m b) -> p m b", b=16)[:, msl, :]
                        fsi = FSs[:, 512:1024].rearrange("p (m b) -> p m b", b=16)[:, msl, :]
                        p8 = bc_last(PR[:, 8 * 32 + 4 * c:8 * 32 + 4 * c + 4], 16)
                        i8_ = bc_last(PI[:, 8 * 32 + 4 * c:8 * 32 + 4 * c + 4], 16)
                        sir = Sin_r[:].rearrange("p (m k) -> p m k", m=4)[:, :, 256:272]
                        sii = Sin_i[:].rearrange("p (m k) -> p m k", m=4)[:, :, 256:272]
                        tq = tX[:, 0:64].rearrange("p (m b) -> p m b", b=16)
                        P.tt("dve", fsr, p8, x0rv, ALU.mult)
                        P.tt("dve", tq, i8_, x0iv, ALU.mult)
                        P.tt("dve", fsr, fsr, tq, ALU.subtract)
                        P.tt("dve", fsr, fsr, sir, ALU.add)
                        P.tt("dve", fsi, p8, x0iv, ALU.mult)
                        P.tt("dve", tq, i8_, x0rv, ALU.mult)
                        P.tt("dve", fsi, fsi, tq, ALU.add)
                        P.tt("dve", fsi, fsi, sii, ALU.add)
                        for j in range(8):
                            acc = psS[j % 2]
                            for s_ in range(j + 1):
                                P.mm(acc[:, 0:NCH], BD[:, (j - s_) * 128:(j - s_ + 1) * 128], uc[:, s_:NT:8], start=(s_ == 0), stop=False)
                            for m4 in range(4):
                                for ri, spt in ((0, Sp_r), (1, Sp_i)):
                                    last = (m4 == 3 and ri == 1)
                                    P.mm(acc[:, 0:NCH], VZ[:, ((j * 4 + m4) * 2 + ri) * 128:((j * 4 + m4) * 2 + ri + 1) * 128],
                                         spt[:, m4 * NCH:(m4 + 1) * NCH], start=False, stop=last)
                            P.stt("dve", yv[:], uc[:, j:NT:8], dsk[:, c:c + 1], acc[:, 0:NCH], ALU.mult, ALU.add)
                            P.tt("pool", y2[:], yv[:], yv[:], ALU.mult)
                            P.ts("dve", y2[:], y2[:], 0.044715, 1.0, ALU.mult, ALU.add)
                            P.tt("pool", y2[:], y2[:], yv[:], ALU.mult)
                            P.actv(y3[:], y2[:], AF.Sigmoid, scale=1.5957691216057308)
                            P.tt("dve", ygc[:, j:NT:8], yv[:], y3[:], ALU.mult)
                        P.cp("pool", uT[:, c * NT:(c + 1) * NT], ygc[:])
                    P.dma(o_ssp.rearrange("r p m -> p r m"), FSp[:].rearrange("p (r m) -> p r m", r=2))
                    P.dma(o_sss.rearrange("r p x -> p r x"), FSs[:].rearrange("p (r x) -> p r x", r=2))
                    P.barrier()

                with ExitStack() as st3:
                    B3 = lambda name, shape, dt=F32: alloc(st3, name, shape, dt)
                    W1 = B3("W1", [128, 8 * 1024], BF16)
                    W2 = B3("W2", [128, 8 * 1024], BF16)
                    Wo2 = B3("Wo2", [128, 8 * 1024], BF16)
                    yt = [B3("ytc%d" % i, [128, 1024]) for i in range(2)]
                    oT = B3("oT", [128, 8 * 512], BF16)
                    sg = B3("sg", [128, 512]); tg = B3("tg", [128, 512])
                    with ExitStack() as stg_:
                        stg = [alloc(stg_, "stgc%d" % i, [128, 1024]) for i in range(2)]
                        k_ = 0
                        for (Wd, src) in ((W1, glu1), (W2, glu2), (Wo2, w_out_o)):
                            for kc in range(8):
                                s_ = stg[k_ % 2]; k_ += 1
                                P.dma(s_[:], src[kc * 128:(kc + 1) * 128, :])
                                P.cp("dve" if kc % 2 == 0 else "pool", Wd[:, kc * 1024:(kc + 1) * 1024], s_[:])
                        P.barrier()
                    for (t0, t1) in blocks:
                        nb = (t1 - t0) * 128
                        col0 = t0 * 128
                        for fc in range(8):
                            p1 = psA[0]; p2 = psA[1]
                            for kc in range(8):
                                P.mm(p1[:, 0:nb], W1[:, kc * 1024 + fc * 128:kc * 1024 + (fc + 1) * 128], uT[:, kc * NT + col0:kc * NT + col0 + nb], start=(kc == 0), stop=(kc == 7))
                            for kc in range(8):
                                P.mm(p2[:, 0:nb], W2[:, kc * 1024 + fc * 128:kc * 1024 + (fc + 1) * 128], uT[:, kc * NT + col0:kc * NT + col0 + nb], start=(kc == 0), stop=(kc == 7))
                            P.actv(sg[:, 0:nb], p2[:, 0:nb], AF.Sigmoid)
                            P.tt("dve", tg[:, 0:nb], p1[:, 0:nb], sg[:, 0:nb], ALU.mult)
                            P.tt("pool", oT[:, fc * 512:fc * 512 + nb], tg[:, 0:nb], szT[:, fc * NT + col0:fc * NT + col0 + nb], ALU.mult)
                        for ti in range(t0, t1):
                            lt = ti - t0
                            ytile = yt[cntb["x"] % 2]; cntb["x"] += 1
                            load_tile(ytile[:], ti)
                            for hf in range(2):
                                pa = psS[hf]
                                for fc in range(8):
                                    P.mm(pa[:, :], oT[:, fc * 512 + lt * 128:fc * 512 + (lt + 1) * 128], Wo2[:, fc * 1024 + hf * 512:fc * 1024 + (hf + 1) * 512], start=(fc == 0), stop=(fc == 7))
                                P.tt("dve", ytile[:, hf * 512:(hf + 1) * 512], ytile[:, hf * 512:(hf + 1) * 512], pa[:, :], ALU.add)
                            if ti < 16:
                                P.dma(yp[ti * 128:(ti + 1) * 128, :], ytile[:], reads=[ytile, ("y0", "p", ti)], writes=[("y0", "p", ti)])
                            else:
                                P.dma(ys, ytile[:], reads=[ytile, ("y0", "s", 0)], writes=[("y0", "s", 0)])
                    P.barrier()
        P.barrier()
        P.flush()
    return nc, in_names


_PROG_CACHE = {}


def _shared_inputs(inputs, consts):
    perm = _perm_even()
    sh = {}
    sh["w_in_e"] = np.ascontiguousarray(inputs["w_in_even"][0][:, perm])
    sh["w_out_e"] = np.ascontiguousarray(inputs["w_out_even"][0])
    sh["norm_e"] = np.ascontiguousarray(inputs["norm_even"][0].reshape(8, 128).T)
    sh["gn_gain"] = np.ascontiguousarray(inputs["ret_gn_gain"][0].reshape(1, 512))
    qn = inputs["nsa_q_norm"][0]
    kn = inputs["nsa_k_norm"][0]
    sh["qk_gain"] = np.concatenate([np.tile(qn, 8), np.tile(kn[0], 2), np.tile(kn[1], 2), np.tile(kn[2], 2)]).reshape(1, 896).astype(np.float32)
    sh["cmp_posT"] = np.ascontiguousarray(inputs["nsa_cmp_pos"][0].transpose(2, 0, 1))
    sh["cmp_w"] = np.ascontiguousarray(inputs["nsa_cmp_w"][0])
    for k, v in consts.items():
        sh["c_" + k] = v
    ca = np.ascontiguousarray
    sh["w_in_o"] = ca(inputs["w_in_odd"][0])
    sh["glu1"] = ca(inputs["glu_w1"][0])
    sh["glu2"] = ca(inputs["glu_w2"][0])
    sh["w_out_o"] = ca(inputs["w_out_odd"][0])
    sh["norm_o"] = ca(inputs["norm_odd"][0].reshape(8, 128).T)
    sh["ssmd"] = ca(inputs["ssm_d"][0].reshape(8, 128).T)
    sh["lamre_A"] = ca(inputs["ssm_lambda_re"][0].reshape(32, 2, 64).transpose(1, 2, 0).reshape(128, 32))
    sh["lamim_A"] = ca(inputs["ssm_lambda_im"][0].reshape(32, 2, 64).transpose(1, 2, 0).reshape(128, 32))
    ls = np.broadcast_to(inputs["ssm_log_step"][0].reshape(32, 2, 1), (32, 2, 64))
    sh["lstep_A"] = ca(ls.transpose(1, 2, 0).reshape(128, 32)).astype(np.float32)
    sh["bA_re"] = ca(inputs["ssm_b_re"][0].reshape(32, 2, 64, 16).transpose(1, 2, 0, 3).reshape(128, 512))
    sh["bA_im"] = ca(inputs["ssm_b_im"][0].reshape(32, 2, 64, 16).transpose(1, 2, 0, 3).reshape(128, 512))
    sh["cA_re"] = ca(inputs["ssm_c_re"][0].reshape(32, 2, 16, 64).transpose(1, 3, 0, 2).reshape(128, 512))
    sh["cA_im"] = ca(inputs["ssm_c_im"][0].reshape(32, 2, 16, 64).transpose(1, 3, 0, 2).reshape(128, 512))
    return sh


def _core_inputs(inputs, c, sh, cache_rows):
    im = dict(sh)
    im["xp"] = np.ascontiguousarray(inputs["x_prompt"][c])
    im["xs"] = np.ascontiguousarray(inputs["x_sample"][16 * c:16 * c + 16].reshape(128, 1024))
    im["cache"] = inputs["cache_nsa_kv"][0].reshape(2560 * 128, 512)[0:cache_rows]
    im["cwin"] = np.ascontiguousarray(inputs["cache_nsa_win"][0, 16 * c:16 * c + 16].reshape(16, 512, 256))
    im["sret"] = np.ascontiguousarray(inputs["state_ret"][0, 16 * c:16 * c + 16])
    im["x0A_re"] = np.ascontiguousarray(inputs["state_ssm_re"][0, 16 * c:16 * c + 16].reshape(16, 32, 2, 64).transpose(2, 3, 1, 0).reshape(128, 512))
    im["x0A_im"] = np.ascontiguousarray(inputs["state_ssm_im"][0, 16 * c:16 * c + 16].reshape(16, 32, 2, 64).transpose(2, 3, 1, 0).reshape(128, 512))
    im["ptab"] = np.ascontiguousarray(inputs["page_table"][16 * c:16 * c + 16].reshape(1, 256)).astype(np.int32)
    return im


def run_cores(inputs, cores, trace=False, **opts):
    consts = make_consts()
    key = tuple(sorted(opts.items()))
    nc, in_names = build_program(consts, **opts)
    sh = _shared_inputs(inputs, consts)
    cache_rows = 2560 * 128 if opts.get("do_sample", True) else 128
    in_maps = [_core_inputs(inputs, c, sh, cache_rows) for c in cores]
    in_maps = [{k: m[k] for k in in_names} for m in in_maps]
    if trace:
        res = run_bass_kernel_spmd(nc, in_maps, core_ids=list(range(len(cores))), trace=True)
        print("EXEC_TIME_NS", res.exec_time_ns)
        return res.results
    res = run_bass_kernel_spmd(nc, in_maps, core_ids=list(range(len(cores))))
    return res.results


def kernel(**inputs):
    inputs = {k: np.asarray(v) for k, v in inputs.items()}
    res = run_cores(inputs, list(range(NCORES)))
    f32 = np.float32
    y_p = np.zeros((8, 2048, 1024), f32)
    y_s = np.zeros((128, 8, 1024), f32)
    ret_p = np.zeros((1, 8, 4, 64, 128), f32)
    ret_s = np.zeros((1, 128, 4, 64, 128), f32)
    kv_p = np.zeros((1, 8, 2048, 4, 2, 64), f32)
    kv_s = np.zeros((1, 128, 8, 4, 2, 64), f32)
    win_p = np.zeros((1, 8, 512, 2, 2, 64), f32)
    win_s = np.zeros((1, 128, 512, 2, 2, 64), f32)
    sre_p = np.zeros((1, 8, 64, 64), f32)
    sim_p = np.zeros((1, 8, 64, 64), f32)
    sre_s = np.zeros((1, 128, 64, 64), f32)
    sim_s = np.zeros((1, 128, 64, 64), f32)
    for c in range(NCORES):
        r = res[c]
        sl = slice(16 * c, 16 * c + 16)
        y_p[c] = r["yp"]
        y_s[sl] = r["ys"].reshape(16, 8, 1024)
        ret_p[0, c] = r["o_retp"]
        ret_s[0, sl] = r["o_rets"]
        kv_p[0, c] = r["o_kvp"].reshape(2048, 4, 2, 64)
        kv_s[0, sl] = r["o_kvs"].reshape(16, 8, 4, 2, 64)
        win_p[0, c] = r["o_winp"].reshape(512, 2, 2, 64)
        win_s[0, sl] = r["o_wins"].reshape(16, 512, 2, 2, 64)
        if "o_ssp" in r:
            sp = r["o_ssp"].reshape(2, 2, 64, 32).transpose(0, 3, 1, 2).reshape(2, 64, 64)
            sre_p[0, c] = sp[0]
            sim_p[0, c] = sp[1]
            ss = r["o_sss"].reshape(2, 2, 64, 32, 16).transpose(0, 4, 3, 1, 2).reshape(2, 16, 64, 64)
            sre_s[0, sl] = ss[0]
            sim_s[0, sl] = ss[1]
    return (y_p, y_s, ret_p, ret_s, kv_p, kv_s, win_p, win_s, sre_p, sim_p, sre_s, sim_s)
```

```python
import numpy as np
import concourse.bass as bass
import concourse.mybir as mybir
from concourse.bass_utils import run_bass_kernel_spmd

F32 = mybir.dt.float32
BF16 = mybir.dt.bfloat16
I32 = mybir.dt.int32
AF = mybir.ActivationFunctionType
ALU = mybir.AluOpType
AX = mybir.AxisListType

ENGS = ("pe", "act", "dve", "pool", "sp")
NDMASEM = 24


class Prog:
    def __init__(self, nc):
        self.nc = nc
        self.ops = {e: [] for e in ENGS}
        self.cnt = {e: 0 for e in ENGS}
        self.sem = {}
        self.seen = {e: {} for e in ENGS}
        self.last_w = {}
        self.readers = {}
        self.dma_i = 0
        self.dma_uses = [0] * NDMASEM
        self.dma_tok = [None] * NDMASEM
        self.stack = None
        self.n_ops = 0

    def setup(self, stack):
        self.stack = stack
        for e in ENGS:
            self.sem[e] = stack.enter_context(self.nc.semaphore("s_" + e))
        for i in range(NDMASEM):
            self.sem["d%d" % i] = stack.enter_context(self.nc.semaphore("s_d%d" % i))

    def _key(self, a):
        if isinstance(a, str):
            return a
        if isinstance(a, tuple):
            return a
        t = getattr(a, 'tensor', None)
        return t.name if t is not None else a.name

    def _deps(self, eng, reads, writes):
        toks = []
        for k in reads:
            k = self._key(k)
            t = self.last_w.get(k)
            if t is not None:
                toks.append(t)
        for k in writes:
            k = self._key(k)
            t = self.last_w.get(k)
            if t is not None:
                toks.append(t)
            toks.extend(self.readers.get(k, ()))
        need = {}
        for (s, v) in toks:
            if eng == "pe" and s == "pe":
                continue
            if v > need.get(s, 0):
                need[s] = v
        waits = []
        seen = self.seen[eng]
        for s, v in need.items():
            if seen.get(s, 0) >= v:
                continue
            seen[s] = v
            waits.append((s, v))
        return waits

    def _commit(self, tok, reads, writes):
        for k in writes:
            k = self._key(k)
            self.last_w[k] = tok
            self.readers[k] = []
        for k in reads:
            k = self._key(k)
            self.readers.setdefault(k, []).append(tok)

    def op(self, eng, fn, reads=(), writes=()):
        waits = self._deps(eng, reads, writes)
        self.cnt[eng] += 1
        tok = (eng, self.cnt[eng])
        self.ops[eng].append((waits, fn, (eng, 1)))
        self._commit(tok, reads, writes)
        self.n_ops += 1

    def dma(self, out, in_, reads=None, writes=None, q="sp", fn=None):
        if reads is None:
            reads = [in_]
        if writes is None:
            writes = [out]
        i = self.dma_i % NDMASEM
        self.dma_i += 1
        sname = "d%d" % i
        waits = self._deps(q, reads, writes)
        prev = self.dma_tok[i]
        if prev is not None and self.seen[q].get(sname, 0) < prev[1]:
            self.seen[q][sname] = prev[1]
            waits.append(prev)
        self.dma_uses[i] += 1
        tok = (sname, 16 * self.dma_uses[i])
        self.dma_tok[i] = tok
        if fn is None:
            fn = lambda e, o=out, a=in_: e.dma_start(out=o, in_=a)
        self.ops[q].append((waits, fn, (sname, 16)))
        self._commit(tok, reads, writes)
        self.n_ops += 1

    def barrier(self):
        for e in ENGS:
            waits = []
            for e2 in ENGS:
                if e2 == e:
                    continue
                v = self.cnt[e2]
                if v > self.seen[e].get(e2, 0):
                    self.seen[e][e2] = v
                    waits.append((e2, v))
            for i in range(NDMASEM):
                t = self.dma_tok[i]
                if t is not None and self.seen[e].get(t[0], 0) < t[1]:
                    self.seen[e][t[0]] = t[1]
                    waits.append(t)
            if waits:
                self.ops[e].append((waits, None, None))

    def flush(self):
        nc = self.nc
        ops = self.ops
        sem = self.sem

        def replay(engh, lst):
            for (waits, fn, inc) in lst:
                for (s, v) in waits:
                    engh.wait_ge(sem[s], v)
                if fn is not None:
                    ins = fn(engh)
                    ins.then_inc(sem[inc[0]], inc[1])

        with nc.Block() as block:
            @block.tensor
            def _(e):
                replay(e, ops["pe"])

            @block.scalar
            def _(e):
                replay(e, ops["act"])

            @block.vector
            def _(e):
                replay(e, ops["dve"])

            @block.gpsimd
            def _(e):
                replay(e, ops["pool"])

            @block.sync
            def _(e):
                replay(e, ops["sp"])
        self.ops = {e: [] for e in ENGS}

    def mm(self, out, lhsT, rhs, start=True, stop=True, reads=None, writes=None):
        if reads is None:
            reads = [lhsT, rhs]
        if writes is None:
            writes = [out]
        self.op("pe", lambda e: e.matmul(out, lhsT, rhs, start=start, stop=stop), reads, writes)

    def tr(self, out, in_, ident, reads=None, writes=None):
        if reads is None:
            reads = [in_, ident]
        if writes is None:
            writes = [out]
        self.op("pe", lambda e: e.transpose(out, in_, ident), reads, writes)

    def actv(self, out, in_, func, bias=None, scale=None, accum_out=None, reads=None, writes=None, eng="act"):
        kw = {}
        if bias is not None:
            kw["bias"] = bias
        if scale is not None:
            kw["scale"] = scale
        if accum_out is not None:
            kw["accum_out"] = accum_out
        if reads is None:
            reads = [in_]
            if bias is not None and not isinstance(bias, (int, float)):
                reads.append(bias)
            if scale is not None and not isinstance(scale, (int, float)):
                reads.append(scale)
        if writes is None:
            writes = [out]
            if accum_out is not None:
                writes.append(accum_out)
        self.op("act", lambda e: e.activation(out, in_, func, **kw), reads, writes)

    def ts(self, eng, out, in0, s1, s2, op0, op1=None, accum_out=None, reads=None, writes=None):
        kw = {}
        if op1 is not None:
            kw["op1"] = op1
        if accum_out is not None:
            kw["accum_out"] = accum_out
        if reads is None:
            reads = [in0]
            for s in (s1, s2):
                if s is not None and not isinstance(s, (int, float)):
                    reads.append(s)
        if writes is None:
            writes = [out]
            if accum_out is not None:
                writes.append(accum_out)
        self.op(eng, lambda e: e.tensor_scalar(out, in0, s1, s2, op0, **kw), reads, writes)

    def tt(self, eng, out, in0, in1, op, reads=None, writes=None):
        if reads is None:
            reads = [in0, in1]
        if writes is None:
            writes = [out]
        self.op(eng, lambda e: e.tensor_tensor(out, in0, in1, op), reads, writes)

    def stt(self, eng, out, in0, scalar, in1, op0, op1, accum_out=None, reads=None, writes=None):
        kw = {}
        if accum_out is not None:
            kw["accum_out"] = accum_out
        if reads is None:
            reads = [in0, in1]
            if not isinstance(scalar, (int, float)):
                reads.append(scalar)
        if writes is None:
            writes = [out]
            if accum_out is not None:
                writes.append(accum_out)
        self.op(eng, lambda e: e.scalar_tensor_tensor(out, in0, scalar, in1, op0, op1, **kw), reads, writes)

    def cp(self, eng, out, in_, reads=None, writes=None):
        if reads is None:
            reads = [in_]
        if writes is None:
            writes = [out]
        if eng == "act":
            self.op(eng, lambda e: e.copy(out, in_), reads, writes)
        else:
            self.op(eng, lambda e: e.tensor_copy(out, in_), reads, writes)

    def memset(self, eng, ap, val, writes=None):
        if writes is None:
            writes = [ap]
        self.op(eng, lambda e: e.memset(ap, val), (), writes)

import math
from contextlib import ExitStack
import ml_dtypes

BF = ml_dtypes.bfloat16
NCORES = 8
BIG = 1.0e4
FORCE = 1.0e4
EPS = 1e-6
SCALE = 0.125
QA, KA, QN, KC, KS, KW, VC, VS, VW, GL, VA, ZA, ZB = 0, 256, 512, 1024, 1152, 1280, 1408, 1536, 1664, 1792, 1816, 2328, 2840
EIN = 3352
GROUPS = [(0, 512), (512, 1024), (1024, 1408), (1408, 1816), (1816, 2328), (2328, 2840), (2840, 3352)]


def _perm_even():
    perm = np.zeros(EIN, np.int64)
    perm[QA:QA + 256] = np.arange(0, 256)
    perm[KA:KA + 256] = np.arange(256, 512)
    o_va, o_za, o_qn, o_kvb, o_gl, o_zb = 512, 1024, 1536, 2048, 2816, 2840
    for j in range(4):
        for g in range(2):
            for dd in range(64):
                perm[QN + (j * 2 + g) * 64 + dd] = o_qn + (4 * g + j) * 64 + dd
    for dst, kind in ((KC, 0), (KS, 2), (KW, 4), (VC, 1), (VS, 3), (VW, 5)):
        perm[dst:dst + 128] = o_kvb + kind * 128 + np.arange(128)
    perm[GL:GL + 24] = o_gl + np.arange(24)
    perm[VA:VA + 512] = o_va + np.arange(512)
    perm[ZA:ZA + 512] = o_za + np.arange(512)
    perm[ZB:ZB + 512] = o_zb + np.arange(512)
    assert len(set(perm.tolist())) == EIN
    return perm


def make_consts():
    c = {}
    f32 = np.float32
    c["ident"] = np.eye(128, dtype=f32).astype(BF)
    c["ident32"] = np.eye(128, dtype=f32)
    half = 32
    inv = (np.float32(10000.0) ** (-np.arange(half, dtype=f32) / f32(half))).astype(f32)
    pos_p = (np.arange(16)[None, :] * 128 + np.arange(128)[:, None]).astype(f32)
    ang = (pos_p[:, :, None] * inv[None, None, :]).astype(f32)
    c["cos_p"] = np.cos(ang).astype(f32)
    c["sin_p"] = np.sin(ang).astype(f32)
    pos_s = (2048 + (np.arange(128) % 8)).astype(f32)
    ang = (pos_s[:, None] * inv[None, :]).astype(f32)
    c["cos_s"] = np.cos(ang).astype(f32)
    c["sin_s"] = np.sin(ang).astype(f32)
    log_g = np.log(1.0 - 2.0 ** (-5.0 - np.arange(4, dtype=np.float64)))
    i = np.arange(128)
    diff = i[None, :] - i[:, None]
    caus = diff >= 0
    DT = np.zeros((128, 4, 128), f32)
    for h in range(4):
        DT[:, h, :] = 0.125 * np.exp(np.where(caus, diff, 0) * log_g[h]) * caus
    c["DTp"] = DT
    c["qdec_p"] = np.exp((i[:, None] + 1.0) * log_g[None, :]).astype(f32)
    c["kdec_p"] = (0.125 * np.exp((127.0 - i[:, None]) * log_g[None, :])).astype(f32)
    cd = np.zeros((128, 2), f32)
    cds = np.zeros((128, 2), f32)
    for p in range(128):
        for pr in range(2):
            h = 2 * pr + p // 64
            cd[p, pr] = np.exp(128.0 * log_g[h])
            cds[p, pr] = np.exp(8.0 * log_g[h])
    c["cdec_p"] = cd
    c["cdec_s"] = cds
    i8 = i % 8
    same = (i[:, None] // 8) == (i[None, :] // 8)
    diff8 = i8[None, :] - i8[:, None]
    caus8 = same & (diff8 >= 0)
    DTs = np.zeros((128, 4, 128), f32)
    for h in range(4):
        DTs[:, h, :] = 0.125 * np.exp(np.where(caus8, diff8, 0) * log_g[h]) * caus8
    c["DTs"] = DTs
    c["qdec_s"] = np.exp((i8[:, None] + 1.0) * log_g[None, :]).astype(f32)
    c["kdec_s"] = (0.125 * np.exp((7.0 - i8[:, None]) * log_g[None, :])).astype(f32)
    c["blkmask"] = ((i[:, None] // 8) == np.arange(16)[None, :]).astype(f32)
    cm = np.zeros((128, 16, 128), f32)
    cm[:, :, :] = ((i[None, :] // 8) == np.arange(16)[:, None])[None, :, :]
    c["colmask"] = cm.astype(BF)
    keys = np.arange(2176)
    E = (keys[None, :] // 64 == np.arange(33)[:, None]).astype(f32)
    c["E"] = E.astype(BF)
    c["CB"] = np.where(i[:, None] <= i[None, :], 0.0, -BIG).astype(f32).astype(BF)
    c["AB"] = np.where(i[:, None] > i[None, :], 0.0, -BIG).astype(f32).astype(BF)
    n = np.arange(128)
    cmpb = np.zeros((128, 16, 128), f32)
    for t in range(16):
        tpos = 128 * t + i
        cmpb[:, t, :] = np.where((16 * n[:, None] + 31) <= tpos[None, :], 0.0, -BIG)
    c["CMPB"] = cmpb.astype(BF)

    def overlap(n_s):
        c_start = np.arange(127) * 16
        s_start = np.arange(n_s) * 64
        return ((c_start[:, None] < s_start[None, :] + 64) & (s_start[None, :] < c_start[:, None] + 32)).astype(f32)
    c["ovl_p"] = overlap(32)
    c["ovl_s"] = overlap(33)
    mulc = np.zeros((128, 16, 32), f32)
    addc = np.zeros((128, 16, 32), f32)
    s_ids = np.arange(32)
    for t in range(16):
        tpos = 128 * t + i
        cur = tpos // 64
        forced = (s_ids[None, :] == 0) | (s_ids[None, :] == cur[:, None]) | (s_ids[None, :] == cur[:, None] - 1)
        valid = (s_ids[None, :] * 64) <= tpos[:, None]
        mulc[:, t, :] = (valid & ~forced)
        addc[:, t, :] = np.where(valid, np.where(forced, FORCE, 0.0), -FORCE)
    c["mulc_p"] = mulc
    c["addc_p"] = addc
    s33 = np.arange(33)
    forced = (s33 == 0) | (s33 == 32) | (s33 == 31)
    c["mulc_s"] = np.tile((~forced).astype(f32)[None, :], (8, 1))
    c["addc_s"] = np.tile(np.where(forced, FORCE, 0.0).astype(f32)[None, :], (8, 1))
    q8 = np.arange(8)
    SB = np.where(((i[:, None, None] // 8) == np.arange(16)[None, :, None]) & ((i[:, None, None] % 8) <= q8[None, None, :]), 0.0, -BIG)
    c["SB"] = SB.astype(f32).astype(BF)
    c["ABs"] = np.where(i[:, None] > q8[None, :], 0.0, -BIG).astype(f32).astype(BF)
    c["iota_p"] = i.astype(f32)[:, None].copy()
    c["ones_row"] = np.ones((1, 128), f32).astype(BF)
    c["zeros_row"] = np.zeros((1, 512), f32).astype(BF)
    jv = np.zeros((128, 9, 32), f32)
    jv[:, :, :] = np.arange(9, dtype=f32)[None, :, None]
    c["JV"] = jv
    c["KI"] = np.tile(np.arange(256, dtype=f32)[None, :], (128, 1))
    c["PM4"] = (np.arange(128)[:, None] // 32 == np.arange(4)[None, :]).astype(f32)
    c["BM16"] = (np.arange(128)[:, None] // 16 == np.arange(128)[None, :] // 16).astype(f32)
    return c


def _dt_of(a):
    if a.dtype == np.float32:
        return F32
    if a.dtype == np.int32:
        return I32
    if a.dtype == BF:
        return BF16
    raise ValueError(a.dtype)


def v3(ap, h):
    return ap.rearrange("p (h d) -> p h d", h=h)


def bc_mid(ap, n):
    return ap.unsqueeze(1).to_broadcast([ap.shape[0], n, ap.shape[1]])


def bc_last(ap, n):
    return ap.unsqueeze(2).to_broadcast([ap.shape[0], ap.shape[1], n])


def build_program(consts, phase_b=True, do_sample=True, n_ptiles=16, stage=99):
    nc = bass.Bass("TRN2", target_bir_lowering=False)

    in_names = []

    def din(name, shape, dt=F32):
        in_names.append(name)
        return nc.dram_tensor(name, list(shape), dt, kind="ExternalInput").ap()

    def dout(name, shape, dt=F32):
        return nc.dram_tensor(name, list(shape), dt, kind="ExternalOutput").ap()

    xp = din("xp", [2048, 1024])
    xs = din("xs", [128, 1024])
    if do_sample:
        cache = din("cache", [2560 * 128, 512])
        cwin = din("cwin", [16, 512, 256])
        sret = din("sret", [16, 4, 64, 128])
        ptab = din("ptab", [1, 256], I32)
    w_in_e = din("w_in_e", [1024, EIN])
    w_out_e = din("w_out_e", [1024, 1024])
    norm_e = din("norm_e", [128, 8])
    gn_gain = din("gn_gain", [1, 512])
    qk_gain = din("qk_gain", [1, 896])
    cmp_posT = din("cmp_posT", [64, 2, 32])
    cmp_w = din("cmp_w", [2, 32, 64, 64])
    S_ONLY = ("cos_s", "sin_s", "DTs", "qdec_s", "kdec_s", "cdec_s", "blkmask", "colmask", "SB", "ABs", "mulc_s", "addc_s", "ovl_s", "iota_p")
    if phase_b:
        w_in_o = din("w_in_o", [1024, 2048])
        glu1 = din("glu1", [1024, 1024])
        glu2 = din("glu2", [1024, 1024])
        w_out_o = din("w_out_o", [1024, 1024])
        norm_o = din("norm_o", [128, 8])
        ssmd = din("ssmd", [128, 8])
        lamre_A = din("lamre_A", [128, 32])
        lamim_A = din("lamim_A", [128, 32])
        lstep_A = din("lstep_A", [128, 32])
        bA_re = din("bA_re", [128, 512])
        bA_im = din("bA_im", [128, 512])
        cA_re = din("cA_re", [128, 512])
        cA_im = din("cA_im", [128, 512])
        x0A_re = din("x0A_re", [128, 512])
        x0A_im = din("x0A_im", [128, 512])
    cd = {k: din("c_" + k, v.shape, _dt_of(v)) for k, v in consts.items() if ((do_sample or k not in S_ONLY) and (phase_b or k not in ("JV", "KI", "PM4", "BM16")))}
    yp = dout("yp", [2048, 1024])
    ys = dout("ys", [128, 1024])
    o_retp = dout("o_retp", [4, 64, 128])
    o_rets = dout("o_rets", [16, 4, 64, 128])
    o_kvp = dout("o_kvp", [2048, 512])
    o_kvs = dout("o_kvs", [128, 512])
    o_winp = dout("o_winp", [512, 256])
    o_wins = dout("o_wins", [16, 512, 256])
    if phase_b:
        o_ssp = dout("o_ssp", [2, 128, 32])
        o_sss = dout("o_sss", [2, 128, 512])

    P = Prog(nc)
    with ExitStack() as st0:
        P.setup(st0)

        def alloc(st, name, shape, dt=F32):
            return st.enter_context(nc.sbuf_tensor(name, list(shape), dt))

        def palloc(st, name, shape, dt=F32):
            return st.enter_context(nc.psum_tensor(name, list(shape), dt))

        psT = palloc(st0, "psT", [128, 1024], BF16)
        psA = [palloc(st0, "psA%d" % i, [128, 512]) for i in range(2)]
        psS = [palloc(st0, "psS%d" % i, [128, 512]) for i in range(2)]
        psV = palloc(st0, "psV", [128, 512])
        psC = palloc(st0, "psC", [128, 512])
        psR = palloc(st0, "psR", [128, 512])

        with ExitStack() as stA:
            A = lambda name, shape, dt=F32: alloc(stA, name, shape, dt)
            cs = {}
            P_ONLY = ("cos_p", "sin_p", "DTp", "qdec_p", "kdec_p", "cdec_p", "CMPB", "mulc_p", "addc_p", "ovl_p", "CB", "AB")

            def load_consts(stx, names, pre="k_"):
                for k in names:
                    v = consts[k]
                    shp = list(v.shape)
                    if len(shp) == 3:
                        tl = alloc(stx, pre + k, [shp[0], shp[1] * shp[2]], _dt_of(v))
                        P.dma(tl[:], cd[k].rearrange("p a b -> p (a b)"))
                    else:
                        tl = alloc(stx, pre + k, shp, _dt_of(v))
                        P.dma(tl[:], cd[k])
                    cs[k] = tl
            B_ONLY = ("JV", "KI", "PM4", "BM16")
            load_consts(stA, [k for k in consts if k not in P_ONLY and k not in S_ONLY and k not in B_ONLY])
            ident = cs["ident"]
            Wob = A("Wob", [128, 8 * 1024], BF16)
            ng = A("ng", [128, 8])
            BDW = A("BDW", [128, 2 * 32 * 128], BF16)
            peb = A("peb", [128, 64], BF16)
            posk = A("posk", [128, 2])
            posv = A("posv", [1, 128], BF16)
            gnb = A("gnb", [128, 512])
            P.dma(gnb[:], gn_gain.partition_broadcast(128))
            qkg = A("qkg", [128, 896])
            P.dma(qkg[:], qk_gain.partition_broadcast(128))

            xt = [A("xt%d" % i, [128, 1024]) for i in range(2)]
            xb = A("xb", [128, 1024], BF16)
            xT = A("xT", [128, 1024], BF16)
            stt_ = A("stats", [128, 64])
            proj = A("proj", [128, EIN])
            rp = A("rp", [128, 1408])
            tmpa = A("tmpa", [128, 1408])
            r16 = A("r16", [128, 1024], BF16)
            vb16 = A("vb16", [128, 512], BF16)
            rT = A("rT", [128, 256], BF16)
            qz = A("qz", [128, 1024], BF16)
            qTz = A("qTz", [128, 1024], BF16)
            scm = A("scm", [128, 512], BF16)
            S32 = A("S32", [128, 256])
            Sb = A("Sb", [128, 256], BF16)
            osb = A("osb", [128, 512])
            ocb = A("ocb", [128, 512])
            sz = A("sz", [128, 512])
            mix = A("mix", [128, 1024], BF16)
            mixT = A("mixT", [128, 1024], BF16)
            n16 = A("n16", [128, 1024], BF16)
            gts = A("gts", [128, 24])
            on = A("on", [128, 3 * 512])
            tmpb = on
            obt = A("obt", [128, 512])
            pt_ = [A("pt%d" % i, [128, 512], BF16) for i in range(3)]
            selT = A("selT", [33, 256], BF16)
            sc_ = A("sc", [128, 80])
            sc2 = A("sc2", [128, 80])
            sc16 = A("sc16", [128, 80], BF16)
            rden = A("rden", [128, 32])
            top8 = A("top8", [128, 16])
            P.memset("dve", S32[:], 0.0)
            P.memset("dve", Sb[:], 0.0)
            P.memset("pool", qz[:], 0.0)
            P.memset("pool", qTz[:], 0.0)

            stWb = ExitStack()
            Wb = alloc(stWb, "Wb", [128, 8 * EIN], BF16)
            P.dma(ng[:], norm_e)
            with ExitStack() as stW:
                stg = [alloc(stW, "stg%d" % i, [128, EIN]) for i in range(2)]
                for kc in range(8):
                    s_ = stg[kc % 2]
                    P.dma(s_[:], w_in_e[kc * 128:(kc + 1) * 128, :])
                    if kc % 2 == 0:
                        P.ts("dve", Wb[:, kc * EIN:(kc + 1) * EIN], s_[:], ng[:, kc:kc + 1], None, ALU.mult)
                    else:
                        P.op("act", lambda e, kc=kc, s_=s_: e.mul(Wb[:, kc * EIN:(kc + 1) * EIN], s_[:], ng[:, kc:kc + 1]), [s_, ng], [Wb])
                for kc in range(8):
                    s_ = stg[kc % 2]
                    P.dma(s_[:, 0:1024], w_out_e[kc * 128:(kc + 1) * 128, :])
                    if kc % 2 == 0:
                        P.cp("dve", Wob[:, kc * 1024:(kc + 1) * 1024], s_[:, 0:1024])
                    else:
                        P.cp("pool", Wob[:, kc * 1024:(kc + 1) * 1024], s_[:, 0:1024])
                P.barrier()
            with ExitStack() as stW:
                P.memset("pool", BDW[:], 0.0)
                wst = alloc(stW, "wst", [128, 2 * 32 * 64])
                srcw = cmp_w.rearrange("c l d e -> d (c l) e")
                P.dma(wst[0:64, :].rearrange("p (a e) -> p a e", e=64), srcw)
                P.dma(wst[64:128, :].rearrange("p (a e) -> p a e", e=64), srcw)
                bdv = BDW[:].rearrange("p (a e) -> p a e", e=128)
                P.cp("dve", bdv[0:64, :, 0:64], wst[0:64, :].rearrange("p (a e) -> p a e", e=64))
                P.cp("dve", bdv[64:128, :, 64:128], wst[64:128, :].rearrange("p (a e) -> p a e", e=64))
                pe32 = alloc(stW, "pe32", [128, 64])
                P.dma(pe32[0:64, :], cmp_posT.rearrange("d c l -> d (c l)"))
                P.dma(pe32[64:128, :], cmp_posT.rearrange("d c l -> d (c l)"))
                P.cp("dve", peb[:], pe32[:])
                for l in range(32):
                    P.mm(psC[:, 0:1], BDW[:, (0 * 32 + l) * 128:(0 * 32 + l + 1) * 128], peb[:, l:l + 1], start=(l == 0), stop=(l == 31))
                for l in range(32):
                    P.mm(psC[:, 1:2], BDW[:, (32 + l) * 128:(32 + l + 1) * 128], peb[:, 32 + l:32 + l + 1], start=(l == 0), stop=(l == 31))
                P.cp("dve", posk[:, 0:2], psC[:, 0:2])
                P.barrier()
            cnt = {"x": 0, "pt": 0, "ps": 0, "pa": 0, "ps3": 0}
            xpref = {}

            def rsqrt_small(out, in_, mult, add):
                P.ts("dve", out, in_, mult, add, ALU.mult, ALU.add)
                P.actv(out, out, AF.Sqrt)
                P.op("dve", lambda e: e.reciprocal(out, out), [out], [out])

            def nsa_tile(nq, groups, ns, mulc, addc, cmp_bias, merge, qall=None):
                W = 4 * nq
                wv = 65 + ns
                R = [dict(cmp=psA[0], o=0), dict(cmp=psA[1], o=40)]
                if merge:
                    accT = {2: [psR, psR], 1: [psV, psV]}
                    aoff = [0, W]
                else:
                    accT = {2: [psR, psC], 1: [psV, psA[0]]}
                    aoff = [0, 0]
                back = {2: [psR, psC], 1: [psV, psA[0]]}
                if merge:
                    oTb = [obt[:, 0:256], obt[:, 256:512]]
                else:
                    oTb = [obt[:, :], sz[:, :]]
                v4 = lambda ap: ap.rearrange("p (j q) -> p j q", j=4)
                v24 = lambda ap: ap.rearrange("p (g j q) -> p g j q", g=2, j=4)
                for g, G in enumerate(groups):
                    r = R[g]; o = r["o"]
                    ps_s = psS[cnt["ps"] % 2]; cnt["ps"] += 1
                    P.mm(v4(ps_s[0:127, 0:W]), G["cmp_k"], G["qTg"], start=True, stop=(cmp_bias is None))
                    if cmp_bias is not None:
                        P.mm(v4(ps_s[0:127, 0:W]), ident[0:127, 0:127], cmp_bias, start=False, stop=True)
                    pt = pt_[cnt["pt"] % 3]; cnt["pt"] += 1
                    P.actv(pt[0:127, 0:W], ps_s[0:127, 0:W], AF.Exp, scale=SCALE)
                    pc = r["cmp"]
                    for j in range(4):
                        P.mm(pc[0:nq, j * wv:(j + 1) * wv], pt[0:127, j * nq:(j + 1) * nq], G["cmp_v"], start=True, stop=True)
                    pcv = pc[0:nq, 0:4 * wv].rearrange("p (j w) -> p j w", j=4)
                    rd = rden[0:nq, 16 * g:16 * g + 4]
                    P.ts("dve", rd, pcv[:, :, 64], 1e-30, None, ALU.add)
                    P.op("dve", lambda e, rd=rd: e.reciprocal(rd, rd), [rden], [rden])
                    P.tt("dve", G["on_dst"](0), pcv[:, :, 0:64], bc_last(rd, 64), ALU.mult)
                    sc = sc_[0:nq, o:o + ns]
                    P.ts("dve", sc, pcv[:, 0, 65:65 + ns], rden[0:nq, 16 * g:16 * g + 1], None, ALU.mult)
                    for j in range(1, 4):
                        P.stt("dve", sc, pcv[:, j, 65:65 + ns], rden[0:nq, 16 * g + j:16 * g + j + 1], sc, ALU.mult, ALU.add)
                    P.tt("dve", sc, sc, mulc, ALU.mult)
                    P.tt("dve", sc, sc, addc, ALU.add)
                    t8 = top8[0:nq, 8 * g:8 * g + 8]
                    P.op("dve", lambda e, t8=t8, sc=sc: e.max(t8, sc), [sc_], [top8])
                    P.ts("dve", sc2[0:nq, o:o + ns], sc, top8[0:nq, 8 * g + 7:8 * g + 8], BIG, ALU.is_ge, ALU.mult)
                    P.ts("dve", sc16[0:nq, o:o + ns], sc2[0:nq, o:o + ns], -BIG, None, ALU.add)

                def scores(stp):
                    gs, br, ci, nch, chs = stp["d"]
                    chk = chs[0]
                    (kT, v1, nk, ecols, bias2d) = chk[0:5]
                    kkey = chk[5] if len(chk) > 5 and chk[5] is not None else kT
                    ps_s = (psS[0], psS[1], psA[1])[cnt["ps3"] % 3]; cnt["ps3"] += 1
                    stp["ps"] = ps_s
                    use_e = (br == 1 and ecols is not None)
                    extra = (1 if use_e else 0) + (1 if bias2d is not None else 0)
                    if len(gs) == 2:
                        outv = v24(ps_s[0:nk, 0:2 * W])
                        qr = qall
                        br_ = None if bias2d is None else bias2d.unsqueeze(1).unsqueeze(1).to_broadcast([nk, 2, 4, nq])
                        er_ = selT[0:ns, 0:256].rearrange("p (g q) -> p g q", g=2)[:, :, 0:nq].unsqueeze(2).to_broadcast([ns, 2, 4, nq])
                    else:
                        g = gs[0]
                        outv = v4(ps_s[0:nk, 0:W])
                        qr = groups[g]["qTg"]
                        br_ = None if bias2d is None else bc_mid(bias2d, 4)
                        er_ = bc_mid(selT[0:ns, 128 * g:128 * g + nq], 4)
                    P.mm(outv, kT, qr, start=True, stop=(extra == 0), reads=[kkey, qTz])
                    if bias2d is not None:
                        extra -= 1
                        P.mm(outv, ident[0:nk, 0:nk], br_, start=False, stop=(extra == 0))
                    if use_e:
                        P.mm(outv, ecols, er_, start=False, stop=True)

                def exp_pv(stp):
                    gs, br, ci, nch, chs = stp["d"]
                    nk = chs[0][2]
                    ps_s = stp["ps"]
                    Wt = W * len(gs)
                    pt = pt_[cnt["pt"] % 3]; cnt["pt"] += 1
                    P.actv(pt[0:nk, 0:Wt], ps_s[0:nk, 0:Wt], AF.Exp, scale=SCALE)
                    for k, g in enumerate(gs):
                        chk = chs[k]
                        v1 = chk[1]
                        vkey = chk[6] if len(chk) > 6 and chk[6] is not None else v1
                        acc = accT[br][g]
                        P.mm(acc[0:65, aoff[g]:aoff[g] + W], v1, pt[0:nk, k * W:(k + 1) * W], start=False, stop=(ci == nch - 1), reads=[pt, vkey])

                def run_steps(steps):
                    n = len(steps)
                    D = 2
                    for i in range(min(D, n)):
                        scores(steps[i])
                    for i in range(n):
                        if i + D < n:
                            scores(steps[i + D])
                        exp_pv(steps[i])

                for br, key in ((2, "win_chunks"), (1, "sel_chunks")):
                    if br == 1:
                        for g in range(2):
                            o = R[g]["o"]
                            pc0 = 384 + 512 * g
                            P.tr(psT[0:ns, pc0:pc0 + nq], sc16[0:nq, o:o + ns], ident[0:nq, 0:nq])
                            P.cp("dve", selT[0:ns, 128 * g:128 * g + nq], psT[0:ns, pc0:pc0 + nq])
                    steps = []
                    if merge:
                        acc = accT[br][0]
                        P.mm(acc[0:65, 0:2 * W], cs["zeros_row"][0:1, 0:65], cs["zeros_row"][0:1, 0:2 * W], start=True, stop=False)
                        c0, c1 = groups[0][key], groups[1][key]
                        for ci in range(len(c0)):
                            steps.append({"d": ([0, 1], br, ci, len(c0), [c0[ci], c1[ci]])})
                    else:
                        for g in range(2):
                            acc = accT[br][g]
                            P.mm(acc[0:65, 0:W], cs["zeros_row"][0:1, 0:65], cs["zeros_row"][0:1, 0:W], start=True, stop=False)
                            chunks = groups[g][key]
                            for ci, ch in enumerate(chunks):
                                steps.append({"d": ([g], br, ci, len(chunks), [ch])})
                    run_steps(steps)
                    for g in range(2):
                        acc = accT[br][g]
                        if g == 0:
                            P.cp("act", oTb[g][0:65, 0:W], acc[0:65, aoff[g]:aoff[g] + W])
                        else:
                            P.cp("dve", oTb[g][0:65, 0:W], acc[0:65, aoff[g]:aoff[g] + W])
                    for g in range(2):
                        bk = back[br][g]
                        for j in range(4):
                            P.tr(bk[0:nq, j * 65:(j + 1) * 65], oTb[g][0:65, j * nq:(j + 1) * nq], cs["ident32"][0:65, 0:65])
                        pvv = bk[0:nq, 0:260].rearrange("p (j w) -> p j w", j=4)
                        rd = rden[0:nq, 16 * g + 4 * br:16 * g + 4 * br + 4]
                        P.ts("dve", rd, pvv[:, :, 64], 1e-30, None, ALU.add)
                        P.op("dve", lambda e, rd=rd: e.reciprocal(rd, rd), [rden], [rden])
                        P.tt("dve", groups[g]["on_dst"](br), pvv[:, :, 0:64], bc_last(rd, 64), ALU.mult)

            def even_tile(mode, t, caches):
                isp = (mode == "p")
                xsrc = xp[t * 128:(t + 1) * 128, :] if isp else xs
                if xpref.get("cur") == (mode, t):
                    xtile = xpref["tile"]
                else:
                    xtile = xt[cnt["x"] % 2]; cnt["x"] += 1
                    P.dma(xtile[:], xsrc)
                nxt = caches.get("next")
                if nxt is not None:
                    ntile = xt[cnt["x"] % 2]; cnt["x"] += 1
                    nsrc = xp[nxt[1] * 128:(nxt[1] + 1) * 128, :] if nxt[0] == "p" else xs
                    P.dma(ntile[:], nsrc)
                    xpref["cur"] = nxt
                    xpref["tile"] = ntile
                P.memset("dve", stt_[:, 0:1], 0.0)
                P.actv(mixT[:], xtile[:], AF.Square, accum_out=stt_[:, 0:1])
                rsqrt_small(stt_[:, 1:2], stt_[:, 0:1], 1.0 / 1024, EPS)
                P.cp("act", xb[:], xtile[:])
                for kc in range(8):
                    P.tr(psT[:, kc * 128:(kc + 1) * 128], xb[:, kc * 128:(kc + 1) * 128], ident[:])
                P.cp("act", xT[:], psT[:])
                for gi, (c0, c1) in enumerate(GROUPS):
                    pa = psA[cnt["pa"] % 2]; cnt["pa"] += 1
                    w = c1 - c0
                    for kc in range(8):
                        P.mm(pa[:, 0:w], xT[:, kc * 128:(kc + 1) * 128], Wb[:, kc * EIN + c0:kc * EIN + c1], start=(kc == 0), stop=(kc == 7))
                    if gi % 2 == 0:
                        P.ts("dve", proj[:, c0:c1], pa[:, 0:w], stt_[:, 1:2], None, ALU.mult)
                    else:
                        P.op("act", lambda e, c0=c0, c1=c1, pa=pa, w=w: e.mul(proj[:, c0:c1], pa[:, 0:w], stt_[:, 1:2]), [pa, stt_], [proj])
                if not isp:
                    caches["hook"]()
                if stage <= 1:
                    return
                nv = v3(proj[:, QN:QN + 896], 14)
                P.tt("dve", tmpa[:, 0:896], proj[:, QN:QN + 896], proj[:, QN:QN + 896], ALU.mult)
                P.op("dve", lambda e: e.reduce_sum(stt_[:, 8:22], v3(tmpa[:, 0:896], 14), AX.X), [tmpa], [stt_])
                rsqrt_small(stt_[:, 8:22], stt_[:, 8:22], 1.0 / 64, EPS)
                P.tt("dve", nv, nv, bc_last(stt_[:, 8:22], 64), ALU.mult)
                P.tt("dve", proj[:, QN:QN + 896], proj[:, QN:QN + 896], qkg[:], ALU.mult)
                if isp:
                    cos = cs["cos_p"][:, t * 32:(t + 1) * 32]
                    sin = cs["sin_p"][:, t * 32:(t + 1) * 32]
                else:
                    cos = cs["cos_s"][:, :]
                    sin = cs["sin_s"][:, :]
                pv = v3(proj[:, 0:1408], 22)
                rv = v3(rp[:, 0:1408], 22)
                ta = tmpa[:, 0:704].rearrange("p (h d) -> p h d", h=22)
                tb = tmpa[:, 704:1408].rearrange("p (h d) -> p h d", h=22)
                tc_ = tmpb[:, 0:704].rearrange("p (h d) -> p h d", h=22)
                td_ = tmpb[:, 704:1408].rearrange("p (h d) -> p h d", h=22)
                cosb = bc_mid(cos, 22)
                sinb = bc_mid(sin, 22)
                P.tt("dve", ta, pv[:, :, 0:32], cosb, ALU.mult)
                P.tt("dve", tb, pv[:, :, 32:64], sinb, ALU.mult)
                P.tt("dve", rv[:, :, 0:32], ta, tb, ALU.subtract)
                P.tt("pool", tc_, pv[:, :, 32:64], cosb, ALU.mult)
                P.tt("pool", td_, pv[:, :, 0:32], sinb, ALU.mult)
                P.tt("pool", rv[:, :, 32:64], tc_, td_, ALU.add)
                if stage <= 2:
                    return
                if isp:
                    okv = o_kvp[t * 128:(t + 1) * 128, :]
                else:
                    okv = o_kvs
                P.dma(okv[:, 0:128], rp[:, KC:KC + 128])
                P.dma(okv[:, 128:256], proj[:, VC:VC + 128])
                P.dma(okv[:, 256:384], rp[:, KS:KS + 128])
                P.dma(okv[:, 384:512], proj[:, VS:VS + 128])
                if isp and t >= 12:
                    ow = o_winp[(t - 12) * 128:(t - 11) * 128, :]
                    P.dma(ow[:, 0:128], rp[:, KW:KW + 128])
                    P.dma(ow[:, 128:256], proj[:, VW:VW + 128])
                if not isp:
                    for b in range(16):
                        P.dma(o_wins[b, 504:512, 0:128], rp[b * 8:(b + 1) * 8, KW:KW + 128])
                        P.dma(o_wins[b, 504:512, 128:256], proj[b * 8:(b + 1) * 8, VW:VW + 128])
                        P.dma(o_wins[b, 0:504, :], cwin[b, 8:512, :])
                if stage <= 3:
                    return
                qdec = cs["qdec_p"] if isp else cs["qdec_s"]
                kdec = cs["kdec_p"] if isp else cs["kdec_s"]
                DTm = cs["DTp"] if isp else cs["DTs"]
                P.cp("act", r16[:, 0:256], rp[:, QA:QA + 256])
                P.tt("dve", v3(r16[:, 256:512], 4), v3(rp[:, QA:QA + 256], 4), bc_last(qdec[:, 0:4], 64), ALU.mult)
                P.cp("act", r16[:, 512:768], rp[:, KA:KA + 256])
                P.tt("dve", v3(r16[:, 768:1024], 4), v3(rp[:, KA:KA + 256], 4), bc_last(kdec[:, 0:4], 64), ALU.mult)
                P.cp("act", vb16[:], proj[:, VA:VA + 512])
                for i6 in range(6):
                    P.tr(psT[:, i6 * 128:(i6 + 1) * 128], r16[:, i6 * 128:(i6 + 1) * 128], ident[:])
                P.cp("act", rT[:, 0:256], psT[:, 512:768])
                qzv = qz[:].rearrange("p (a h q) -> p a h q", a=4, h=2)
                P.cp("act", qzv[0:64, :, 0, :], psT[0:64, 0:512].rearrange("p (a q) -> p a q", a=4))
                P.cp("dve", qzv[64:128, :, 1, :], psT[64:128, 0:512].rearrange("p (a q) -> p a q", a=4))
                if stage <= 3.1:
                    return
                for h in range(4):
                    hp, pr = h % 2, h // 2
                    P.mm(psR[:, h * 128:(h + 1) * 128], rT[:, pr * 128:(pr + 1) * 128],
                         qz[:, ((0 * 2 + pr) * 2 + hp) * 128:((0 * 2 + pr) * 2 + hp + 1) * 128])
                if stage <= 3.2:
                    return
                P.tt("dve", scm[:], psR[:], DTm[:], ALU.mult)
                if isp:
                    for h in range(4):
                        hp, pr = h % 2, h // 2
                        P.mm(psC[:, h * 128:(h + 1) * 128], scm[:, h * 128:(h + 1) * 128], vb16[:, h * 128:(h + 1) * 128], start=True, stop=False)
                        P.mm(psC[:, h * 128:(h + 1) * 128], qz[:, ((1 * 2 + pr) * 2 + hp) * 128:((1 * 2 + pr) * 2 + hp + 1) * 128],
                             Sb[:, pr * 128:(pr + 1) * 128], start=False, stop=True)
                else:
                    S0b = caches["S0b"]
                    qdTm = caches["qdTm"]
                    for h in range(4):
                        hp, pr = h % 2, h // 2
                        if hp == 0:
                            for b in range(16):
                                P.tt("dve" if b % 2 == 0 else "pool", qdTm[:, b * 256:(b + 1) * 256].rearrange("p (a q) -> p a q", a=2),
                                     qz[:, 512 + pr * 256:512 + (pr + 1) * 256].rearrange("p (a q) -> p a q", a=2),
                                     bc_mid(cs["colmask"][:, b * 128:(b + 1) * 128], 2), ALU.mult)
                        P.mm(psC[:, h * 128:(h + 1) * 128], scm[:, h * 128:(h + 1) * 128], vb16[:, h * 128:(h + 1) * 128], start=True, stop=False)
                        for b in range(16):
                            P.mm(psC[:, h * 128:(h + 1) * 128], qdTm[:, b * 256 + hp * 128:b * 256 + (hp + 1) * 128],
                                 S0b[:, (b * 2 + pr) * 128:(b * 2 + pr + 1) * 128], start=False, stop=(b == 15))
                if stage <= 3.4:
                    return
                P.cp("act", osb[:], psC[:])
                if stage <= 3.5:
                    return
                if isp:
                    for h in range(4):
                        hp, pr = h % 2, h // 2
                        P.mm(psR[:, h * 128:(h + 1) * 128], r16[:, 768 + pr * 128:768 + (pr + 1) * 128], vb16[:, h * 128:(h + 1) * 128])
                    for h in range(4):
                        hp, pr = h % 2, h // 2
                        rows = slice(hp * 64, (hp + 1) * 64)
                        P.stt("dve", S32[rows, pr * 128:(pr + 1) * 128], S32[rows, pr * 128:(pr + 1) * 128], cs["cdec_p"][rows, pr:pr + 1],
                              psR[rows, h * 128:(h + 1) * 128], ALU.mult, ALU.add)
                    P.cp("act", Sb[:], S32[:])
                    if t == n_ptiles - 1:
                        for h in range(4):
                            hp, pr = h % 2, h // 2
                            P.dma(o_retp[h, :, :], S32[hp * 64:(hp + 1) * 64, pr * 128:(pr + 1) * 128])
                else:
                    S0 = caches["S0"]
                    vblk = caches["vblk"]
                    Sn = caches["Sn"]
                    for h in range(4):
                        hp, pr = h % 2, h // 2
                        rows = slice(hp * 64, (hp + 1) * 64)
                        P.tt("dve", vblk[:].rearrange("p (b e) -> p b e", b=16), bc_mid(vb16[:, h * 128:(h + 1) * 128], 16),
                             bc_last(cs["blkmask"][:, 0:16], 128), ALU.mult)
                        for q4 in range(4):
                            pa = psA[cnt["pa"] % 2]; cnt["pa"] += 1
                            P.mm(pa[:, :], r16[:, 768 + pr * 128:768 + (pr + 1) * 128], vblk[:, q4 * 512:(q4 + 1) * 512])
                            s0v = S0[rows, :].rearrange("p (b a e) -> p b a e", b=16, a=2)[:, q4 * 4:(q4 + 1) * 4, pr, :]
                            P.stt("dve", Sn[rows, q4 * 512:(q4 + 1) * 512].rearrange("p (b e) -> p b e", b=4), s0v, cs["cdec_s"][rows, pr:pr + 1],
                                  pa[rows, :].rearrange("p (b e) -> p b e", b=4), ALU.mult, ALU.add)
                        P.dma(o_rets[:, h, :, :].rearrange("b d e -> d b e"), Sn[rows, :].rearrange("p (b e) -> p b e", b=16))
                if stage <= 4:
                    return
                P.op("dve", lambda e: e.reduce_sum(stt_[:, 24:28], v3(osb[:], 4), AX.X), [osb], [stt_])
                P.ts("dve", stt_[:, 24:28], stt_[:, 24:28], -1.0 / 128, None, ALU.mult)
                P.tt("dve", v3(ocb[:], 4), v3(osb[:], 4), bc_last(stt_[:, 24:28], 128), ALU.add)
                P.tt("dve", osb[:], ocb[:], ocb[:], ALU.mult)
                P.op("dve", lambda e: e.reduce_sum(stt_[:, 28:32], v3(osb[:], 4), AX.X), [osb], [stt_])
                rsqrt_small(stt_[:, 28:32], stt_[:, 28:32], 1.0 / 128, EPS)
                P.tt("dve", v3(ocb[:], 4), v3(ocb[:], 4), bc_last(stt_[:, 28:32], 128), ALU.mult)
                P.tt("dve", ocb[:], ocb[:], gnb[:], ALU.mult)
                P.actv(sz[:], proj[:, ZA:ZA + 512], AF.Silu)
                P.tt("dve", mix[:, 0:512], ocb[:], sz[:], ALU.mult)
                if stage <= 5:
                    return
                P.actv(gts[:], proj[:, GL:GL + 24], AF.Sigmoid)
                P.cp("act", n16[:, 0:896], rp[:, QN:QN + 896])
                P.cp("dve", n16[:, 896:1024], proj[:, VC:VC + 128])
                for i4 in range(4):
                    P.tr(psT[:, i4 * 128:(i4 + 1) * 128], n16[:, i4 * 128:(i4 + 1) * 128], ident[:])
                P.cp("act", qTz[0:64, 0:512], psT[0:64, 0:512])
                P.cp("dve", qTz[64:128, 512:1024], psT[64:128, 0:512])
                for i4 in range(4):
                    P.tr(psT[:, i4 * 128:(i4 + 1) * 128], n16[:, 512 + i4 * 128:512 + (i4 + 1) * 128], ident[:])
                if isp:
                    cT = caches["cT"]; vs1 = caches["vs1"]; vw1 = caches["vw1"]
                    P.cp("act", cT[:].rearrange("p (k n) -> p k n", k=4)[:, :, t * 128:(t + 1) * 128], psT[:, 0:512].rearrange("p (k n) -> p k n", k=4))
                    P.cp("dve", vs1[:].rearrange("p (c g w) -> p c g w", c=16, g=2)[:, t, :, 0:64], v3(proj[:, VS:VS + 128], 2))
                    P.cp("dve", vw1[:].rearrange("p (c g w) -> p c g w", c=16, g=2)[:, t, :, 0:64], v3(proj[:, VW:VW + 128], 2))
                    ckT = caches["ckT"]; cvx = caches["cvx"]
                    compress(cT, 0, cT, 3 * 2048, ckT, caches["cvT"], cvx, 97, max(0, 8 * t - 1), 8 * t + 6)
                    groups = []
                    for g in range(2):
                        qTg = qTz[:, g * 512:(g + 1) * 512].rearrange("p (j q) -> p j q", j=4)
                        selc = []
                        for c in range(t + 1):
                            bias = cs["CB"][:, :] if c == t else None
                            selc.append((cT[:, 2048 + c * 128:2048 + (c + 1) * 128], vs1[:, (c * 2 + g) * 65:(c * 2 + g + 1) * 65], 128,
                                         cs["E"][0:32, c * 128:(c + 1) * 128], bias))
                        winc = []
                        for c in range(max(0, t - 4), t + 1):
                            bias = cs["CB"][:, :] if c == t else (cs["AB"][:, :] if c == t - 4 else None)
                            winc.append((cT[:, 4096 + c * 128:4096 + (c + 1) * 128], vw1[:, (c * 2 + g) * 65:(c * 2 + g + 1) * 65], 128, None, bias))
                        groups.append(dict(qTg=qTg, cmp_k=ckT[:, 0:127], cmp_v=cvx[0:127, g * 97:(g + 1) * 97], sel_chunks=selc, win_chunks=winc,
                                           on_dst=lambda x, g=g: on[:, x * 512 + g * 256:x * 512 + (g + 1) * 256].rearrange("p (j d) -> p j d", j=4)))
                    cmpb = bc_mid(cs["CMPB"][0:127, t * 128:(t + 1) * 128], 4)
                    nsa_tile(128, groups, 32, cs["mulc_p"][:, t * 32:(t + 1) * 32], cs["addc_p"][:, t * 32:(t + 1) * 32], cmpb, False)
                else:
                    cTs = caches["cTs"]; vs1s = caches["vs1s"]
                    P.cp("act", cTs[:], psT[:, 0:512])
                    P.cp("dve", vs1s[:].rearrange("p (k g w) -> p k g w", k=2, g=2)[:, 0, :, 0:64], v3(proj[:, VS:VS + 128], 2))
                    P.cp("dve", vs1s[:].rearrange("p (k g w) -> p k g w", k=2, g=2)[:, 1, :, 0:64], v3(proj[:, VW:VW + 128], 2))
                    sample_nsa(caches)
                if stage <= 6:
                    return
                gv = gts[:].rearrange("p (h x) -> p h x", h=8)
                for x in range(3):
                    P.tt("dve", v3(on[:, x * 512:(x + 1) * 512], 8), v3(on[:, x * 512:(x + 1) * 512], 8), bc_last(gv[:, :, x], 64), ALU.mult)
                P.tt("dve", obt[:], on[:, 0:512], on[:, 512:1024], ALU.add)
                P.tt("dve", obt[:], obt[:], on[:, 1024:1536], ALU.add)
                P.actv(sz[:], proj[:, ZB:ZB + 512], AF.Silu)
                P.tt("dve", mix[:, 512:1024], obt[:], sz[:], ALU.mult)
                if stage <= 7:
                    return
                for kc in range(8):
                    P.tr(psT[:, kc * 128:(kc + 1) * 128], mix[:, kc * 128:(kc + 1) * 128], ident[:])
                P.cp("act", mixT[:], psT[:])
                for hf in range(2):
                    pa = psA[cnt["pa"] % 2]; cnt["pa"] += 1
                    for kc in range(8):
                        P.mm(pa[:, :], mixT[:, kc * 128:(kc + 1) * 128], Wob[:, kc * 1024 + hf * 512:kc * 1024 + (hf + 1) * 512], start=(kc == 0), stop=(kc == 7))
                    P.tt("dve", xtile[:, hf * 512:(hf + 1) * 512], xtile[:, hf * 512:(hf + 1) * 512], pa[:, :], ALU.add)
                ydst = yp[t * 128:(t + 1) * 128, :] if isp else ys
                P.dma(ydst, xtile[:], writes=[("y0", mode, t)])

            def compress(kc_t, kc_off, vc_t, vc_off, ckT, cvT, cvx, wv, n0, n1, rkeys=None):
                nn = n1 - n0 + 1
                rd = ([BDW] + rkeys) if rkeys else None
                for l in range(32):
                    a0 = kc_off + 16 * n0 + l
                    P.mm(psC[:, 0:nn], BDW[:, l * 128:(l + 1) * 128], kc_t[:, a0:a0 + 16 * (nn - 1) + 1:16], start=(l == 0), stop=(l == 31), reads=rd)
                for l in range(32):
                    a0 = vc_off + 16 * n0 + l
                    P.mm(psC[:, 128:128 + nn], BDW[:, (32 + l) * 128:(33 + l) * 128], vc_t[:, a0:a0 + 16 * (nn - 1) + 1:16], start=(l == 0), stop=(l == 31), reads=rd)
                P.ts("dve", ckT[:, n0:n1 + 1], psC[:, 0:nn], posk[:, 0:1], None, ALU.add)
                P.ts("dve", cvT[:, n0:n1 + 1], psC[:, 128:128 + nn], posk[:, 1:2], None, ALU.add)
                P.tr(psT[0:127, 0:128], cvT[:, 0:127], ident[:])
                P.cp("act", cvx[0:127, :].rearrange("p (g w) -> p g w", g=2)[:, :, 0:64], psT[0:127, 0:128].rearrange("p (g d) -> p g d", g=2))

            def sample_nsa(caches):
                C = caches
                cTq, kwTq, vs1q, vw1q, ckTq, cvxq = C["cTq"], C["kwTq"], C["vs1q"], C["vw1q"], C["ckTq"], C["cvxq"]
                cTs, vs1s, idx = C["cTs"], C["vs1s"], C["idx"]
                on8 = [osb, ocb, sz]
                wst32, w16 = C["wst32"], C["w16"]
                P.barrier()
                C["stR"].close()
                stPG = C["stPG"]
                pgb = [alloc(stPG, "pgb%d" % i, [128, 16 * 384], BF16) for i in range(2)]
                pg = [alloc(stPG, "pgf%d" % i, [128, 512]) for i in range(6)]
                for b in range(16):
                    pb = pgb[b % 2]
                    for i in range(16):
                        pgt = pg[(b * 16 + i) % 6]
                        col = b * 16 + i
                        P.dma(pgt[:], cache, reads=[cache, idx], q="pool",
                              fn=lambda e, pgt=pgt, col=col: e.indirect_dma_start(
                                  out=pgt[:, :], out_offset=None, in_=cache[:, :],
                                  in_offset=bass.IndirectOffsetOnAxis(ap=idx[:, col:col + 1], axis=0)))
                        P.cp("act", pb[:, i * 384:(i + 1) * 384], pgt[:, 0:384], writes=[("pgb", b % 2, i)])
                        P.cp("dve", vs1q[:].rearrange("p (c g w) -> p c g w", c=16, g=2)[:, i, :, 0:64], v3(pgt[:, 384:512], 2), writes=[("vs1q", i)])
                    psRb = psR[:].bitcast(BF16)
                    for i in range(16):
                        pdst = psT[:, 0:384] if i % 2 == 0 else psRb[:, 0:384]
                        pkey = psT if i % 2 == 0 else psR
                        for k in range(3):
                            P.tr(pdst[:, k * 128:(k + 1) * 128], pb[:, i * 384 + k * 128:i * 384 + (k + 1) * 128], ident[:],
                                 reads=[("pgb", b % 2, i), ident], writes=[pkey])
                        P.cp("act" if i % 2 == 0 else "dve", cTq[:].rearrange("p (k n) -> p k n", k=3)[:, :, i * 128:(i + 1) * 128],
                             pdst.rearrange("p (k n) -> p k n", k=3), reads=[pkey], writes=[("cTq", i)])
                    P.dma(wst32[:].rearrange("p (c w) -> p c w", c=4), cwin[b].rearrange("(c r) w -> r c w", r=128))
                    P.cp("dve", w16[:], wst32[:])
                    for c in range(4):
                        P.tr(psT[:, 512 + c * 128:512 + (c + 1) * 128], w16[:, c * 256:c * 256 + 128], ident[:])
                    P.cp("act", kwTq[:], psT[:, 512:1024])
                    for c in range(4):
                        P.cp("dve", vw1q[:, c * 130:(c + 1) * 130].rearrange("p (g w) -> p g w", g=2)[:, :, 0:64], v3(w16[:, c * 256 + 128:(c + 1) * 256], 2))
                    compress(cTq, 0, cTq, 2048, ckTq, C["cvTq"], cvxq, 98, 0, 126, rkeys=[("cTq", i) for i in range(16)])
                    groups = []
                    for g in range(2):
                        qTg = qTz[:, g * 512:(g + 1) * 512].rearrange("p (j q) -> p j q", j=4)[:, :, 8 * b:8 * b + 8]
                        sbias = cs["SB"][:, b * 8:(b + 1) * 8]
                        selc = []
                        for c in range(16):
                            selc.append((cTq[:, 4096 + c * 128:4096 + (c + 1) * 128], vs1q[:, (c * 2 + g) * 65:(c * 2 + g + 1) * 65], 128,
                                         cs["E"][0:33, c * 128:(c + 1) * 128], None, ("cTq", c), ("vs1q", c)))
                        selc.append((cTs[:, 128:256], vs1s[:, (0 * 2 + g) * 65:(0 * 2 + g + 1) * 65], 128, None, sbias))
                        winc = []
                        for c in range(4):
                            bias = cs["ABs"][:, :] if c == 0 else None
                            winc.append((kwTq[:, c * 128:(c + 1) * 128], vw1q[:, (c * 2 + g) * 65:(c * 2 + g + 1) * 65], 128, None, bias))
                        winc.append((cTs[:, 256:384], vs1s[:, (1 * 2 + g) * 65:(1 * 2 + g + 1) * 65], 128, None, sbias))
                        groups.append(dict(qTg=qTg, cmp_k=ckTq[:, 0:127], cmp_v=cvxq[0:127, g * 98:(g + 1) * 98], sel_chunks=selc, win_chunks=winc,
                                           on_dst=lambda x, g=g: on8[x][0:8, g * 256:(g + 1) * 256].rearrange("p (j d) -> p j d", j=4)))
                    qall = qTz[:, :].rearrange("p (g j q) -> p g j q", g=2, j=4)[:, :, :, 8 * b:8 * b + 8]
                    nsa_tile(8, groups, 33, cs["mulc_s"][:, :], cs["addc_s"][:, :], None, True, qall)
                    for x in range(3):
                        P.dma(on[b * 8:(b + 1) * 8, x * 512:(x + 1) * 512], on8[x][0:8, :])
                P.barrier()

            with ExitStack() as stP:
                Ap = lambda name, shape, dt=F32: alloc(stP, name, shape, dt)
                load_consts(stP, P_ONLY)
                cT = Ap("cT", [128, 4 * 2048], BF16)
                vs1 = Ap("vs1", [128, 16 * 2 * 65], BF16)
                vw1 = Ap("vw1", [128, 16 * 2 * 65], BF16)
                ckT = Ap("ckT", [128, 128], BF16)
                cvT = Ap("cvT", [128, 128], BF16)
                cvx = Ap("cvx", [128, 2 * 97], BF16)
                ov32 = Ap("ov32", [128, 32])
                P.memset("pool", cT[:], 0.0)
                P.memset("pool", vs1[:], 1.0)
                P.memset("pool", vw1[:], 1.0)
                P.memset("pool", cvx[:], 1.0)
                P.memset("pool", ckT[:], 0.0)
                P.memset("pool", cvT[:], 0.0)
                for g in range(2):
                    P.cp("dve", cvx[0:127, g * 97 + 65:(g + 1) * 97], cs["ovl_p"][0:127, :])
                caches = {"cT": cT, "vs1": vs1, "vw1": vw1, "ckT": ckT, "cvx": cvx, "cvT": cvT}
                for t in range(n_ptiles):
                    caches["next"] = ("p", t + 1) if t + 1 < n_ptiles else (("s", 0) if do_sample else None)
                    even_tile("p", t, caches)
                P.barrier()

            if do_sample:
                stS = ExitStack()
                stPG = ExitStack()
                scaches = {}

                def sample_hook():
                    P.barrier()
                    stWb.close()
                    As = lambda name, shape, dt=F32: alloc(stS, name, shape, dt)
                    load_consts(stS, S_ONLY)
                    C = scaches
                    C["cTq"] = As("cTq", [128, 3 * 2048], BF16)
                    C["kwTq"] = As("kwTq", [128, 512], BF16)
                    C["vs1q"] = As("vs1q", [128, 16 * 130], BF16)
                    C["vw1q"] = As("vw1q", [128, 4 * 130], BF16)
                    C["ckTq"] = As("ckTq", [128, 128], BF16)
                    C["cvTq"] = As("cvTq", [128, 128], BF16)
                    C["cvxq"] = As("cvxq", [128, 2 * 98], BF16)
                    C["cTs"] = As("cTs", [128, 512], BF16)
                    C["vs1s"] = As("vs1s", [128, 4 * 65], BF16)
                    C["wst32"] = As("wst32", [128, 1024])
                    C["w16"] = As("w16", [128, 1024], BF16)
                    C["idx"] = As("idx", [128, 256], I32)
                    pti = As("pti", [128, 256], I32)
                    stR = ExitStack()
                    C["stR"] = stR
                    C["stPG"] = stPG
                    Ar = lambda name, shape, dt=F32: alloc(stR, name, shape, dt)
                    C["S0"] = Ar("S0", [128, 4096])
                    C["S0b"] = Ar("S0b", [128, 4096], BF16)
                    C["qdTm"] = Ar("qdTm", [128, 16 * 256], BF16)
                    C["vblk"] = Ar("vblk", [128, 2048], BF16)
                    C["Sn"] = Ar("Sn", [128, 2048])
                    ptf = tmpa[:, 0:256]
                    P.memset("pool", C["vs1q"][:], 1.0)
                    P.memset("pool", C["vw1q"][:], 1.0)
                    P.memset("pool", C["vs1s"][:], 1.0)
                    P.memset("pool", C["cvxq"][:], 1.0)
                    for g in range(2):
                        P.cp("dve", C["cvxq"][0:127, g * 98 + 65:(g + 1) * 98], cs["ovl_s"][0:127, :])
                    for hp in range(2):
                        P.dma(C["S0"][hp * 64:(hp + 1) * 64, :].rearrange("p (b a e) -> p b a e", b=16, a=2),
                              sret[:, hp::2, :, :].rearrange("b a d e -> d b a e"))
                    P.cp("dve", C["S0b"][:], C["S0"][:])
                    P.dma(pti[:], ptab.partition_broadcast(128))
                    P.cp("dve", ptf, pti[:])
                    P.ts("dve", ptf, ptf, 128.0, cs["iota_p"][:, 0:1], ALU.mult, ALU.add)
                    P.cp("dve", C["idx"][:], ptf)

                scaches["hook"] = sample_hook
                even_tile("s", 0, scaches)
                P.barrier()
                stPG.close()
                stS.close()
            else:
                stWb.close()
            P.barrier()
        if phase_b:
            NT = 2176
            NCH = 272
            TWO_PI = 2.0 * math.pi
            with ExitStack() as stB:
                Bf = lambda name, shape, dt=F32: alloc(stB, name, shape, dt)
                load_consts(stB, ["ident", "JV", "KI", "PM4", "BM16"], pre="kb_")
                ident = cs["ident"]
                uT = Bf("uT", [128, 8 * NT], BF16)
                szT = Bf("szT", [128, 8 * NT], BF16)
                ngo = Bf("ngo", [128, 8])
                dsk = Bf("dsk", [128, 8])
                statb = Bf("statb", [128, 8])
                FSp = Bf("FSp", [128, 64])
                FSs = Bf("FSs", [128, 1024])
                P.dma(ngo[:], norm_o)
                P.dma(dsk[:], ssmd)
                cntb = {"x": 0, "pa": 0}

                def load_tile(dst, ti):
                    if ti < 16:
                        P.dma(dst, yp[ti * 128:(ti + 1) * 128, :], reads=[("y0", "p", ti)])
                    else:
                        P.dma(dst, ys, reads=[("y0", "s", 0)])

                with ExitStack() as st1:
                    B1 = lambda name, shape, dt=F32: alloc(st1, name, shape, dt)
                    Wodd = B1("Wodd", [128, 8 * 2048], BF16)
                    yt = [B1("yt%d" % i, [128, 1024]) for i in range(2)]
                    hb = B1("hb", [128, 1024], BF16)
                    hT = B1("hT", [128, 8 * 512], BF16)
                    with ExitStack() as stg_:
                        stg = [alloc(stg_, "stgb%d" % i, [128, 2048]) for i in range(2)]
                        for kc in range(8):
                            s_ = stg[kc % 2]
                            P.dma(s_[:], w_in_o[kc * 128:(kc + 1) * 128, :])
                            if kc % 2 == 0:
                                P.ts("dve", Wodd[:, kc * 2048:(kc + 1) * 2048], s_[:], ngo[:, kc:kc + 1], None, ALU.mult)
                            else:
                                P.op("act", lambda e, kc=kc, s_=s_: e.mul(Wodd[:, kc * 2048:(kc + 1) * 2048], s_[:], ngo[:, kc:kc + 1]), [s_, ngo], [Wodd])
                        P.barrier()
                    blocks = [(0, 4), (4, 8), (8, 12), (12, 16), (16, 17)]
                    for (t0, t1) in blocks:
                        nb = (t1 - t0) * 128
                        col0 = t0 * 128
                        for ti in range(t0, t1):
                            ytile = yt[cntb["x"] % 2]; cntb["x"] += 1
                            load_tile(ytile[:], ti)
                            P.memset("dve", statb[:, 0:1], 0.0)
                            P.actv(hb[:], ytile[:], AF.Square, accum_out=statb[:, 0:1])
                            P.ts("dve", statb[:, 1:2], statb[:, 0:1], 1.0 / 1024, EPS, ALU.mult, ALU.add)
                            P.actv(statb[:, 1:2], statb[:, 1:2], AF.Sqrt)
                            P.op("dve", lambda e: e.reciprocal(statb[:, 1:2], statb[:, 1:2]), [statb], [statb])
                            P.ts("dve", hb[:], ytile[:], statb[:, 1:2], None, ALU.mult)
                            for kc in range(8):
                                P.tr(psT[:, kc * 128:(kc + 1) * 128], hb[:, kc * 128:(kc + 1) * 128], ident[:])
                            lt = ti - t0
                            P.cp("act", hT[:].rearrange("p (k n) -> p k n", k=8)[:, :, lt * 128:(lt + 1) * 128], psT[:].rearrange("p (k n) -> p k n", k=8))
                        for oc in range(16):
                            pa = psA[cntb["pa"] % 2]; cntb["pa"] += 1
                            for kc in range(8):
                                P.mm(pa[:, 0:nb], Wodd[:, kc * 2048 + oc * 128:kc * 2048 + (oc + 1) * 128], hT[:, kc * 512:kc * 512 + nb], start=(kc == 0), stop=(kc == 7))
                            if oc < 8:
                                P.cp("dve", uT[:, oc * NT + col0:oc * NT + col0 + nb], pa[:, 0:nb])
                            else:
                                P.actv(szT[:, (oc - 8) * NT + col0:(oc - 8) * NT + col0 + nb], pa[:, 0:nb], AF.Silu)
                    P.barrier()

                with ExitStack() as st2:
                    B2 = lambda name, shape, dt=F32: alloc(st2, name, shape, dt)
                    lr = B2("lr", [128, 32]); li = B2("li", [128, 32]); ls = B2("ls", [128, 32])
                    bre = B2("bre", [128, 512]); bim = B2("bim", [128, 512])
                    cre = B2("cre", [128, 512]); cim = B2("cim", [128, 512])
                    x0r = B2("x0r", [128, 512]); x0i = B2("x0i", [128, 512])
                    for tl, src in ((lr, lamre_A), (li, lamim_A), (ls, lstep_A), (bre, bA_re), (bim, bA_im), (cre, cA_re), (cim, cA_im), (x0r, x0A_re), (x0i, x0A_im)):
                        P.dma(tl[:], src)
                    aa = B2("aa", [128, 32]); th = B2("th", [128, 32])
                    A9 = B2("A9", [128, 288]); T9 = B2("T9", [128, 288]); T9c = B2("T9c", [128, 288])
                    PR = B2("PR", [128, 288]); PI = B2("PI", [128, 288])
                    rri = B2("rri", [128, 1024], I32); rri2 = B2("rri2", [128, 1024], I32)
                    npi = B2("npi", [128, 1])

                    hpi = B2("hpi", [128, 1])
                    P.memset("dve", hpi[:], math.pi / 2)

                    def range_reduce(x, n):
                        P.ts("dve", rri[:, 0:n], x, 1.0 / TWO_PI, None, ALU.mult)
                        P.stt("dve", x, rri[:, 0:n], -TWO_PI, x, ALU.mult, ALU.add)

                    def sincos(x, n, sin_out, cos_out):
                        P.ts("dve", rri[:, 0:n], x, 1.0 / TWO_PI, None, ALU.mult)
                        P.stt("dve", sin_out, rri[:, 0:n], -TWO_PI, x, ALU.mult, ALU.add)
                        P.actv(sin_out, sin_out, AF.Sin)
                        P.ts("dve", rri2[:, 0:n], x, 1.0 / TWO_PI, 0.25, ALU.mult, ALU.add)
                        P.stt("dve", cos_out, rri2[:, 0:n], -TWO_PI, x, ALU.mult, ALU.add)
                        P.actv(cos_out, cos_out, AF.Sin, bias=hpi[:, 0:1])

                    P.actv(ls[:], ls[:], AF.Exp)
                    P.tt("dve", aa[:], lr[:], ls[:], ALU.mult)
                    P.tt("dve", th[:], li[:], ls[:], ALU.mult)
                    JV = cs["JV"]
                    P.tt("dve", A9[:].rearrange("p (j m) -> p j m", j=9), JV[:].rearrange("p (j m) -> p j m", j=9), bc_mid(aa[:, :], 9), ALU.mult)
                    P.actv(A9[:], A9[:], AF.Exp)
                    range_reduce(th[:, :], 32)
                    P.tt("dve", T9[:].rearrange("p (j m) -> p j m", j=9), JV[:].rearrange("p (j m) -> p j m", j=9), bc_mid(th[:, :], 9), ALU.mult)
                    sincos(T9[:, :], 288, PI[:, :], T9c[:, :])
                    P.tt("dve", PR[:], A9[:], T9c[:], ALU.mult)
                    P.tt("dve", PI[:], A9[:], PI[:], ALU.mult)
                    PRv = PR[:].rearrange("p (j m) -> p j m", j=9)
                    PIv = PI[:].rearrange("p (j m) -> p j m", j=9)
                    den = B2("den", [128, 32]); nr = B2("nr", [128, 32]); fre = B2("fre", [128, 32]); fim = B2("fim", [128, 32]); t32 = B2("t32", [128, 32])
                    P.tt("dve", den[:], lr[:], lr[:], ALU.mult)
                    P.tt("dve", t32[:], li[:], li[:], ALU.mult)
                    P.tt("dve", den[:], den[:], t32[:], ALU.add)
                    P.op("dve", lambda e: e.reciprocal(den[:], den[:]), [den], [den])
                    P.ts("dve", nr[:], PR[:, 32:64], -1.0, None, ALU.add)
                    P.tt("dve", fre[:], nr[:], lr[:], ALU.mult)
                    P.tt("dve", t32[:], PI[:, 32:64], li[:], ALU.mult)
                    P.tt("dve", fre[:], fre[:], t32[:], ALU.add)
                    P.tt("dve", fre[:], fre[:], den[:], ALU.mult)
                    P.tt("dve", fim[:], PI[:, 32:64], lr[:], ALU.mult)
                    P.tt("dve", t32[:], nr[:], li[:], ALU.mult)
                    P.tt("dve", fim[:], fim[:], t32[:], ALU.subtract)
                    P.tt("dve", fim[:], fim[:], den[:], ALU.mult)
                    Bre = B2("Bre", [128, 512]); Bim = B2("Bim", [128, 512]); t512 = B2("t512", [128, 512])
                    v16 = lambda ap: ap.rearrange("p (m c) -> p m c", c=16)
                    P.tt("dve", v16(Bre[:]), v16(bre[:]), bc_last(fre[:, :], 16), ALU.mult)
                    P.tt("dve", v16(t512[:]), v16(bim[:]), bc_last(fim[:, :], 16), ALU.mult)
                    P.tt("dve", Bre[:], Bre[:], t512[:], ALU.subtract)
                    P.tt("dve", v16(Bim[:]), v16(bim[:]), bc_last(fre[:, :], 16), ALU.mult)
                    P.tt("dve", v16(t512[:]), v16(bre[:]), bc_last(fim[:, :], 16), ALU.mult)
                    P.tt("dve", Bim[:], Bim[:], t512[:], ALU.add)
                    Cbd_re = B2("Cbd_re", [128, 32 * 32], BF16); Cbd_nim = B2("Cbd_nim", [128, 32 * 32], BF16)
                    P.memset("pool", Cbd_re[:], 0.0)
                    P.memset("pool", Cbd_nim[:], 0.0)
                    cbv = lambda t_: t_[:].rearrange("p (m c) -> p m c", c=32)
                    for hp in range(2):
                        rows = slice(hp * 64, (hp + 1) * 64)
                        P.cp("dve", cbv(Cbd_re)[rows, :, hp * 16:(hp + 1) * 16], v16(cre[:])[rows, :, :])
                        P.ts("dve", cbv(Cbd_nim)[rows, :, hp * 16:(hp + 1) * 16], v16(cim[:])[rows, :, :], -1.0, None, ALU.mult)
                    th8 = B2("th8", [128, 32])
                    P.ts("dve", th8[:], th[:], 8.0, None, ALU.mult)
                    range_reduce(th8[:, :], 32)
                    Xre = B2("Xre", [128, 512]); Xim = B2("Xim", [128, 512]); tX = B2("tX", [128, 512])
                    XBD = B2("XBD", [128, 2 * 8 * 4 * 32], BF16)
                    VZ = B2("VZ", [128, 8 * 4 * 2 * 128], BF16)
                    WS = B2("WS", [128, 8 * 2 * 128], BF16)
                    uzb = [B2("uz%d" % i, [128, NT], BF16) for i in range(2)]
                    BD = B2("BD", [128, 8 * 128], BF16)
                    Sin_r = B2("Sin_r", [128, 4 * NCH]); Sin_i = B2("Sin_i", [128, 4 * NCH])
                    cosT = B2("cosT", [128, 1024]); sinT = B2("sinT", [128, 1024])
                    c_r = B2("c_r", [128, 1024]); c_i = B2("c_i", [128, 1024]); t1k = B2("t1k", [128, 1024])
                    w_r = B2("w_r", [128, 1024]); w_i = B2("w_i", [128, 1024])
                    Sp_r = B2("Sp_r", [128, 4 * NCH], BF16); Sp_i = B2("Sp_i", [128, 4 * NCH], BF16)
                    yvb = [B2("yv%d" % i, [128, NCH]) for i in range(2)]; y2b = [B2("y2%d" % i, [128, NCH]) for i in range(2)]; y3b = [B2("y3%d" % i, [128, NCH]) for i in range(2)]
                    ygc = B2("ygc", [128, NT], BF16)
                    P.memset("pool", XBD[:], 0.0)
                    P.memset("pool", VZ[:], 0.0)
                    for c in range(8):
                        msl = slice(4 * c, 4 * c + 4)
                        x4 = lambda t_: t_[:].rearrange("p (t m c) -> p t m c", t=8, m=4)
                        prb = PRv[:, 0:8, msl].unsqueeze(3).to_broadcast([128, 8, 4, 16])
                        pib = PIv[:, 0:8, msl].unsqueeze(3).to_broadcast([128, 8, 4, 16])
                        brb = v16(Bre[:])[:, msl, :].unsqueeze(1).to_broadcast([128, 8, 4, 16])
                        bib = v16(Bim[:])[:, msl, :].unsqueeze(1).to_broadcast([128, 8, 4, 16])
                        P.tt("dve", x4(Xre), prb, brb, ALU.mult)
                        P.tt("dve", x4(tX), pib, bib, ALU.mult)
                        P.tt("dve", Xre[:], Xre[:], tX[:], ALU.subtract)
                        P.tt("dve", x4(Xim), prb, bib, ALU.mult)
                        P.tt("dve", x4(tX), pib, brb, ALU.mult)
                        P.tt("dve", Xim[:], Xim[:], tX[:], ALU.add)
                        xbv = XBD[:].rearrange("p (r t m c) -> p r t m c", r=2, t=8, m=4)
                        for hp in range(2):
                            rows = slice(hp * 64, (hp + 1) * 64)
                            P.cp("dve", xbv[rows, 0, :, :, hp * 16:(hp + 1) * 16], x4(Xre)[rows])
                            P.cp("pool", xbv[rows, 1, :, :, hp * 16:(hp + 1) * 16], x4(Xim)[rows])
                        prb = PRv[:, 1:9, msl].unsqueeze(3).to_broadcast([128, 8, 4, 16])
                        pib = PIv[:, 1:9, msl].unsqueeze(3).to_broadcast([128, 8, 4, 16])
                        crb = v16(cre[:])[:, msl, :].unsqueeze(1).to_broadcast([128, 8, 4, 16])
                        cib = v16(cim[:])[:, msl, :].unsqueeze(1).to_broadcast([128, 8, 4, 16])
                        P.tt("dve", x4(Xre), prb, crb, ALU.mult)
                        P.tt("dve", x4(tX), pib, cib, ALU.mult)
                        P.tt("dve", Xre[:], Xre[:], tX[:], ALU.subtract)
                        P.tt("dve", x4(Xim), pib, crb, ALU.mult)
                        P.tt("dve", x4(tX), prb, cib, ALU.mult)
                        P.stt("dve", Xim[:], Xim[:], -1.0, tX[:], ALU.mult, ALU.subtract)
                        vzv = VZ[:].rearrange("p (t m r n) -> p t m r n", t=8, m=4, r=2)
                        for hp in range(2):
                            rows = slice(hp * 64, (hp + 1) * 64)
                            for m4 in range(4):
                                c0_ = 32 * m4 + 16 * hp
                                P.cp("dve", vzv[rows, :, m4, 0, c0_:c0_ + 16], x4(Xre)[rows, :, m4, :])
                                P.cp("pool", vzv[rows, :, m4, 1, c0_:c0_ + 16], x4(Xim)[rows, :, m4, :])
                        for th2 in range(2):
                            for tl_ in range(4):
                                tau = th2 * 4 + tl_
                                for ri in range(2):
                                    P.tr(psT[:, (tl_ * 2 + ri) * 128:(tl_ * 2 + ri + 1) * 128], XBD[:, (ri * 8 + tau) * 128:(ri * 8 + tau + 1) * 128], ident[:])
                            P.cp("act", WS[:, th2 * 1024:(th2 + 1) * 1024], psT[:])
                        for th2 in range(2):
                            for tl_ in range(4):
                                tau = th2 * 4 + tl_
                                outp = psC[:, tl_ * 128:(tl_ + 1) * 128]
                                P.mm(outp, XBD[:, (0 * 8 + tau) * 128:(0 * 8 + tau + 1) * 128], Cbd_re[:, c * 128:(c + 1) * 128], start=True, stop=False)
                                P.mm(outp, XBD[:, (1 * 8 + tau) * 128:(1 * 8 + tau + 1) * 128], Cbd_nim[:, c * 128:(c + 1) * 128], start=False, stop=True)
                            P.tt("dve", BD[:, th2 * 512:(th2 + 1) * 512].rearrange("p (t n) -> p t n", t=4), psC[:, 0:512].rearrange("p (t n) -> p t n", t=4),
                                 bc_mid(cs["BM16"][:, :], 4), ALU.mult)
                        uc = uT[:, c * NT:(c + 1) * NT]
                        for m4 in range(4):
                            uz = uzb[m4 % 2]
                            P.op("act", lambda e, uz=uz, uc=uc, m4=m4: e.mul(uz[:], uc, cs["PM4"][:, m4:m4 + 1]), [uT, cs["PM4"]], [uz])
                            for ri, dst in ((0, Sin_r), (1, Sin_i)):
                                pa = psA[ri]
                                for s_ in range(8):
                                    tau = 7 - s_
                                    P.mm(pa[:, 0:NCH], WS[:, (tau * 2 + ri) * 128:(tau * 2 + ri + 1) * 128], uz[:, s_:NT:8], start=(s_ == 0), stop=(s_ == 7))
                                if ri == 0:
                                    P.cp("act", dst[:, m4 * NCH:(m4 + 1) * NCH], pa[:, 0:NCH])
                                else:
                                    P.cp("dve", dst[:, m4 * NCH:(m4 + 1) * NCH], pa[:, 0:NCH])
                        a3 = lambda t_: t_[:].rearrange("p (m k) -> p m k", m=4)
                        P.tt("dve", a3(t1k), bc_last(th8[:, msl], 256), bc_mid(cs["KI"][:, :], 4), ALU.mult)
                        sincos(t1k[:, :], 1024, sinT[:, :], cosT[:, :])
                        s3 = lambda t_: t_[:].rearrange("p (m k) -> p m k", m=4)[:, :, 0:256]
                        P.tt("dve", a3(c_r), a3(cosT), s3(Sin_r), ALU.mult)
                        P.tt("dve", a3(t1k), a3(sinT), s3(Sin_i), ALU.mult)
                        P.tt("dve", c_r[:], c_r[:], t1k[:], ALU.add)
                        P.tt("dve", a3(c_i), a3(cosT), s3(Sin_i), ALU.mult)
                        P.tt("dve", a3(t1k), a3(sinT), s3(Sin_r), ALU.mult)
                        P.tt("dve", c_i[:], c_i[:], t1k[:], ALU.subtract)
                        for m4 in range(4):
                            m = 4 * c + m4
                            r8b = A9[:, 8 * 32 + m:8 * 32 + m + 1].to_broadcast([128, 256])
                            for tl_, wo_ in ((c_r, w_r), (c_i, w_i)):
                                seg = tl_[:, m4 * 256:(m4 + 1) * 256]
                                oseg = wo_[:, m4 * 256:(m4 + 1) * 256]
                                P.op("dve", lambda e, seg=seg, oseg=oseg, r8b=r8b: e.tensor_tensor_scan(oseg, r8b, seg, 0.0, ALU.mult, ALU.add), [tl_, A9], [wo_])
                        P.tt("dve", t1k[:], cosT[:], w_r[:], ALU.mult)
                        P.tt("pool", c_r[:], sinT[:], w_i[:], ALU.mult)
                        P.tt("dve", t1k[:], t1k[:], c_r[:], ALU.subtract)
                        P.tt("pool", c_i[:], cosT[:], w_i[:], ALU.mult)
                        P.tt("dve", c_r[:], sinT[:], w_r[:], ALU.mult)
                        P.tt("dve", c_i[:], c_i[:], c_r[:], ALU.add)
                        spr = Sp_r[:].rearrange("p (m k) -> p m k", m=4)
                        spi = Sp_i[:].rearrange("p (m k) -> p m k", m=4)
                        P.memset("pool", spr[:, :, 0:1], 0.0)
                        P.memset("pool", spi[:, :, 0:1], 0.0)
                        P.cp("dve", spr[:, :, 1:256], a3(t1k)[:, :, 0:255])
                        P.cp("pool", spi[:, :, 1:256], a3(c_i)[:, :, 0:255])
                        x0rv = x0r[:].rearrange("p (m b) -> p m b", b=16)[:, msl, :]
                        x0iv = x0i[:].rearrange("p (m b) -> p m b", b=16)[:, msl, :]
                        P.cp("dve", spr[:, :, 256:272], x0rv)
                        P.cp("pool", spi[:, :, 256:272], x0iv)
                        P.cp("dve", FSp[:, 4 * c:4 * c + 4], a3(t1k)[:, :, 255])
                        P.cp("dve", FSp[:, 32 + 4 * c:32 + 4 * c + 4], a3(c_i)[:, :, 255])
                        fsr = FSs[:, 0:512].rearrange("p (m b) -> p m b", b=16)[:, msl, :]
                        fsi = FSs[:, 512:1024].rearrange("p (m b) -> p m b", b=16)[:, msl, :]
                        p8 = bc_last(PR[:, 8 * 32 + 4 * c:8 * 32 + 4 * c + 4], 16)
                        i8_ = bc_last(PI[:, 8 * 32 + 4 * c:8 * 32 + 4 * c + 4], 16)
                        sir = Sin_r[:].rearrange("p (m k) -> p m k", m=4)[:, :, 256:272]
                        sii = Sin_i[:].rearrange("p (m k) -> p m k", m=4)[:, :, 256:272]
                        tq = tX[:, 0:64].rearrange("p (m b) -> p m b", b=16)
                        P.tt("dve", fsr, p8, x0rv, ALU.mult)
                        P.tt("dve", tq, i8_, x0iv, ALU.mult)
                        P.tt("dve", fsr, fsr, tq, ALU.subtract)
                        P.tt("dve", fsr, fsr, sir, ALU.add)
                        P.tt("dve", fsi, p8, x0iv, ALU.mult)
                        P.tt("dve", tq, i8_, x0rv, ALU.mult)
                        P.tt("dve", fsi, fsi, tq, ALU.add)
                        P.tt("dve", fsi, fsi, sii, ALU.add)
                        for j in range(8):
                            acc = psS[j % 2]
                            for s_ in range(j + 1):
                                P.mm(acc[:, 0:NCH], BD[:, (j - s_) * 128:(j - s_ + 1) * 128], uc[:, s_:NT:8], start=(s_ == 0), stop=False)
                            for m4 in range(4):
                                for ri, spt in ((0, Sp_r), (1, Sp_i)):
                                    last = (m4 == 3 and ri == 1)
                                    P.mm(acc[:, 0:NCH], VZ[:, ((j * 4 + m4) * 2 + ri) * 128:((j * 4 + m4) * 2 + ri + 1) * 128],
                                         spt[:, m4 * NCH:(m4 + 1) * NCH], start=False, stop=last)
                            yv, y2, y3 = yvb[j % 2], y2b[j % 2], y3b[j % 2]
                            P.stt("dve", yv[:], uc[:, j:NT:8], dsk[:, c:c + 1], acc[:, 0:NCH], ALU.mult, ALU.add)
                            P.actv(y2[:], yv[:], AF.Square)
                            P.ts("dve", y2[:], y2[:], 0.044715, 1.0, ALU.mult, ALU.add)
                            P.tt("dve", y2[:], y2[:], yv[:], ALU.mult)
                            P.actv(y3[:], y2[:], AF.Sigmoid, scale=1.5957691216057308)
                            P.tt("dve", ygc[:, j:NT:8], yv[:], y3[:], ALU.mult)
                        P.cp("pool", uT[:, c * NT:(c + 1) * NT], ygc[:])
                    P.dma(o_ssp.rearrange("r p m -> p r m"), FSp[:].rearrange("p (r m) -> p r m", r=2))
                    P.dma(o_sss.rearrange("r p x -> p r x"), FSs[:].rearrange("p (r x) -> p r x", r=2))
                    P.barrier()

                with ExitStack() as st3:
                    B3 = lambda name, shape, dt=F32: alloc(st3, name, shape, dt)
                    W1 = B3("W1", [128, 8 * 1024], BF16)
                    W2 = B3("W2", [128, 8 * 1024], BF16)
                    Wo2 = B3("Wo2", [128, 8 * 1024], BF16)
                    yt = [B3("ytc%d" % i, [128, 1024]) for i in range(2)]
                    oT = B3("oT", [128, 8 * 512], BF16)
                    sg = B3("sg", [128, 512]); tg = B3("tg", [128, 512])
                    with ExitStack() as stg_:
                        stg = [alloc(stg_, "stgc%d" % i, [128, 1024]) for i in range(2)]
                        k_ = 0
                        for (Wd, src) in ((W1, glu1), (W2, glu2), (Wo2, w_out_o)):
                            for kc in range(8):
                                s_ = stg[k_ % 2]; k_ += 1
                                P.dma(s_[:], src[kc * 128:(kc + 1) * 128, :])
                                P.cp("dve" if kc % 2 == 0 else "act", Wd[:, kc * 1024:(kc + 1) * 1024], s_[:])
                        P.barrier()
                    for (t0, t1) in blocks:
                        nb = (t1 - t0) * 128
                        col0 = t0 * 128
                        for fc in range(8):
                            p1 = psA[0]; p2 = psA[1]
                            for kc in range(8):
                                P.mm(p1[:, 0:nb], W1[:, kc * 1024 + fc * 128:kc * 1024 + (fc + 1) * 128], uT[:, kc * NT + col0:kc * NT + col0 + nb], start=(kc == 0), stop=(kc == 7))
                            for kc in range(8):
                                P.mm(p2[:, 0:nb], W2[:, kc * 1024 + fc * 128:kc * 1024 + (fc + 1) * 128], uT[:, kc * NT + col0:kc * NT + col0 + nb], start=(kc == 0), stop=(kc == 7))
                            P.actv(sg[:, 0:nb], p2[:, 0:nb], AF.Sigmoid)
                            P.tt("dve", tg[:, 0:nb], p1[:, 0:nb], sg[:, 0:nb], ALU.mult)
                            P.tt("pool", oT[:, fc * 512:fc * 512 + nb], tg[:, 0:nb], szT[:, fc * NT + col0:fc * NT + col0 + nb], ALU.mult)
                        for ti in range(t0, t1):
                            lt = ti - t0
                            ytile = yt[cntb["x"] % 2]; cntb["x"] += 1
                            load_tile(ytile[:], ti)
                            for hf in range(2):
                                pa = psS[hf]
                                for fc in range(8):
                                    P.mm(pa[:, :], oT[:, fc * 512 + lt * 128:fc * 512 + (lt + 1) * 128], Wo2[:, fc * 1024 + hf * 512:fc * 1024 + (hf + 1) * 512], start=(fc == 0), stop=(fc == 7))
                                P.tt("dve", ytile[:, hf * 512:(hf + 1) * 512], ytile[:, hf * 512:(hf + 1) * 512], pa[:, :], ALU.add)
                            if ti < 16:
                                P.dma(yp[ti * 128:(ti + 1) * 128, :], ytile[:], reads=[ytile, ("y0", "p", ti)], writes=[("y0", "p", ti)])
                            else:
                                P.dma(ys, ytile[:], reads=[ytile, ("y0", "s", 0)], writes=[("y0", "s", 0)])
                    P.barrier()
        P.barrier()
        P.flush()
    return nc, in_names


_PROG_CACHE = {}


def _shared_inputs(inputs, consts):
    perm = _perm_even()
    sh = {}
    sh["w_in_e"] = np.ascontiguousarray(inputs["w_in_even"][0][:, perm])
    sh["w_out_e"] = np.ascontiguousarray(inputs["w_out_even"][0])
    sh["norm_e"] = np.ascontiguousarray(inputs["norm_even"][0].reshape(8, 128).T)
    sh["gn_gain"] = np.ascontiguousarray(inputs["ret_gn_gain"][0].reshape(1, 512))
    qn = inputs["nsa_q_norm"][0]
    kn = inputs["nsa_k_norm"][0]
    sh["qk_gain"] = np.concatenate([np.tile(qn, 8), np.tile(kn[0], 2), np.tile(kn[1], 2), np.tile(kn[2], 2)]).reshape(1, 896).astype(np.float32)
    sh["cmp_posT"] = np.ascontiguousarray(inputs["nsa_cmp_pos"][0].transpose(2, 0, 1))
    sh["cmp_w"] = np.ascontiguousarray(inputs["nsa_cmp_w"][0])
    for k, v in consts.items():
        sh["c_" + k] = v
    ca = np.ascontiguousarray
    sh["w_in_o"] = ca(inputs["w_in_odd"][0])
    sh["glu1"] = ca(inputs["glu_w1"][0])
    sh["glu2"] = ca(inputs["glu_w2"][0])
    sh["w_out_o"] = ca(inputs["w_out_odd"][0])
    sh["norm_o"] = ca(inputs["norm_odd"][0].reshape(8, 128).T)
    sh["ssmd"] = ca(inputs["ssm_d"][0].reshape(8, 128).T)
    sh["lamre_A"] = ca(inputs["ssm_lambda_re"][0].reshape(32, 2, 64).transpose(1, 2, 0).reshape(128, 32))
    sh["lamim_A"] = ca(inputs["ssm_lambda_im"][0].reshape(32, 2, 64).transpose(1, 2, 0).reshape(128, 32))
    ls = np.broadcast_to(inputs["ssm_log_step"][0].reshape(32, 2, 1), (32, 2, 64))
    sh["lstep_A"] = ca(ls.transpose(1, 2, 0).reshape(128, 32)).astype(np.float32)
    sh["bA_re"] = ca(inputs["ssm_b_re"][0].reshape(32, 2, 64, 16).transpose(1, 2, 0, 3).reshape(128, 512))
    sh["bA_im"] = ca(inputs["ssm_b_im"][0].reshape(32, 2, 64, 16).transpose(1, 2, 0, 3).reshape(128, 512))
    sh["cA_re"] = ca(inputs["ssm_c_re"][0].reshape(32, 2, 16, 64).transpose(1, 3, 0, 2).reshape(128, 512))
    sh["cA_im"] = ca(inputs["ssm_c_im"][0].reshape(32, 2, 16, 64).transpose(1, 3, 0, 2).reshape(128, 512))
    return sh


def _core_inputs(inputs, c, sh, cache_rows):
    im = dict(sh)
    im["xp"] = np.ascontiguousarray(inputs["x_prompt"][c])
    im["xs"] = np.ascontiguousarray(inputs["x_sample"][16 * c:16 * c + 16].reshape(128, 1024))
    im["cache"] = inputs["cache_nsa_kv"][0].reshape(2560 * 128, 512)[0:cache_rows]
    im["cwin"] = np.ascontiguousarray(inputs["cache_nsa_win"][0, 16 * c:16 * c + 16].reshape(16, 512, 256))
    im["sret"] = np.ascontiguousarray(inputs["state_ret"][0, 16 * c:16 * c + 16])
    im["x0A_re"] = np.ascontiguousarray(inputs["state_ssm_re"][0, 16 * c:16 * c + 16].reshape(16, 32, 2, 64).transpose(2, 3, 1, 0).reshape(128, 512))
    im["x0A_im"] = np.ascontiguousarray(inputs["state_ssm_im"][0, 16 * c:16 * c + 16].reshape(16, 32, 2, 64).transpose(2, 3, 1, 0).reshape(128, 512))
    im["ptab"] = np.ascontiguousarray(inputs["page_table"][16 * c:16 * c + 16].reshape(1, 256)).astype(np.int32)
    return im


def run_cores(inputs, cores, trace=False, **opts):
    consts = make_consts()
    key = tuple(sorted(opts.items()))
    nc, in_names = build_program(consts, **opts)
    sh = _shared_inputs(inputs, consts)
    cache_rows = 2560 * 128 if opts.get("do_sample", True) else 128
    in_maps = [_core_inputs(inputs, c, sh, cache_rows) for c in cores]
    in_maps = [{k: m[k] for k in in_names} for m in in_maps]
    if trace:
        res = run_bass_kernel_spmd(nc, in_maps, core_ids=list(range(len(cores))), trace=True)
        print("EXEC_TIME_NS", res.exec_time_ns)
        return res.results
    res = run_bass_kernel_spmd(nc, in_maps, core_ids=list(range(len(cores))))
    return res.results


def kernel(**inputs):
    inputs = {k: np.asarray(v) for k, v in inputs.items()}
    res = run_cores(inputs, list(range(NCORES)))
    f32 = np.float32
    y_p = np.zeros((8, 2048, 1024), f32)
    y_s = np.zeros((128, 8, 1024), f32)
    ret_p = np.zeros((1, 8, 4, 64, 128), f32)
    ret_s = np.zeros((1, 128, 4, 64, 128), f32)
    kv_p = np.zeros((1, 8, 2048, 4, 2, 64), f32)
    kv_s = np.zeros((1, 128, 8, 4, 2, 64), f32)
    win_p = np.zeros((1, 8, 512, 2, 2, 64), f32)
    win_s = np.zeros((1, 128, 512, 2, 2, 64), f32)
    sre_p = np.zeros((1, 8, 64, 64), f32)
    sim_p = np.zeros((1, 8, 64, 64), f32)
    sre_s = np.zeros((1, 128, 64, 64), f32)
    sim_s = np.zeros((1, 128, 64, 64), f32)
    for c in range(NCORES):
        r = res[c]
        sl = slice(16 * c, 16 * c + 16)
        y_p[c] = r["yp"]
        y_s[sl] = r["ys"].reshape(16, 8, 1024)
        ret_p[0, c] = r["o_retp"]
        ret_s[0, sl] = r["o_rets"]
        kv_p[0, c] = r["o_kvp"].reshape(2048, 4, 2, 64)
        kv_s[0, sl] = r["o_kvs"].reshape(16, 8, 4, 2, 64)
        win_p[0, c] = r["o_winp"].reshape(512, 2, 2, 64)
        win_s[0, sl] = r["o_wins"].reshape(16, 512, 2, 2, 64)
        if "o_ssp" in r:
            sp = r["o_ssp"].reshape(2, 2, 64, 32).transpose(0, 3, 1, 2).reshape(2, 64, 64)
            sre_p[0, c] = sp[0]
            sim_p[0, c] = sp[1]
            ss = r["o_sss"].reshape(2, 2, 64, 32, 16).transpose(0, 4, 3, 1, 2).reshape(2, 16, 64, 64)
            sre_s[0, sl] = ss[0]
            sim_s[0, sl] = ss[1]
    return (y_p, y_s, ret_p, ret_s, kv_p, kv_s, win_p, win_s, sre_p, sim_p, sre_s, sim_s)
```

```python
import numpy as np
import concourse.bass as bass
import concourse.mybir as mybir
from concourse.bass_utils import run_bass_kernel_spmd

F32 = mybir.dt.float32
BF16 = mybir.dt.bfloat16
I32 = mybir.dt.int32
AF = mybir.ActivationFunctionType
ALU = mybir.AluOpType
AX = mybir.AxisListType

ENGS = ("pe", "act", "dve", "pool", "sp")
NDMASEM = 24
NSWSEM = 8


class Prog:
    def __init__(self, nc):
        self.nc = nc
        self.ops = {e: [] for e in ENGS}
        self.cnt = {e: 0 for e in ENGS}
        self.sem = {}
        self.seen = {e: {} for e in ENGS}
        self.last_w = {}
        self.readers = {}
        self.dma_i = 0
        self.dma_uses = [0] * NDMASEM
        self.dma_tok = [None] * NDMASEM
        self.sw_i = 0
        self.sw_uses = [0] * NSWSEM
        self.sw_tok = [None] * NSWSEM
        self.stack = None
        self.n_ops = 0

    def setup(self, stack):
        self.stack = stack
        for e in ENGS:
            self.sem[e] = stack.enter_context(self.nc.semaphore("s_" + e))
        for i in range(NDMASEM):
            self.sem["d%d" % i] = stack.enter_context(self.nc.semaphore("s_d%d" % i))
        for i in range(NSWSEM):
            self.sem["w%d" % i] = stack.enter_context(self.nc.semaphore("s_w%d" % i))

    def _key(self, a):
        if isinstance(a, str):
            return a
        if isinstance(a, tuple):
            return a
        t = getattr(a, 'tensor', None)
        return t.name if t is not None else a.name

    def _deps(self, eng, reads, writes):
        toks = []
        for k in reads:
            k = self._key(k)
            t = self.last_w.get(k)
            if t is not None:
                toks.append(t)
        for k in writes:
            k = self._key(k)
            t = self.last_w.get(k)
            if t is not None:
                toks.append(t)
            toks.extend(self.readers.get(k, ()))
        need = {}
        for (s, v) in toks:
            if eng == "pe" and s == "pe":
                continue
            if v > need.get(s, 0):
                need[s] = v
        waits = []
        seen = self.seen[eng]
        for s, v in need.items():
            if seen.get(s, 0) >= v:
                continue
            seen[s] = v
            waits.append((s, v))
        return waits

    def _commit(self, tok, reads, writes):
        for k in writes:
            k = self._key(k)
            self.last_w[k] = tok
            self.readers[k] = []
        for k in reads:
            k = self._key(k)
            self.readers.setdefault(k, []).append(tok)

    def op(self, eng, fn, reads=(), writes=()):
        waits = self._deps(eng, reads, writes)
        self.cnt[eng] += 1
        tok = (eng, self.cnt[eng])
        self.ops[eng].append((waits, fn, (eng, 1)))
        self._commit(tok, reads, writes)
        self.n_ops += 1

    def dma(self, out, in_, reads=None, writes=None, q="sp", fn=None):
        if reads is None:
            reads = [in_]
        if writes is None:
            writes = [out]
        if q == "pool":
            i = self.sw_i % NSWSEM
            self.sw_i += 1
            sname = "w%d" % i
            uses, toks = self.sw_uses, self.sw_tok
        else:
            i = self.dma_i % NDMASEM
            self.dma_i += 1
            sname = "d%d" % i
            uses, toks = self.dma_uses, self.dma_tok
        waits = self._deps(q, reads, writes)
        prev = toks[i]
        if prev is not None and self.seen[q].get(sname, 0) < prev[1]:
            self.seen[q][sname] = prev[1]
            waits.append(prev)
        uses[i] += 1
        tok = (sname, 16 * uses[i])
        toks[i] = tok
        if fn is None:
            fn = lambda e, o=out, a=in_: e.dma_start(out=o, in_=a)
        self.ops[q].append((waits, fn, (sname, 16)))
        self._commit(tok, reads, writes)
        self.n_ops += 1

    def barrier(self):
        for e in ENGS:
            waits = []
            for e2 in ENGS:
                if e2 == e:
                    continue
                v = self.cnt[e2]
                if v > self.seen[e].get(e2, 0):
                    self.seen[e][e2] = v
                    waits.append((e2, v))
            for t in list(self.dma_tok) + list(self.sw_tok):
                if t is not None and self.seen[e].get(t[0], 0) < t[1]:
                    self.seen[e][t[0]] = t[1]
                    waits.append(t)
            if waits:
                self.ops[e].append((waits, None, None))

    def flush(self):
        nc = self.nc
        ops = self.ops
        sem = self.sem

        def replay(engh, lst):
            for (waits, fn, inc) in lst:
                for (s, v) in waits:
                    engh.wait_ge(sem[s], v)
                if fn is not None:
                    ins = fn(engh)
                    ins.then_inc(sem[inc[0]], inc[1])

        with nc.Block() as block:
            @block.tensor
            def _(e):
                replay(e, ops["pe"])

            @block.scalar
            def _(e):
                replay(e, ops["act"])

            @block.vector
            def _(e):
                replay(e, ops["dve"])

            @block.gpsimd
            def _(e):
                replay(e, ops["pool"])

            @block.sync
            def _(e):
                replay(e, ops["sp"])
        self.ops = {e: [] for e in ENGS}

    def mm(self, out, lhsT, rhs, start=True, stop=True, reads=None, writes=None):
        if reads is None:
            reads = [lhsT, rhs]
        if writes is None:
            writes = [out]
        self.op("pe", lambda e: e.matmul(out, lhsT, rhs, start=start, stop=stop), reads, writes)

    def tr(self, out, in_, ident, reads=None, writes=None):
        if reads is None:
            reads = [in_, ident]
        if writes is None:
            writes = [out]
        self.op("pe", lambda e: e.transpose(out, in_, ident), reads, writes)

    def actv(self, out, in_, func, bias=None, scale=None, accum_out=None, reads=None, writes=None, eng="act"):
        kw = {}
        if bias is not None:
            kw["bias"] = bias
        if scale is not None:
            kw["scale"] = scale
        if accum_out is not None:
            kw["accum_out"] = accum_out
        if reads is None:
            reads = [in_]
            if bias is not None and not isinstance(bias, (int, float)):
                reads.append(bias)
            if scale is not None and not isinstance(scale, (int, float)):
                reads.append(scale)
        if writes is None:
            writes = [out]
            if accum_out is not None:
                writes.append(accum_out)
        self.op("act", lambda e: e.activation(out, in_, func, **kw), reads, writes)

    def ts(self, eng, out, in0, s1, s2, op0, op1=None, accum_out=None, reads=None, writes=None):
        kw = {}
        if op1 is not None:
            kw["op1"] = op1
        if accum_out is not None:
            kw["accum_out"] = accum_out
        if reads is None:
            reads = [in0]
            for s in (s1, s2):
                if s is not None and not isinstance(s, (int, float)):
                    reads.append(s)
        if writes is None:
            writes = [out]
            if accum_out is not None:
                writes.append(accum_out)
        self.op(eng, lambda e: e.tensor_scalar(out, in0, s1, s2, op0, **kw), reads, writes)

    def tt(self, eng, out, in0, in1, op, reads=None, writes=None):
        if reads is None:
            reads = [in0, in1]
        if writes is None:
            writes = [out]
        self.op(eng, lambda e: e.tensor_tensor(out, in0, in1, op), reads, writes)

    def stt(self, eng, out, in0, scalar, in1, op0, op1, accum_out=None, reads=None, writes=None):
        kw = {}
        if accum_out is not None:
            kw["accum_out"] = accum_out
        if reads is None:
            reads = [in0, in1]
            if not isinstance(scalar, (int, float)):
                reads.append(scalar)
        if writes is None:
            writes = [out]
            if accum_out is not None:
                writes.append(accum_out)
        self.op(eng, lambda e: e.scalar_tensor_tensor(out, in0, scalar, in1, op0, op1, **kw), reads, writes)

    def cp(self, eng, out, in_, reads=None, writes=None):
        if reads is None:
            reads = [in_]
        if writes is None:
            writes = [out]
        if eng == "act":
            self.op(eng, lambda e: e.copy(out, in_), reads, writes)
        else:
            self.op(eng, lambda e: e.tensor_copy(out, in_), reads, writes)

    def memset(self, eng, ap, val, writes=None):
        if writes is None:
            writes = [ap]
        self.op(eng, lambda e: e.memset(ap, val), (), writes)

import math
from contextlib import ExitStack
import ml_dtypes

BF = ml_dtypes.bfloat16
NCORES = 8
BIG = 1.0e4
FORCE = 1.0e4
EPS = 1e-6
SCALE = 0.125
QA, KA, QN, KC, KS, KW, VC, VS, VW, GL, VA, ZA, ZB = 0, 256, 512, 1024, 1152, 1280, 1408, 1536, 1664, 1792, 1816, 2328, 2840
EIN = 3352
GROUPS = [(0, 512), (512, 1024), (1024, 1408), (1408, 1816), (1816, 2328), (2328, 2840), (2840, 3352)]


def _perm_even():
    perm = np.zeros(EIN, np.int64)
    perm[QA:QA + 256] = np.arange(0, 256)
    perm[KA:KA + 256] = np.arange(256, 512)
    o_va, o_za, o_qn, o_kvb, o_gl, o_zb = 512, 1024, 1536, 2048, 2816, 2840
    for j in range(4):
        for g in range(2):
            for dd in range(64):
                perm[QN + (j * 2 + g) * 64 + dd] = o_qn + (4 * g + j) * 64 + dd
    for dst, kind in ((KC, 0), (KS, 2), (KW, 4), (VC, 1), (VS, 3), (VW, 5)):
        perm[dst:dst + 128] = o_kvb + kind * 128 + np.arange(128)
    perm[GL:GL + 24] = o_gl + np.arange(24)
    perm[VA:VA + 512] = o_va + np.arange(512)
    perm[ZA:ZA + 512] = o_za + np.arange(512)
    perm[ZB:ZB + 512] = o_zb + np.arange(512)
    assert len(set(perm.tolist())) == EIN
    return perm


def make_consts():
    c = {}
    f32 = np.float32
    c["ident"] = np.eye(128, dtype=f32).astype(BF)
    c["ident32"] = np.eye(128, dtype=f32)
    half = 32
    inv = (np.float32(10000.0) ** (-np.arange(half, dtype=f32) / f32(half))).astype(f32)
    pos_p = (np.arange(16)[None, :] * 128 + np.arange(128)[:, None]).astype(f32)
    ang = (pos_p[:, :, None] * inv[None, None, :]).astype(f32)
    c["cos_p"] = np.cos(ang).astype(f32)
    c["sin_p"] = np.sin(ang).astype(f32)
    pos_s = (2048 + (np.arange(128) % 8)).astype(f32)
    ang = (pos_s[:, None] * inv[None, :]).astype(f32)
    c["cos_s"] = np.cos(ang).astype(f32)
    c["sin_s"] = np.sin(ang).astype(f32)
    log_g = np.log(1.0 - 2.0 ** (-5.0 - np.arange(4, dtype=np.float64)))
    i = np.arange(128)
    diff = i[None, :] - i[:, None]
    caus = diff >= 0
    DT = np.zeros((128, 4, 128), f32)
    for h in range(4):
        DT[:, h, :] = 0.125 * np.exp(np.where(caus, diff, 0) * log_g[h]) * caus
    c["DTp"] = DT
    c["qdec_p"] = np.exp((i[:, None] + 1.0) * log_g[None, :]).astype(f32)
    c["kdec_p"] = (0.125 * np.exp((127.0 - i[:, None]) * log_g[None, :])).astype(f32)
    cd = np.zeros((128, 2), f32)
    cds = np.zeros((128, 2), f32)
    for p in range(128):
        for pr in range(2):
            h = 2 * pr + p // 64
            cd[p, pr] = np.exp(128.0 * log_g[h])
            cds[p, pr] = np.exp(8.0 * log_g[h])
    c["cdec_p"] = cd
    c["cdec_s"] = cds
    i8 = i % 8
    same = (i[:, None] // 8) == (i[None, :] // 8)
    diff8 = i8[None, :] - i8[:, None]
    caus8 = same & (diff8 >= 0)
    DTs = np.zeros((128, 4, 128), f32)
    for h in range(4):
        DTs[:, h, :] = 0.125 * np.exp(np.where(caus8, diff8, 0) * log_g[h]) * caus8
    c["DTs"] = DTs
    c["qdec_s"] = np.exp((i8[:, None] + 1.0) * log_g[None, :]).astype(f32)
    c["kdec_s"] = (0.125 * np.exp((7.0 - i8[:, None]) * log_g[None, :])).astype(f32)
    c["blkmask"] = ((i[:, None] // 8) == np.arange(16)[None, :]).astype(f32)
    cm = np.zeros((128, 16, 128), f32)
    cm[:, :, :] = ((i[None, :] // 8) == np.arange(16)[:, None])[None, :, :]
    c["colmask"] = cm.astype(BF)
    keys = np.arange(2176)
    E = (keys[None, :] // 64 == np.arange(33)[:, None]).astype(f32)
    c["E"] = E.astype(BF)
    c["CB"] = np.where(i[:, None] <= i[None, :], 0.0, -BIG).astype(f32).astype(BF)
    c["AB"] = np.where(i[:, None] > i[None, :], 0.0, -BIG).astype(f32).astype(BF)
    n = np.arange(128)
    cmpb = np.zeros((128, 16, 128), f32)
    for t in range(16):
        tpos = 128 * t + i
        cmpb[:, t, :] = np.where((16 * n[:, None] + 31) <= tpos[None, :], 0.0, -BIG)
    c["CMPB"] = cmpb.astype(BF)

    def overlap(n_s):
        c_start = np.arange(127) * 16
        s_start = np.arange(n_s) * 64
        return ((c_start[:, None] < s_start[None, :] + 64) & (s_start[None, :] < c_start[:, None] + 32)).astype(f32)
    c["ovl_p"] = overlap(32)
    c["ovl_s"] = overlap(33)
    mulc = np.zeros((128, 16, 32), f32)
    addc = np.zeros((128, 16, 32), f32)
    s_ids = np.arange(32)
    for t in range(16):
        tpos = 128 * t + i
        cur = tpos // 64
        forced = (s_ids[None, :] == 0) | (s_ids[None, :] == cur[:, None]) | (s_ids[None, :] == cur[:, None] - 1)
        valid = (s_ids[None, :] * 64) <= tpos[:, None]
        mulc[:, t, :] = (valid & ~forced)
        addc[:, t, :] = np.where(valid, np.where(forced, FORCE, 0.0), -FORCE)
    c["mulc_p"] = mulc
    c["addc_p"] = addc
    s33 = np.arange(33)
    forced = (s33 == 0) | (s33 == 32) | (s33 == 31)
    c["mulc_s"] = np.tile((~forced).astype(f32)[None, :], (8, 1))
    c["addc_s"] = np.tile(np.where(forced, FORCE, 0.0).astype(f32)[None, :], (8, 1))
    q8 = np.arange(8)
    SB = np.where(((i[:, None, None] // 8) == np.arange(16)[None, :, None]) & ((i[:, None, None] % 8) <= q8[None, None, :]), 0.0, -BIG)
    c["SB"] = SB.astype(f32).astype(BF)
    c["ABs"] = np.where(i[:, None] > q8[None, :], 0.0, -BIG).astype(f32).astype(BF)
    c["iota_p"] = i.astype(f32)[:, None].copy()
    c["ones_row"] = np.ones((1, 128), f32).astype(BF)
    c["zeros_row"] = np.zeros((1, 512), f32).astype(BF)
    jv = np.zeros((128, 9, 32), f32)
    jv[:, :, :] = np.arange(9, dtype=f32)[None, :, None]
    c["JV"] = jv
    c["KI"] = np.tile(np.arange(256, dtype=f32)[None, :], (128, 1))
    c["PM4"] = (np.arange(128)[:, None] // 32 == np.arange(4)[None, :]).astype(f32)
    c["BM16"] = (np.arange(128)[:, None] // 16 == np.arange(128)[None, :] // 16).astype(f32)
    return c


def _dt_of(a):
    if a.dtype == np.float32:
        return F32
    if a.dtype == np.int32:
        return I32
    if a.dtype == BF:
        return BF16
    raise ValueError(a.dtype)


def v3(ap, h):
    return ap.rearrange("p (h d) -> p h d", h=h)


def bc_mid(ap, n):
    return ap.unsqueeze(1).to_broadcast([ap.shape[0], n, ap.shape[1]])


def bc_last(ap, n):
    return ap.unsqueeze(2).to_broadcast([ap.shape[0], ap.shape[1], n])


def build_program(consts, phase_b=True, do_sample=True, n_ptiles=16, stage=99):
    nc = bass.Bass("TRN2", target_bir_lowering=False)

    in_names = []

    def din(name, shape, dt=F32):
        in_names.append(name)
        return nc.dram_tensor(name, list(shape), dt, kind="ExternalInput").ap()

    def dout(name, shape, dt=F32):
        return nc.dram_tensor(name, list(shape), dt, kind="ExternalOutput").ap()

    xp = din("xp", [2048, 1024])
    xs = din("xs", [128, 1024])
    if do_sample:
        cache = din("cache", [2560 * 128, 512])
        cwin = din("cwin", [16, 512, 256])
        sret = din("sret", [16, 4, 64, 128])
        ptab = din("ptab", [1, 256], I32)
    w_in_e = din("w_in_e", [1024, EIN])
    w_out_e = din("w_out_e", [1024, 1024])
    norm_e = din("norm_e", [128, 8])
    gn_gain = din("gn_gain", [1, 512])
    qk_gain = din("qk_gain", [1, 896])
    cmp_posT = din("cmp_posT", [64, 2, 32])
    cmp_w = din("cmp_w", [2, 32, 64, 64])
    S_ONLY = ("cos_s", "sin_s", "DTs", "qdec_s", "kdec_s", "cdec_s", "blkmask", "colmask", "SB", "ABs", "mulc_s", "addc_s", "ovl_s", "iota_p")
    if phase_b:
        w_in_o = din("w_in_o", [1024, 2048])
        glu1 = din("glu1", [1024, 1024])
        glu2 = din("glu2", [1024, 1024])
        w_out_o = din("w_out_o", [1024, 1024])
        norm_o = din("norm_o", [128, 8])
        ssmd = din("ssmd", [128, 8])
        lamre_A = din("lamre_A", [128, 32])
        lamim_A = din("lamim_A", [128, 32])
        lstep_A = din("lstep_A", [128, 32])
        bA_re = din("bA_re", [128, 512])
        bA_im = din("bA_im", [128, 512])
        cA_re = din("cA_re", [128, 512])
        cA_im = din("cA_im", [128, 512])
        x0A_re = din("x0A_re", [128, 512])
        x0A_im = din("x0A_im", [128, 512])
    cd = {k: din("c_" + k, v.shape, _dt_of(v)) for k, v in consts.items() if ((do_sample or k not in S_ONLY) and (phase_b or k not in ("JV", "KI", "PM4", "BM16")))}
    yp = dout("yp", [2048, 1024])
    ys = dout("ys", [128, 1024])
    o_retp = dout("o_retp", [4, 64, 128])
    o_rets = dout("o_rets", [16, 4, 64, 128])
    o_kvp = dout("o_kvp", [2048, 512])
    o_kvs = dout("o_kvs", [128, 512])
    o_winp = dout("o_winp", [512, 256])
    o_wins = dout("o_wins", [16, 512, 256])
    if phase_b:
        o_ssp = dout("o_ssp", [2, 128, 32])
        o_sss = dout("o_sss", [2, 128, 512])

    P = Prog(nc)
    with ExitStack() as st0:
        P.setup(st0)

        def alloc(st, name, shape, dt=F32):
            return st.enter_context(nc.sbuf_tensor(name, list(shape), dt))

        def palloc(st, name, shape, dt=F32):
            return st.enter_context(nc.psum_tensor(name, list(shape), dt))

        psT = palloc(st0, "psT", [128, 1024], BF16)
        psA = [palloc(st0, "psA%d" % i, [128, 512]) for i in range(2)]
        psS = [palloc(st0, "psS%d" % i, [128, 512]) for i in range(2)]
        psV = palloc(st0, "psV", [128, 512])
        psC = palloc(st0, "psC", [128, 512])
        psR = palloc(st0, "psR", [128, 512])

        with ExitStack() as stA:
            A = lambda name, shape, dt=F32: alloc(stA, name, shape, dt)
            cs = {}
            P_ONLY = ("cos_p", "sin_p", "DTp", "qdec_p", "kdec_p", "cdec_p", "CMPB", "mulc_p", "addc_p", "ovl_p", "CB", "AB")

            def load_consts(stx, names, pre="k_"):
                for k in names:
                    v = consts[k]
                    shp = list(v.shape)
                    if len(shp) == 3:
                        tl = alloc(stx, pre + k, [shp[0], shp[1] * shp[2]], _dt_of(v))
                        P.dma(tl[:], cd[k].rearrange("p a b -> p (a b)"))
                    else:
                        tl = alloc(stx, pre + k, shp, _dt_of(v))
                        P.dma(tl[:], cd[k])
                    cs[k] = tl
            B_ONLY = ("JV", "KI", "PM4", "BM16")
            load_consts(stA, [k for k in consts if k not in P_ONLY and k not in S_ONLY and k not in B_ONLY])
            ident = cs["ident"]
            Wob = A("Wob", [128, 8 * 1024], BF16)
            ng = A("ng", [128, 8])
            BDW = A("BDW", [128, 2 * 32 * 128], BF16)
            peb = A("peb", [128, 64], BF16)
            posk = A("posk", [128, 2])
            posv = A("posv", [1, 128], BF16)
            gnb = A("gnb", [128, 512])
            P.dma(gnb[:], gn_gain.partition_broadcast(128))
            qkg = A("qkg", [128, 896])
            P.dma(qkg[:], qk_gain.partition_broadcast(128))

            xt = [A("xt%d" % i, [128, 1024]) for i in range(2)]
            xb = A("xb", [128, 1024], BF16)
            xT = A("xT", [128, 1024], BF16)
            stt_ = A("stats", [128, 64])
            proj = A("proj", [128, EIN])
            rp = A("rp", [128, 1408])
            tmpa = A("tmpa", [128, 1408])
            r16 = A("r16", [128, 1024], BF16)
            vb16 = A("vb16", [128, 512], BF16)
            rT = A("rT", [128, 256], BF16)
            qz = A("qz", [128, 1024], BF16)
            qTz = A("qTz", [128, 1024], BF16)
            scm = A("scm", [128, 512], BF16)
            S32 = A("S32", [128, 256])
            Sb = A("Sb", [128, 256], BF16)
            osb = A("osb", [128, 512])
            ocb = A("ocb", [128, 512])
            sz = A("sz", [128, 512])
            mix = A("mix", [128, 1024], BF16)
            mixT = A("mixT", [128, 1024], BF16)
            n16 = A("n16", [128, 1024], BF16)
            gts = A("gts", [128, 24])
            on = A("on", [128, 3 * 512])
            tmpb = on
            obt = A("obt", [128, 512])
            pt_ = [A("pt%d" % i, [128, 512], BF16) for i in range(3)]
            selT = A("selT", [33, 256], BF16)
            sc_ = A("sc", [128, 80])
            sc2 = A("sc2", [128, 80])
            sc16 = A("sc16", [128, 80], BF16)
            rden = A("rden", [128, 32])
            top8 = A("top8", [128, 16])
            P.memset("dve", S32[:], 0.0)
            P.memset("dve", Sb[:], 0.0)
            P.memset("pool", qz[:], 0.0)
            P.memset("pool", qTz[:], 0.0)

            stWb = ExitStack()
            Wb = alloc(stWb, "Wb", [128, 8 * EIN], BF16)
            P.dma(ng[:], norm_e)
            with ExitStack() as stW:
                stg = [alloc(stW, "stg%d" % i, [128, EIN]) for i in range(2)]
                for kc in range(8):
                    s_ = stg[kc % 2]
                    P.dma(s_[:], w_in_e[kc * 128:(kc + 1) * 128, :])
                    if kc % 2 == 0:
                        P.ts("dve", Wb[:, kc * EIN:(kc + 1) * EIN], s_[:], ng[:, kc:kc + 1], None, ALU.mult)
                    else:
                        P.op("act", lambda e, kc=kc, s_=s_: e.mul(Wb[:, kc * EIN:(kc + 1) * EIN], s_[:], ng[:, kc:kc + 1]), [s_, ng], [Wb])
                for kc in range(8):
                    s_ = stg[kc % 2]
                    P.dma(s_[:, 0:1024], w_out_e[kc * 128:(kc + 1) * 128, :])
                    if kc % 2 == 0:
                        P.cp("dve", Wob[:, kc * 1024:(kc + 1) * 1024], s_[:, 0:1024])
                    else:
                        P.cp("pool", Wob[:, kc * 1024:(kc + 1) * 1024], s_[:, 0:1024])
                P.barrier()
            with ExitStack() as stW:
                P.memset("pool", BDW[:], 0.0)
                wst = alloc(stW, "wst", [128, 2 * 32 * 64])
                srcw = cmp_w.rearrange("c l d e -> d (c l) e")
                P.dma(wst[0:64, :].rearrange("p (a e) -> p a e", e=64), srcw)
                P.dma(wst[64:128, :].rearrange("p (a e) -> p a e", e=64), srcw)
                bdv = BDW[:].rearrange("p (a e) -> p a e", e=128)
                P.cp("dve", bdv[0:64, :, 0:64], wst[0:64, :].rearrange("p (a e) -> p a e", e=64))
                P.cp("dve", bdv[64:128, :, 64:128], wst[64:128, :].rearrange("p (a e) -> p a e", e=64))
                pe32 = alloc(stW, "pe32", [128, 64])
                P.dma(pe32[0:64, :], cmp_posT.rearrange("d c l -> d (c l)"))
                P.dma(pe32[64:128, :], cmp_posT.rearrange("d c l -> d (c l)"))
                P.cp("dve", peb[:], pe32[:])
                for l in range(32):
                    P.mm(psC[:, 0:1], BDW[:, (0 * 32 + l) * 128:(0 * 32 + l + 1) * 128], peb[:, l:l + 1], start=(l == 0), stop=(l == 31))
                for l in range(32):
                    P.mm(psC[:, 1:2], BDW[:, (32 + l) * 128:(32 + l + 1) * 128], peb[:, 32 + l:32 + l + 1], start=(l == 0), stop=(l == 31))
                P.cp("dve", posk[:, 0:2], psC[:, 0:2])
                P.barrier()
            cnt = {"x": 0, "pt": 0, "ps": 0, "pa": 0, "ps3": 0}
            xpref = {}

            def rsqrt_small(out, in_, mult, add):
                P.ts("dve", out, in_, mult, add, ALU.mult, ALU.add)
                P.actv(out, out, AF.Sqrt)
                P.op("dve", lambda e: e.reciprocal(out, out), [out], [out])

            def nsa_tile(nq, groups, ns, mulc, addc, cmp_bias, merge, qall=None):
                W = 4 * nq
                wv = 65 + ns
                R = [dict(cmp=psA[0], o=0), dict(cmp=psA[1], o=40)]
                if merge:
                    accT = {2: [psR, psR], 1: [psV, psV]}
                    aoff = [0, W]
                else:
                    accT = {2: [psR, psC], 1: [psV, psA[0]]}
                    aoff = [0, 0]
                back = {2: [psR, psC], 1: [psV, psA[0]]}
                if merge:
                    oTb = [obt[:, 0:256], obt[:, 256:512]]
                else:
                    oTb = [obt[:, :], sz[:, :]]
                v4 = lambda ap: ap.rearrange("p (j q) -> p j q", j=4)
                v24 = lambda ap: ap.rearrange("p (g j q) -> p g j q", g=2, j=4)
                for g, G in enumerate(groups):
                    r = R[g]; o = r["o"]
                    ps_s = psS[cnt["ps"] % 2]; cnt["ps"] += 1
                    P.mm(v4(ps_s[0:127, 0:W]), G["cmp_k"], G["qTg"], start=True, stop=(cmp_bias is None))
                    if cmp_bias is not None:
                        P.mm(v4(ps_s[0:127, 0:W]), ident[0:127, 0:127], cmp_bias, start=False, stop=True)
                    pt = pt_[cnt["pt"] % 3]; cnt["pt"] += 1
                    P.actv(pt[0:127, 0:W], ps_s[0:127, 0:W], AF.Exp, scale=SCALE)
                    pc = r["cmp"]
                    for j in range(4):
                        P.mm(pc[0:nq, j * wv:(j + 1) * wv], pt[0:127, j * nq:(j + 1) * nq], G["cmp_v"], start=True, stop=True)
                    pcv = pc[0:nq, 0:4 * wv].rearrange("p (j w) -> p j w", j=4)
                    rd = rden[0:nq, 16 * g:16 * g + 4]
                    P.ts("dve", rd, pcv[:, :, 64], 1e-30, None, ALU.add)
                    P.op("dve", lambda e, rd=rd: e.reciprocal(rd, rd), [rden], [rden])
                    P.tt("dve", G["on_dst"](0), pcv[:, :, 0:64], bc_last(rd, 64), ALU.mult)
                    sc = sc_[0:nq, o:o + ns]
                    P.ts("dve", sc, pcv[:, 0, 65:65 + ns], rden[0:nq, 16 * g:16 * g + 1], None, ALU.mult)
                    for j in range(1, 4):
                        P.stt("dve", sc, pcv[:, j, 65:65 + ns], rden[0:nq, 16 * g + j:16 * g + j + 1], sc, ALU.mult, ALU.add)
                    P.tt("dve", sc, sc, mulc, ALU.mult)
                    P.tt("dve", sc, sc, addc, ALU.add)
                    t8 = top8[0:nq, 8 * g:8 * g + 8]
                    P.op("dve", lambda e, t8=t8, sc=sc: e.max(t8, sc), [sc_], [top8])
                    P.ts("dve", sc2[0:nq, o:o + ns], sc, top8[0:nq, 8 * g + 7:8 * g + 8], BIG, ALU.is_ge, ALU.mult)
                    P.ts("dve", sc16[0:nq, o:o + ns], sc2[0:nq, o:o + ns], -BIG, None, ALU.add)

                def scores(stp):
                    gs, br, ci, nch, chs = stp["d"]
                    chk = chs[0]
                    (kT, v1, nk, ecols, bias2d) = chk[0:5]
                    kkey = chk[5] if len(chk) > 5 and chk[5] is not None else kT
                    ps_s = (psS[0], psS[1], psA[1])[cnt["ps3"] % 3]; cnt["ps3"] += 1
                    stp["ps"] = ps_s
                    use_e = (br == 1 and ecols is not None)
                    extra = (1 if use_e else 0) + (1 if bias2d is not None else 0)
                    if len(gs) == 2:
                        outv = v24(ps_s[0:nk, 0:2 * W])
                        qr = qall
                        br_ = None if bias2d is None else bias2d.unsqueeze(1).unsqueeze(1).to_broadcast([nk, 2, 4, nq])
                        er_ = selT[0:ns, 0:256].rearrange("p (g q) -> p g q", g=2)[:, :, 0:nq].unsqueeze(2).to_broadcast([ns, 2, 4, nq])
                    else:
                        g = gs[0]
                        outv = v4(ps_s[0:nk, 0:W])
                        qr = groups[g]["qTg"]
                        br_ = None if bias2d is None else bc_mid(bias2d, 4)
                        er_ = bc_mid(selT[0:ns, 128 * g:128 * g + nq], 4)
                    P.mm(outv, kT, qr, start=True, stop=(extra == 0), reads=[kkey, qTz])
                    if bias2d is not None:
                        extra -= 1
                        P.mm(outv, ident[0:nk, 0:nk], br_, start=False, stop=(extra == 0))
                    if use_e:
                        P.mm(outv, ecols, er_, start=False, stop=True)

                def exp_pv(stp):
                    gs, br, ci, nch, chs = stp["d"]
                    nk = chs[0][2]
                    ps_s = stp["ps"]
                    Wt = W * len(gs)
                    pt = pt_[cnt["pt"] % 3]; cnt["pt"] += 1
                    P.actv(pt[0:nk, 0:Wt], ps_s[0:nk, 0:Wt], AF.Exp, scale=SCALE)
                    for k, g in enumerate(gs):
                        chk = chs[k]
                        v1 = chk[1]
                        vkey = chk[6] if len(chk) > 6 and chk[6] is not None else v1
                        acc = accT[br][g]
                        P.mm(acc[0:65, aoff[g]:aoff[g] + W], v1, pt[0:nk, k * W:(k + 1) * W], start=False, stop=(ci == nch - 1), reads=[pt, vkey])

                def run_steps(steps):
                    n = len(steps)
                    D = 2
                    for i in range(min(D, n)):
                        scores(steps[i])
                    for i in range(n):
                        if i + D < n:
                            scores(steps[i + D])
                        exp_pv(steps[i])

                for br, key in ((2, "win_chunks"), (1, "sel_chunks")):
                    if br == 1:
                        for g in range(2):
                            o = R[g]["o"]
                            pc0 = 384 + 512 * g
                            P.tr(psT[0:ns, pc0:pc0 + nq], sc16[0:nq, o:o + ns], ident[0:nq, 0:nq])
                            P.cp("dve", selT[0:ns, 128 * g:128 * g + nq], psT[0:ns, pc0:pc0 + nq])
                    steps = []
                    if merge:
                        acc = accT[br][0]
                        P.mm(acc[0:65, 0:2 * W], cs["zeros_row"][0:1, 0:65], cs["zeros_row"][0:1, 0:2 * W], start=True, stop=False)
                        c0, c1 = groups[0][key], groups[1][key]
                        for ci in range(len(c0)):
                            steps.append({"d": ([0, 1], br, ci, len(c0), [c0[ci], c1[ci]])})
                    else:
                        for g in range(2):
                            acc = accT[br][g]
                            P.mm(acc[0:65, 0:W], cs["zeros_row"][0:1, 0:65], cs["zeros_row"][0:1, 0:W], start=True, stop=False)
                            chunks = groups[g][key]
                            for ci, ch in enumerate(chunks):
                                steps.append({"d": ([g], br, ci, len(chunks), [ch])})
                    run_steps(steps)
                    for g in range(2):
                        acc = accT[br][g]
                        if g == 0:
                            P.cp("act", oTb[g][0:65, 0:W], acc[0:65, aoff[g]:aoff[g] + W])
                        else:
                            P.cp("dve", oTb[g][0:65, 0:W], acc[0:65, aoff[g]:aoff[g] + W])
                    for g in range(2):
                        bk = back[br][g]
                        for j in range(4):
                            P.tr(bk[0:nq, j * 65:(j + 1) * 65], oTb[g][0:65, j * nq:(j + 1) * nq], cs["ident32"][0:65, 0:65])
                        pvv = bk[0:nq, 0:260].rearrange("p (j w) -> p j w", j=4)
                        rd = rden[0:nq, 16 * g + 4 * br:16 * g + 4 * br + 4]
                        P.ts("dve", rd, pvv[:, :, 64], 1e-30, None, ALU.add)
                        P.op("dve", lambda e, rd=rd: e.reciprocal(rd, rd), [rden], [rden])
                        P.tt("dve", groups[g]["on_dst"](br), pvv[:, :, 0:64], bc_last(rd, 64), ALU.mult)

            def even_tile(mode, t, caches):
                isp = (mode == "p")
                xsrc = xp[t * 128:(t + 1) * 128, :] if isp else xs
                if xpref.get("cur") == (mode, t):
                    xtile = xpref["tile"]
                else:
                    xtile = xt[cnt["x"] % 2]; cnt["x"] += 1
                    P.dma(xtile[:], xsrc)
                nxt = caches.get("next")
                if nxt is not None:
                    ntile = xt[cnt["x"] % 2]; cnt["x"] += 1
                    nsrc = xp[nxt[1] * 128:(nxt[1] + 1) * 128, :] if nxt[0] == "p" else xs
                    P.dma(ntile[:], nsrc)
                    xpref["cur"] = nxt
                    xpref["tile"] = ntile
                P.memset("dve", stt_[:, 0:1], 0.0)
                P.actv(mixT[:], xtile[:], AF.Square, accum_out=stt_[:, 0:1])
                rsqrt_small(stt_[:, 1:2], stt_[:, 0:1], 1.0 / 1024, EPS)
                P.cp("act", xb[:], xtile[:])
                for kc in range(8):
                    P.tr(psT[:, kc * 128:(kc + 1) * 128], xb[:, kc * 128:(kc + 1) * 128], ident[:])
                P.cp("act", xT[:], psT[:])
                for gi, (c0, c1) in enumerate(GROUPS):
                    pa = psA[cnt["pa"] % 2]; cnt["pa"] += 1
                    w = c1 - c0
                    for kc in range(8):
                        P.mm(pa[:, 0:w], xT[:, kc * 128:(kc + 1) * 128], Wb[:, kc * EIN + c0:kc * EIN + c1], start=(kc == 0), stop=(kc == 7))
                    if gi % 2 == 0:
                        P.ts("dve", proj[:, c0:c1], pa[:, 0:w], stt_[:, 1:2], None, ALU.mult)
                    else:
                        P.op("act", lambda e, c0=c0, c1=c1, pa=pa, w=w: e.mul(proj[:, c0:c1], pa[:, 0:w], stt_[:, 1:2]), [pa, stt_], [proj])
                if not isp:
                    caches["hook"]()
                if stage <= 1:
                    return
                nv = v3(proj[:, QN:QN + 896], 14)
                P.tt("dve", tmpa[:, 0:896], proj[:, QN:QN + 896], proj[:, QN:QN + 896], ALU.mult)
                P.op("dve", lambda e: e.reduce_sum(stt_[:, 8:22], v3(tmpa[:, 0:896], 14), AX.X), [tmpa], [stt_])
                rsqrt_small(stt_[:, 8:22], stt_[:, 8:22], 1.0 / 64, EPS)
                P.tt("dve", nv, nv, bc_last(stt_[:, 8:22], 64), ALU.mult)
                P.tt("dve", proj[:, QN:QN + 896], proj[:, QN:QN + 896], qkg[:], ALU.mult)
                if isp:
                    cos = cs["cos_p"][:, t * 32:(t + 1) * 32]
                    sin = cs["sin_p"][:, t * 32:(t + 1) * 32]
                else:
                    cos = cs["cos_s"][:, :]
                    sin = cs["sin_s"][:, :]
                pv = v3(proj[:, 0:1408], 22)
                rv = v3(rp[:, 0:1408], 22)
                ta = tmpa[:, 0:704].rearrange("p (h d) -> p h d", h=22)
                tb = tmpa[:, 704:1408].rearrange("p (h d) -> p h d", h=22)
                tc_ = tmpb[:, 0:704].rearrange("p (h d) -> p h d", h=22)
                td_ = tmpb[:, 704:1408].rearrange("p (h d) -> p h d", h=22)
                cosb = bc_mid(cos, 22)
                sinb = bc_mid(sin, 22)
                P.tt("dve", ta, pv[:, :, 0:32], cosb, ALU.mult)
                P.tt("dve", tb, pv[:, :, 32:64], sinb, ALU.mult)
                P.tt("dve", rv[:, :, 0:32], ta, tb, ALU.subtract)
                P.tt("pool", tc_, pv[:, :, 32:64], cosb, ALU.mult)
                P.tt("pool", td_, pv[:, :, 0:32], sinb, ALU.mult)
                P.tt("pool", rv[:, :, 32:64], tc_, td_, ALU.add)
                if stage <= 2:
                    return
                if isp:
                    okv = o_kvp[t * 128:(t + 1) * 128, :]
                else:
                    okv = o_kvs
                P.dma(okv[:, 0:128], rp[:, KC:KC + 128])
                P.dma(okv[:, 128:256], proj[:, VC:VC + 128])
                P.dma(okv[:, 256:384], rp[:, KS:KS + 128])
                P.dma(okv[:, 384:512], proj[:, VS:VS + 128])
                if isp and t >= 12:
                    ow = o_winp[(t - 12) * 128:(t - 11) * 128, :]
                    P.dma(ow[:, 0:128], rp[:, KW:KW + 128])
                    P.dma(ow[:, 128:256], proj[:, VW:VW + 128])
                if not isp:
                    for b in range(16):
                        P.dma(o_wins[b, 504:512, 0:128], rp[b * 8:(b + 1) * 8, KW:KW + 128])
                        P.dma(o_wins[b, 504:512, 128:256], proj[b * 8:(b + 1) * 8, VW:VW + 128])
                        P.dma(o_wins[b, 0:504, :], cwin[b, 8:512, :])
                if stage <= 3:
                    return
                qdec = cs["qdec_p"] if isp else cs["qdec_s"]
                kdec = cs["kdec_p"] if isp else cs["kdec_s"]
                DTm = cs["DTp"] if isp else cs["DTs"]
                P.cp("act", r16[:, 0:256], rp[:, QA:QA + 256])
                P.tt("dve", v3(r16[:, 256:512], 4), v3(rp[:, QA:QA + 256], 4), bc_last(qdec[:, 0:4], 64), ALU.mult)
                P.cp("act", r16[:, 512:768], rp[:, KA:KA + 256])
                P.tt("dve", v3(r16[:, 768:1024], 4), v3(rp[:, KA:KA + 256], 4), bc_last(kdec[:, 0:4], 64), ALU.mult)
                P.cp("act", vb16[:], proj[:, VA:VA + 512])
                for i6 in range(6):
                    P.tr(psT[:, i6 * 128:(i6 + 1) * 128], r16[:, i6 * 128:(i6 + 1) * 128], ident[:])
                P.cp("act", rT[:, 0:256], psT[:, 512:768])
                qzv = qz[:].rearrange("p (a h q) -> p a h q", a=4, h=2)
                P.cp("act", qzv[0:64, :, 0, :], psT[0:64, 0:512].rearrange("p (a q) -> p a q", a=4))
                P.cp("dve", qzv[64:128, :, 1, :], psT[64:128, 0:512].rearrange("p (a q) -> p a q", a=4))
                if stage <= 3.1:
                    return
                for h in range(4):
                    hp, pr = h % 2, h // 2
                    P.mm(psR[:, h * 128:(h + 1) * 128], rT[:, pr * 128:(pr + 1) * 128],
                         qz[:, ((0 * 2 + pr) * 2 + hp) * 128:((0 * 2 + pr) * 2 + hp + 1) * 128])
                if stage <= 3.2:
                    return
                P.tt("dve", scm[:], psR[:], DTm[:], ALU.mult)
                if isp:
                    for h in range(4):
                        hp, pr = h % 2, h // 2
                        P.mm(psC[:, h * 128:(h + 1) * 128], scm[:, h * 128:(h + 1) * 128], vb16[:, h * 128:(h + 1) * 128], start=True, stop=False)
                        P.mm(psC[:, h * 128:(h + 1) * 128], qz[:, ((1 * 2 + pr) * 2 + hp) * 128:((1 * 2 + pr) * 2 + hp + 1) * 128],
                             Sb[:, pr * 128:(pr + 1) * 128], start=False, stop=True)
                else:
                    S0b = caches["S0b"]
                    qdTm = caches["qdTm"]
                    for h in range(4):
                        hp, pr = h % 2, h // 2
                        if hp == 0:
                            for b in range(16):
                                P.tt("dve" if b % 2 == 0 else "pool", qdTm[:, b * 256:(b + 1) * 256].rearrange("p (a q) -> p a q", a=2),
                                     qz[:, 512 + pr * 256:512 + (pr + 1) * 256].rearrange("p (a q) -> p a q", a=2),
                                     bc_mid(cs["colmask"][:, b * 128:(b + 1) * 128], 2), ALU.mult)
                        P.mm(psC[:, h * 128:(h + 1) * 128], scm[:, h * 128:(h + 1) * 128], vb16[:, h * 128:(h + 1) * 128], start=True, stop=False)
                        for b in range(16):
                            P.mm(psC[:, h * 128:(h + 1) * 128], qdTm[:, b * 256 + hp * 128:b * 256 + (hp + 1) * 128],
                                 S0b[:, (b * 2 + pr) * 128:(b * 2 + pr + 1) * 128], start=False, stop=(b == 15))
                if stage <= 3.4:
                    return
                P.cp("act", osb[:], psC[:])
                if stage <= 3.5:
                    return
                if isp:
                    for h in range(4):
                        hp, pr = h % 2, h // 2
                        P.mm(psR[:, h * 128:(h + 1) * 128], r16[:, 768 + pr * 128:768 + (pr + 1) * 128], vb16[:, h * 128:(h + 1) * 128])
                    for h in range(4):
                        hp, pr = h % 2, h // 2
                        rows = slice(hp * 64, (hp + 1) * 64)
                        P.stt("dve", S32[rows, pr * 128:(pr + 1) * 128], S32[rows, pr * 128:(pr + 1) * 128], cs["cdec_p"][rows, pr:pr + 1],
                              psR[rows, h * 128:(h + 1) * 128], ALU.mult, ALU.add)
                    P.cp("act", Sb[:], S32[:])
                    if t == n_ptiles - 1:
                        for h in range(4):
                            hp, pr = h % 2, h // 2
                            P.dma(o_retp[h, :, :], S32[hp * 64:(hp + 1) * 64, pr * 128:(pr + 1) * 128])
                else:
                    S0 = caches["S0"]
                    vblk = caches["vblk"]
                    Sn = caches["Sn"]
                    for h in range(4):
                        hp, pr = h % 2, h // 2
                        rows = slice(hp * 64, (hp + 1) * 64)
                        P.tt("dve", vblk[:].rearrange("p (b e) -> p b e", b=16), bc_mid(vb16[:, h * 128:(h + 1) * 128], 16),
                             bc_last(cs["blkmask"][:, 0:16], 128), ALU.mult)
                        for q4 in range(4):
                            pa = psA[cnt["pa"] % 2]; cnt["pa"] += 1
                            P.mm(pa[:, :], r16[:, 768 + pr * 128:768 + (pr + 1) * 128], vblk[:, q4 * 512:(q4 + 1) * 512])
                            s0v = S0[rows, :].rearrange("p (b a e) -> p b a e", b=16, a=2)[:, q4 * 4:(q4 + 1) * 4, pr, :]
                            P.stt("dve", Sn[rows, q4 * 512:(q4 + 1) * 512].rearrange("p (b e) -> p b e", b=4), s0v, cs["cdec_s"][rows, pr:pr + 1],
                                  pa[rows, :].rearrange("p (b e) -> p b e", b=4), ALU.mult, ALU.add)
                        P.dma(o_rets[:, h, :, :].rearrange("b d e -> d b e"), Sn[rows, :].rearrange("p (b e) -> p b e", b=16))
                if stage <= 4:
                    return
                P.op("dve", lambda e: e.reduce_sum(stt_[:, 24:28], v3(osb[:], 4), AX.X), [osb], [stt_])
                P.ts("dve", stt_[:, 24:28], stt_[:, 24:28], -1.0 / 128, None, ALU.mult)
                P.tt("dve", v3(ocb[:], 4), v3(osb[:], 4), bc_last(stt_[:, 24:28], 128), ALU.add)
                P.tt("dve", osb[:], ocb[:], ocb[:], ALU.mult)
                P.op("dve", lambda e: e.reduce_sum(stt_[:, 28:32], v3(osb[:], 4), AX.X), [osb], [stt_])
                rsqrt_small(stt_[:, 28:32], stt_[:, 28:32], 1.0 / 128, EPS)
                P.tt("dve", v3(ocb[:], 4), v3(ocb[:], 4), bc_last(stt_[:, 28:32], 128), ALU.mult)
                P.tt("dve", ocb[:], ocb[:], gnb[:], ALU.mult)
                P.actv(sz[:], proj[:, ZA:ZA + 512], AF.Silu)
                P.tt("dve", mix[:, 0:512], ocb[:], sz[:], ALU.mult)
                if stage <= 5:
                    return
                P.actv(gts[:], proj[:, GL:GL + 24], AF.Sigmoid)
                P.cp("act", n16[:, 0:896], rp[:, QN:QN + 896])
                P.cp("dve", n16[:, 896:1024], proj[:, VC:VC + 128])
                for i4 in range(4):
                    P.tr(psT[:, i4 * 128:(i4 + 1) * 128], n16[:, i4 * 128:(i4 + 1) * 128], ident[:])
                P.cp("act", qTz[0:64, 0:512], psT[0:64, 0:512])
                P.cp("dve", qTz[64:128, 512:1024], psT[64:128, 0:512])
                for i4 in range(4):
                    P.tr(psT[:, i4 * 128:(i4 + 1) * 128], n16[:, 512 + i4 * 128:512 + (i4 + 1) * 128], ident[:])
                if isp:
                    cT = caches["cT"]; vs1 = caches["vs1"]; vw1 = caches["vw1"]
                    P.cp("act", cT[:].rearrange("p (k n) -> p k n", k=4)[:, :, t * 128:(t + 1) * 128], psT[:, 0:512].rearrange("p (k n) -> p k n", k=4))
                    P.cp("dve", vs1[:].rearrange("p (c g w) -> p c g w", c=16, g=2)[:, t, :, 0:64], v3(proj[:, VS:VS + 128], 2))
                    P.cp("dve", vw1[:].rearrange("p (c g w) -> p c g w", c=16, g=2)[:, t, :, 0:64], v3(proj[:, VW:VW + 128], 2))
                    ckT = caches["ckT"]; cvx = caches["cvx"]
                    compress(cT, 0, cT, 3 * 2048, ckT, caches["cvT"], cvx, 97, max(0, 8 * t - 1), 8 * t + 6)
                    groups = []
                    for g in range(2):
                        qTg = qTz[:, g * 512:(g + 1) * 512].rearrange("p (j q) -> p j q", j=4)
                        selc = []
                        for c in range(t + 1):
                            bias = cs["CB"][:, :] if c == t else None
                            selc.append((cT[:, 2048 + c * 128:2048 + (c + 1) * 128], vs1[:, (c * 2 + g) * 65:(c * 2 + g + 1) * 65], 128,
                                         cs["E"][0:32, c * 128:(c + 1) * 128], bias))
                        winc = []
                        for c in range(max(0, t - 4), t + 1):
                            bias = cs["CB"][:, :] if c == t else (cs["AB"][:, :] if c == t - 4 else None)
                            winc.append((cT[:, 4096 + c * 128:4096 + (c + 1) * 128], vw1[:, (c * 2 + g) * 65:(c * 2 + g + 1) * 65], 128, None, bias))
                        groups.append(dict(qTg=qTg, cmp_k=ckT[:, 0:127], cmp_v=cvx[0:127, g * 97:(g + 1) * 97], sel_chunks=selc, win_chunks=winc,
                                           on_dst=lambda x, g=g: on[:, x * 512 + g * 256:x * 512 + (g + 1) * 256].rearrange("p (j d) -> p j d", j=4)))
                    cmpb = bc_mid(cs["CMPB"][0:127, t * 128:(t + 1) * 128], 4)
                    nsa_tile(128, groups, 32, cs["mulc_p"][:, t * 32:(t + 1) * 32], cs["addc_p"][:, t * 32:(t + 1) * 32], cmpb, False)
                else:
                    cTs = caches["cTs"]; vs1s = caches["vs1s"]
                    P.cp("act", cTs[:], psT[:, 0:512])
                    P.cp("dve", vs1s[:].rearrange("p (k g w) -> p k g w", k=2, g=2)[:, 0, :, 0:64], v3(proj[:, VS:VS + 128], 2))
                    P.cp("dve", vs1s[:].rearrange("p (k g w) -> p k g w", k=2, g=2)[:, 1, :, 0:64], v3(proj[:, VW:VW + 128], 2))
                    sample_nsa(caches)
                if stage <= 6:
                    return
                gv = gts[:].rearrange("p (h x) -> p h x", h=8)
                for x in range(3):
                    P.tt("dve", v3(on[:, x * 512:(x + 1) * 512], 8), v3(on[:, x * 512:(x + 1) * 512], 8), bc_last(gv[:, :, x], 64), ALU.mult)
                P.tt("dve", obt[:], on[:, 0:512], on[:, 512:1024], ALU.add)
                P.tt("dve", obt[:], obt[:], on[:, 1024:1536], ALU.add)
                P.actv(sz[:], proj[:, ZB:ZB + 512], AF.Silu)
                P.tt("dve", mix[:, 512:1024], obt[:], sz[:], ALU.mult)
                if stage <= 7:
                    return
                for kc in range(8):
                    P.tr(psT[:, kc * 128:(kc + 1) * 128], mix[:, kc * 128:(kc + 1) * 128], ident[:])
                P.cp("act", mixT[:], psT[:])
                for hf in range(2):
                    pa = psA[cnt["pa"] % 2]; cnt["pa"] += 1
                    for kc in range(8):
                        P.mm(pa[:, :], mixT[:, kc * 128:(kc + 1) * 128], Wob[:, kc * 1024 + hf * 512:kc * 1024 + (hf + 1) * 512], start=(kc == 0), stop=(kc == 7))
                    P.tt("dve", xtile[:, hf * 512:(hf + 1) * 512], xtile[:, hf * 512:(hf + 1) * 512], pa[:, :], ALU.add)
                ydst = yp[t * 128:(t + 1) * 128, :] if isp else ys
                P.dma(ydst, xtile[:], writes=[("y0", mode, t)])

            def compress(kc_t, kc_off, vc_t, vc_off, ckT, cvT, cvx, wv, n0, n1, rkeys=None):
                nn = n1 - n0 + 1
                rd = ([BDW] + rkeys) if rkeys else None
                for l in range(32):
                    a0 = kc_off + 16 * n0 + l
                    P.mm(psC[:, 0:nn], BDW[:, l * 128:(l + 1) * 128], kc_t[:, a0:a0 + 16 * (nn - 1) + 1:16], start=(l == 0), stop=(l == 31), reads=rd)
                for l in range(32):
                    a0 = vc_off + 16 * n0 + l
                    P.mm(psC[:, 128:128 + nn], BDW[:, (32 + l) * 128:(33 + l) * 128], vc_t[:, a0:a0 + 16 * (nn - 1) + 1:16], start=(l == 0), stop=(l == 31), reads=rd)
                P.ts("dve", ckT[:, n0:n1 + 1], psC[:, 0:nn], posk[:, 0:1], None, ALU.add)
                P.ts("dve", cvT[:, n0:n1 + 1], psC[:, 128:128 + nn], posk[:, 1:2], None, ALU.add)
                P.tr(psT[0:127, 0:128], cvT[:, 0:127], ident[:])
                P.cp("act", cvx[0:127, :].rearrange("p (g w) -> p g w", g=2)[:, :, 0:64], psT[0:127, 0:128].rearrange("p (g d) -> p g d", g=2))

            def sample_nsa(caches):
                C = caches
                cTq, kwTq, vs1q, vw1q, ckTq, cvxq = C["cTq"], C["kwTq"], C["vs1q"], C["vw1q"], C["ckTq"], C["cvxq"]
                cTs, vs1s, idx = C["cTs"], C["vs1s"], C["idx"]
                on8 = [osb, ocb, sz]
                wst32, w16 = C["wst32"], C["w16"]
                P.barrier()
                C["stR"].close()
                stPG = C["stPG"]
                pgb = [alloc(stPG, "pgb%d" % i, [128, 16 * 384], BF16) for i in range(2)]
                pg = [alloc(stPG, "pgf%d" % i, [128, 512]) for i in range(6)]
                for b in range(16):
                    pb = pgb[b % 2]
                    for i in range(16):
                        pgt = pg[(b * 16 + i) % 6]
                        col = b * 16 + i
                        P.dma(pgt[:], cache, reads=[cache, idx], q="pool",
                              fn=lambda e, pgt=pgt, col=col: e.indirect_dma_start(
                                  out=pgt[:, :], out_offset=None, in_=cache[:, :],
                                  in_offset=bass.IndirectOffsetOnAxis(ap=idx[:, col:col + 1], axis=0)))
                        P.cp("act", pb[:, i * 384:(i + 1) * 384], pgt[:, 0:384], writes=[("pgb", b % 2, i)])
                        P.cp("dve", vs1q[:].rearrange("p (c g w) -> p c g w", c=16, g=2)[:, i, :, 0:64], v3(pgt[:, 384:512], 2), writes=[("vs1q", i)])
                    psRb = psR[:].bitcast(BF16)
                    for i in range(16):
                        pdst = psT[:, 0:384] if i % 2 == 0 else psRb[:, 0:384]
                        pkey = psT if i % 2 == 0 else psR
                        for k in range(3):
                            P.tr(pdst[:, k * 128:(k + 1) * 128], pb[:, i * 384 + k * 128:i * 384 + (k + 1) * 128], ident[:],
                                 reads=[("pgb", b % 2, i), ident], writes=[pkey])
                        P.cp("act" if i % 2 == 0 else "dve", cTq[:].rearrange("p (k n) -> p k n", k=3)[:, :, i * 128:(i + 1) * 128],
                             pdst.rearrange("p (k n) -> p k n", k=3), reads=[pkey], writes=[("cTq", i)])
                    P.dma(wst32[:].rearrange("p (c w) -> p c w", c=4), cwin[b].rearrange("(c r) w -> r c w", r=128))
                    P.cp("dve", w16[:], wst32[:])
                    for c in range(4):
                        P.tr(psT[:, 512 + c * 128:512 + (c + 1) * 128], w16[:, c * 256:c * 256 + 128], ident[:])
                    P.cp("act", kwTq[:], psT[:, 512:1024])
                    for c in range(4):
                        P.cp("dve", vw1q[:, c * 130:(c + 1) * 130].rearrange("p (g w) -> p g w", g=2)[:, :, 0:64], v3(w16[:, c * 256 + 128:(c + 1) * 256], 2))
                    compress(cTq, 0, cTq, 2048, ckTq, C["cvTq"], cvxq, 98, 0, 126, rkeys=[("cTq", i) for i in range(16)])
                    groups = []
                    for g in range(2):
                        qTg = qTz[:, g * 512:(g + 1) * 512].rearrange("p (j q) -> p j q", j=4)[:, :, 8 * b:8 * b + 8]
                        sbias = cs["SB"][:, b * 8:(b + 1) * 8]
                        selc = []
                        for c in range(16):
                            selc.append((cTq[:, 4096 + c * 128:4096 + (c + 1) * 128], vs1q[:, (c * 2 + g) * 65:(c * 2 + g + 1) * 65], 128,
                                         cs["E"][0:33, c * 128:(c + 1) * 128], None, ("cTq", c), ("vs1q", c)))
                        selc.append((cTs[:, 128:256], vs1s[:, (0 * 2 + g) * 65:(0 * 2 + g + 1) * 65], 128, None, sbias))
                        winc = []
                        for c in range(4):
                            bias = cs["ABs"][:, :] if c == 0 else None
                            winc.append((kwTq[:, c * 128:(c + 1) * 128], vw1q[:, (c * 2 + g) * 65:(c * 2 + g + 1) * 65], 128, None, bias))
                        winc.append((cTs[:, 256:384], vs1s[:, (1 * 2 + g) * 65:(1 * 2 + g + 1) * 65], 128, None, sbias))
                        groups.append(dict(qTg=qTg, cmp_k=ckTq[:, 0:127], cmp_v=cvxq[0:127, g * 98:(g + 1) * 98], sel_chunks=selc, win_chunks=winc,
                                           on_dst=lambda x, g=g: on8[x][0:8, g * 256:(g + 1) * 256].rearrange("p (j d) -> p j d", j=4)))
                    qall = qTz[:, :].rearrange("p (g j q) -> p g j q", g=2, j=4)[:, :, :, 8 * b:8 * b + 8]
                    nsa_tile(8, groups, 33, cs["mulc_s"][:, :], cs["addc_s"][:, :], None, True, qall)
                    for x in range(3):
                        P.dma(on[b * 8:(b + 1) * 8, x * 512:(x + 1) * 512], on8[x][0:8, :])
                P.barrier()

            with ExitStack() as stP:
                Ap = lambda name, shape, dt=F32: alloc(stP, name, shape, dt)
                load_consts(stP, P_ONLY)
                cT = Ap("cT", [128, 4 * 2048], BF16)
                vs1 = Ap("vs1", [128, 16 * 2 * 65], BF16)
                vw1 = Ap("vw1", [128, 16 * 2 * 65], BF16)
                ckT = Ap("ckT", [128, 128], BF16)
                cvT = Ap("cvT", [128, 128], BF16)
                cvx = Ap("cvx", [128, 2 * 97], BF16)
                ov32 = Ap("ov32", [128, 32])
                P.memset("pool", cT[:], 0.0)
                P.memset("pool", vs1[:], 1.0)
                P.memset("pool", vw1[:], 1.0)
                P.memset("pool", cvx[:], 1.0)
                P.memset("pool", ckT[:], 0.0)
                P.memset("pool", cvT[:], 0.0)
                for g in range(2):
                    P.cp("dve", cvx[0:127, g * 97 + 65:(g + 1) * 97], cs["ovl_p"][0:127, :])
                caches = {"cT": cT, "vs1": vs1, "vw1": vw1, "ckT": ckT, "cvx": cvx, "cvT": cvT}
                for t in range(n_ptiles):
                    caches["next"] = ("p", t + 1) if t + 1 < n_ptiles else (("s", 0) if do_sample else None)
                    even_tile("p", t, caches)
                P.barrier()

            if do_sample:
                stS = ExitStack()
                stPG = ExitStack()
                scaches = {}

                def sample_hook():
                    P.barrier()
                    stWb.close()
                    As = lambda name, shape, dt=F32: alloc(stS, name, shape, dt)
                    load_consts(stS, S_ONLY)
                    C = scaches
                    C["cTq"] = As("cTq", [128, 3 * 2048], BF16)
                    C["kwTq"] = As("kwTq", [128, 512], BF16)
                    C["vs1q"] = As("vs1q", [128, 16 * 130], BF16)
                    C["vw1q"] = As("vw1q", [128, 4 * 130], BF16)
                    C["ckTq"] = As("ckTq", [128, 128], BF16)
                    C["cvTq"] = As("cvTq", [128, 128], BF16)
                    C["cvxq"] = As("cvxq", [128, 2 * 98], BF16)
                    C["cTs"] = As("cTs", [128, 512], BF16)
                    C["vs1s"] = As("vs1s", [128, 4 * 65], BF16)
                    C["wst32"] = As("wst32", [128, 1024])
                    C["w16"] = As("w16", [128, 1024], BF16)
                    C["idx"] = As("idx", [128, 256], I32)
                    pti = As("pti", [128, 256], I32)
                    stR = ExitStack()
                    C["stR"] = stR
                    C["stPG"] = stPG
                    Ar = lambda name, shape, dt=F32: alloc(stR, name, shape, dt)
                    C["S0"] = Ar("S0", [128, 4096])
                    C["S0b"] = Ar("S0b", [128, 4096], BF16)
                    C["qdTm"] = Ar("qdTm", [128, 16 * 256], BF16)
                    C["vblk"] = Ar("vblk", [128, 2048], BF16)
                    C["Sn"] = Ar("Sn", [128, 2048])
                    ptf = tmpa[:, 0:256]
                    P.memset("pool", C["vs1q"][:], 1.0)
                    P.memset("pool", C["vw1q"][:], 1.0)
                    P.memset("pool", C["vs1s"][:], 1.0)
                    P.memset("pool", C["cvxq"][:], 1.0)
                    for g in range(2):
                        P.cp("dve", C["cvxq"][0:127, g * 98 + 65:(g + 1) * 98], cs["ovl_s"][0:127, :])
                    for hp in range(2):
                        P.dma(C["S0"][hp * 64:(hp + 1) * 64, :].rearrange("p (b a e) -> p b a e", b=16, a=2),
                              sret[:, hp::2, :, :].rearrange("b a d e -> d b a e"))
                    P.cp("dve", C["S0b"][:], C["S0"][:])
                    P.dma(pti[:], ptab.partition_broadcast(128))
                    P.cp("dve", ptf, pti[:])
                    P.ts("dve", ptf, ptf, 128.0, cs["iota_p"][:, 0:1], ALU.mult, ALU.add)
                    P.cp("dve", C["idx"][:], ptf)

                scaches["hook"] = sample_hook
                even_tile("s", 0, scaches)
                P.barrier()
                stPG.close()
                stS.close()
            else:
                stWb.close()
            P.barrier()
        if phase_b:
            NT = 2176
            NCH = 272
            TWO_PI = 2.0 * math.pi
            with ExitStack() as stB:
                Bf = lambda name, shape, dt=F32: alloc(stB, name, shape, dt)
                load_consts(stB, ["ident", "JV", "KI", "PM4", "BM16"], pre="kb_")
                ident = cs["ident"]
                uT = Bf("uT", [128, 8 * NT], BF16)
                szT = Bf("szT", [128, 8 * NT], BF16)
                ngo = Bf("ngo", [128, 8])
                dsk = Bf("dsk", [128, 8])
                statb = Bf("statb", [128, 8])
                FSp = Bf("FSp", [128, 64])
                FSs = Bf("FSs", [128, 1024])
                P.dma(ngo[:], norm_o)
                P.dma(dsk[:], ssmd)
                cntb = {"x": 0, "pa": 0}

                def load_tile(dst, ti):
                    if ti < 16:
                        P.dma(dst, yp[ti * 128:(ti + 1) * 128, :], reads=[("y0", "p", ti)])
                    else:
                        P.dma(dst, ys, reads=[("y0", "s", 0)])

                with ExitStack() as st1:
                    B1 = lambda name, shape, dt=F32: alloc(st1, name, shape, dt)
                    Wodd = B1("Wodd", [128, 8 * 2048], BF16)
                    yt = [B1("yt%d" % i, [128, 1024]) for i in range(2)]
                    hb = B1("hb", [128, 1024], BF16)
                    hT = B1("hT", [128, 8 * 512], BF16)
                    with ExitStack() as stg_:
                        stg = [alloc(stg_, "stgb%d" % i, [128, 2048]) for i in range(2)]
                        for kc in range(8):
                            s_ = stg[kc % 2]
                            P.dma(s_[:], w_in_o[kc * 128:(kc + 1) * 128, :])
                            if kc % 2 == 0:
                                P.ts("dve", Wodd[:, kc * 2048:(kc + 1) * 2048], s_[:], ngo[:, kc:kc + 1], None, ALU.mult)
                            else:
                                P.op("act", lambda e, kc=kc, s_=s_: e.mul(Wodd[:, kc * 2048:(kc + 1) * 2048], s_[:], ngo[:, kc:kc + 1]), [s_, ngo], [Wodd])
                        P.barrier()
                    blocks = [(0, 4), (4, 8), (8, 12), (12, 16), (16, 17)]
                    for (t0, t1) in blocks:
                        nb = (t1 - t0) * 128
                        col0 = t0 * 128
                        for ti in range(t0, t1):
                            ytile = yt[cntb["x"] % 2]; cntb["x"] += 1
                            load_tile(ytile[:], ti)
                            P.memset("dve", statb[:, 0:1], 0.0)
                            P.actv(hb[:], ytile[:], AF.Square, accum_out=statb[:, 0:1])
                            P.ts("dve", statb[:, 1:2], statb[:, 0:1], 1.0 / 1024, EPS, ALU.mult, ALU.add)
                            P.actv(statb[:, 1:2], statb[:, 1:2], AF.Sqrt)
                            P.op("dve", lambda e: e.reciprocal(statb[:, 1:2], statb[:, 1:2]), [statb], [statb])
                            P.ts("dve", hb[:], ytile[:], statb[:, 1:2], None, ALU.mult)
                            for kc in range(8):
                                P.tr(psT[:, kc * 128:(kc + 1) * 128], hb[:, kc * 128:(kc + 1) * 128], ident[:])
                            lt = ti - t0
                            P.cp("act", hT[:].rearrange("p (k n) -> p k n", k=8)[:, :, lt * 128:(lt + 1) * 128], psT[:].rearrange("p (k n) -> p k n", k=8))
                        for oc in range(16):
                            pa = psA[cntb["pa"] % 2]; cntb["pa"] += 1
                            for kc in range(8):
                                P.mm(pa[:, 0:nb], Wodd[:, kc * 2048 + oc * 128:kc * 2048 + (oc + 1) * 128], hT[:, kc * 512:kc * 512 + nb], start=(kc == 0), stop=(kc == 7))
                            if oc < 8:
                                P.cp("dve", uT[:, oc * NT + col0:oc * NT + col0 + nb], pa[:, 0:nb])
                            else:
                                P.actv(szT[:, (oc - 8) * NT + col0:(oc - 8) * NT + col0 + nb], pa[:, 0:nb], AF.Silu)
                    P.barrier()

                with ExitStack() as st2:
                    B2 = lambda name, shape, dt=F32: alloc(st2, name, shape, dt)
                    lr = B2("lr", [128, 32]); li = B2("li", [128, 32]); ls = B2("ls", [128, 32])
                    bre = B2("bre", [128, 512]); bim = B2("bim", [128, 512])
                    cre = B2("cre", [128, 512]); cim = B2("cim", [128, 512])
                    x0r = B2("x0r", [128, 512]); x0i = B2("x0i", [128, 512])
                    for tl, src in ((lr, lamre_A), (li, lamim_A), (ls, lstep_A), (bre, bA_re), (bim, bA_im), (cre, cA_re), (cim, cA_im), (x0r, x0A_re), (x0i, x0A_im)):
                        P.dma(tl[:], src)
                    aa = B2("aa", [128, 32]); th = B2("th", [128, 32])
                    A9 = B2("A9", [128, 288]); T9 = B2("T9", [128, 288]); T9c = B2("T9c", [128, 288])
                    PR = B2("PR", [128, 288]); PI = B2("PI", [128, 288])
                    rri = B2("rri", [128, 1024], I32); rri2 = B2("rri2", [128, 1024], I32)
                    npi = B2("npi", [128, 1])

                    hpi = B2("hpi", [128, 1])
                    P.memset("dve", hpi[:], math.pi / 2)

                    def range_reduce(x, n):
                        P.ts("dve", rri[:, 0:n], x, 1.0 / TWO_PI, None, ALU.mult)
                        P.stt("dve", x, rri[:, 0:n], -TWO_PI, x, ALU.mult, ALU.add)

                    def sincos(x, n, sin_out, cos_out):
                        P.ts("dve", rri[:, 0:n], x, 1.0 / TWO_PI, None, ALU.mult)
                        P.stt("dve", sin_out, rri[:, 0:n], -TWO_PI, x, ALU.mult, ALU.add)
                        P.actv(sin_out, sin_out, AF.Sin)
                        P.ts("dve", rri2[:, 0:n], x, 1.0 / TWO_PI, 0.25, ALU.mult, ALU.add)
                        P.stt("dve", cos_out, rri2[:, 0:n], -TWO_PI, x, ALU.mult, ALU.add)
                        P.actv(cos_out, cos_out, AF.Sin, bias=hpi[:, 0:1])

                    P.actv(ls[:], ls[:], AF.Exp)
                    P.tt("dve", aa[:], lr[:], ls[:], ALU.mult)
                    P.tt("dve", th[:], li[:], ls[:], ALU.mult)
                    JV = cs["JV"]
                    P.tt("dve", A9[:].rearrange("p (j m) -> p j m", j=9), JV[:].rearrange("p (j m) -> p j m", j=9), bc_mid(aa[:, :], 9), ALU.mult)
                    P.actv(A9[:], A9[:], AF.Exp)
                    range_reduce(th[:, :], 32)
                    P.tt("dve", T9[:].rearrange("p (j m) -> p j m", j=9), JV[:].rearrange("p (j m) -> p j m", j=9), bc_mid(th[:, :], 9), ALU.mult)
                    sincos(T9[:, :], 288, PI[:, :], T9c[:, :])
                    P.tt("dve", PR[:], A9[:], T9c[:], ALU.mult)
                    P.tt("dve", PI[:], A9[:], PI[:], ALU.mult)
                    PRv = PR[:].rearrange("p (j m) -> p j m", j=9)
                    PIv = PI[:].rearrange("p (j m) -> p j m", j=9)
                    den = B2("den", [128, 32]); nr = B2("nr", [128, 32]); fre = B2("fre", [128, 32]); fim = B2("fim", [128, 32]); t32 = B2("t32", [128, 32])
                    P.tt("dve", den[:], lr[:], lr[:], ALU.mult)
                    P.tt("dve", t32[:], li[:], li[:], ALU.mult)
                    P.tt("dve", den[:], den[:], t32[:], ALU.add)
                    P.op("dve", lambda e: e.reciprocal(den[:], den[:]), [den], [den])
                    P.ts("dve", nr[:], PR[:, 32:64], -1.0, None, ALU.add)
                    P.tt("dve", fre[:], nr[:], lr[:], ALU.mult)
                    P.tt("dve", t32[:], PI[:, 32:64], li[:], ALU.mult)
                    P.tt("dve", fre[:], fre[:], t32[:], ALU.add)
                    P.tt("dve", fre[:], fre[:], den[:], ALU.mult)
                    P.tt("dve", fim[:], PI[:, 32:64], lr[:], ALU.mult)
                    P.tt("dve", t32[:], nr[:], li[:], ALU.mult)
                    P.tt("dve", fim[:], fim[:], t32[:], ALU.subtract)
                    P.tt("dve", fim[:], fim[:], den[:], ALU.mult)
                    Bre = B2("Bre", [128, 512]); Bim = B2("Bim", [128, 512]); t512 = B2("t512", [128, 512])
                    v16 = lambda ap: ap.rearrange("p (m c) -> p m c", c=16)
                    P.tt("dve", v16(Bre[:]), v16(bre[:]), bc_last(fre[:, :], 16), ALU.mult)
                    P.tt("dve", v16(t512[:]), v16(bim[:]), bc_last(fim[:, :], 16), ALU.mult)
                    P.tt("dve", Bre[:], Bre[:], t512[:], ALU.subtract)
                    P.tt("dve", v16(Bim[:]), v16(bim[:]), bc_last(fre[:, :], 16), ALU.mult)
                    P.tt("dve", v16(t512[:]), v16(bre[:]), bc_last(fim[:, :], 16), ALU.mult)
                    P.tt("dve", Bim[:], Bim[:], t512[:], ALU.add)
                    Cbd_re = B2("Cbd_re", [128, 32 * 32], BF16); Cbd_nim = B2("Cbd_nim", [128, 32 * 32], BF16)
                    P.memset("pool", Cbd_re[:], 0.0)
                    P.memset("pool", Cbd_nim[:], 0.0)
                    cbv = lambda t_: t_[:].rearrange("p (m c) -> p m c", c=32)
                    for hp in range(2):
                        rows = slice(hp * 64, (hp + 1) * 64)
                        P.cp("dve", cbv(Cbd_re)[rows, :, hp * 16:(hp + 1) * 16], v16(cre[:])[rows, :, :])
                        P.ts("dve", cbv(Cbd_nim)[rows, :, hp * 16:(hp + 1) * 16], v16(cim[:])[rows, :, :], -1.0, None, ALU.mult)
                    th8 = B2("th8", [128, 32])
                    P.ts("dve", th8[:], th[:], 8.0, None, ALU.mult)
                    range_reduce(th8[:, :], 32)
                    Xre = B2("Xre", [128, 512]); Xim = B2("Xim", [128, 512]); tX = B2("tX", [128, 512])
                    XBD = B2("XBD", [128, 2 * 8 * 4 * 32], BF16)
                    VZ = B2("VZ", [128, 8 * 4 * 2 * 128], BF16)
                    WS = B2("WS", [128, 8 * 2 * 128], BF16)
                    uzb = [B2("uz%d" % i, [128, NT], BF16) for i in range(2)]
                    BD = B2("BD", [128, 8 * 128], BF16)
                    Sin_r = B2("Sin_r", [128, 4 * NCH]); Sin_i = B2("Sin_i", [128, 4 * NCH])
                    cosT = B2("cosT", [128, 1024]); sinT = B2("sinT", [128, 1024])
                    c_r = B2("c_r", [128, 1024]); c_i = B2("c_i", [128, 1024]); t1k = B2("t1k", [128, 1024])
                    w_r = B2("w_r", [128, 1024]); w_i = B2("w_i", [128, 1024])
                    Sp_r = B2("Sp_r", [128, 4 * NCH], BF16); Sp_i = B2("Sp_i", [128, 4 * NCH], BF16)
                    yvb = [B2("yv%d" % i, [128, NCH]) for i in range(2)]; y2b = [B2("y2%d" % i, [128, NCH]) for i in range(2)]; y3b = [B2("y3%d" % i, [128, NCH]) for i in range(2)]
                    ygc = B2("ygc", [128, NT], BF16)
                    P.memset("pool", XBD[:], 0.0)
                    P.memset("pool", VZ[:], 0.0)
                    for c in range(8):
                        msl = slice(4 * c, 4 * c + 4)
                        x4 = lambda t_: t_[:].rearrange("p (t m c) -> p t m c", t=8, m=4)
                        prb = PRv[:, 0:8, msl].unsqueeze(3).to_broadcast([128, 8, 4, 16])
                        pib = PIv[:, 0:8, msl].unsqueeze(3).to_broadcast([128, 8, 4, 16])
                        brb = v16(Bre[:])[:, msl, :].unsqueeze(1).to_broadcast([128, 8, 4, 16])
                        bib = v16(Bim[:])[:, msl, :].unsqueeze(1).to_broadcast([128, 8, 4, 16])
                        P.tt("dve", x4(Xre), prb, brb, ALU.mult)
                        P.tt("dve", x4(tX), pib, bib, ALU.mult)
                        P.tt("dve", Xre[:], Xre[:], tX[:], ALU.subtract)
                        P.tt("dve", x4(Xim), prb, bib, ALU.mult)
                        P.tt("dve", x4(tX), pib, brb, ALU.mult)
                        P.tt("dve", Xim[:], Xim[:], tX[:], ALU.add)
                        xbv = XBD[:].rearrange("p (r t m c) -> p r t m c", r=2, t=8, m=4)
                        for hp in range(2):
                            rows = slice(hp * 64, (hp + 1) * 64)
                            P.cp("dve", xbv[rows, 0, :, :, hp * 16:(hp + 1) * 16], x4(Xre)[rows])
                            P.cp("pool", xbv[rows, 1, :, :, hp * 16:(hp + 1) * 16], x4(Xim)[rows])
                        prb = PRv[:, 1:9, msl].unsqueeze(3).to_broadcast([128, 8, 4, 16])
                        pib = PIv[:, 1:9, msl].unsqueeze(3).to_broadcast([128, 8, 4, 16])
                        crb = v16(cre[:])[:, msl, :].unsqueeze(1).to_broadcast([128, 8, 4, 16])
                        cib = v16(cim[:])[:, msl, :].unsqueeze(1).to_broadcast([128, 8, 4, 16])
                        P.tt("dve", x4(Xre), prb, crb, ALU.mult)
                        P.tt("dve", x4(tX), pib, cib, ALU.mult)
                        P.tt("dve", Xre[:], Xre[:], tX[:], ALU.subtract)
                        P.tt("dve", x4(Xim), pib, crb, ALU.mult)
                        P.tt("dve", x4(tX), prb, cib, ALU.mult)
                        P.stt("dve", Xim[:], Xim[:], -1.0, tX[:], ALU.mult, ALU.subtract)
                        vzv = VZ[:].rearrange("p (t m r n) -> p t m r n", t=8, m=4, r=2)
                        for hp in range(2):
                            rows = slice(hp * 64, (hp + 1) * 64)
                            for m4 in range(4):
                                c0_ = 32 * m4 + 16 * hp
                                P.cp("dve", vzv[rows, :, m4, 0, c0_:c0_ + 16], x4(Xre)[rows, :, m4, :])
                                P.cp("pool", vzv[rows, :, m4, 1, c0_:c0_ + 16], x4(Xim)[rows, :, m4, :])
                        for th2 in range(2):
                            for tl_ in range(4):
                                tau = th2 * 4 + tl_
                                for ri in range(2):
                                    P.tr(psT[:, (tl_ * 2 + ri) * 128:(tl_ * 2 + ri + 1) * 128], XBD[:, (ri * 8 + tau) * 128:(ri * 8 + tau + 1) * 128], ident[:])
                            P.cp("act", WS[:, th2 * 1024:(th2 + 1) * 1024], psT[:])
                        for th2 in range(2):
                            for tl_ in range(4):
                                tau = th2 * 4 + tl_
                                outp = psC[:, tl_ * 128:(tl_ + 1) * 128]
                                P.mm(outp, XBD[:, (0 * 8 + tau) * 128:(0 * 8 + tau + 1) * 128], Cbd_re[:, c * 128:(c + 1) * 128], start=True, stop=False)
                                P.mm(outp, XBD[:, (1 * 8 + tau) * 128:(1 * 8 + tau + 1) * 128], Cbd_nim[:, c * 128:(c + 1) * 128], start=False, stop=True)
                            P.tt("dve", BD[:, th2 * 512:(th2 + 1) * 512].rearrange("p (t n) -> p t n", t=4), psC[:, 0:512].rearrange("p (t n) -> p t n", t=4),
                                 bc_mid(cs["BM16"][:, :], 4), ALU.mult)
                        uc = uT[:, c * NT:(c + 1) * NT]
                        for m4 in range(4):
                            uz = uzb[m4 % 2]
                            P.op("act", lambda e, uz=uz, uc=uc, m4=m4: e.mul(uz[:], uc, cs["PM4"][:, m4:m4 + 1]), [uT, cs["PM4"]], [uz])
                            for ri, dst in ((0, Sin_r), (1, Sin_i)):
                                pa = psA[ri]
                                for s_ in range(8):
                                    tau = 7 - s_
                                    P.mm(pa[:, 0:NCH], WS[:, (tau * 2 + ri) * 128:(tau * 2 + ri + 1) * 128], uz[:, s_:NT:8], start=(s_ == 0), stop=(s_ == 7))
                                if ri == 0:
                                    P.cp("act", dst[:, m4 * NCH:(m4 + 1) * NCH], pa[:, 0:NCH])
                                else:
                                    P.cp("dve", dst[:, m4 * NCH:(m4 + 1) * NCH], pa[:, 0:NCH])
                        a3 = lambda t_: t_[:].rearrange("p (m k) -> p m k", m=4)
                        P.tt("dve", a3(t1k), bc_last(th8[:, msl], 256), bc_mid(cs["KI"][:, :], 4), ALU.mult)
                        sincos(t1k[:, :], 1024, sinT[:, :], cosT[:, :])
                        s3 = lambda t_: t_[:].rearrange("p (m k) -> p m k", m=4)[:, :, 0:256]
                        P.tt("dve", a3(c_r), a3(cosT), s3(Sin_r), ALU.mult)
                        P.tt("dve", a3(t1k), a3(sinT), s3(Sin_i), ALU.mult)
                        P.tt("dve", c_r[:], c_r[:], t1k[:], ALU.add)
                        P.tt("dve", a3(c_i), a3(cosT), s3(Sin_i), ALU.mult)
                        P.tt("dve", a3(t1k), a3(sinT), s3(Sin_r), ALU.mult)
                        P.tt("dve", c_i[:], c_i[:], t1k[:], ALU.subtract)
                        for m4 in range(4):
                            m = 4 * c + m4
                            r8b = A9[:, 8 * 32 + m:8 * 32 + m + 1].to_broadcast([128, 256])
                            for tl_, wo_ in ((c_r, w_r), (c_i, w_i)):
                                seg = tl_[:, m4 * 256:(m4 + 1) * 256]
                                oseg = wo_[:, m4 * 256:(m4 + 1) * 256]
                                P.op("dve", lambda e, seg=seg, oseg=oseg, r8b=r8b: e.tensor_tensor_scan(oseg, r8b, seg, 0.0, ALU.mult, ALU.add), [tl_, A9], [wo_])
                        P.tt("dve", t1k[:], cosT[:], w_r[:], ALU.mult)
                        P.tt("pool", c_r[:], sinT[:], w_i[:], ALU.mult)
                        P.tt("dve", t1k[:], t1k[:], c_r[:], ALU.subtract)
                        P.tt("pool", c_i[:], cosT[:], w_i[:], ALU.mult)
                        P.tt("dve", c_r[:], sinT[:], w_r[:], ALU.mult)
                        P.tt("dve", c_i[:], c_i[:], c_r[:], ALU.add)
                        spr = Sp_r[:].rearrange("p (m k) -> p m k", m=4)
                        spi = Sp_i[:].rearrange("p (m k) -> p m k", m=4)
                        P.memset("pool", spr[:, :, 0:1], 0.0)
                        P.memset("pool", spi[:, :, 0:1], 0.0)
                        P.cp("dve", spr[:, :, 1:256], a3(t1k)[:, :, 0:255])
                        P.cp("pool", spi[:, :, 1:256], a3(c_i)[:, :, 0:255])
                        x0rv = x0r[:].rearrange("p (m b) -> p m b", b=16)[:, msl, :]
                        x0iv = x0i[:].rearrange("p (m b) -> p m b", b=16)[:, msl, :]
                        P.cp("dve", spr[:, :, 256:272], x0rv)
                        P.cp("pool", spi[:, :, 256:272], x0iv)
                        P.cp("dve", FSp[:, 4 * c:4 * c + 4], a3(t1k)[:, :, 255])
                        P.cp("dve", FSp[:, 32 + 4 * c:32 + 4 * c + 4], a3(c_i)[:, :, 255])
                        fsr = FSs[:, 0:512].rearrange("p (m b) -> p m b", b=16)[:, msl, :]
                        fsi = FSs[:, 512:1024].rearrange("p (m b) -> p m b", b=16)[:, msl, :]
                        p8 = bc_last(PR[:, 8 * 32 + 4 * c:8 * 32 + 4 * c + 4], 16)
                        i8_ = bc_last(PI[:, 8 * 32 + 4 * c:8 * 32 + 4 * c + 4], 16)
                        sir = Sin_r[:].rearrange("p (m k) -> p m k", m=4)[:, :, 256:272]
                        sii = Sin_i[:].rearrange("p (m k) -> p m k", m=4)[:, :, 256:272]
                        tq = tX[:, 0:64].rearrange("p (m b) -> p m b", b=16)
                        P.tt("dve", fsr, p8, x0rv, ALU.mult)
                        P.tt("dve", tq, i8_, x0iv, ALU.mult)
                        P.tt("dve", fsr, fsr, tq, ALU.subtract)
                        P.tt("dve", fsr, fsr, sir, ALU.add)
                        P.tt("dve", fsi, p8, x0iv, ALU.mult)
                        P.tt("dve", tq, i8_, x0rv, ALU.mult)
                        P.tt("dve", fsi, fsi, tq, ALU.add)
                        P.tt("dve", fsi, fsi, sii, ALU.add)
                        for j in range(8):
                            acc = psS[j % 2]
                            for s_ in range(j + 1):
                                P.mm(acc[:, 0:NCH], BD[:, (j - s_) * 128:(j - s_ + 1) * 128], uc[:, s_:NT:8], start=(s_ == 0), stop=False)
                            for m4 in range(4):
                                for ri, spt in ((0, Sp_r), (1, Sp_i)):
                                    last = (m4 == 3 and ri == 1)
                                    P.mm(acc[:, 0:NCH], VZ[:, ((j * 4 + m4) * 2 + ri) * 128:((j * 4 + m4) * 2 + ri + 1) * 128],
                                         spt[:, m4 * NCH:(m4 + 1) * NCH], start=False, stop=last)
                            yv, y2, y3 = yvb[j % 2], y2b[j % 2], y3b[j % 2]
                            P.stt("dve", yv[:], uc[:, j:NT:8], dsk[:, c:c + 1], acc[:, 0:NCH], ALU.mult, ALU.add)
                            P.actv(y2[:], yv[:], AF.Square)
                            P.ts("dve", y2[:], y2[:], 0.044715, 1.0, ALU.mult, ALU.add)
                            P.tt("dve", y2[:], y2[:], yv[:], ALU.mult)
                            P.actv(y3[:], y2[:], AF.Sigmoid, scale=1.5957691216057308)
                            P.tt("dve", ygc[:, j:NT:8], yv[:], y3[:], ALU.mult)
                        P.cp("pool", uT[:, c * NT:(c + 1) * NT], ygc[:])
                    P.dma(o_ssp.rearrange("r p m -> p r m"), FSp[:].rearrange("p (r m) -> p r m", r=2))
                    P.dma(o_sss.rearrange("r p x -> p r x"), FSs[:].rearrange("p (r x) -> p r x", r=2))
                    P.barrier()

                with ExitStack() as st3:
                    B3 = lambda name, shape, dt=F32: alloc(st3, name, shape, dt)
                    W1 = B3("W1", [128, 8 * 1024], BF16)
                    W2 = B3("W2", [128, 8 * 1024], BF16)
                    Wo2 = B3("Wo2", [128, 8 * 1024], BF16)
                    yt = [B3("ytc%d" % i, [128, 1024]) for i in range(2)]
                    oT = B3("oT", [128, 8 * 512], BF16)
                    sg = B3("sg", [128, 512]); tg = B3("tg", [128, 512])
                    with ExitStack() as stg_:
                        stg = [alloc(stg_, "stgc%d" % i, [128, 1024]) for i in range(2)]
                        k_ = 0
                        for (Wd, src) in ((W1, glu1), (W2, glu2), (Wo2, w_out_o)):
                            for kc in range(8):
                                s_ = stg[k_ % 2]; k_ += 1
                                P.dma(s_[:], src[kc * 128:(kc + 1) * 128, :])
                                P.cp("dve" if kc % 2 == 0 else "act", Wd[:, kc * 1024:(kc + 1) * 1024], s_[:])
                        P.barrier()
                    for (t0, t1) in blocks:
                        nb = (t1 - t0) * 128
                        col0 = t0 * 128
                        for fc in range(8):
                            p1 = psA[0]; p2 = psA[1]
                            for kc in range(8):
                                P.mm(p1[:, 0:nb], W1[:, kc * 1024 + fc * 128:kc * 1024 + (fc + 1) * 128], uT[:, kc * NT + col0:kc * NT + col0 + nb], start=(kc == 0), stop=(kc == 7))
                            for kc in range(8):
                                P.mm(p2[:, 0:nb], W2[:, kc * 1024 + fc * 128:kc * 1024 + (fc + 1) * 128], uT[:, kc * NT + col0:kc * NT + col0 + nb], start=(kc == 0), stop=(kc == 7))
                            P.actv(sg[:, 0:nb], p2[:, 0:nb], AF.Sigmoid)
                            P.tt("dve", tg[:, 0:nb], p1[:, 0:nb], sg[:, 0:nb], ALU.mult)
                            P.tt("pool", oT[:, fc * 512:fc * 512 + nb], tg[:, 0:nb], szT[:, fc * NT + col0:fc * NT + col0 + nb], ALU.mult)
                        for ti in range(t0, t1):
                            lt = ti - t0
                            ytile = yt[cntb["x"] % 2]; cntb["x"] += 1
                            load_tile(ytile[:], ti)
                            for hf in range(2):
                                pa = psS[hf]
                                for fc in range(8):
                                    P.mm(pa[:, :], oT[:, fc * 512 + lt * 128:fc * 512 + (lt + 1) * 128], Wo2[:, fc * 1024 + hf * 512:fc * 1024 + (hf + 1) * 512], start=(fc == 0), stop=(fc == 7))
                                P.tt("dve", ytile[:, hf * 512:(hf + 1) * 512], ytile[:, hf * 512:(hf + 1) * 512], pa[:, :], ALU.add)
                            if ti < 16:
                                P.dma(yp[ti * 128:(ti + 1) * 128, :], ytile[:], reads=[ytile, ("y0", "p", ti)], writes=[("y0", "p", ti)])
                            else:
                                P.dma(ys, ytile[:], reads=[ytile, ("y0", "s", 0)], writes=[("y0", "s", 0)])
                    P.barrier()
        P.barrier()
        P.flush()
    return nc, in_names


_PROG_CACHE = {}


def _shared_inputs(inputs, consts):
    perm = _perm_even()
    sh = {}
    sh["w_in_e"] = np.ascontiguousarray(inputs["w_in_even"][0][:, perm])
    sh["w_out_e"] = np.ascontiguousarray(inputs["w_out_even"][0])
    sh["norm_e"] = np.ascontiguousarray(inputs["norm_even"][0].reshape(8, 128).T)
    sh["gn_gain"] = np.ascontiguousarray(inputs["ret_gn_gain"][0].reshape(1, 512))
    qn = inputs["nsa_q_norm"][0]
    kn = inputs["nsa_k_norm"][0]
    sh["qk_gain"] = np.concatenate([np.tile(qn, 8), np.tile(kn[0], 2), np.tile(kn[1], 2), np.tile(kn[2], 2)]).reshape(1, 896).astype(np.float32)
    sh["cmp_posT"] = np.ascontiguousarray(inputs["nsa_cmp_pos"][0].transpose(2, 0, 1))
    sh["cmp_w"] = np.ascontiguousarray(inputs["nsa_cmp_w"][0])
    for k, v in consts.items():
        sh["c_" + k] = v
    ca = np.ascontiguousarray
    sh["w_in_o"] = ca(inputs["w_in_odd"][0])
    sh["glu1"] = ca(inputs["glu_w1"][0])
    sh["glu2"] = ca(inputs["glu_w2"][0])
    sh["w_out_o"] = ca(inputs["w_out_odd"][0])
    sh["norm_o"] = ca(inputs["norm_odd"][0].reshape(8, 128).T)
    sh["ssmd"] = ca(inputs["ssm_d"][0].reshape(8, 128).T)
    sh["lamre_A"] = ca(inputs["ssm_lambda_re"][0].reshape(32, 2, 64).transpose(1, 2, 0).reshape(128, 32))
    sh["lamim_A"] = ca(inputs["ssm_lambda_im"][0].reshape(32, 2, 64).transpose(1, 2, 0).reshape(128, 32))
    ls = np.broadcast_to(inputs["ssm_log_step"][0].reshape(32, 2, 1), (32, 2, 64))
    sh["lstep_A"] = ca(ls.transpose(1, 2, 0).reshape(128, 32)).astype(np.float32)
    sh["bA_re"] = ca(inputs["ssm_b_re"][0].reshape(32, 2, 64, 16).transpose(1, 2, 0, 3).reshape(128, 512))
    sh["bA_im"] = ca(inputs["ssm_b_im"][0].reshape(32, 2, 64, 16).transpose(1, 2, 0, 3).reshape(128, 512))
    sh["cA_re"] = ca(inputs["ssm_c_re"][0].reshape(32, 2, 16, 64).transpose(1, 3, 0, 2).reshape(128, 512))
    sh["cA_im"] = ca(inputs["ssm_c_im"][0].reshape(32, 2, 16, 64).transpose(1, 3, 0, 2).reshape(128, 512))
    return sh


def _core_inputs(inputs, c, sh, cache_rows):
    im = dict(sh)
    im["xp"] = np.ascontiguousarray(inputs["x_prompt"][c])
    im["xs"] = np.ascontiguousarray(inputs["x_sample"][16 * c:16 * c + 16].reshape(128, 1024))
    im["cache"] = inputs["cache_nsa_kv"][0].reshape(2560 * 128, 512)[0:cache_rows]
    im["cwin"] = np.ascontiguousarray(inputs["cache_nsa_win"][0, 16 * c:16 * c + 16].reshape(16, 512, 256))
    im["sret"] = np.ascontiguousarray(inputs["state_ret"][0, 16 * c:16 * c + 16])
    im["x0A_re"] = np.ascontiguousarray(inputs["state_ssm_re"][0, 16 * c:16 * c + 16].reshape(16, 32, 2, 64).transpose(2, 3, 1, 0).reshape(128, 512))
    im["x0A_im"] = np.ascontiguousarray(inputs["state_ssm_im"][0, 16 * c:16 * c + 16].reshape(16, 32, 2, 64).transpose(2, 3, 1, 0).reshape(128, 512))
    im["ptab"] = np.ascontiguousarray(inputs["page_table"][16 * c:16 * c + 16].reshape(1, 256)).astype(np.int32)
    return im


def run_cores(inputs, cores, trace=False, **opts):
    consts = make_consts()
    key = tuple(sorted(opts.items()))
    nc, in_names = build_program(consts, **opts)
    sh = _shared_inputs(inputs, consts)
    cache_rows = 2560 * 128 if opts.get("do_sample", True) else 128
    in_maps = [_core_inputs(inputs, c, sh, cache_rows) for c in cores]
    in_maps = [{k: m[k] for k in in_names} for m in in_maps]
    if trace:
        res = run_bass_kernel_spmd(nc, in_maps, core_ids=list(range(len(cores))), trace=True)
        print("EXEC_TIME_NS", res.exec_time_ns)
        return res.results
    res = run_bass_kernel_spmd(nc, in_maps, core_ids=list(range(len(cores))))
    return res.results


def kernel(**inputs):
    inputs = {k: np.asarray(v) for k, v in inputs.items()}
    res = run_cores(inputs, list(range(NCORES)))
    f32 = np.float32
    y_p = np.zeros((8, 2048, 1024), f32)
    y_s = np.zeros((128, 8, 1024), f32)
    ret_p = np.zeros((1, 8, 4, 64, 128), f32)
    ret_s = np.zeros((1, 128, 4, 64, 128), f32)
    kv_p = np.zeros((1, 8, 2048, 4, 2, 64), f32)
    kv_s = np.zeros((1, 128, 8, 4, 2, 64), f32)
    win_p = np.zeros((1, 8, 512, 2, 2, 64), f32)
    win_s = np.zeros((1, 128, 512, 2, 2, 64), f32)
    sre_p = np.zeros((1, 8, 64, 64), f32)
    sim_p = np.zeros((1, 8, 64, 64), f32)
    sre_s = np.zeros((1, 128, 64, 64), f32)
    sim_s = np.zeros((1, 128, 64, 64), f32)
    for c in range(NCORES):
        r = res[c]
        sl = slice(16 * c, 16 * c + 16)
        y_p[c] = r["yp"]
        y_s[sl] = r["ys"].reshape(16, 8, 1024)
        ret_p[0, c] = r["o_retp"]
        ret_s[0, sl] = r["o_rets"]
        kv_p[0, c] = r["o_kvp"].reshape(2048, 4, 2, 64)
        kv_s[0, sl] = r["o_kvs"].reshape(16, 8, 4, 2, 64)
        win_p[0, c] = r["o_winp"].reshape(512, 2, 2, 64)
        win_s[0, sl] = r["o_wins"].reshape(16, 512, 2, 2, 64)
        if "o_ssp" in r:
            sp = r["o_ssp"].reshape(2, 2, 64, 32).transpose(0, 3, 1, 2).reshape(2, 64, 64)
            sre_p[0, c] = sp[0]
            sim_p[0, c] = sp[1]
            ss = r["o_sss"].reshape(2, 2, 64, 32, 16).transpose(0, 4, 3, 1, 2).reshape(2, 16, 64, 64)
            sre_s[0, sl] = ss[0]
            sim_s[0, sl] = ss[1]
    return (y_p, y_s, ret_p, ret_s, kv_p, kv_s, win_p, win_s, sre_p, sim_p, sre_s, sim_s)
```

```python
import numpy as np
import concourse.bass as bass
import concourse.mybir as mybir
from concourse.bass_utils import run_bass_kernel_spmd

F32 = mybir.dt.float32
BF16 = mybir.dt.bfloat16
I32 = mybir.dt.int32
AF = mybir.ActivationFunctionType
ALU = mybir.AluOpType
AX = mybir.AxisListType

ENGS = ("pe", "act", "dve", "pool", "sp")
NDMASEM = 24
NSWSEM = 8


class Prog:
    def __init__(self, nc):
        self.nc = nc
        self.ops = {e: [] for e in ENGS}
        self.cnt = {e: 0 for e in ENGS}
        self.sem = {}
        self.seen = {e: {} for e in ENGS}
        self.last_w = {}
        self.readers = {}
        self.dma_i = 0
        self.dma_uses = [0] * NDMASEM
        self.dma_tok = [None] * NDMASEM
        self.sw_i = 0
        self.sw_uses = [0] * NSWSEM
        self.sw_tok = [None] * NSWSEM
        self.stack = None
        self.n_ops = 0

    def setup(self, stack):
        self.stack = stack
        for e in ENGS:
            self.sem[e] = stack.enter_context(self.nc.semaphore("s_" + e))
        for i in range(NDMASEM):
            self.sem["d%d" % i] = stack.enter_context(self.nc.semaphore("s_d%d" % i))
        for i in range(NSWSEM):
            self.sem["w%d" % i] = stack.enter_context(self.nc.semaphore("s_w%d" % i))

    def _key(self, a):
        if isinstance(a, str):
            return a
        if isinstance(a, tuple):
            return a
        t = getattr(a, 'tensor', None)
        return t.name if t is not None else a.name

    def _deps(self, eng, reads, writes):
        toks = []
        for k in reads:
            k = self._key(k)
            t = self.last_w.get(k)
            if t is not None:
                toks.append(t)
        for k in writes:
            k = self._key(k)
            t = self.last_w.get(k)
            if t is not None:
                toks.append(t)
            toks.extend(self.readers.get(k, ()))
        need = {}
        for (s, v) in toks:
            if eng == "pe" and s == "pe":
                continue
            if v > need.get(s, 0):
                need[s] = v
        waits = []
        seen = self.seen[eng]
        for s, v in need.items():
            if seen.get(s, 0) >= v:
                continue
            seen[s] = v
            waits.append((s, v))
        return waits

    def _commit(self, tok, reads, writes):
        for k in writes:
            k = self._key(k)
            self.last_w[k] = tok
            self.readers[k] = []
        for k in reads:
            k = self._key(k)
            self.readers.setdefault(k, []).append(tok)

    def op(self, eng, fn, reads=(), writes=()):
        waits = self._deps(eng, reads, writes)
        self.cnt[eng] += 1
        tok = (eng, self.cnt[eng])
        self.ops[eng].append((waits, fn, (eng, 1)))
        self._commit(tok, reads, writes)
        self.n_ops += 1

    def dma(self, out, in_, reads=None, writes=None, q="sp", fn=None):
        if reads is None:
            reads = [in_]
        if writes is None:
            writes = [out]
        if q == "pool":
            i = self.sw_i % NSWSEM
            self.sw_i += 1
            sname = "w%d" % i
            uses, toks = self.sw_uses, self.sw_tok
        else:
            i = self.dma_i % NDMASEM
            self.dma_i += 1
            sname = "d%d" % i
            uses, toks = self.dma_uses, self.dma_tok
        waits = self._deps(q, reads, writes)
        prev = toks[i]
        if prev is not None and self.seen[q].get(sname, 0) < prev[1]:
            self.seen[q][sname] = prev[1]
            waits.append(prev)
        uses[i] += 1
        tok = (sname, 16 * uses[i])
        toks[i] = tok
        if fn is None:
            fn = lambda e, o=out, a=in_: e.dma_start(out=o, in_=a)
        self.ops[q].append((waits, fn, (sname, 16)))
        self._commit(tok, reads, writes)
        self.n_ops += 1

    def barrier(self):
        for e in ENGS:
            waits = []
            for e2 in ENGS:
                if e2 == e:
                    continue
                v = self.cnt[e2]
                if v > self.seen[e].get(e2, 0):
                    self.seen[e][e2] = v
                    waits.append((e2, v))
            for t in list(self.dma_tok) + list(self.sw_tok):
                if t is not None and self.seen[e].get(t[0], 0) < t[1]:
                    self.seen[e][t[0]] = t[1]
                    waits.append(t)
            if waits:
                self.ops[e].append((waits, None, None))

    def flush(self):
        nc = self.nc
        ops = self.ops
        sem = self.sem

        def replay(engh, lst):
            for (waits, fn, inc) in lst:
                for (s, v) in waits:
                    engh.wait_ge(sem[s], v)
                if fn is not None:
                    ins = fn(engh)
                    ins.then_inc(sem[inc[0]], inc[1])

        with nc.Block() as block:
            @block.tensor
            def _(e):
                replay(e, ops["pe"])

            @block.scalar
            def _(e):
                replay(e, ops["act"])

            @block.vector
            def _(e):
                replay(e, ops["dve"])

            @block.gpsimd
            def _(e):
                replay(e, ops["pool"])

            @block.sync
            def _(e):
                replay(e, ops["sp"])
        self.ops = {e: [] for e in ENGS}

    def mm(self, out, lhsT, rhs, start=True, stop=True, reads=None, writes=None):
        if reads is None:
            reads = [lhsT, rhs]
        if writes is None:
            writes = [out]
        self.op("pe", lambda e: e.matmul(out, lhsT, rhs, start=start, stop=stop), reads, writes)

    def tr(self, out, in_, ident, reads=None, writes=None):
        if reads is None:
            reads = [in_, ident]
        if writes is None:
            writes = [out]
        self.op("pe", lambda e: e.transpose(out, in_, ident), reads, writes)

    def actv(self, out, in_, func, bias=None, scale=None, accum_out=None, reads=None, writes=None, eng="act"):
        kw = {}
        if bias is not None:
            kw["bias"] = bias
        if scale is not None:
            kw["scale"] = scale
        if accum_out is not None:
            kw["accum_out"] = accum_out
        if reads is None:
            reads = [in_]
            if bias is not None and not isinstance(bias, (int, float)):
                reads.append(bias)
            if scale is not None and not isinstance(scale, (int, float)):
                reads.append(scale)
        if writes is None:
            writes = [out]
            if accum_out is not None:
                writes.append(accum_out)
        self.op("act", lambda e: e.activation(out, in_, func, **kw), reads, writes)

    def ts(self, eng, out, in0, s1, s2, op0, op1=None, accum_out=None, reads=None, writes=None):
        kw = {}
        if op1 is not None:
            kw["op1"] = op1
        if accum_out is not None:
            kw["accum_out"] = accum_out
        if reads is None:
            reads = [in0]
            for s in (s1, s2):
                if s is not None and not isinstance(s, (int, float)):
                    reads.append(s)
        if writes is None:
            writes = [out]
            if accum_out is not None:
                writes.append(accum_out)
        self.op(eng, lambda e: e.tensor_scalar(out, in0, s1, s2, op0, **kw), reads, writes)

    def tt(self, eng, out, in0, in1, op, reads=None, writes=None):
        if reads is None:
            reads = [in0, in1]
        if writes is None:
            writes = [out]
        self.op(eng, lambda e: e.tensor_tensor(out, in0, in1, op), reads, writes)

    def stt(self, eng, out, in0, scalar, in1, op0, op1, accum_out=None, reads=None, writes=None):
        kw = {}
        if accum_out is not None:
            kw["accum_out"] = accum_out
        if reads is None:
            reads = [in0, in1]
            if not isinstance(scalar, (int, float)):
                reads.append(scalar)
        if writes is None:
            writes = [out]
            if accum_out is not None:
                writes.append(accum_out)
        self.op(eng, lambda e: e.scalar_tensor_tensor(out, in0, scalar, in1, op0, op1, **kw), reads, writes)

    def cp(self, eng, out, in_, reads=None, writes=None):
        if reads is None:
            reads = [in_]
        if writes is None:
            writes = [out]
        if eng == "act":
            self.op(eng, lambda e: e.copy(out, in_), reads, writes)
        else:
            self.op(eng, lambda e: e.tensor_copy(out, in_), reads, writes)

    def memset(self, eng, ap, val, writes=None):
        if writes is None:
            writes = [ap]
        self.op(eng, lambda e: e.memset(ap, val), (), writes)

import math
from contextlib import ExitStack
import ml_dtypes

BF = ml_dtypes.bfloat16
NCORES = 8
BIG = 1.0e4
FORCE = 1.0e4
EPS = 1e-6
SCALE = 0.125
QA, KA, QN, KC, KS, KW, VC, VS, VW, GL, VA, ZA, ZB = 0, 256, 512, 1024, 1152, 1280, 1408, 1536, 1664, 1792, 1816, 2328, 2840
EIN = 3352
GROUPS = [(0, 512), (512, 1024), (1024, 1408), (1408, 1816), (1816, 2328), (2328, 2840), (2840, 3352)]


def _perm_even():
    perm = np.zeros(EIN, np.int64)
    perm[QA:QA + 256] = np.arange(0, 256)
    perm[KA:KA + 256] = np.arange(256, 512)
    o_va, o_za, o_qn, o_kvb, o_gl, o_zb = 512, 1024, 1536, 2048, 2816, 2840
    for j in range(4):
        for g in range(2):
            for dd in range(64):
                perm[QN + (j * 2 + g) * 64 + dd] = o_qn + (4 * g + j) * 64 + dd
    for dst, kind in ((KC, 0), (KS, 2), (KW, 4), (VC, 1), (VS, 3), (VW, 5)):
        perm[dst:dst + 128] = o_kvb + kind * 128 + np.arange(128)
    perm[GL:GL + 24] = o_gl + np.arange(24)
    perm[VA:VA + 512] = o_va + np.arange(512)
    perm[ZA:ZA + 512] = o_za + np.arange(512)
    perm[ZB:ZB + 512] = o_zb + np.arange(512)
    assert len(set(perm.tolist())) == EIN
    return perm


def make_consts():
    c = {}
    f32 = np.float32
    c["ident"] = np.eye(128, dtype=f32).astype(BF)
    c["ident32"] = np.eye(128, dtype=f32)
    half = 32
    inv = (np.float32(10000.0) ** (-np.arange(half, dtype=f32) / f32(half))).astype(f32)
    pos_p = (np.arange(16)[None, :] * 128 + np.arange(128)[:, None]).astype(f32)
    ang = (pos_p[:, :, None] * inv[None, None, :]).astype(f32)
    c["cos_p"] = np.cos(ang).astype(f32)
    c["sin_p"] = np.sin(ang).astype(f32)
    pos_s = (2048 + (np.arange(128) % 8)).astype(f32)
    ang = (pos_s[:, None] * inv[None, :]).astype(f32)
    c["cos_s"] = np.cos(ang).astype(f32)
    c["sin_s"] = np.sin(ang).astype(f32)
    log_g = np.log(1.0 - 2.0 ** (-5.0 - np.arange(4, dtype=np.float64)))
    i = np.arange(128)
    diff = i[None, :] - i[:, None]
    caus = diff >= 0
    DT = np.zeros((128, 4, 128), f32)
    for h in range(4):
        DT[:, h, :] = 0.125 * np.exp(np.where(caus, diff, 0) * log_g[h]) * caus
    c["DTp"] = DT
    c["qdec_p"] = np.exp((i[:, None] + 1.0) * log_g[None, :]).astype(f32)
    c["kdec_p"] = (0.125 * np.exp((127.0 - i[:, None]) * log_g[None, :])).astype(f32)
    cd = np.zeros((128, 2), f32)
    cds = np.zeros((128, 2), f32)
    for p in range(128):
        for pr in range(2):
            h = 2 * pr + p // 64
            cd[p, pr] = np.exp(128.0 * log_g[h])
            cds[p, pr] = np.exp(8.0 * log_g[h])
    c["cdec_p"] = cd
    c["cdec_s"] = cds
    i8 = i % 8
    same = (i[:, None] // 8) == (i[None, :] // 8)
    diff8 = i8[None, :] - i8[:, None]
    caus8 = same & (diff8 >= 0)
    DTs = np.zeros((128, 4, 128), f32)
    for h in range(4):
        DTs[:, h, :] = 0.125 * np.exp(np.where(caus8, diff8, 0) * log_g[h]) * caus8
    c["DTs"] = DTs
    c["qdec_s"] = np.exp((i8[:, None] + 1.0) * log_g[None, :]).astype(f32)
    c["kdec_s"] = (0.125 * np.exp((7.0 - i8[:, None]) * log_g[None, :])).astype(f32)
    c["blkmask"] = ((i[:, None] // 8) == np.arange(16)[None, :]).astype(f32)
    cm = np.zeros((128, 16, 128), f32)
    cm[:, :, :] = ((i[None, :] // 8) == np.arange(16)[:, None])[None, :, :]
    c["colmask"] = cm.astype(BF)
    keys = np.arange(2176)
    E = (keys[None, :] // 64 == np.arange(33)[:, None]).astype(f32)
    c["E"] = E.astype(BF)
    c["CB"] = np.where(i[:, None] <= i[None, :], 0.0, -BIG).astype(f32).astype(BF)
    c["AB"] = np.where(i[:, None] > i[None, :], 0.0, -BIG).astype(f32).astype(BF)
    n = np.arange(128)
    cmpb = np.zeros((128, 16, 128), f32)
    for t in range(16):
        tpos = 128 * t + i
        cmpb[:, t, :] = np.where((16 * n[:, None] + 31) <= tpos[None, :], 0.0, -BIG)
    c["CMPB"] = cmpb.astype(BF)

    def overlap(n_s):
        c_start = np.arange(127) * 16
        s_start = np.arange(n_s) * 64
        return ((c_start[:, None] < s_start[None, :] + 64) & (s_start[None, :] < c_start[:, None] + 32)).astype(f32)
    c["ovl_p"] = overlap(32)
    c["ovl_s"] = overlap(33)
    mulc = np.zeros((128, 16, 32), f32)
    addc = np.zeros((128, 16, 32), f32)
    s_ids = np.arange(32)
    for t in range(16):
        tpos = 128 * t + i
        cur = tpos // 64
        forced = (s_ids[None, :] == 0) | (s_ids[None, :] == cur[:, None]) | (s_ids[None, :] == cur[:, None] - 1)
        valid = (s_ids[None, :] * 64) <= tpos[:, None]
        mulc[:, t, :] = (valid & ~forced)
        addc[:, t, :] = np.where(valid, np.where(forced, FORCE, 0.0), -FORCE)
    c["mulc_p"] = mulc
    c["addc_p"] = addc
    s33 = np.arange(33)
    forced = (s33 == 0) | (s33 == 32) | (s33 == 31)
    c["mulc_s"] = np.tile((~forced).astype(f32)[None, :], (8, 1))
    c["addc_s"] = np.tile(np.where(forced, FORCE, 0.0).astype(f32)[None, :], (8, 1))
    q8 = np.arange(8)
    SB = np.where(((i[:, None, None] // 8) == np.arange(16)[None, :, None]) & ((i[:, None, None] % 8) <= q8[None, None, :]), 0.0, -BIG)
    c["SB"] = SB.astype(f32).astype(BF)
    c["ABs"] = np.where(i[:, None] > q8[None, :], 0.0, -BIG).astype(f32).astype(BF)
    c["iota_p"] = i.astype(f32)[:, None].copy()
    c["ones_row"] = np.ones((1, 128), f32).astype(BF)
    c["zeros_row"] = np.zeros((1, 512), f32).astype(BF)
    jv = np.zeros((128, 9, 32), f32)
    jv[:, :, :] = np.arange(9, dtype=f32)[None, :, None]
    c["JV"] = jv
    c["KI"] = np.tile(np.arange(256, dtype=f32)[None, :], (128, 1))
    c["PM4"] = (np.arange(128)[:, None] // 32 == np.arange(4)[None, :]).astype(f32)
    c["BM16"] = (np.arange(128)[:, None] // 16 == np.arange(128)[None, :] // 16).astype(f32)
    return c


def _dt_of(a):
    if a.dtype == np.float32:
        return F32
    if a.dtype == np.int32:
        return I32
    if a.dtype == BF:
        return BF16
    raise ValueError(a.dtype)


def v3(ap, h):
    return ap.rearrange("p (h d) -> p h d", h=h)


def bc_mid(ap, n):
    return ap.unsqueeze(1).to_broadcast([ap.shape[0], n, ap.shape[1]])


def bc_last(ap, n):
    return ap.unsqueeze(2).to_broadcast([ap.shape[0], ap.shape[1], n])


def build_program(consts, phase_b=True, do_sample=True, n_ptiles=16, stage=99):
    nc = bass.Bass("TRN2", target_bir_lowering=False)

    in_names = []

    def din(name, shape, dt=F32):
        in_names.append(name)
        return nc.dram_tensor(name, list(shape), dt, kind="ExternalInput").ap()

    def dout(name, shape, dt=F32):
        return nc.dram_tensor(name, list(shape), dt, kind="ExternalOutput").ap()

    xp = din("xp", [2048, 1024])
    xs = din("xs", [128, 1024])
    if do_sample:
        cache = din("cache", [2560 * 128, 512])
        cwin = din("cwin", [16, 512, 256])
        sret = din("sret", [16, 4, 64, 128])
        ptab = din("ptab", [1, 256], I32)
    w_in_e = din("w_in_e", [1024, EIN])
    w_out_e = din("w_out_e", [1024, 1024])
    norm_e = din("norm_e", [128, 8])
    gn_gain = din("gn_gain", [1, 512])
    qk_gain = din("qk_gain", [1, 896])
    cmp_posT = din("cmp_posT", [64, 2, 32])
    cmp_w = din("cmp_w", [2, 32, 64, 64])
    S_ONLY = ("cos_s", "sin_s", "DTs", "qdec_s", "kdec_s", "cdec_s", "blkmask", "colmask", "SB", "ABs", "mulc_s", "addc_s", "ovl_s", "iota_p")
    if phase_b:
        w_in_o = din("w_in_o", [1024, 2048])
        glu1 = din("glu1", [1024, 1024])
        glu2 = din("glu2", [1024, 1024])
        w_out_o = din("w_out_o", [1024, 1024])
        norm_o = din("norm_o", [128, 8])
        ssmd = din("ssmd", [128, 8])
        lamre_A = din("lamre_A", [128, 32])
        lamim_A = din("lamim_A", [128, 32])
        lstep_A = din("lstep_A", [128, 32])
        bA_re = din("bA_re", [128, 512])
        bA_im = din("bA_im", [128, 512])
        cA_re = din("cA_re", [128, 512])
        cA_im = din("cA_im", [128, 512])
        x0A_re = din("x0A_re", [128, 512])
        x0A_im = din("x0A_im", [128, 512])
    cd = {k: din("c_" + k, v.shape, _dt_of(v)) for k, v in consts.items() if ((do_sample or k not in S_ONLY) and (phase_b or k not in ("JV", "KI", "PM4", "BM16")))}
    yp = dout("yp", [2048, 1024])
    ys = dout("ys", [128, 1024])
    o_retp = dout("o_retp", [4, 64, 128])
    o_rets = dout("o_rets", [16, 4, 64, 128])
    o_kvp = dout("o_kvp", [2048, 512])
    o_kvs = dout("o_kvs", [128, 512])
    o_winp = dout("o_winp", [512, 256])
    o_wins = dout("o_wins", [16, 512, 256])
    if phase_b:
        o_ssp = dout("o_ssp", [2, 128, 32])
        o_sss = dout("o_sss", [2, 128, 512])

    P = Prog(nc)
    with ExitStack() as st0:
        P.setup(st0)

        def alloc(st, name, shape, dt=F32):
            return st.enter_context(nc.sbuf_tensor(name, list(shape), dt))

        def palloc(st, name, shape, dt=F32):
            return st.enter_context(nc.psum_tensor(name, list(shape), dt))

        psT = palloc(st0, "psT", [128, 1024], BF16)
        psA = [palloc(st0, "psA%d" % i, [128, 512]) for i in range(2)]
        psS = [palloc(st0, "psS%d" % i, [128, 512]) for i in range(2)]
        psV = palloc(st0, "psV", [128, 512])
        psC = palloc(st0, "psC", [128, 512])
        psR = palloc(st0, "psR", [128, 512])

        with ExitStack() as stA:
            A = lambda name, shape, dt=F32: alloc(stA, name, shape, dt)
            cs = {}
            P_ONLY = ("cos_p", "sin_p", "DTp", "qdec_p", "kdec_p", "cdec_p", "CMPB", "mulc_p", "addc_p", "ovl_p", "CB", "AB")

            def load_consts(stx, names, pre="k_"):
                for k in names:
                    v = consts[k]
                    shp = list(v.shape)
                    if len(shp) == 3:
                        tl = alloc(stx, pre + k, [shp[0], shp[1] * shp[2]], _dt_of(v))
                        P.dma(tl[:], cd[k].rearrange("p a b -> p (a b)"))
                    else:
                        tl = alloc(stx, pre + k, shp, _dt_of(v))
                        P.dma(tl[:], cd[k])
                    cs[k] = tl
            B_ONLY = ("JV", "KI", "PM4", "BM16")
            load_consts(stA, [k for k in consts if k not in P_ONLY and k not in S_ONLY and k not in B_ONLY])
            ident = cs["ident"]
            Wob = A("Wob", [128, 8 * 1024], BF16)
            ng = A("ng", [128, 8])
            BDW = A("BDW", [128, 2 * 32 * 128], BF16)
            peb = A("peb", [128, 64], BF16)
            posk = A("posk", [128, 2])
            posv = A("posv", [1, 128], BF16)
            gnb = A("gnb", [128, 512])
            P.dma(gnb[:], gn_gain.partition_broadcast(128))
            qkg = A("qkg", [128, 896])
            P.dma(qkg[:], qk_gain.partition_broadcast(128))

            xt = [A("xt%d" % i, [128, 1024]) for i in range(2)]
            xb = A("xb", [128, 1024], BF16)
            xT = A("xT", [128, 1024], BF16)
            stt_ = A("stats", [128, 64])
            proj = A("proj", [128, EIN])
            rp = A("rp", [128, 1408])
            tmpa = A("tmpa", [128, 1408])
            r16 = A("r16", [128, 1024], BF16)
            vb16 = A("vb16", [128, 512], BF16)
            rT = A("rT", [128, 256], BF16)
            qz = A("qz", [128, 1024], BF16)
            qTz = A("qTz", [128, 1024], BF16)
            scm = A("scm", [128, 512], BF16)
            S32 = A("S32", [128, 256])
            Sb = A("Sb", [128, 256], BF16)
            osb = A("osb", [128, 512])
            ocb = A("ocb", [128, 512])
            sz = A("sz", [128, 512])
            mix = A("mix", [128, 1024], BF16)
            mixT = A("mixT", [128, 1024], BF16)
            n16 = A("n16", [128, 1024], BF16)
            gts = A("gts", [128, 24])
            on = A("on", [128, 3 * 512])
            tmpb = on
            obt = A("obt", [128, 512])
            pt_ = [A("pt%d" % i, [128, 512], BF16) for i in range(3)]
            selT = A("selT", [33, 256], BF16)
            sc_ = A("sc", [128, 80])
            sc2 = A("sc2", [128, 80])
            sc16 = A("sc16", [128, 80], BF16)
            rden = A("rden", [128, 32])
            top8 = A("top8", [128, 16])
            P.memset("dve", S32[:], 0.0)
            P.memset("dve", Sb[:], 0.0)
            P.memset("pool", qz[:], 0.0)
            P.memset("pool", qTz[:], 0.0)

            stWb = ExitStack()
            Wb = alloc(stWb, "Wb", [128, 8 * EIN], BF16)
            P.dma(ng[:], norm_e)
            with ExitStack() as stW:
                stg = [alloc(stW, "stg%d" % i, [128, EIN]) for i in range(2)]
                for kc in range(8):
                    s_ = stg[kc % 2]
                    P.dma(s_[:], w_in_e[kc * 128:(kc + 1) * 128, :])
                    if kc % 2 == 0:
                        P.ts("dve", Wb[:, kc * EIN:(kc + 1) * EIN], s_[:], ng[:, kc:kc + 1], None, ALU.mult)
                    else:
                        P.op("act", lambda e, kc=kc, s_=s_: e.mul(Wb[:, kc * EIN:(kc + 1) * EIN], s_[:], ng[:, kc:kc + 1]), [s_, ng], [Wb])
                for kc in range(8):
                    s_ = stg[kc % 2]
                    P.dma(s_[:, 0:1024], w_out_e[kc * 128:(kc + 1) * 128, :])
                    if kc % 2 == 0:
                        P.cp("dve", Wob[:, kc * 1024:(kc + 1) * 1024], s_[:, 0:1024])
                    else:
                        P.cp("pool", Wob[:, kc * 1024:(kc + 1) * 1024], s_[:, 0:1024])
                P.barrier()
            with ExitStack() as stW:
                P.memset("pool", BDW[:], 0.0)
                wst = alloc(stW, "wst", [128, 2 * 32 * 64])
                srcw = cmp_w.rearrange("c l d e -> d (c l) e")
                P.dma(wst[0:64, :].rearrange("p (a e) -> p a e", e=64), srcw)
                P.dma(wst[64:128, :].rearrange("p (a e) -> p a e", e=64), srcw)
                bdv = BDW[:].rearrange("p (a e) -> p a e", e=128)
                P.cp("dve", bdv[0:64, :, 0:64], wst[0:64, :].rearrange("p (a e) -> p a e", e=64))
                P.cp("dve", bdv[64:128, :, 64:128], wst[64:128, :].rearrange("p (a e) -> p a e", e=64))
                pe32 = alloc(stW, "pe32", [128, 64])
                P.dma(pe32[0:64, :], cmp_posT.rearrange("d c l -> d (c l)"))
                P.dma(pe32[64:128, :], cmp_posT.rearrange("d c l -> d (c l)"))
                P.cp("dve", peb[:], pe32[:])
                for l in range(32):
                    P.mm(psC[:, 0:1], BDW[:, (0 * 32 + l) * 128:(0 * 32 + l + 1) * 128], peb[:, l:l + 1], start=(l == 0), stop=(l == 31))
                for l in range(32):
                    P.mm(psC[:, 1:2], BDW[:, (32 + l) * 128:(32 + l + 1) * 128], peb[:, 32 + l:32 + l + 1], start=(l == 0), stop=(l == 31))
                P.cp("dve", posk[:, 0:2], psC[:, 0:2])
                P.barrier()
            cnt = {"x": 0, "pt": 0, "ps": 0, "pa": 0, "ps3": 0}
            xpref = {}

            def rsqrt_small(out, in_, mult, add):
                P.ts("dve", out, in_, mult, add, ALU.mult, ALU.add)
                P.actv(out, out, AF.Sqrt)
                P.op("dve", lambda e: e.reciprocal(out, out), [out], [out])

            def nsa_tile(nq, groups, ns, mulc, addc, cmp_bias, merge, qall=None):
                W = 4 * nq
                wv = 65 + ns
                R = [dict(cmp=psA[0], o=0), dict(cmp=psA[1], o=40)]
                if merge:
                    accT = {2: [psR, psR], 1: [psV, psV]}
                    aoff = [0, W]
                else:
                    accT = {2: [psR, psC], 1: [psV, psA[0]]}
                    aoff = [0, 0]
                back = {2: [psR, psC], 1: [psV, psA[0]]}
                if merge:
                    oTb = [obt[:, 0:256], obt[:, 256:512]]
                else:
                    oTb = [obt[:, :], sz[:, :]]
                v4 = lambda ap: ap.rearrange("p (j q) -> p j q", j=4)
                v24 = lambda ap: ap.rearrange("p (g j q) -> p g j q", g=2, j=4)
                for g, G in enumerate(groups):
                    r = R[g]; o = r["o"]
                    ps_s = psS[cnt["ps"] % 2]; cnt["ps"] += 1
                    P.mm(v4(ps_s[0:127, 0:W]), G["cmp_k"], G["qTg"], start=True, stop=(cmp_bias is None))
                    if cmp_bias is not None:
                        P.mm(v4(ps_s[0:127, 0:W]), ident[0:127, 0:127], cmp_bias, start=False, stop=True)
                    pt = pt_[cnt["pt"] % 3]; cnt["pt"] += 1
                    P.actv(pt[0:127, 0:W], ps_s[0:127, 0:W], AF.Exp, scale=SCALE)
                    pc = r["cmp"]
                    for j in range(4):
                        P.mm(pc[0:nq, j * wv:(j + 1) * wv], pt[0:127, j * nq:(j + 1) * nq], G["cmp_v"], start=True, stop=True)
                    pcv = pc[0:nq, 0:4 * wv].rearrange("p (j w) -> p j w", j=4)
                    rd = rden[0:nq, 16 * g:16 * g + 4]
                    P.ts("dve", rd, pcv[:, :, 64], 1e-30, None, ALU.add)
                    P.op("dve", lambda e, rd=rd: e.reciprocal(rd, rd), [rden], [rden])
                    P.tt("dve", G["on_dst"](0), pcv[:, :, 0:64], bc_last(rd, 64), ALU.mult)
                    sc = sc_[0:nq, o:o + ns]
                    P.ts("dve", sc, pcv[:, 0, 65:65 + ns], rden[0:nq, 16 * g:16 * g + 1], None, ALU.mult)
                    for j in range(1, 4):
                        P.stt("dve", sc, pcv[:, j, 65:65 + ns], rden[0:nq, 16 * g + j:16 * g + j + 1], sc, ALU.mult, ALU.add)
                    P.tt("dve", sc, sc, mulc, ALU.mult)
                    P.tt("dve", sc, sc, addc, ALU.add)
                    t8 = top8[0:nq, 8 * g:8 * g + 8]
                    P.op("dve", lambda e, t8=t8, sc=sc: e.max(t8, sc), [sc_], [top8])
                    P.ts("dve", sc2[0:nq, o:o + ns], sc, top8[0:nq, 8 * g + 7:8 * g + 8], BIG, ALU.is_ge, ALU.mult)
                    P.ts("dve", sc16[0:nq, o:o + ns], sc2[0:nq, o:o + ns], -BIG, None, ALU.add)

                def scores(stp):
                    gs, br, ci, nch, chs = stp["d"]
                    chk = chs[0]
                    (kT, v1, nk, ecols, bias2d) = chk[0:5]
                    kkey = chk[5] if len(chk) > 5 and chk[5] is not None else kT
                    ps_s = (psS[0], psS[1], psA[1])[cnt["ps3"] % 3]; cnt["ps3"] += 1
                    stp["ps"] = ps_s
                    use_e = (br == 1 and ecols is not None)
                    extra = (1 if use_e else 0) + (1 if bias2d is not None else 0)
                    if len(gs) == 2:
                        outv = v24(ps_s[0:nk, 0:2 * W])
                        qr = qall
                        br_ = None if bias2d is None else bias2d.unsqueeze(1).unsqueeze(1).to_broadcast([nk, 2, 4, nq])
                        er_ = selT[0:ns, 0:256].rearrange("p (g q) -> p g q", g=2)[:, :, 0:nq].unsqueeze(2).to_broadcast([ns, 2, 4, nq])
                    else:
                        g = gs[0]
                        outv = v4(ps_s[0:nk, 0:W])
                        qr = groups[g]["qTg"]
                        br_ = None if bias2d is None else bc_mid(bias2d, 4)
                        er_ = bc_mid(selT[0:ns, 128 * g:128 * g + nq], 4)
                    P.mm(outv, kT, qr, start=True, stop=(extra == 0), reads=[kkey, qTz])
                    if bias2d is not None:
                        extra -= 1
                        P.mm(outv, ident[0:nk, 0:nk], br_, start=False, stop=(extra == 0))
                    if use_e:
                        P.mm(outv, ecols, er_, start=False, stop=True)

                def exp_pv(stp):
                    gs, br, ci, nch, chs = stp["d"]
                    nk = chs[0][2]
                    ps_s = stp["ps"]
                    Wt = W * len(gs)
                    pt = pt_[cnt["pt"] % 3]; cnt["pt"] += 1
                    P.actv(pt[0:nk, 0:Wt], ps_s[0:nk, 0:Wt], AF.Exp, scale=SCALE)
                    for k, g in enumerate(gs):
                        chk = chs[k]
                        v1 = chk[1]
                        vkey = chk[6] if len(chk) > 6 and chk[6] is not None else v1
                        acc = accT[br][g]
                        P.mm(acc[0:65, aoff[g]:aoff[g] + W], v1, pt[0:nk, k * W:(k + 1) * W], start=False, stop=(ci == nch - 1 and k == len(gs) - 1), reads=[pt, vkey])

                def run_steps(steps):
                    n = len(steps)
                    D = 2
                    for i in range(min(D, n)):
                        scores(steps[i])
                    for i in range(n):
                        if i + D < n:
                            scores(steps[i + D])
                        exp_pv(steps[i])

                for br, key in ((2, "win_chunks"), (1, "sel_chunks")):
                    if br == 1:
                        for g in range(2):
                            o = R[g]["o"]
                            pc0 = 384 + 512 * g
                            P.tr(psT[0:ns, pc0:pc0 + nq], sc16[0:nq, o:o + ns], ident[0:nq, 0:nq])
                            P.cp("dve", selT[0:ns, 128 * g:128 * g + nq], psT[0:ns, pc0:pc0 + nq])
                    steps = []
                    if merge:
                        acc = accT[br][0]
                        P.mm(acc[0:65, 0:2 * W], cs["zeros_row"][0:1, 0:65], cs["zeros_row"][0:1, 0:2 * W], start=True, stop=False)
                        c0, c1 = groups[0][key], groups[1][key]
                        for ci in range(len(c0)):
                            steps.append({"d": ([0, 1], br, ci, len(c0), [c0[ci], c1[ci]])})
                    else:
                        for g in range(2):
                            acc = accT[br][g]
                            P.mm(acc[0:65, 0:W], cs["zeros_row"][0:1, 0:65], cs["zeros_row"][0:1, 0:W], start=True, stop=False)
                            chunks = groups[g][key]
                            for ci, ch in enumerate(chunks):
                                steps.append({"d": ([g], br, ci, len(chunks), [ch])})
                    run_steps(steps)
                    for g in range(2):
                        acc = accT[br][g]
                        if g == 0:
                            P.cp("act", oTb[g][0:65, 0:W], acc[0:65, aoff[g]:aoff[g] + W])
                        else:
                            P.cp("dve", oTb[g][0:65, 0:W], acc[0:65, aoff[g]:aoff[g] + W])
                    for g in range(2):
                        bk = back[br][g]
                        for j in range(4):
                            P.tr(bk[0:nq, j * 65:(j + 1) * 65], oTb[g][0:65, j * nq:(j + 1) * nq], cs["ident32"][0:65, 0:65])
                        pvv = bk[0:nq, 0:260].rearrange("p (j w) -> p j w", j=4)
                        rd = rden[0:nq, 16 * g + 4 * br:16 * g + 4 * br + 4]
                        P.ts("dve", rd, pvv[:, :, 64], 1e-30, None, ALU.add)
                        P.op("dve", lambda e, rd=rd: e.reciprocal(rd, rd), [rden], [rden])
                        P.tt("dve", groups[g]["on_dst"](br), pvv[:, :, 0:64], bc_last(rd, 64), ALU.mult)

            def even_tile(mode, t, caches):
                isp = (mode == "p")
                xsrc = xp[t * 128:(t + 1) * 128, :] if isp else xs
                if xpref.get("cur") == (mode, t):
                    xtile = xpref["tile"]
                else:
                    xtile = xt[cnt["x"] % 2]; cnt["x"] += 1
                    P.dma(xtile[:], xsrc)
                nxt = caches.get("next")
                if nxt is not None:
                    ntile = xt[cnt["x"] % 2]; cnt["x"] += 1
                    nsrc = xp[nxt[1] * 128:(nxt[1] + 1) * 128, :] if nxt[0] == "p" else xs
                    P.dma(ntile[:], nsrc)
                    xpref["cur"] = nxt
                    xpref["tile"] = ntile
                P.memset("dve", stt_[:, 0:1], 0.0)
                P.actv(mixT[:], xtile[:], AF.Square, accum_out=stt_[:, 0:1])
                rsqrt_small(stt_[:, 1:2], stt_[:, 0:1], 1.0 / 1024, EPS)
                P.cp("act", xb[:], xtile[:])
                for kc in range(8):
                    P.tr(psT[:, kc * 128:(kc + 1) * 128], xb[:, kc * 128:(kc + 1) * 128], ident[:])
                P.cp("act", xT[:], psT[:])
                for gi, (c0, c1) in enumerate(GROUPS):
                    pa = psA[cnt["pa"] % 2]; cnt["pa"] += 1
                    w = c1 - c0
                    for kc in range(8):
                        P.mm(pa[:, 0:w], xT[:, kc * 128:(kc + 1) * 128], Wb[:, kc * EIN + c0:kc * EIN + c1], start=(kc == 0), stop=(kc == 7))
                    if gi % 2 == 0:
                        P.ts("dve", proj[:, c0:c1], pa[:, 0:w], stt_[:, 1:2], None, ALU.mult)
                    else:
                        P.op("act", lambda e, c0=c0, c1=c1, pa=pa, w=w: e.mul(proj[:, c0:c1], pa[:, 0:w], stt_[:, 1:2]), [pa, stt_], [proj])
                if not isp:
                    caches["hook"]()
                if stage <= 1:
                    return
                nv = v3(proj[:, QN:QN + 896], 14)
                P.tt("dve", tmpa[:, 0:896], proj[:, QN:QN + 896], proj[:, QN:QN + 896], ALU.mult)
                P.op("dve", lambda e: e.reduce_sum(stt_[:, 8:22], v3(tmpa[:, 0:896], 14), AX.X), [tmpa], [stt_])
                rsqrt_small(stt_[:, 8:22], stt_[:, 8:22], 1.0 / 64, EPS)
                P.tt("dve", nv, nv, bc_last(stt_[:, 8:22], 64), ALU.mult)
                P.tt("dve", proj[:, QN:QN + 896], proj[:, QN:QN + 896], qkg[:], ALU.mult)
                if isp:
                    cos = cs["cos_p"][:, t * 32:(t + 1) * 32]
                    sin = cs["sin_p"][:, t * 32:(t + 1) * 32]
                else:
                    cos = cs["cos_s"][:, :]
                    sin = cs["sin_s"][:, :]
                pv = v3(proj[:, 0:1408], 22)
                rv = v3(rp[:, 0:1408], 22)
                ta = tmpa[:, 0:704].rearrange("p (h d) -> p h d", h=22)
                tb = tmpa[:, 704:1408].rearrange("p (h d) -> p h d", h=22)
                tc_ = tmpb[:, 0:704].rearrange("p (h d) -> p h d", h=22)
                td_ = tmpb[:, 704:1408].rearrange("p (h d) -> p h d", h=22)
                cosb = bc_mid(cos, 22)
                sinb = bc_mid(sin, 22)
                P.tt("dve", ta, pv[:, :, 0:32], cosb, ALU.mult)
                P.tt("dve", tb, pv[:, :, 32:64], sinb, ALU.mult)
                P.tt("dve", rv[:, :, 0:32], ta, tb, ALU.subtract)
                P.tt("pool", tc_, pv[:, :, 32:64], cosb, ALU.mult)
                P.tt("pool", td_, pv[:, :, 0:32], sinb, ALU.mult)
                P.tt("pool", rv[:, :, 32:64], tc_, td_, ALU.add)
                if stage <= 2:
                    return
                if isp:
                    okv = o_kvp[t * 128:(t + 1) * 128, :]
                else:
                    okv = o_kvs
                P.dma(okv[:, 0:128], rp[:, KC:KC + 128])
                P.dma(okv[:, 128:256], proj[:, VC:VC + 128])
                P.dma(okv[:, 256:384], rp[:, KS:KS + 128])
                P.dma(okv[:, 384:512], proj[:, VS:VS + 128])
                if isp and t >= 12:
                    ow = o_winp[(t - 12) * 128:(t - 11) * 128, :]
                    P.dma(ow[:, 0:128], rp[:, KW:KW + 128])
                    P.dma(ow[:, 128:256], proj[:, VW:VW + 128])
                if not isp:
                    for b in range(16):
                        P.dma(o_wins[b, 504:512, 0:128], rp[b * 8:(b + 1) * 8, KW:KW + 128])
                        P.dma(o_wins[b, 504:512, 128:256], proj[b * 8:(b + 1) * 8, VW:VW + 128])
                        P.dma(o_wins[b, 0:504, :], cwin[b, 8:512, :])
                if stage <= 3:
                    return
                qdec = cs["qdec_p"] if isp else cs["qdec_s"]
                kdec = cs["kdec_p"] if isp else cs["kdec_s"]
                DTm = cs["DTp"] if isp else cs["DTs"]
                P.cp("act", r16[:, 0:256], rp[:, QA:QA + 256])
                P.tt("dve", v3(r16[:, 256:512], 4), v3(rp[:, QA:QA + 256], 4), bc_last(qdec[:, 0:4], 64), ALU.mult)
                P.cp("act", r16[:, 512:768], rp[:, KA:KA + 256])
                P.tt("dve", v3(r16[:, 768:1024], 4), v3(rp[:, KA:KA + 256], 4), bc_last(kdec[:, 0:4], 64), ALU.mult)
                P.cp("act", vb16[:], proj[:, VA:VA + 512])
                for i6 in range(6):
                    P.tr(psT[:, i6 * 128:(i6 + 1) * 128], r16[:, i6 * 128:(i6 + 1) * 128], ident[:])
                P.cp("act", rT[:, 0:256], psT[:, 512:768])
                qzv = qz[:].rearrange("p (a h q) -> p a h q", a=4, h=2)
                P.cp("act", qzv[0:64, :, 0, :], psT[0:64, 0:512].rearrange("p (a q) -> p a q", a=4))
                P.cp("dve", qzv[64:128, :, 1, :], psT[64:128, 0:512].rearrange("p (a q) -> p a q", a=4))
                if stage <= 3.1:
                    return
                for h in range(4):
                    hp, pr = h % 2, h // 2
                    P.mm(psR[:, h * 128:(h + 1) * 128], rT[:, pr * 128:(pr + 1) * 128],
                         qz[:, ((0 * 2 + pr) * 2 + hp) * 128:((0 * 2 + pr) * 2 + hp + 1) * 128])
                if stage <= 3.2:
                    return
                P.tt("dve", scm[:], psR[:], DTm[:], ALU.mult)
                if isp:
                    for h in range(4):
                        hp, pr = h % 2, h // 2
                        P.mm(psC[:, h * 128:(h + 1) * 128], scm[:, h * 128:(h + 1) * 128], vb16[:, h * 128:(h + 1) * 128], start=True, stop=False)
                        P.mm(psC[:, h * 128:(h + 1) * 128], qz[:, ((1 * 2 + pr) * 2 + hp) * 128:((1 * 2 + pr) * 2 + hp + 1) * 128],
                             Sb[:, pr * 128:(pr + 1) * 128], start=False, stop=True)
                else:
                    S0b = caches["S0b"]
                    qdTm = caches["qdTm"]
                    for h in range(4):
                        hp, pr = h % 2, h // 2
                        if hp == 0:
                            for b in range(16):
                                P.tt("dve" if b % 2 == 0 else "pool", qdTm[:, b * 256:(b + 1) * 256].rearrange("p (a q) -> p a q", a=2),
                                     qz[:, 512 + pr * 256:512 + (pr + 1) * 256].rearrange("p (a q) -> p a q", a=2),
                                     bc_mid(cs["colmask"][:, b * 128:(b + 1) * 128], 2), ALU.mult)
                        P.mm(psC[:, h * 128:(h + 1) * 128], scm[:, h * 128:(h + 1) * 128], vb16[:, h * 128:(h + 1) * 128], start=True, stop=False)
                        for b in range(16):
                            P.mm(psC[:, h * 128:(h + 1) * 128], qdTm[:, b * 256 + hp * 128:b * 256 + (hp + 1) * 128],
                                 S0b[:, (b * 2 + pr) * 128:(b * 2 + pr + 1) * 128], start=False, stop=(b == 15))
                if stage <= 3.4:
                    return
                P.cp("act", osb[:], psC[:])
                if stage <= 3.5:
                    return
                if isp:
                    for h in range(4):
                        hp, pr = h % 2, h // 2
                        P.mm(psR[:, h * 128:(h + 1) * 128], r16[:, 768 + pr * 128:768 + (pr + 1) * 128], vb16[:, h * 128:(h + 1) * 128])
                    for h in range(4):
                        hp, pr = h % 2, h // 2
                        rows = slice(hp * 64, (hp + 1) * 64)
                        P.stt("dve", S32[rows, pr * 128:(pr + 1) * 128], S32[rows, pr * 128:(pr + 1) * 128], cs["cdec_p"][rows, pr:pr + 1],
                              psR[rows, h * 128:(h + 1) * 128], ALU.mult, ALU.add)
                    P.cp("act", Sb[:], S32[:])
                    if t == n_ptiles - 1:
                        for h in range(4):
                            hp, pr = h % 2, h // 2
                            P.dma(o_retp[h, :, :], S32[hp * 64:(hp + 1) * 64, pr * 128:(pr + 1) * 128])
                else:
                    S0 = caches["S0"]
                    vblk = caches["vblk"]
                    Sn = caches["Sn"]
                    for h in range(4):
                        hp, pr = h % 2, h // 2
                        rows = slice(hp * 64, (hp + 1) * 64)
                        P.tt("dve", vblk[:].rearrange("p (b e) -> p b e", b=16), bc_mid(vb16[:, h * 128:(h + 1) * 128], 16),
                             bc_last(cs["blkmask"][:, 0:16], 128), ALU.mult)
                        for q4 in range(4):
                            pa = psA[cnt["pa"] % 2]; cnt["pa"] += 1
                            P.mm(pa[:, :], r16[:, 768 + pr * 128:768 + (pr + 1) * 128], vblk[:, q4 * 512:(q4 + 1) * 512])
                            s0v = S0[rows, :].rearrange("p (b a e) -> p b a e", b=16, a=2)[:, q4 * 4:(q4 + 1) * 4, pr, :]
                            P.stt("dve", Sn[rows, q4 * 512:(q4 + 1) * 512].rearrange("p (b e) -> p b e", b=4), s0v, cs["cdec_s"][rows, pr:pr + 1],
                                  pa[rows, :].rearrange("p (b e) -> p b e", b=4), ALU.mult, ALU.add)
                        P.dma(o_rets[:, h, :, :].rearrange("b d e -> d b e"), Sn[rows, :].rearrange("p (b e) -> p b e", b=16))
                if stage <= 4:
                    return
                P.op("dve", lambda e: e.reduce_sum(stt_[:, 24:28], v3(osb[:], 4), AX.X), [osb], [stt_])
                P.ts("dve", stt_[:, 24:28], stt_[:, 24:28], -1.0 / 128, None, ALU.mult)
                P.tt("dve", v3(ocb[:], 4), v3(osb[:], 4), bc_last(stt_[:, 24:28], 128), ALU.add)
                P.tt("dve", osb[:], ocb[:], ocb[:], ALU.mult)
                P.op("dve", lambda e: e.reduce_sum(stt_[:, 28:32], v3(osb[:], 4), AX.X), [osb], [stt_])
                rsqrt_small(stt_[:, 28:32], stt_[:, 28:32], 1.0 / 128, EPS)
                P.tt("dve", v3(ocb[:], 4), v3(ocb[:], 4), bc_last(stt_[:, 28:32], 128), ALU.mult)
                P.tt("dve", ocb[:], ocb[:], gnb[:], ALU.mult)
                P.actv(sz[:], proj[:, ZA:ZA + 512], AF.Silu)
                P.tt("dve", mix[:, 0:512], ocb[:], sz[:], ALU.mult)
                if stage <= 5:
                    return
                P.actv(gts[:], proj[:, GL:GL + 24], AF.Sigmoid)
                P.cp("act", n16[:, 0:896], rp[:, QN:QN + 896])
                P.cp("dve", n16[:, 896:1024], proj[:, VC:VC + 128])
                for i4 in range(4):
                    P.tr(psT[:, i4 * 128:(i4 + 1) * 128], n16[:, i4 * 128:(i4 + 1) * 128], ident[:])
                P.cp("act", qTz[0:64, 0:512], psT[0:64, 0:512])
                P.cp("dve", qTz[64:128, 512:1024], psT[64:128, 0:512])
                for i4 in range(4):
                    P.tr(psT[:, i4 * 128:(i4 + 1) * 128], n16[:, 512 + i4 * 128:512 + (i4 + 1) * 128], ident[:])
                if isp:
                    cT = caches["cT"]; vs1 = caches["vs1"]; vw1 = caches["vw1"]
                    P.cp("act", cT[:].rearrange("p (k n) -> p k n", k=4)[:, :, t * 128:(t + 1) * 128], psT[:, 0:512].rearrange("p (k n) -> p k n", k=4))
                    P.cp("dve", vs1[:].rearrange("p (c g w) -> p c g w", c=16, g=2)[:, t, :, 0:64], v3(proj[:, VS:VS + 128], 2))
                    P.cp("dve", vw1[:].rearrange("p (c g w) -> p c g w", c=16, g=2)[:, t, :, 0:64], v3(proj[:, VW:VW + 128], 2))
                    ckT = caches["ckT"]; cvx = caches["cvx"]
                    compress(cT, 0, cT, 3 * 2048, ckT, caches["cvT"], cvx, 97, max(0, 8 * t - 1), 8 * t + 6)
                    groups = []
                    for g in range(2):
                        qTg = qTz[:, g * 512:(g + 1) * 512].rearrange("p (j q) -> p j q", j=4)
                        selc = []
                        for c in range(t + 1):
                            bias = cs["CB"][:, :] if c == t else None
                            selc.append((cT[:, 2048 + c * 128:2048 + (c + 1) * 128], vs1[:, (c * 2 + g) * 65:(c * 2 + g + 1) * 65], 128,
                                         cs["E"][0:32, c * 128:(c + 1) * 128], bias))
                        winc = []
                        for c in range(max(0, t - 4), t + 1):
                            bias = cs["CB"][:, :] if c == t else (cs["AB"][:, :] if c == t - 4 else None)
                            winc.append((cT[:, 4096 + c * 128:4096 + (c + 1) * 128], vw1[:, (c * 2 + g) * 65:(c * 2 + g + 1) * 65], 128, None, bias))
                        groups.append(dict(qTg=qTg, cmp_k=ckT[:, 0:127], cmp_v=cvx[0:127, g * 97:(g + 1) * 97], sel_chunks=selc, win_chunks=winc,
                                           on_dst=lambda x, g=g: on[:, x * 512 + g * 256:x * 512 + (g + 1) * 256].rearrange("p (j d) -> p j d", j=4)))
                    cmpb = bc_mid(cs["CMPB"][0:127, t * 128:(t + 1) * 128], 4)
                    nsa_tile(128, groups, 32, cs["mulc_p"][:, t * 32:(t + 1) * 32], cs["addc_p"][:, t * 32:(t + 1) * 32], cmpb, False)
                else:
                    cTs = caches["cTs"]; vs1s = caches["vs1s"]
                    P.cp("act", cTs[:], psT[:, 0:512])
                    P.cp("dve", vs1s[:].rearrange("p (k g w) -> p k g w", k=2, g=2)[:, 0, :, 0:64], v3(proj[:, VS:VS + 128], 2))
                    P.cp("dve", vs1s[:].rearrange("p (k g w) -> p k g w", k=2, g=2)[:, 1, :, 0:64], v3(proj[:, VW:VW + 128], 2))
                    sample_nsa(caches)
                if stage <= 6:
                    return
                gv = gts[:].rearrange("p (h x) -> p h x", h=8)
                for x in range(3):
                    P.tt("dve", v3(on[:, x * 512:(x + 1) * 512], 8), v3(on[:, x * 512:(x + 1) * 512], 8), bc_last(gv[:, :, x], 64), ALU.mult)
                P.tt("dve", obt[:], on[:, 0:512], on[:, 512:1024], ALU.add)
                P.tt("dve", obt[:], obt[:], on[:, 1024:1536], ALU.add)
                P.actv(sz[:], proj[:, ZB:ZB + 512], AF.Silu)
                P.tt("dve", mix[:, 512:1024], obt[:], sz[:], ALU.mult)
                if stage <= 7:
                    return
                for kc in range(8):
                    P.tr(psT[:, kc * 128:(kc + 1) * 128], mix[:, kc * 128:(kc + 1) * 128], ident[:])
                P.cp("act", mixT[:], psT[:])
                for hf in range(2):
                    pa = psA[cnt["pa"] % 2]; cnt["pa"] += 1
                    for kc in range(8):
                        P.mm(pa[:, :], mixT[:, kc * 128:(kc + 1) * 128], Wob[:, kc * 1024 + hf * 512:kc * 1024 + (hf + 1) * 512], start=(kc == 0), stop=(kc == 7))
                    P.tt("dve", xtile[:, hf * 512:(hf + 1) * 512], xtile[:, hf * 512:(hf + 1) * 512], pa[:, :], ALU.add)
                ydst = yp[t * 128:(t + 1) * 128, :] if isp else ys
                P.dma(ydst, xtile[:], writes=[("y0", mode, t)])

            def compress(kc_t, kc_off, vc_t, vc_off, ckT, cvT, cvx, wv, n0, n1, rkeys=None):
                nn = n1 - n0 + 1
                rd = ([BDW] + rkeys) if rkeys else None
                for l in range(32):
                    a0 = kc_off + 16 * n0 + l
                    P.mm(psC[:, 0:nn], BDW[:, l * 128:(l + 1) * 128], kc_t[:, a0:a0 + 16 * (nn - 1) + 1:16], start=(l == 0), stop=(l == 31), reads=rd)
                for l in range(32):
                    a0 = vc_off + 16 * n0 + l
                    P.mm(psC[:, 128:128 + nn], BDW[:, (32 + l) * 128:(33 + l) * 128], vc_t[:, a0:a0 + 16 * (nn - 1) + 1:16], start=(l == 0), stop=(l == 31), reads=rd)
                P.ts("dve", ckT[:, n0:n1 + 1], psC[:, 0:nn], posk[:, 0:1], None, ALU.add)
                P.ts("dve", cvT[:, n0:n1 + 1], psC[:, 128:128 + nn], posk[:, 1:2], None, ALU.add)
                P.tr(psT[0:127, 0:128], cvT[:, 0:127], ident[:])
                P.cp("act", cvx[0:127, :].rearrange("p (g w) -> p g w", g=2)[:, :, 0:64], psT[0:127, 0:128].rearrange("p (g d) -> p g d", g=2))

            def sample_nsa(caches):
                C = caches
                cTq, kwTq, vs1q, vw1q, ckTq, cvxq = C["cTq"], C["kwTq"], C["vs1q"], C["vw1q"], C["ckTq"], C["cvxq"]
                cTs, vs1s, idx = C["cTs"], C["vs1s"], C["idx"]
                on8 = [osb, ocb, sz]
                wst32, w16 = C["wst32"], C["w16"]
                P.barrier()
                C["stR"].close()
                stPG = C["stPG"]
                pgb = [alloc(stPG, "pgb%d" % i, [128, 16 * 384], BF16) for i in range(2)]
                pg = [alloc(stPG, "pgf%d" % i, [128, 512]) for i in range(6)]
                for b in range(16):
                    pb = pgb[b % 2]
                    for i in range(16):
                        pgt = pg[(b * 16 + i) % 6]
                        col = b * 16 + i
                        P.dma(pgt[:], cache, reads=[cache, idx], q="pool",
                              fn=lambda e, pgt=pgt, col=col: e.indirect_dma_start(
                                  out=pgt[:, :], out_offset=None, in_=cache[:, :],
                                  in_offset=bass.IndirectOffsetOnAxis(ap=idx[:, col:col + 1], axis=0)))
                        P.cp("act", pb[:, i * 384:(i + 1) * 384], pgt[:, 0:384], writes=[("pgb", b % 2, i)])
                        P.cp("dve", vs1q[:].rearrange("p (c g w) -> p c g w", c=16, g=2)[:, i, :, 0:64], v3(pgt[:, 384:512], 2), writes=[("vs1q", i)])
                    psRb = psR[:].bitcast(BF16)
                    for i in range(16):
                        pdst = psT[:, 0:384] if i % 2 == 0 else psRb[:, 0:384]
                        pkey = psT if i % 2 == 0 else psR
                        for k in range(3):
                            P.tr(pdst[:, k * 128:(k + 1) * 128], pb[:, i * 384 + k * 128:i * 384 + (k + 1) * 128], ident[:],
                                 reads=[("pgb", b % 2, i), ident], writes=[pkey])
                        P.cp("act" if i % 2 == 0 else "dve", cTq[:].rearrange("p (k n) -> p k n", k=3)[:, :, i * 128:(i + 1) * 128],
                             pdst.rearrange("p (k n) -> p k n", k=3), reads=[pkey], writes=[("cTq", i)])
                    P.dma(wst32[:].rearrange("p (c w) -> p c w", c=4), cwin[b].rearrange("(c r) w -> r c w", r=128))
                    P.cp("dve", w16[:], wst32[:])
                    for c in range(4):
                        P.tr(psT[:, 512 + c * 128:512 + (c + 1) * 128], w16[:, c * 256:c * 256 + 128], ident[:])
                    P.cp("act", kwTq[:], psT[:, 512:1024])
                    for c in range(4):
                        P.cp("dve", vw1q[:, c * 130:(c + 1) * 130].rearrange("p (g w) -> p g w", g=2)[:, :, 0:64], v3(w16[:, c * 256 + 128:(c + 1) * 256], 2))
                    compress(cTq, 0, cTq, 2048, ckTq, C["cvTq"], cvxq, 98, 0, 126, rkeys=[("cTq", i) for i in range(16)])
                    groups = []
                    for g in range(2):
                        qTg = qTz[:, g * 512:(g + 1) * 512].rearrange("p (j q) -> p j q", j=4)[:, :, 8 * b:8 * b + 8]
                        sbias = cs["SB"][:, b * 8:(b + 1) * 8]
                        selc = []
                        for c in range(16):
                            selc.append((cTq[:, 4096 + c * 128:4096 + (c + 1) * 128], vs1q[:, (c * 2 + g) * 65:(c * 2 + g + 1) * 65], 128,
                                         cs["E"][0:33, c * 128:(c + 1) * 128], None, ("cTq", c), ("vs1q", c)))
                        selc.append((cTs[:, 128:256], vs1s[:, (0 * 2 + g) * 65:(0 * 2 + g + 1) * 65], 128, None, sbias))
                        winc = []
                        for c in range(4):
                            bias = cs["ABs"][:, :] if c == 0 else None
                            winc.append((kwTq[:, c * 128:(c + 1) * 128], vw1q[:, (c * 2 + g) * 65:(c * 2 + g + 1) * 65], 128, None, bias))
                        winc.append((cTs[:, 256:384], vs1s[:, (1 * 2 + g) * 65:(1 * 2 + g + 1) * 65], 128, None, sbias))
                        groups.append(dict(qTg=qTg, cmp_k=ckTq[:, 0:127], cmp_v=cvxq[0:127, g * 98:(g + 1) * 98], sel_chunks=selc, win_chunks=winc,
                                           on_dst=lambda x, g=g: on8[x][0:8, g * 256:(g + 1) * 256].rearrange("p (j d) -> p j d", j=4)))
                    qall = qTz[:, :].rearrange("p (g j q) -> p g j q", g=2, j=4)[:, :, :, 8 * b:8 * b + 8]
                    nsa_tile(8, groups, 33, cs["mulc_s"][:, :], cs["addc_s"][:, :], None, True, qall)
                    for x in range(3):
                        P.dma(on[b * 8:(b + 1) * 8, x * 512:(x + 1) * 512], on8[x][0:8, :])
                P.barrier()

            with ExitStack() as stP:
                Ap = lambda name, shape, dt=F32: alloc(stP, name, shape, dt)
                load_consts(stP, P_ONLY)
                cT = Ap("cT", [128, 4 * 2048], BF16)
                vs1 = Ap("vs1", [128, 16 * 2 * 65], BF16)
                vw1 = Ap("vw1", [128, 16 * 2 * 65], BF16)
                ckT = Ap("ckT", [128, 128], BF16)
                cvT = Ap("cvT", [128, 128], BF16)
                cvx = Ap("cvx", [128, 2 * 97], BF16)
                ov32 = Ap("ov32", [128, 32])
                P.memset("pool", cT[:], 0.0)
                P.memset("pool", vs1[:], 1.0)
                P.memset("pool", vw1[:], 1.0)
                P.memset("pool", cvx[:], 1.0)
                P.memset("pool", ckT[:], 0.0)
                P.memset("pool", cvT[:], 0.0)
                for g in range(2):
                    P.cp("dve", cvx[0:127, g * 97 + 65:(g + 1) * 97], cs["ovl_p"][0:127, :])
                caches = {"cT": cT, "vs1": vs1, "vw1": vw1, "ckT": ckT, "cvx": cvx, "cvT": cvT}
                for t in range(n_ptiles):
                    caches["next"] = ("p", t + 1) if t + 1 < n_ptiles else (("s", 0) if do_sample else None)
                    even_tile("p", t, caches)
                P.barrier()

            if do_sample:
                stS = ExitStack()
                stPG = ExitStack()
                scaches = {}

                def sample_hook():
                    P.barrier()
                    stWb.close()
                    As = lambda name, shape, dt=F32: alloc(stS, name, shape, dt)
                    load_consts(stS, S_ONLY)
                    C = scaches
                    C["cTq"] = As("cTq", [128, 3 * 2048], BF16)
                    C["kwTq"] = As("kwTq", [128, 512], BF16)
                    C["vs1q"] = As("vs1q", [128, 16 * 130], BF16)
                    C["vw1q"] = As("vw1q", [128, 4 * 130], BF16)
                    C["ckTq"] = As("ckTq", [128, 128], BF16)
                    C["cvTq"] = As("cvTq", [128, 128], BF16)
                    C["cvxq"] = As("cvxq", [128, 2 * 98], BF16)
                    C["cTs"] = As("cTs", [128, 512], BF16)
                    C["vs1s"] = As("vs1s", [128, 4 * 65], BF16)
                    C["wst32"] = As("wst32", [128, 1024])
                    C["w16"] = As("w16", [128, 1024], BF16)
                    C["idx"] = As("idx", [128, 256], I32)
                    pti = As("pti", [128, 256], I32)
                    stR = ExitStack()
                    C["stR"] = stR
                    C["stPG"] = stPG
                    Ar = lambda name, shape, dt=F32: alloc(stR, name, shape, dt)
                    C["S0"] = Ar("S0", [128, 4096])
                    C["S0b"] = Ar("S0b", [128, 4096], BF16)
                    C["qdTm"] = Ar("qdTm", [128, 16 * 256], BF16)
                    C["vblk"] = Ar("vblk", [128, 2048], BF16)
                    C["Sn"] = Ar("Sn", [128, 2048])
                    ptf = tmpa[:, 0:256]
                    P.memset("pool", C["vs1q"][:], 1.0)
                    P.memset("pool", C["vw1q"][:], 1.0)
                    P.memset("pool", C["vs1s"][:], 1.0)
                    P.memset("pool", C["cvxq"][:], 1.0)
                    for g in range(2):
                        P.cp("dve", C["cvxq"][0:127, g * 98 + 65:(g + 1) * 98], cs["ovl_s"][0:127, :])
                    for hp in range(2):
                        P.dma(C["S0"][hp * 64:(hp + 1) * 64, :].rearrange("p (b a e) -> p b a e", b=16, a=2),
                              sret[:, hp::2, :, :].rearrange("b a d e -> d b a e"))
                    P.cp("dve", C["S0b"][:], C["S0"][:])
                    P.dma(pti[:], ptab.partition_broadcast(128))
                    P.cp("dve", ptf, pti[:])
                    P.ts("dve", ptf, ptf, 128.0, cs["iota_p"][:, 0:1], ALU.mult, ALU.add)
                    P.cp("dve", C["idx"][:], ptf)

                scaches["hook"] = sample_hook
                even_tile("s", 0, scaches)
                P.barrier()
                stPG.close()
                stS.close()
            else:
                stWb.close()
            P.barrier()
        if phase_b:
            NT = 2176
            NCH = 272
            TWO_PI = 2.0 * math.pi
            with ExitStack() as stB:
                Bf = lambda name, shape, dt=F32: alloc(stB, name, shape, dt)
                load_consts(stB, ["ident", "JV", "KI", "PM4", "BM16"], pre="kb_")
                ident = cs["ident"]
                uT = Bf("uT", [128, 8 * NT], BF16)
                szT = Bf("szT", [128, 8 * NT], BF16)
                ngo = Bf("ngo", [128, 8])
                dsk = Bf("dsk", [128, 8])
                statb = Bf("statb", [128, 8])
                FSp = Bf("FSp", [128, 64])
                FSs = Bf("FSs", [128, 1024])
                P.dma(ngo[:], norm_o)
                P.dma(dsk[:], ssmd)
                cntb = {"x": 0, "pa": 0}

                def load_tile(dst, ti):
                    if ti < 16:
                        P.dma(dst, yp[ti * 128:(ti + 1) * 128, :], reads=[("y0", "p", ti)])
                    else:
                        P.dma(dst, ys, reads=[("y0", "s", 0)])

                with ExitStack() as st1:
                    B1 = lambda name, shape, dt=F32: alloc(st1, name, shape, dt)
                    Wodd = B1("Wodd", [128, 8 * 2048], BF16)
                    yt = [B1("yt%d" % i, [128, 1024]) for i in range(2)]
                    hb = B1("hb", [128, 1024], BF16)
                    hT = B1("hT", [128, 8 * 512], BF16)
                    with ExitStack() as stg_:
                        stg = [alloc(stg_, "stgb%d" % i, [128, 2048]) for i in range(2)]
                        for kc in range(8):
                            s_ = stg[kc % 2]
                            P.dma(s_[:], w_in_o[kc * 128:(kc + 1) * 128, :])
                            if kc % 2 == 0:
                                P.ts("dve", Wodd[:, kc * 2048:(kc + 1) * 2048], s_[:], ngo[:, kc:kc + 1], None, ALU.mult)
                            else:
                                P.op("act", lambda e, kc=kc, s_=s_: e.mul(Wodd[:, kc * 2048:(kc + 1) * 2048], s_[:], ngo[:, kc:kc + 1]), [s_, ngo], [Wodd])
                        P.barrier()
                    blocks = [(0, 4), (4, 8), (8, 12), (12, 16), (16, 17)]
                    for (t0, t1) in blocks:
                        nb = (t1 - t0) * 128
                        col0 = t0 * 128
                        for ti in range(t0, t1):
                            ytile = yt[cntb["x"] % 2]; cntb["x"] += 1
                            load_tile(ytile[:], ti)
                            P.memset("dve", statb[:, 0:1], 0.0)
                            P.actv(hb[:], ytile[:], AF.Square, accum_out=statb[:, 0:1])
                            P.ts("dve", statb[:, 1:2], statb[:, 0:1], 1.0 / 1024, EPS, ALU.mult, ALU.add)
                            P.actv(statb[:, 1:2], statb[:, 1:2], AF.Sqrt)
                            P.op("dve", lambda e: e.reciprocal(statb[:, 1:2], statb[:, 1:2]), [statb], [statb])
                            P.ts("dve", hb[:], ytile[:], statb[:, 1:2], None, ALU.mult)
                            for kc in range(8):
                                P.tr(psT[:, kc * 128:(kc + 1) * 128], hb[:, kc * 128:(kc + 1) * 128], ident[:])
                            lt = ti - t0
                            P.cp("act", hT[:].rearrange("p (k n) -> p k n", k=8)[:, :, lt * 128:(lt + 1) * 128], psT[:].rearrange("p (k n) -> p k n", k=8))
                        for oc in range(16):
                            pa = psA[cntb["pa"] % 2]; cntb["pa"] += 1
                            for kc in range(8):
                                P.mm(pa[:, 0:nb], Wodd[:, kc * 2048 + oc * 128:kc * 2048 + (oc + 1) * 128], hT[:, kc * 512:kc * 512 + nb], start=(kc == 0), stop=(kc == 7))
                            if oc < 8:
                                P.cp("dve", uT[:, oc * NT + col0:oc * NT + col0 + nb], pa[:, 0:nb])
                            else:
                                P.actv(szT[:, (oc - 8) * NT + col0:(oc - 8) * NT + col0 + nb], pa[:, 0:nb], AF.Silu)
                    P.barrier()

                with ExitStack() as st2:
                    B2 = lambda name, shape, dt=F32: alloc(st2, name, shape, dt)
                    lr = B2("lr", [128, 32]); li = B2("li", [128, 32]); ls = B2("ls", [128, 32])
                    bre = B2("bre", [128, 512]); bim = B2("bim", [128, 512])
                    cre = B2("cre", [128, 512]); cim = B2("cim", [128, 512])
                    x0r = B2("x0r", [128, 512]); x0i = B2("x0i", [128, 512])
                    for tl, src in ((lr, lamre_A), (li, lamim_A), (ls, lstep_A), (bre, bA_re), (bim, bA_im), (cre, cA_re), (cim, cA_im), (x0r, x0A_re), (x0i, x0A_im)):
                        P.dma(tl[:], src)
                    aa = B2("aa", [128, 32]); th = B2("th", [128, 32])
                    A9 = B2("A9", [128, 288]); T9 = B2("T9", [128, 288]); T9c = B2("T9c", [128, 288])
                    PR = B2("PR", [128, 288]); PI = B2("PI", [128, 288])
                    rri = B2("rri", [128, 1024], I32); rri2 = B2("rri2", [128, 1024], I32)
                    npi = B2("npi", [128, 1])

                    hpi = B2("hpi", [128, 1])
                    P.memset("dve", hpi[:], math.pi / 2)

                    def range_reduce(x, n):
                        P.ts("dve", rri[:, 0:n], x, 1.0 / TWO_PI, None, ALU.mult)
                        P.stt("dve", x, rri[:, 0:n], -TWO_PI, x, ALU.mult, ALU.add)

                    def sincos(x, n, sin_out, cos_out):
                        P.ts("dve", rri[:, 0:n], x, 1.0 / TWO_PI, None, ALU.mult)
                        P.stt("dve", sin_out, rri[:, 0:n], -TWO_PI, x, ALU.mult, ALU.add)
                        P.actv(sin_out, sin_out, AF.Sin)
                        P.ts("dve", rri2[:, 0:n], x, 1.0 / TWO_PI, 0.25, ALU.mult, ALU.add)
                        P.stt("dve", cos_out, rri2[:, 0:n], -TWO_PI, x, ALU.mult, ALU.add)
                        P.actv(cos_out, cos_out, AF.Sin, bias=hpi[:, 0:1])

                    P.actv(ls[:], ls[:], AF.Exp)
                    P.tt("dve", aa[:], lr[:], ls[:], ALU.mult)
                    P.tt("dve", th[:], li[:], ls[:], ALU.mult)
                    JV = cs["JV"]
                    P.tt("dve", A9[:].rearrange("p (j m) -> p j m", j=9), JV[:].rearrange("p (j m) -> p j m", j=9), bc_mid(aa[:, :], 9), ALU.mult)
                    P.actv(A9[:], A9[:], AF.Exp)
                    range_reduce(th[:, :], 32)
                    P.tt("dve", T9[:].rearrange("p (j m) -> p j m", j=9), JV[:].rearrange("p (j m) -> p j m", j=9), bc_mid(th[:, :], 9), ALU.mult)
                    sincos(T9[:, :], 288, PI[:, :], T9c[:, :])
                    P.tt("dve", PR[:], A9[:], T9c[:], ALU.mult)
                    P.tt("dve", PI[:], A9[:], PI[:], ALU.mult)
                    PRv = PR[:].rearrange("p (j m) -> p j m", j=9)
                    PIv = PI[:].rearrange("p (j m) -> p j m", j=9)
                    den = B2("den", [128, 32]); nr = B2("nr", [128, 32]); fre = B2("fre", [128, 32]); fim = B2("fim", [128, 32]); t32 = B2("t32", [128, 32])
                    P.tt("dve", den[:], lr[:], lr[:], ALU.mult)
                    P.tt("dve", t32[:], li[:], li[:], ALU.mult)
                    P.tt("dve", den[:], den[:], t32[:], ALU.add)
                    P.op("dve", lambda e: e.reciprocal(den[:], den[:]), [den], [den])
                    P.ts("dve", nr[:], PR[:, 32:64], -1.0, None, ALU.add)
                    P.tt("dve", fre[:], nr[:], lr[:], ALU.mult)
                    P.tt("dve", t32[:], PI[:, 32:64], li[:], ALU.mult)
                    P.tt("dve", fre[:], fre[:], t32[:], ALU.add)
                    P.tt("dve", fre[:], fre[:], den[:], ALU.mult)
                    P.tt("dve", fim[:], PI[:, 32:64], lr[:], ALU.mult)
                    P.tt("dve", t32[:], nr[:], li[:], ALU.mult)
                    P.tt("dve", fim[:], fim[:], t32[:], ALU.subtract)
                    P.tt("dve", fim[:], fim[:], den[:], ALU.mult)
                    Bre = B2("Bre", [128, 512]); Bim = B2("Bim", [128, 512]); t512 = B2("t512", [128, 512])
                    v16 = lambda ap: ap.rearrange("p (m c) -> p m c", c=16)
                    P.tt("dve", v16(Bre[:]), v16(bre[:]), bc_last(fre[:, :], 16), ALU.mult)
                    P.tt("dve", v16(t512[:]), v16(bim[:]), bc_last(fim[:, :], 16), ALU.mult)
                    P.tt("dve", Bre[:], Bre[:], t512[:], ALU.subtract)
                    P.tt("dve", v16(Bim[:]), v16(bim[:]), bc_last(fre[:, :], 16), ALU.mult)
                    P.tt("dve", v16(t512[:]), v16(bre[:]), bc_last(fim[:, :], 16), ALU.mult)
                    P.tt("dve", Bim[:], Bim[:], t512[:], ALU.add)
                    Cbd_re = B2("Cbd_re", [128, 32 * 32], BF16); Cbd_nim = B2("Cbd_nim", [128, 32 * 32], BF16)
                    P.memset("pool", Cbd_re[:], 0.0)
                    P.memset("pool", Cbd_nim[:], 0.0)
                    cbv = lambda t_: t_[:].rearrange("p (m c) -> p m c", c=32)
                    for hp in range(2):
                        rows = slice(hp * 64, (hp + 1) * 64)
                        P.cp("dve", cbv(Cbd_re)[rows, :, hp * 16:(hp + 1) * 16], v16(cre[:])[rows, :, :])
                        P.ts("dve", cbv(Cbd_nim)[rows, :, hp * 16:(hp + 1) * 16], v16(cim[:])[rows, :, :], -1.0, None, ALU.mult)
                    th8 = B2("th8", [128, 32])
                    P.ts("dve", th8[:], th[:], 8.0, None, ALU.mult)
                    range_reduce(th8[:, :], 32)
                    Xre = B2("Xre", [128, 512]); Xim = B2("Xim", [128, 512]); tX = B2("tX", [128, 512])
                    XBD = B2("XBD", [128, 2 * 8 * 4 * 32], BF16)
                    VZ = B2("VZ", [128, 8 * 4 * 2 * 128], BF16)
                    WS = B2("WS", [128, 8 * 2 * 128], BF16)
                    uzb = [B2("uz%d" % i, [128, NT], BF16) for i in range(2)]
                    BD = B2("BD", [128, 8 * 128], BF16)
                    Sin_r = B2("Sin_r", [128, 4 * NCH]); Sin_i = B2("Sin_i", [128, 4 * NCH])
                    cosT = B2("cosT", [128, 1024]); sinT = B2("sinT", [128, 1024])
                    c_r = B2("c_r", [128, 1024]); c_i = B2("c_i", [128, 1024]); t1k = B2("t1k", [128, 1024])
                    w_r = B2("w_r", [128, 1024]); w_i = B2("w_i", [128, 1024])
                    Sp_r = B2("Sp_r", [128, 4 * NCH], BF16); Sp_i = B2("Sp_i", [128, 4 * NCH], BF16)
                    yvb = [B2("yv%d" % i, [128, NCH]) for i in range(2)]; y2b = [B2("y2%d" % i, [128, NCH]) for i in range(2)]; y3b = [B2("y3%d" % i, [128, NCH]) for i in range(2)]
                    ygc = B2("ygc", [128, NT], BF16)
                    P.memset("pool", XBD[:], 0.0)
                    P.memset("pool", VZ[:], 0.0)
                    for c in range(8):
                        msl = slice(4 * c, 4 * c + 4)
                        x4 = lambda t_: t_[:].rearrange("p (t m c) -> p t m c", t=8, m=4)
                        prb = PRv[:, 0:8, msl].unsqueeze(3).to_broadcast([128, 8, 4, 16])
                        pib = PIv[:, 0:8, msl].unsqueeze(3).to_broadcast([128, 8, 4, 16])
                        brb = v16(Bre[:])[:, msl, :].unsqueeze(1).to_broadcast([128, 8, 4, 16])
                        bib = v16(Bim[:])[:, msl, :].unsqueeze(1).to_broadcast([128, 8, 4, 16])
                        P.tt("dve", x4(Xre), prb, brb, ALU.mult)
                        P.tt("dve", x4(tX), pib, bib, ALU.mult)
                        P.tt("dve", Xre[:], Xre[:], tX[:], ALU.subtract)
                        P.tt("dve", x4(Xim), prb, bib, ALU.mult)
                        P.tt("dve", x4(tX), pib, brb, ALU.mult)
                        P.tt("dve", Xim[:], Xim[:], tX[:], ALU.add)
                        xbv = XBD[:].rearrange("p (r t m c) -> p r t m c", r=2, t=8, m=4)
                        for hp in range(2):
                            rows = slice(hp * 64, (hp + 1) * 64)
                            P.cp("dve", xbv[rows, 0, :, :, hp * 16:(hp + 1) * 16], x4(Xre)[rows])
                            P.cp("pool", xbv[rows, 1, :, :, hp * 16:(hp + 1) * 16], x4(Xim)[rows])
                        prb = PRv[:, 1:9, msl].unsqueeze(3).to_broadcast([128, 8, 4, 16])
                        pib = PIv[:, 1:9, msl].unsqueeze(3).to_broadcast([128, 8, 4, 16])
                        crb = v16(cre[:])[:, msl, :].unsqueeze(1).to_broadcast([128, 8, 4, 16])
                        cib = v16(cim[:])[:, msl, :].unsqueeze(1).to_broadcast([128, 8, 4, 16])
                        P.tt("dve", x4(Xre), prb, crb, ALU.mult)
                        P.tt("dve", x4(tX), pib, cib, ALU.mult)
                        P.tt("dve", Xre[:], Xre[:], tX[:], ALU.subtract)
                        P.tt("dve", x4(Xim), pib, crb, ALU.mult)
                        P.tt("dve", x4(tX), prb, cib, ALU.mult)
                        P.stt("dve", Xim[:], Xim[:], -1.0, tX[:], ALU.mult, ALU.subtract)
                        vzv = VZ[:].rearrange("p (t m r n) -> p t m r n", t=8, m=4, r=2)
                        for hp in range(2):
                            rows = slice(hp * 64, (hp + 1) * 64)
                            for m4 in range(4):
                                c0_ = 32 * m4 + 16 * hp
                                P.cp("dve", vzv[rows, :, m4, 0, c0_:c0_ + 16], x4(Xre)[rows, :, m4, :])
                                P.cp("pool", vzv[rows, :, m4, 1, c0_:c0_ + 16], x4(Xim)[rows, :, m4, :])
                        for th2 in range(2):
                            for tl_ in range(4):
                                tau = th2 * 4 + tl_
                                for ri in range(2):
                                    P.tr(psT[:, (tl_ * 2 + ri) * 128:(tl_ * 2 + ri + 1) * 128], XBD[:, (ri * 8 + tau) * 128:(ri * 8 + tau + 1) * 128], ident[:])
                            P.cp("act", WS[:, th2 * 1024:(th2 + 1) * 1024], psT[:])
                        for th2 in range(2):
                            for tl_ in range(4):
                                tau = th2 * 4 + tl_
                                outp = psC[:, tl_ * 128:(tl_ + 1) * 128]
                                P.mm(outp, XBD[:, (0 * 8 + tau) * 128:(0 * 8 + tau + 1) * 128], Cbd_re[:, c * 128:(c + 1) * 128], start=True, stop=False)
                                P.mm(outp, XBD[:, (1 * 8 + tau) * 128:(1 * 8 + tau + 1) * 128], Cbd_nim[:, c * 128:(c + 1) * 128], start=False, stop=True)
                            P.tt("dve", BD[:, th2 * 512:(th2 + 1) * 512].rearrange("p (t n) -> p t n", t=4), psC[:, 0:512].rearrange("p (t n) -> p t n", t=4),
                                 bc_mid(cs["BM16"][:, :], 4), ALU.mult)
                        uc = uT[:, c * NT:(c + 1) * NT]
                        for m4 in range(4):
                            uz = uzb[m4 % 2]
                            P.op("act", lambda e, uz=uz, uc=uc, m4=m4: e.mul(uz[:], uc, cs["PM4"][:, m4:m4 + 1]), [uT, cs["PM4"]], [uz])
                            for ri, dst in ((0, Sin_r), (1, Sin_i)):
                                pa = psA[ri]
                                for s_ in range(8):
                                    tau = 7 - s_
                                    P.mm(pa[:, 0:NCH], WS[:, (tau * 2 + ri) * 128:(tau * 2 + ri + 1) * 128], uz[:, s_:NT:8], start=(s_ == 0), stop=(s_ == 7))
                                if ri == 0:
                                    P.cp("act", dst[:, m4 * NCH:(m4 + 1) * NCH], pa[:, 0:NCH])
                                else:
                                    P.cp("dve", dst[:, m4 * NCH:(m4 + 1) * NCH], pa[:, 0:NCH])
                        a3 = lambda t_: t_[:].rearrange("p (m k) -> p m k", m=4)
                        P.tt("dve", a3(t1k), bc_last(th8[:, msl], 256), bc_mid(cs["KI"][:, :], 4), ALU.mult)
                        sincos(t1k[:, :], 1024, sinT[:, :], cosT[:, :])
                        s3 = lambda t_: t_[:].rearrange("p (m k) -> p m k", m=4)[:, :, 0:256]
                        P.tt("dve", a3(c_r), a3(cosT), s3(Sin_r), ALU.mult)
                        P.tt("dve", a3(t1k), a3(sinT), s3(Sin_i), ALU.mult)
                        P.tt("dve", c_r[:], c_r[:], t1k[:], ALU.add)
                        P.tt("dve", a3(c_i), a3(cosT), s3(Sin_i), ALU.mult)
                        P.tt("dve", a3(t1k), a3(sinT), s3(Sin_r), ALU.mult)
                        P.tt("dve", c_i[:], c_i[:], t1k[:], ALU.subtract)
                        for m4 in range(4):
                            m = 4 * c + m4
                            r8b = A9[:, 8 * 32 + m:8 * 32 + m + 1].to_broadcast([128, 256])
                            for tl_, wo_ in ((c_r, w_r), (c_i, w_i)):
                                seg = tl_[:, m4 * 256:(m4 + 1) * 256]
                                oseg = wo_[:, m4 * 256:(m4 + 1) * 256]
                                P.op("dve", lambda e, seg=seg, oseg=oseg, r8b=r8b: e.tensor_tensor_scan(oseg, r8b, seg, 0.0, ALU.mult, ALU.add), [tl_, A9], [wo_])
                        P.tt("dve", t1k[:], cosT[:], w_r[:], ALU.mult)
                        P.tt("pool", c_r[:], sinT[:], w_i[:], ALU.mult)
                        P.tt("dve", t1k[:], t1k[:], c_r[:], ALU.subtract)
                        P.tt("pool", c_i[:], cosT[:], w_i[:], ALU.mult)
                        P.tt("dve", c_r[:], sinT[:], w_r[:], ALU.mult)
                        P.tt("dve", c_i[:], c_i[:], c_r[:], ALU.add)
                        spr = Sp_r[:].rearrange("p (m k) -> p m k", m=4)
                        spi = Sp_i[:].rearrange("p (m k) -> p m k", m=4)
                        P.memset("pool", spr[:, :, 0:1], 0.0)
                        P.memset("pool", spi[:, :, 0:1], 0.0)
                        P.cp("dve", spr[:, :, 1:256], a3(t1k)[:, :, 0:255])
                        P.cp("pool", spi[:, :, 1:256], a3(c_i)[:, :, 0:255])
                        x0rv = x0r[:].rearrange("p (m b) -> p m b", b=16)[:, msl, :]
                        x0iv = x0i[:].rearrange("p (m b) -> p m b", b=16)[:, msl, :]
                        P.cp("dve", spr[:, :, 256:272], x0rv)
                        P.cp("pool", spi[:, :, 256:272], x0iv)
                        P.cp("dve", FSp[:, 4 * c:4 * c + 4], a3(t1k)[:, :, 255])
                        P.cp("dve", FSp[:, 32 + 4 * c:32 + 4 * c + 4], a3(c_i)[:, :, 255])
                        fsr = FSs[:, 0:512].rearrange("p (m b) -> p m b", b=16)[:, msl, :]
                        fsi = FSs[:, 512:1024].rearrange("p (m b) -> p m b", b=16)[:, msl, :]
                        p8 = bc_last(PR[:, 8 * 32 + 4 * c:8 * 32 + 4 * c + 4], 16)
                        i8_ = bc_last(PI[:, 8 * 32 + 4 * c:8 * 32 + 4 * c + 4], 16)
                        sir = Sin_r[:].rearrange("p (m k) -> p m k", m=4)[:, :, 256:272]
                        sii = Sin_i[:].rearrange("p (m k) -> p m k", m=4)[:, :, 256:272]
                        tq = tX[:, 0:64].rearrange("p (m b) -> p m b", b=16)
                        P.tt("dve", fsr, p8, x0rv, ALU.mult)
                        P.tt("dve", tq, i8_, x0iv, ALU.mult)
                        P.tt("dve", fsr, fsr, tq, ALU.subtract)
                        P.tt("dve", fsr, fsr, sir, ALU.add)
                        P.tt("dve", fsi, p8, x0iv, ALU.mult)
                        P.tt("dve", tq, i8_, x0rv, ALU.mult)
                        P.tt("dve", fsi, fsi, tq, ALU.add)
                        P.tt("dve", fsi, fsi, sii, ALU.add)
                        for j in range(8):
                            acc = psS[j % 2]
                            for s_ in range(j + 1):
                                P.mm(acc[:, 0:NCH], BD[:, (j - s_) * 128:(j - s_ + 1) * 128], uc[:, s_:NT:8], start=(s_ == 0), stop=False)
                            for m4 in range(4):
                                for ri, spt in ((0, Sp_r), (1, Sp_i)):
                                    last = (m4 == 3 and ri == 1)
                                    P.mm(acc[:, 0:NCH], VZ[:, ((j * 4 + m4) * 2 + ri) * 128:((j * 4 + m4) * 2 + ri + 1) * 128],
                                         spt[:, m4 * NCH:(m4 + 1) * NCH], start=False, stop=last)
                            yv, y2, y3 = yvb[j % 2], y2b[j % 2], y3b[j % 2]
                            P.stt("dve", yv[:], uc[:, j:NT:8], dsk[:, c:c + 1], acc[:, 0:NCH], ALU.mult, ALU.add)
                            P.actv(y2[:], yv[:], AF.Square)
                            P.ts("dve", y2[:], y2[:], 0.044715, 1.0, ALU.mult, ALU.add)
                            P.tt("dve", y2[:], y2[:], yv[:], ALU.mult)
                            P.actv(y3[:], y2[:], AF.Sigmoid, scale=1.5957691216057308)
                            P.tt("dve", ygc[:, j:NT:8], yv[:], y3[:], ALU.mult)
                        P.cp("pool", uT[:, c * NT:(c + 1) * NT], ygc[:])
                    P.dma(o_ssp.rearrange("r p m -> p r m"), FSp[:].rearrange("p (r m) -> p r m", r=2))
                    P.dma(o_sss.rearrange("r p x -> p r x"), FSs[:].rearrange("p (r x) -> p r x", r=2))
                    P.barrier()

                with ExitStack() as st3:
                    B3 = lambda name, shape, dt=F32: alloc(st3, name, shape, dt)
                    W1 = B3("W1", [128, 8 * 1024], BF16)
                    W2 = B3("W2", [128, 8 * 1024], BF16)
                    Wo2 = B3("Wo2", [128, 8 * 1024], BF16)
                    yt = [B3("ytc%d" % i, [128, 1024]) for i in range(2)]
                    oT = B3("oT", [128, 8 * 512], BF16)
                    sg = B3("sg", [128, 512]); tg = B3("tg", [128, 512])
                    with ExitStack() as stg_:
                        stg = [alloc(stg_, "stgc%d" % i, [128, 1024]) for i in range(2)]
                        k_ = 0
                        for (Wd, src) in ((W1, glu1), (W2, glu2), (Wo2, w_out_o)):
                            for kc in range(8):
                                s_ = stg[k_ % 2]; k_ += 1
                                P.dma(s_[:], src[kc * 128:(kc + 1) * 128, :])
                                P.cp("dve" if kc % 2 == 0 else "act", Wd[:, kc * 1024:(kc + 1) * 1024], s_[:])
                        P.barrier()
                    for (t0, t1) in blocks:
                        nb = (t1 - t0) * 128
                        col0 = t0 * 128
                        for fc in range(8):
                            p1 = psA[0]; p2 = psA[1]
                            for kc in range(8):
                                P.mm(p1[:, 0:nb], W1[:, kc * 1024 + fc * 128:kc * 1024 + (fc + 1) * 128], uT[:, kc * NT + col0:kc * NT + col0 + nb], start=(kc == 0), stop=(kc == 7))
                            for kc in range(8):
                                P.mm(p2[:, 0:nb], W2[:, kc * 1024 + fc * 128:kc * 1024 + (fc + 1) * 128], uT[:, kc * NT + col0:kc * NT + col0 + nb], start=(kc == 0), stop=(kc == 7))
                            P.actv(sg[:, 0:nb], p2[:, 0:nb], AF.Sigmoid)
                            P.tt("dve", tg[:, 0:nb], p1[:, 0:nb], sg[:, 0:nb], ALU.mult)
                            P.tt("pool", oT[:, fc * 512:fc * 512 + nb], tg[:, 0:nb], szT[:, fc * NT + col0:fc * NT + col0 + nb], ALU.mult)
                        for ti in range(t0, t1):
                            lt = ti - t0
                            ytile = yt[cntb["x"] % 2]; cntb["x"] += 1
                            load_tile(ytile[:], ti)
                            for hf in range(2):
                                pa = psS[hf]
                                for fc in range(8):
                                    P.mm(pa[:, :], oT[:, fc * 512 + lt * 128:fc * 512 + (lt + 1) * 128], Wo2[:, fc * 1024 + hf * 512:fc * 1024 + (hf + 1) * 512], start=(fc == 0), stop=(fc == 7))
                                P.tt("dve", ytile[:, hf * 512:(hf + 1) * 512], ytile[:, hf * 512:(hf + 1) * 512], pa[:, :], ALU.add)
                            if ti < 16:
                                P.dma(yp[ti * 128:(ti + 1) * 128, :], ytile[:], reads=[ytile, ("y0", "p", ti)], writes=[("y0", "p", ti)])
                            else:
                                P.dma(ys, ytile[:], reads=[ytile, ("y0", "s", 0)], writes=[("y0", "s", 0)])
                    P.barrier()
        P.barrier()
        P.flush()
    return nc, in_names


_PROG_CACHE = {}


def _shared_inputs(inputs, consts):
    perm = _perm_even()
    sh = {}
    sh["w_in_e"] = np.ascontiguousarray(inputs["w_in_even"][0][:, perm])
    sh["w_out_e"] = np.ascontiguousarray(inputs["w_out_even"][0])
    sh["norm_e"] = np.ascontiguousarray(inputs["norm_even"][0].reshape(8, 128).T)
    sh["gn_gain"] = np.ascontiguousarray(inputs["ret_gn_gain"][0].reshape(1, 512))
    qn = inputs["nsa_q_norm"][0]
    kn = inputs["nsa_k_norm"][0]
    sh["qk_gain"] = np.concatenate([np.tile(qn, 8), np.tile(kn[0], 2), np.tile(kn[1], 2), np.tile(kn[2], 2)]).reshape(1, 896).astype(np.float32)
    sh["cmp_posT"] = np.ascontiguousarray(inputs["nsa_cmp_pos"][0].transpose(2, 0, 1))
    sh["cmp_w"] = np.ascontiguousarray(inputs["nsa_cmp_w"][0])
    for k, v in consts.items():
        sh["c_" + k] = v
    ca = np.ascontiguousarray
    sh["w_in_o"] = ca(inputs["w_in_odd"][0])
    sh["glu1"] = ca(inputs["glu_w1"][0])
    sh["glu2"] = ca(inputs["glu_w2"][0])
    sh["w_out_o"] = ca(inputs["w_out_odd"][0])
    sh["norm_o"] = ca(inputs["norm_odd"][0].reshape(8, 128).T)
    sh["ssmd"] = ca(inputs["ssm_d"][0].reshape(8, 128).T)
    sh["lamre_A"] = ca(inputs["ssm_lambda_re"][0].reshape(32, 2, 64).transpose(1, 2, 0).reshape(128, 32))
    sh["lamim_A"] = ca(inputs["ssm_lambda_im"][0].reshape(32, 2, 64).transpose(1, 2, 0).reshape(128, 32))
    ls = np.broadcast_to(inputs["ssm_log_step"][0].reshape(32, 2, 1), (32, 2, 64))
    sh["lstep_A"] = ca(ls.transpose(1, 2, 0).reshape(128, 32)).astype(np.float32)
    sh["bA_re"] = ca(inputs["ssm_b_re"][0].reshape(32, 2, 64, 16).transpose(1, 2, 0, 3).reshape(128, 512))
    sh["bA_im"] = ca(inputs["ssm_b_im"][0].reshape(32, 2, 64, 16).transpose(1, 2, 0, 3).reshape(128, 512))
    sh["cA_re"] = ca(inputs["ssm_c_re"][0].reshape(32, 2, 16, 64).transpose(1, 3, 0, 2).reshape(128, 512))
    sh["cA_im"] = ca(inputs["ssm_c_im"][0].reshape(32, 2, 16, 64).transpose(1, 3, 0, 2).reshape(128, 512))
    return sh


def _core_inputs(inputs, c, sh, cache_rows):
    im = dict(sh)
    im["xp"] = np.ascontiguousarray(inputs["x_prompt"][c])
    im["xs"] = np.ascontiguousarray(inputs["x_sample"][16 * c:16 * c + 16].reshape(128, 1024))
    im["cache"] = inputs["cache_nsa_kv"][0].reshape(2560 * 128, 512)[0:cache_rows]
    im["cwin"] = np.ascontiguousarray(inputs["cache_nsa_win"][0, 16 * c:16 * c + 16].reshape(16, 512, 256))
    im["sret"] = np.ascontiguousarray(inputs["state_ret"][0, 16 * c:16 * c + 16])
    im["x0A_re"] = np.ascontiguousarray(inputs["state_ssm_re"][0, 16 * c:16 * c + 16].reshape(16, 32, 2, 64).transpose(2, 3, 1, 0).reshape(128, 512))
    im["x0A_im"] = np.ascontiguousarray(inputs["state_ssm_im"][0, 16 * c:16 * c + 16].reshape(16, 32, 2, 64).transpose(2, 3, 1, 0).reshape(128, 512))
    im["ptab"] = np.ascontiguousarray(inputs["page_table"][16 * c:16 * c + 16].reshape(1, 256)).astype(np.int32)
    return im


def run_cores(inputs, cores, trace=False, **opts):
    consts = make_consts()
    key = tuple(sorted(opts.items()))
    nc, in_names = build_program(consts, **opts)
    sh = _shared_inputs(inputs, consts)
    cache_rows = 2560 * 128 if opts.get("do_sample", True) else 128
    in_maps = [_core_inputs(inputs, c, sh, cache_rows) for c in cores]
    in_maps = [{k: m[k] for k in in_names} for m in in_maps]
    if trace:
        res = run_bass_kernel_spmd(nc, in_maps, core_ids=list(range(len(cores))), trace=True)
        print("EXEC_TIME_NS", res.exec_time_ns)
        return res.results
    res = run_bass_kernel_spmd(nc, in_maps, core_ids=list(range(len(cores))))
    return res.results


def kernel(**inputs):
    inputs = {k: np.asarray(v) for k, v in inputs.items()}
    res = run_cores(inputs, list(range(NCORES)))
    f32 = np.float32
    y_p = np.zeros((8, 2048, 1024), f32)
    y_s = np.zeros((128, 8, 1024), f32)
    ret_p = np.zeros((1, 8, 4, 64, 128), f32)
    ret_s = np.zeros((1, 128, 4, 64, 128), f32)
    kv_p = np.zeros((1, 8, 2048, 4, 2, 64), f32)
    kv_s = np.zeros((1, 128, 8, 4, 2, 64), f32)
    win_p = np.zeros((1, 8, 512, 2, 2, 64), f32)
    win_s = np.zeros((1, 128, 512, 2, 2, 64), f32)
    sre_p = np.zeros((1, 8, 64, 64), f32)
    sim_p = np.zeros((1, 8, 64, 64), f32)
    sre_s = np.zeros((1, 128, 64, 64), f32)
    sim_s = np.zeros((1, 128, 64, 64), f32)
    for c in range(NCORES):
        r = res[c]
        sl = slice(16 * c, 16 * c + 16)
        y_p[c] = r["yp"]
        y_s[sl] = r["ys"].reshape(16, 8, 1024)
        ret_p[0, c] = r["o_retp"]
        ret_s[0, sl] = r["o_rets"]
        kv_p[0, c] = r["o_kvp"].reshape(2048, 4, 2, 64)
        kv_s[0, sl] = r["o_kvs"].reshape(16, 8, 4, 2, 64)
        win_p[0, c] = r["o_winp"].reshape(512, 2, 2, 64)
        win_s[0, sl] = r["o_wins"].reshape(16, 512, 2, 2, 64)
        if "o_ssp" in r:
            sp = r["o_ssp"].reshape(2, 2, 64, 32).transpose(0, 3, 1, 2).reshape(2, 64, 64)
            sre_p[0, c] = sp[0]
            sim_p[0, c] = sp[1]
            ss = r["o_sss"].reshape(2, 2, 64, 32, 16).transpose(0, 4, 3, 1, 2).reshape(2, 16, 64, 64)
            sre_s[0, sl] = ss[0]
            sim_s[0, sl] = ss[1]
    return (y_p, y_s, ret_p, ret_s, kv_p, kv_s, win_p, win_s, sre_p, sim_p, sre_s, sim_s)
```

```python
import numpy as np
import concourse.bass as bass
import concourse.mybir as mybir
from concourse.bass_utils import run_bass_kernel_spmd

F32 = mybir.dt.float32
BF16 = mybir.dt.bfloat16
I32 = mybir.dt.int32
AF = mybir.ActivationFunctionType
ALU = mybir.AluOpType
AX = mybir.AxisListType

ENGS = ("pe", "act", "dve", "pool", "sp")
NDMASEM = 24
NSWSEM = 8


class Prog:
    def __init__(self, nc):
        self.nc = nc
        self.ops = {e: [] for e in ENGS}
        self.cnt = {e: 0 for e in ENGS}
        self.sem = {}
        self.seen = {e: {} for e in ENGS}
        self.last_w = {}
        self.readers = {}
        self.dma_i = 0
        self.dma_uses = [0] * NDMASEM
        self.dma_tok = [None] * NDMASEM
        self.sw_i = 0
        self.sw_uses = [0] * NSWSEM
        self.sw_tok = [None] * NSWSEM
        self.stack = None
        self.n_ops = 0

    def setup(self, stack):
        self.stack = stack
        for e in ENGS:
            self.sem[e] = stack.enter_context(self.nc.semaphore("s_" + e))
        for i in range(NDMASEM):
            self.sem["d%d" % i] = stack.enter_context(self.nc.semaphore("s_d%d" % i))
        for i in range(NSWSEM):
            self.sem["w%d" % i] = stack.enter_context(self.nc.semaphore("s_w%d" % i))

    def _key(self, a):
        if isinstance(a, str):
            return a
        if isinstance(a, tuple):
            return a
        t = getattr(a, 'tensor', None)
        return t.name if t is not None else a.name

    def _deps(self, eng, reads, writes):
        toks = []
        for k in reads:
            k = self._key(k)
            t = self.last_w.get(k)
            if t is not None:
                toks.append(t)
        for k in writes:
            k = self._key(k)
            t = self.last_w.get(k)
            if t is not None:
                toks.append(t)
            toks.extend(self.readers.get(k, ()))
        need = {}
        for (s, v) in toks:
            if eng == "pe" and s == "pe":
                continue
            if v > need.get(s, 0):
                need[s] = v
        waits = []
        seen = self.seen[eng]
        for s, v in need.items():
            if seen.get(s, 0) >= v:
                continue
            seen[s] = v
            waits.append((s, v))
        return waits

    def _commit(self, tok, reads, writes):
        for k in writes:
            k = self._key(k)
            self.last_w[k] = tok
            self.readers[k] = []
        for k in reads:
            k = self._key(k)
            self.readers.setdefault(k, []).append(tok)

    def op(self, eng, fn, reads=(), writes=()):
        waits = self._deps(eng, reads, writes)
        self.cnt[eng] += 1
        tok = (eng, self.cnt[eng])
        self.ops[eng].append((waits, fn, (eng, 1)))
        self._commit(tok, reads, writes)
        self.n_ops += 1

    def dma(self, out, in_, reads=None, writes=None, q="sp", fn=None):
        if reads is None:
            reads = [in_]
        if writes is None:
            writes = [out]
        if q == "pool":
            i = self.sw_i % NSWSEM
            self.sw_i += 1
            sname = "w%d" % i
            uses, toks = self.sw_uses, self.sw_tok
        else:
            i = self.dma_i % NDMASEM
            self.dma_i += 1
            sname = "d%d" % i
            uses, toks = self.dma_uses, self.dma_tok
        waits = self._deps(q, reads, writes)
        prev = toks[i]
        if prev is not None and self.seen[q].get(sname, 0) < prev[1]:
            self.seen[q][sname] = prev[1]
            waits.append(prev)
        uses[i] += 1
        tok = (sname, 16 * uses[i])
        toks[i] = tok
        if fn is None:
            fn = lambda e, o=out, a=in_: e.dma_start(out=o, in_=a)
        self.ops[q].append((waits, fn, (sname, 16)))
        self._commit(tok, reads, writes)
        self.n_ops += 1

    def barrier(self):
        for e in ENGS:
            waits = []
            for e2 in ENGS:
                if e2 == e:
                    continue
                v = self.cnt[e2]
                if v > self.seen[e].get(e2, 0):
                    self.seen[e][e2] = v
                    waits.append((e2, v))
            for t in list(self.dma_tok) + list(self.sw_tok):
                if t is not None and self.seen[e].get(t[0], 0) < t[1]:
                    self.seen[e][t[0]] = t[1]
                    waits.append(t)
            if waits:
                self.ops[e].append((waits, None, None))

    def flush(self):
        nc = self.nc
        ops = self.ops
        sem = self.sem

        def replay(engh, lst):
            for (waits, fn, inc) in lst:
                for (s, v) in waits:
                    engh.wait_ge(sem[s], v)
                if fn is not None:
                    ins = fn(engh)
                    ins.then_inc(sem[inc[0]], inc[1])

        with nc.Block() as block:
            @block.tensor
            def _(e):
                replay(e, ops["pe"])

            @block.scalar
            def _(e):
                replay(e, ops["act"])

            @block.vector
            def _(e):
                replay(e, ops["dve"])

            @block.gpsimd
            def _(e):
                replay(e, ops["pool"])

            @block.sync
            def _(e):
                replay(e, ops["sp"])
        self.ops = {e: [] for e in ENGS}

    def mm(self, out, lhsT, rhs, start=True, stop=True, reads=None, writes=None):
        if reads is None:
            reads = [lhsT, rhs]
        if writes is None:
            writes = [out]
        self.op("pe", lambda e: e.matmul(out, lhsT, rhs, start=start, stop=stop), reads, writes)

    def tr(self, out, in_, ident, reads=None, writes=None):
        if reads is None:
            reads = [in_, ident]
        if writes is None:
            writes = [out]
        self.op("pe", lambda e: e.transpose(out, in_, ident), reads, writes)

    def actv(self, out, in_, func, bias=None, scale=None, accum_out=None, reads=None, writes=None, eng="act"):
        kw = {}
        if bias is not None:
            kw["bias"] = bias
        if scale is not None:
            kw["scale"] = scale
        if accum_out is not None:
            kw["accum_out"] = accum_out
        if reads is None:
            reads = [in_]
            if bias is not None and not isinstance(bias, (int, float)):
                reads.append(bias)
            if scale is not None and not isinstance(scale, (int, float)):
                reads.append(scale)
        if writes is None:
            writes = [out]
            if accum_out is not None:
                writes.append(accum_out)
        self.op("act", lambda e: e.activation(out, in_, func, **kw), reads, writes)

    def ts(self, eng, out, in0, s1, s2, op0, op1=None, accum_out=None, reads=None, writes=None):
        kw = {}
        if op1 is not None:
            kw["op1"] = op1
        if accum_out is not None:
            kw["accum_out"] = accum_out
        if reads is None:
            reads = [in0]
            for s in (s1, s2):
                if s is not None and not isinstance(s, (int, float)):
                    reads.append(s)
        if writes is None:
            writes = [out]
            if accum_out is not None:
                writes.append(accum_out)
        self.op(eng, lambda e: e.tensor_scalar(out, in0, s1, s2, op0, **kw), reads, writes)

    def tt(self, eng, out, in0, in1, op, reads=None, writes=None):
        if reads is None:
            reads = [in0, in1]
        if writes is None:
            writes = [out]
        self.op(eng, lambda e: e.tensor_tensor(out, in0, in1, op), reads, writes)

    def stt(self, eng, out, in0, scalar, in1, op0, op1, accum_out=None, reads=None, writes=None):
        kw = {}
        if accum_out is not None:
            kw["accum_out"] = accum_out
        if reads is None:
            reads = [in0, in1]
            if not isinstance(scalar, (int, float)):
                reads.append(scalar)
        if writes is None:
            writes = [out]
            if accum_out is not None:
                writes.append(accum_out)
        self.op(eng, lambda e: e.scalar_tensor_tensor(out, in0, scalar, in1, op0, op1, **kw), reads, writes)

    def cp(self, eng, out, in_, reads=None, writes=None):
        if reads is None:
            reads = [in_]
        if writes is None:
            writes = [out]
        if eng == "act":
            self.op(eng, lambda e: e.copy(out, in_), reads, writes)
        else:
            self.op(eng, lambda e: e.tensor_copy(out, in_), reads, writes)

    def memset(self, eng, ap, val, writes=None):
        if writes is None:
            writes = [ap]
        self.op(eng, lambda e: e.memset(ap, val), (), writes)

import math
from contextlib import ExitStack
import ml_dtypes

BF = ml_dtypes.bfloat16
NCORES = 8
BIG = 1.0e4
FORCE = 1.0e4
EPS = 1e-6
SCALE = 0.125
QA, KA, QN, KC, KS, KW, VC, VS, VW, GL, VA, ZA, ZB = 0, 256, 512, 1024, 1152, 1280, 1408, 1536, 1664, 1792, 1816, 2328, 2840
EIN = 3352
GROUPS = [(0, 512), (512, 1024), (1024, 1408), (1408, 1816), (1816, 2328), (2328, 2840), (2840, 3352)]


def _perm_even():
    perm = np.zeros(EIN, np.int64)
    perm[QA:QA + 256] = np.arange(0, 256)
    perm[KA:KA + 256] = np.arange(256, 512)
    o_va, o_za, o_qn, o_kvb, o_gl, o_zb = 512, 1024, 1536, 2048, 2816, 2840
    for j in range(4):
        for g in range(2):
            for dd in range(64):
                perm[QN + (j * 2 + g) * 64 + dd] = o_qn + (4 * g + j) * 64 + dd
    for dst, kind in ((KC, 0), (KS, 2), (KW, 4), (VC, 1), (VS, 3), (VW, 5)):
        perm[dst:dst + 128] = o_kvb + kind * 128 + np.arange(128)
    perm[GL:GL + 24] = o_gl + np.arange(24)
    perm[VA:VA + 512] = o_va + np.arange(512)
    perm[ZA:ZA + 512] = o_za + np.arange(512)
    perm[ZB:ZB + 512] = o_zb + np.arange(512)
    assert len(set(perm.tolist())) == EIN
    return perm


def make_consts():
    c = {}
    f32 = np.float32
    c["ident"] = np.eye(128, dtype=f32).astype(BF)
    c["ident32"] = np.eye(128, dtype=f32)
    half = 32
    inv = (np.float32(10000.0) ** (-np.arange(half, dtype=f32) / f32(half))).astype(f32)
    pos_p = (np.arange(16)[None, :] * 128 + np.arange(128)[:, None]).astype(f32)
    ang = (pos_p[:, :, None] * inv[None, None, :]).astype(f32)
    c["cos_p"] = np.cos(ang).astype(f32)
    c["sin_p"] = np.sin(ang).astype(f32)
    pos_s = (2048 + (np.arange(128) % 8)).astype(f32)
    ang = (pos_s[:, None] * inv[None, :]).astype(f32)
    c["cos_s"] = np.cos(ang).astype(f32)
    c["sin_s"] = np.sin(ang).astype(f32)
    log_g = np.log(1.0 - 2.0 ** (-5.0 - np.arange(4, dtype=np.float64)))
    i = np.arange(128)
    diff = i[None, :] - i[:, None]
    caus = diff >= 0
    DT = np.zeros((128, 4, 128), f32)
    for h in range(4):
        DT[:, h, :] = 0.125 * np.exp(np.where(caus, diff, 0) * log_g[h]) * caus
    c["DTp"] = DT
    c["qdec_p"] = np.exp((i[:, None] + 1.0) * log_g[None, :]).astype(f32)
    c["kdec_p"] = (0.125 * np.exp((127.0 - i[:, None]) * log_g[None, :])).astype(f32)
    cd = np.zeros((128, 2), f32)
    cds = np.zeros((128, 2), f32)
    for p in range(128):
        for pr in range(2):
            h = 2 * pr + p // 64
            cd[p, pr] = np.exp(128.0 * log_g[h])
            cds[p, pr] = np.exp(8.0 * log_g[h])
    c["cdec_p"] = cd
    c["cdec_s"] = cds
    i8 = i % 8
    same = (i[:, None] // 8) == (i[None, :] // 8)
    diff8 = i8[None, :] - i8[:, None]
    caus8 = same & (diff8 >= 0)
    DTs = np.zeros((128, 4, 128), f32)
    for h in range(4):
        DTs[:, h, :] = 0.125 * np.exp(np.where(caus8, diff8, 0) * log_g[h]) * caus8
    c["DTs"] = DTs
    c["qdec_s"] = np.exp((i8[:, None] + 1.0) * log_g[None, :]).astype(f32)
    c["kdec_s"] = (0.125 * np.exp((7.0 - i8[:, None]) * log_g[None, :])).astype(f32)
    c["blkmask"] = ((i[:, None] // 8) == np.arange(16)[None, :]).astype(f32)
    cm = np.zeros((128, 16, 128), f32)
    cm[:, :, :] = ((i[None, :] // 8) == np.arange(16)[:, None])[None, :, :]
    c["colmask"] = cm.astype(BF)
    keys = np.arange(2176)
    E = (keys[None, :] // 64 == np.arange(33)[:, None]).astype(f32)
    c["E"] = E.astype(BF)
    c["CB"] = np.where(i[:, None] <= i[None, :], 0.0, -BIG).astype(f32).astype(BF)
    c["AB"] = np.where(i[:, None] > i[None, :], 0.0, -BIG).astype(f32).astype(BF)
    n = np.arange(128)
    cmpb = np.zeros((128, 16, 128), f32)
    for t in range(16):
        tpos = 128 * t + i
        cmpb[:, t, :] = np.where((16 * n[:, None] + 31) <= tpos[None, :], 0.0, -BIG)
    c["CMPB"] = cmpb.astype(BF)

    def overlap(n_s):
        c_start = np.arange(127) * 16
        s_start = np.arange(n_s) * 64
        return ((c_start[:, None] < s_start[None, :] + 64) & (s_start[None, :] < c_start[:, None] + 32)).astype(f32)
    c["ovl_p"] = overlap(32)
    c["ovl_s"] = overlap(33)
    mulc = np.zeros((128, 16, 32), f32)
    addc = np.zeros((128, 16, 32), f32)
    s_ids = np.arange(32)
    for t in range(16):
        tpos = 128 * t + i
        cur = tpos // 64
        forced = (s_ids[None, :] == 0) | (s_ids[None, :] == cur[:, None]) | (s_ids[None, :] == cur[:, None] - 1)
        valid = (s_ids[None, :] * 64) <= tpos[:, None]
        mulc[:, t, :] = (valid & ~forced)
        addc[:, t, :] = np.where(valid, np.where(forced, FORCE, 0.0), -FORCE)
    c["mulc_p"] = mulc
    c["addc_p"] = addc
    s33 = np.arange(33)
    forced = (s33 == 0) | (s33 == 32) | (s33 == 31)
    c["mulc_s"] = np.tile((~forced).astype(f32)[None, :], (8, 1))
    c["addc_s"] = np.tile(np.where(forced, FORCE, 0.0).astype(f32)[None, :], (8, 1))
    q8 = np.arange(8)
    SB = np.where(((i[:, None, None] // 8) == np.arange(16)[None, :, None]) & ((i[:, None, None] % 8) <= q8[None, None, :]), 0.0, -BIG)
    c["SB"] = SB.astype(f32).astype(BF)
    c["ABs"] = np.where(i[:, None] > q8[None, :], 0.0, -BIG).astype(f32).astype(BF)
    c["iota_p"] = i.astype(f32)[:, None].copy()
    c["ones_row"] = np.ones((1, 128), f32).astype(BF)
    c["zeros_row"] = np.zeros((1, 512), f32).astype(BF)
    jv = np.zeros((128, 9, 32), f32)
    jv[:, :, :] = np.arange(9, dtype=f32)[None, :, None]
    c["JV"] = jv
    c["KI"] = np.tile(np.arange(256, dtype=f32)[None, :], (128, 1))
    c["PM4"] = (np.arange(128)[:, None] // 32 == np.arange(4)[None, :]).astype(f32)
    c["BM16"] = (np.arange(128)[:, None] // 16 == np.arange(128)[None, :] // 16).astype(f32)
    return c


def _dt_of(a):
    if a.dtype == np.float32:
        return F32
    if a.dtype == np.int32:
        return I32
    if a.dtype == BF:
        return BF16
    raise ValueError(a.dtype)


def v3(ap, h):
    return ap.rearrange("p (h d) -> p h d", h=h)


def bc_mid(ap, n):
    return ap.unsqueeze(1).to_broadcast([ap.shape[0], n, ap.shape[1]])


def bc_last(ap, n):
    return ap.unsqueeze(2).to_broadcast([ap.shape[0], ap.shape[1], n])


def build_program(consts, phase_b=True, do_sample=True, n_ptiles=16, stage=99):
    nc = bass.Bass("TRN2", target_bir_lowering=False)

    in_names = []

    def din(name, shape, dt=F32):
        in_names.append(name)
        return nc.dram_tensor(name, list(shape), dt, kind="ExternalInput").ap()

    def dout(name, shape, dt=F32):
        return nc.dram_tensor(name, list(shape), dt, kind="ExternalOutput").ap()

    xp = din("xp", [2048, 1024])
    xs = din("xs", [128, 1024])
    if do_sample:
        cache = din("cache", [2560 * 128, 512])
        cwin = din("cwin", [16, 512, 256])
        sret = din("sret", [16, 4, 64, 128])
        ptab = din("ptab", [1, 256], I32)
    w_in_e = din("w_in_e", [1024, EIN])
    w_out_e = din("w_out_e", [1024, 1024])
    norm_e = din("norm_e", [128, 8])
    gn_gain = din("gn_gain", [1, 512])
    qk_gain = din("qk_gain", [1, 896])
    cmp_posT = din("cmp_posT", [64, 2, 32])
    cmp_w = din("cmp_w", [2, 32, 64, 64])
    S_ONLY = ("cos_s", "sin_s", "DTs", "qdec_s", "kdec_s", "cdec_s", "blkmask", "colmask", "SB", "ABs", "mulc_s", "addc_s", "ovl_s", "iota_p")
    if phase_b:
        w_in_o = din("w_in_o", [1024, 2048])
        glu1 = din("glu1", [1024, 1024])
        glu2 = din("glu2", [1024, 1024])
        w_out_o = din("w_out_o", [1024, 1024])
        norm_o = din("norm_o", [128, 8])
        ssmd = din("ssmd", [128, 8])
        lamre_A = din("lamre_A", [128, 32])
        lamim_A = din("lamim_A", [128, 32])
        lstep_A = din("lstep_A", [128, 32])
        bA_re = din("bA_re", [128, 512])
        bA_im = din("bA_im", [128, 512])
        cA_re = din("cA_re", [128, 512])
        cA_im = din("cA_im", [128, 512])
        x0A_re = din("x0A_re", [128, 512])
        x0A_im = din("x0A_im", [128, 512])
    cd = {k: din("c_" + k, v.shape, _dt_of(v)) for k, v in consts.items() if ((do_sample or k not in S_ONLY) and (phase_b or k not in ("JV", "KI", "PM4", "BM16")))}
    yp = dout("yp", [2048, 1024])
    ys = dout("ys", [128, 1024])
    o_retp = dout("o_retp", [4, 64, 128])
    o_rets = dout("o_rets", [16, 4, 64, 128])
    o_kvp = dout("o_kvp", [2048, 512])
    o_kvs = dout("o_kvs", [128, 512])
    o_winp = dout("o_winp", [512, 256])
    o_wins = dout("o_wins", [16, 512, 256])
    if phase_b:
        o_ssp = dout("o_ssp", [2, 128, 32])
        o_sss = dout("o_sss", [2, 128, 512])

    P = Prog(nc)
    with ExitStack() as st0:
        P.setup(st0)

        def alloc(st, name, shape, dt=F32):
            return st.enter_context(nc.sbuf_tensor(name, list(shape), dt))

        def palloc(st, name, shape, dt=F32):
            return st.enter_context(nc.psum_tensor(name, list(shape), dt))

        psT = palloc(st0, "psT", [128, 1024], BF16)
        psA = [palloc(st0, "psA%d" % i, [128, 512]) for i in range(2)]
        psS = [palloc(st0, "psS%d" % i, [128, 512]) for i in range(2)]
        psV = palloc(st0, "psV", [128, 512])
        psC = palloc(st0, "psC", [128, 512])
        psR = palloc(st0, "psR", [128, 512])

        with ExitStack() as stA:
            A = lambda name, shape, dt=F32: alloc(stA, name, shape, dt)
            cs = {}
            P_ONLY = ("cos_p", "sin_p", "DTp", "qdec_p", "kdec_p", "cdec_p", "CMPB", "mulc_p", "addc_p", "ovl_p", "CB", "AB")

            def load_consts(stx, names, pre="k_"):
                for k in names:
                    v = consts[k]
                    shp = list(v.shape)
                    if len(shp) == 3:
                        tl = alloc(stx, pre + k, [shp[0], shp[1] * shp[2]], _dt_of(v))
                        P.dma(tl[:], cd[k].rearrange("p a b -> p (a b)"))
                    else:
                        tl = alloc(stx, pre + k, shp, _dt_of(v))
                        P.dma(tl[:], cd[k])
                    cs[k] = tl
            B_ONLY = ("JV", "KI", "PM4", "BM16")
            load_consts(stA, [k for k in consts if k not in P_ONLY and k not in S_ONLY and k not in B_ONLY])
            ident = cs["ident"]
            Wob = A("Wob", [128, 8 * 1024], BF16)
            ng = A("ng", [128, 8])
            BDW = A("BDW", [128, 2 * 32 * 128], BF16)
            peb = A("peb", [128, 64], BF16)
            posk = A("posk", [128, 2])
            posv = A("posv", [1, 128], BF16)
            gnb = A("gnb", [128, 512])
            P.dma(gnb[:], gn_gain.partition_broadcast(128))
            qkg = A("qkg", [128, 896])
            P.dma(qkg[:], qk_gain.partition_broadcast(128))

            xt = [A("xt%d" % i, [128, 1024]) for i in range(2)]
            xb = A("xb", [128, 1024], BF16)
            xT = A("xT", [128, 1024], BF16)
            stt_ = A("stats", [128, 64])
            proj = A("proj", [128, EIN])
            rp = A("rp", [128, 1408])
            tmpa = A("tmpa", [128, 1408])
            r16 = A("r16", [128, 1024], BF16)
            vb16 = A("vb16", [128, 512], BF16)
            rT = A("rT", [128, 256], BF16)
            qz = A("qz", [128, 1024], BF16)
            qTz = A("qTz", [128, 1024], BF16)
            scm = A("scm", [128, 512], BF16)
            S32 = A("S32", [128, 256])
            Sb = A("Sb", [128, 256], BF16)
            osb = A("osb", [128, 512])
            ocb = A("ocb", [128, 512])
            sz = A("sz", [128, 512])
            mix = A("mix", [128, 1024], BF16)
            mixT = A("mixT", [128, 1024], BF16)
            n16 = A("n16", [128, 1024], BF16)
            gts = A("gts", [128, 24])
            on = A("on", [128, 3 * 512])
            tmpb = on
            obt = A("obt", [128, 512])
            pt_ = [A("pt%d" % i, [128, 512], BF16) for i in range(3)]
            selT = A("selT", [33, 256], BF16)
            sc_ = A("sc", [128, 80])
            sc2 = A("sc2", [128, 80])
            sc16 = A("sc16", [128, 80], BF16)
            rden = A("rden", [128, 32])
            top8 = A("top8", [128, 16])
            P.memset("dve", S32[:], 0.0)
            P.memset("dve", Sb[:], 0.0)
            P.memset("pool", qz[:], 0.0)
            P.memset("pool", qTz[:], 0.0)

            stWb = ExitStack()
            Wb = alloc(stWb, "Wb", [128, 8 * EIN], BF16)
            P.dma(ng[:], norm_e)
            with ExitStack() as stW:
                stg = [alloc(stW, "stg%d" % i, [128, EIN]) for i in range(2)]
                for kc in range(8):
                    s_ = stg[kc % 2]
                    P.dma(s_[:], w_in_e[kc * 128:(kc + 1) * 128, :])
                    if kc % 2 == 0:
                        P.ts("dve", Wb[:, kc * EIN:(kc + 1) * EIN], s_[:], ng[:, kc:kc + 1], None, ALU.mult)
                    else:
                        P.op("act", lambda e, kc=kc, s_=s_: e.mul(Wb[:, kc * EIN:(kc + 1) * EIN], s_[:], ng[:, kc:kc + 1]), [s_, ng], [Wb])
                for kc in range(8):
                    s_ = stg[kc % 2]
                    P.dma(s_[:, 0:1024], w_out_e[kc * 128:(kc + 1) * 128, :])
                    if kc % 2 == 0:
                        P.cp("dve", Wob[:, kc * 1024:(kc + 1) * 1024], s_[:, 0:1024])
                    else:
                        P.cp("pool", Wob[:, kc * 1024:(kc + 1) * 1024], s_[:, 0:1024])
                P.barrier()
            with ExitStack() as stW:
                P.memset("pool", BDW[:], 0.0)
                wst = alloc(stW, "wst", [128, 2 * 32 * 64])
                srcw = cmp_w.rearrange("c l d e -> d (c l) e")
                P.dma(wst[0:64, :].rearrange("p (a e) -> p a e", e=64), srcw)
                P.dma(wst[64:128, :].rearrange("p (a e) -> p a e", e=64), srcw)
                bdv = BDW[:].rearrange("p (a e) -> p a e", e=128)
                P.cp("dve", bdv[0:64, :, 0:64], wst[0:64, :].rearrange("p (a e) -> p a e", e=64))
                P.cp("dve", bdv[64:128, :, 64:128], wst[64:128, :].rearrange("p (a e) -> p a e", e=64))
                pe32 = alloc(stW, "pe32", [128, 64])
                P.dma(pe32[0:64, :], cmp_posT.rearrange("d c l -> d (c l)"))
                P.dma(pe32[64:128, :], cmp_posT.rearrange("d c l -> d (c l)"))
                P.cp("dve", peb[:], pe32[:])
                for l in range(32):
                    P.mm(psC[:, 0:1], BDW[:, (0 * 32 + l) * 128:(0 * 32 + l + 1) * 128], peb[:, l:l + 1], start=(l == 0), stop=(l == 31))
                for l in range(32):
                    P.mm(psC[:, 1:2], BDW[:, (32 + l) * 128:(32 + l + 1) * 128], peb[:, 32 + l:32 + l + 1], start=(l == 0), stop=(l == 31))
                P.cp("dve", posk[:, 0:2], psC[:, 0:2])
                P.barrier()
            cnt = {"x": 0, "pt": 0, "ps": 0, "pa": 0, "ps3": 0}
            xpref = {}

            def rsqrt_small(out, in_, mult, add):
                P.ts("dve", out, in_, mult, add, ALU.mult, ALU.add)
                P.actv(out, out, AF.Sqrt)
                P.op("dve", lambda e: e.reciprocal(out, out), [out], [out])

            def nsa_tile(nq, groups, ns, mulc, addc, cmp_bias, merge, qall=None):
                W = 4 * nq
                wv = 65 + ns
                R = [dict(cmp=psA[0], o=0), dict(cmp=psA[1], o=40)]
                if merge:
                    accT = {2: [psR, psR], 1: [psV, psV]}
                    aoff = [0, W]
                else:
                    accT = {2: [psR, psC], 1: [psV, psA[0]]}
                    aoff = [0, 0]
                back = {2: [psR, psC], 1: [psV, psA[0]]}
                if merge:
                    oTb = [obt[:, 0:256], obt[:, 256:512]]
                else:
                    oTb = [obt[:, :], sz[:, :]]
                v4 = lambda ap: ap.rearrange("p (j q) -> p j q", j=4)
                v24 = lambda ap: ap.rearrange("p (g j q) -> p g j q", g=2, j=4)
                for g, G in enumerate(groups):
                    r = R[g]; o = r["o"]
                    ps_s = psS[cnt["ps"] % 2]; cnt["ps"] += 1
                    P.mm(v4(ps_s[0:127, 0:W]), G["cmp_k"], G["qTg"], start=True, stop=(cmp_bias is None))
                    if cmp_bias is not None:
                        P.mm(v4(ps_s[0:127, 0:W]), ident[0:127, 0:127], cmp_bias, start=False, stop=True)
                    pt = pt_[cnt["pt"] % 3]; cnt["pt"] += 1
                    P.actv(pt[0:127, 0:W], ps_s[0:127, 0:W], AF.Exp, scale=SCALE)
                    pc = r["cmp"]
                    for j in range(4):
                        P.mm(pc[0:nq, j * wv:(j + 1) * wv], pt[0:127, j * nq:(j + 1) * nq], G["cmp_v"], start=True, stop=True)
                    pcv = pc[0:nq, 0:4 * wv].rearrange("p (j w) -> p j w", j=4)
                    rd = rden[0:nq, 16 * g:16 * g + 4]
                    P.ts("dve", rd, pcv[:, :, 64], 1e-30, None, ALU.add)
                    P.op("dve", lambda e, rd=rd: e.reciprocal(rd, rd), [rden], [rden])
                    P.tt("dve", G["on_dst"](0), pcv[:, :, 0:64], bc_last(rd, 64), ALU.mult)
                    sc = sc_[0:nq, o:o + ns]
                    P.ts("dve", sc, pcv[:, 0, 65:65 + ns], rden[0:nq, 16 * g:16 * g + 1], None, ALU.mult)
                    for j in range(1, 4):
                        P.stt("dve", sc, pcv[:, j, 65:65 + ns], rden[0:nq, 16 * g + j:16 * g + j + 1], sc, ALU.mult, ALU.add)
                    P.tt("dve", sc, sc, mulc, ALU.mult)
                    P.tt("dve", sc, sc, addc, ALU.add)
                    t8 = top8[0:nq, 8 * g:8 * g + 8]
                    P.op("dve", lambda e, t8=t8, sc=sc: e.max(t8, sc), [sc_], [top8])
                    P.ts("dve", sc2[0:nq, o:o + ns], sc, top8[0:nq, 8 * g + 7:8 * g + 8], BIG, ALU.is_ge, ALU.mult)
                    P.ts("dve", sc16[0:nq, o:o + ns], sc2[0:nq, o:o + ns], -BIG, None, ALU.add)

                def scores(stp):
                    gs, br, ci, nch, chs = stp["d"]
                    chk = chs[0]
                    (kT, v1, nk, ecols, bias2d) = chk[0:5]
                    kkey = chk[5] if len(chk) > 5 and chk[5] is not None else kT
                    ps_s = (psS[0], psS[1], psA[1])[cnt["ps3"] % 3]; cnt["ps3"] += 1
                    stp["ps"] = ps_s
                    use_e = (br == 1 and ecols is not None)
                    extra = (1 if use_e else 0) + (1 if bias2d is not None else 0)
                    if len(gs) == 2:
                        outv = v24(ps_s[0:nk, 0:2 * W])
                        qr = qall
                        br_ = None if bias2d is None else bias2d.unsqueeze(1).unsqueeze(1).to_broadcast([nk, 2, 4, nq])
                        er_ = selT[0:ns, 0:256].rearrange("p (g q) -> p g q", g=2)[:, :, 0:nq].unsqueeze(2).to_broadcast([ns, 2, 4, nq])
                    else:
                        g = gs[0]
                        outv = v4(ps_s[0:nk, 0:W])
                        qr = groups[g]["qTg"]
                        br_ = None if bias2d is None else bc_mid(bias2d, 4)
                        er_ = bc_mid(selT[0:ns, 128 * g:128 * g + nq], 4)
                    P.mm(outv, kT, qr, start=True, stop=(extra == 0), reads=[kkey, qTz])
                    if bias2d is not None:
                        extra -= 1
                        P.mm(outv, ident[0:nk, 0:nk], br_, start=False, stop=(extra == 0))
                    if use_e:
                        P.mm(outv, ecols, er_, start=False, stop=True)

                def exp_pv(stp):
                    gs, br, ci, nch, chs = stp["d"]
                    nk = chs[0][2]
                    ps_s = stp["ps"]
                    Wt = W * len(gs)
                    pt = pt_[cnt["pt"] % 3]; cnt["pt"] += 1
                    P.actv(pt[0:nk, 0:Wt], ps_s[0:nk, 0:Wt], AF.Exp, scale=SCALE)
                    for k, g in enumerate(gs):
                        chk = chs[k]
                        v1 = chk[1]
                        vkey = chk[6] if len(chk) > 6 and chk[6] is not None else v1
                        acc = accT[br][g]
                        P.mm(acc[0:65, aoff[g]:aoff[g] + W], v1, pt[0:nk, k * W:(k + 1) * W], start=False, stop=(ci == nch - 1 and k == len(gs) - 1), reads=[pt, vkey])

                def run_steps(steps):
                    n = len(steps)
                    D = 2
                    for i in range(min(D, n)):
                        scores(steps[i])
                    for i in range(n):
                        if i + D < n:
                            scores(steps[i + D])
                        exp_pv(steps[i])

                for br, key in ((2, "win_chunks"), (1, "sel_chunks")):
                    if br == 1:
                        for g in range(2):
                            o = R[g]["o"]
                            pc0 = 384 + 512 * g
                            P.tr(psT[0:ns, pc0:pc0 + nq], sc16[0:nq, o:o + ns], ident[0:nq, 0:nq])
                            P.cp("dve", selT[0:ns, 128 * g:128 * g + nq], psT[0:ns, pc0:pc0 + nq])
                    steps = []
                    if merge:
                        acc = accT[br][0]
                        P.mm(acc[0:65, 0:2 * W], cs["zeros_row"][0:1, 0:65], cs["zeros_row"][0:1, 0:2 * W], start=True, stop=False)
                        c0, c1 = groups[0][key], groups[1][key]
                        for ci in range(len(c0)):
                            steps.append({"d": ([0, 1], br, ci, len(c0), [c0[ci], c1[ci]])})
                    else:
                        for g in range(2):
                            acc = accT[br][g]
                            P.mm(acc[0:65, 0:W], cs["zeros_row"][0:1, 0:65], cs["zeros_row"][0:1, 0:W], start=True, stop=False)
                            chunks = groups[g][key]
                            for ci, ch in enumerate(chunks):
                                steps.append({"d": ([g], br, ci, len(chunks), [ch])})
                    run_steps(steps)
                    for g in range(2):
                        acc = accT[br][g]
                        if g == 0:
                            P.cp("act", oTb[g][0:65, 0:W], acc[0:65, aoff[g]:aoff[g] + W])
                        else:
                            P.cp("dve", oTb[g][0:65, 0:W], acc[0:65, aoff[g]:aoff[g] + W])
                    for g in range(2):
                        bk = back[br][g]
                        for j in range(4):
                            P.tr(bk[0:nq, j * 65:(j + 1) * 65], oTb[g][0:65, j * nq:(j + 1) * nq], cs["ident32"][0:65, 0:65])
                        pvv = bk[0:nq, 0:260].rearrange("p (j w) -> p j w", j=4)
                        rd = rden[0:nq, 16 * g + 4 * br:16 * g + 4 * br + 4]
                        P.ts("dve", rd, pvv[:, :, 64], 1e-30, None, ALU.add)
                        P.op("dve", lambda e, rd=rd: e.reciprocal(rd, rd), [rden], [rden])
                        P.tt("dve", groups[g]["on_dst"](br), pvv[:, :, 0:64], bc_last(rd, 64), ALU.mult)

            def even_tile(mode, t, caches):
                isp = (mode == "p")
                xsrc = xp[t * 128:(t + 1) * 128, :] if isp else xs
                if xpref.get("cur") == (mode, t):
                    xtile = xpref["tile"]
                else:
                    xtile = xt[cnt["x"] % 2]; cnt["x"] += 1
                    P.dma(xtile[:], xsrc)
                nxt = caches.get("next")
                if nxt is not None:
                    ntile = xt[cnt["x"] % 2]; cnt["x"] += 1
                    nsrc = xp[nxt[1] * 128:(nxt[1] + 1) * 128, :] if nxt[0] == "p" else xs
                    P.dma(ntile[:], nsrc)
                    xpref["cur"] = nxt
                    xpref["tile"] = ntile
                P.memset("dve", stt_[:, 0:1], 0.0)
                P.actv(mixT[:], xtile[:], AF.Square, accum_out=stt_[:, 0:1])
                rsqrt_small(stt_[:, 1:2], stt_[:, 0:1], 1.0 / 1024, EPS)
                P.cp("act", xb[:], xtile[:])
                for kc in range(8):
                    P.tr(psT[:, kc * 128:(kc + 1) * 128], xb[:, kc * 128:(kc + 1) * 128], ident[:])
                P.cp("act", xT[:], psT[:])
                for gi, (c0, c1) in enumerate(GROUPS):
                    pa = psA[cnt["pa"] % 2]; cnt["pa"] += 1
                    w = c1 - c0
                    for kc in range(8):
                        P.mm(pa[:, 0:w], xT[:, kc * 128:(kc + 1) * 128], Wb[:, kc * EIN + c0:kc * EIN + c1], start=(kc == 0), stop=(kc == 7))
                    if gi % 2 == 0:
                        P.ts("dve", proj[:, c0:c1], pa[:, 0:w], stt_[:, 1:2], None, ALU.mult)
                    else:
                        P.op("act", lambda e, c0=c0, c1=c1, pa=pa, w=w: e.mul(proj[:, c0:c1], pa[:, 0:w], stt_[:, 1:2]), [pa, stt_], [proj])
                if not isp:
                    caches["hook"]()
                if stage <= 1:
                    return
                nv = v3(proj[:, QN:QN + 896], 14)
                P.tt("dve", tmpa[:, 0:896], proj[:, QN:QN + 896], proj[:, QN:QN + 896], ALU.mult)
                P.op("dve", lambda e: e.reduce_sum(stt_[:, 8:22], v3(tmpa[:, 0:896], 14), AX.X), [tmpa], [stt_])
                rsqrt_small(stt_[:, 8:22], stt_[:, 8:22], 1.0 / 64, EPS)
                P.tt("dve", nv, nv, bc_last(stt_[:, 8:22], 64), ALU.mult)
                P.tt("dve", proj[:, QN:QN + 896], proj[:, QN:QN + 896], qkg[:], ALU.mult)
                if isp:
                    cos = cs["cos_p"][:, t * 32:(t + 1) * 32]
                    sin = cs["sin_p"][:, t * 32:(t + 1) * 32]
                else:
                    cos = cs["cos_s"][:, :]
                    sin = cs["sin_s"][:, :]
                pv = v3(proj[:, 0:1408], 22)
                rv = v3(rp[:, 0:1408], 22)
                ta = tmpa[:, 0:704].rearrange("p (h d) -> p h d", h=22)
                tb = tmpa[:, 704:1408].rearrange("p (h d) -> p h d", h=22)
                tc_ = tmpb[:, 0:704].rearrange("p (h d) -> p h d", h=22)
                td_ = tmpb[:, 704:1408].rearrange("p (h d) -> p h d", h=22)
                cosb = bc_mid(cos, 22)
                sinb = bc_mid(sin, 22)
                P.tt("dve", ta, pv[:, :, 0:32], cosb, ALU.mult)
                P.tt("dve", tb, pv[:, :, 32:64], sinb, ALU.mult)
                P.tt("dve", rv[:, :, 0:32], ta, tb, ALU.subtract)
                P.tt("pool", tc_, pv[:, :, 32:64], cosb, ALU.mult)
                P.tt("pool", td_, pv[:, :, 0:32], sinb, ALU.mult)
                P.tt("pool", rv[:, :, 32:64], tc_, td_, ALU.add)
                if stage <= 2:
                    return
                if isp:
                    okv = o_kvp[t * 128:(t + 1) * 128, :]
                else:
                    okv = o_kvs
                P.dma(okv[:, 0:128], rp[:, KC:KC + 128])
                P.dma(okv[:, 128:256], proj[:, VC:VC + 128])
                P.dma(okv[:, 256:384], rp[:, KS:KS + 128])
                P.dma(okv[:, 384:512], proj[:, VS:VS + 128])
                if isp and t >= 12:
                    ow = o_winp[(t - 12) * 128:(t - 11) * 128, :]
                    P.dma(ow[:, 0:128], rp[:, KW:KW + 128])
                    P.dma(ow[:, 128:256], proj[:, VW:VW + 128])
                if not isp:
                    for b in range(16):
                        P.dma(o_wins[b, 504:512, 0:128], rp[b * 8:(b + 1) * 8, KW:KW + 128])
                        P.dma(o_wins[b, 504:512, 128:256], proj[b * 8:(b + 1) * 8, VW:VW + 128])
                        P.dma(o_wins[b, 0:504, :], cwin[b, 8:512, :])
                if stage <= 3:
                    return
                qdec = cs["qdec_p"] if isp else cs["qdec_s"]
                kdec = cs["kdec_p"] if isp else cs["kdec_s"]
                DTm = cs["DTp"] if isp else cs["DTs"]
                P.cp("act", r16[:, 0:256], rp[:, QA:QA + 256])
                P.tt("dve", v3(r16[:, 256:512], 4), v3(rp[:, QA:QA + 256], 4), bc_last(qdec[:, 0:4], 64), ALU.mult)
                P.cp("act", r16[:, 512:768], rp[:, KA:KA + 256])
                P.tt("dve", v3(r16[:, 768:1024], 4), v3(rp[:, KA:KA + 256], 4), bc_last(kdec[:, 0:4], 64), ALU.mult)
                P.cp("act", vb16[:], proj[:, VA:VA + 512])
                for i6 in range(6):
                    P.tr(psT[:, i6 * 128:(i6 + 1) * 128], r16[:, i6 * 128:(i6 + 1) * 128], ident[:])
                P.cp("act", rT[:, 0:256], psT[:, 512:768])
                qzv = qz[:].rearrange("p (a h q) -> p a h q", a=4, h=2)
                P.cp("act", qzv[0:64, :, 0, :], psT[0:64, 0:512].rearrange("p (a q) -> p a q", a=4))
                P.cp("dve", qzv[64:128, :, 1, :], psT[64:128, 0:512].rearrange("p (a q) -> p a q", a=4))
                if stage <= 3.1:
                    return
                for h in range(4):
                    hp, pr = h % 2, h // 2
                    P.mm(psR[:, h * 128:(h + 1) * 128], rT[:, pr * 128:(pr + 1) * 128],
                         qz[:, ((0 * 2 + pr) * 2 + hp) * 128:((0 * 2 + pr) * 2 + hp + 1) * 128])
                if stage <= 3.2:
                    return
                P.tt("dve", scm[:], psR[:], DTm[:], ALU.mult)
                if isp:
                    for h in range(4):
                        hp, pr = h % 2, h // 2
                        P.mm(psC[:, h * 128:(h + 1) * 128], scm[:, h * 128:(h + 1) * 128], vb16[:, h * 128:(h + 1) * 128], start=True, stop=False)
                        P.mm(psC[:, h * 128:(h + 1) * 128], qz[:, ((1 * 2 + pr) * 2 + hp) * 128:((1 * 2 + pr) * 2 + hp + 1) * 128],
                             Sb[:, pr * 128:(pr + 1) * 128], start=False, stop=True)
                else:
                    S0b = caches["S0b"]
                    qdTm = caches["qdTm"]
                    for h in range(4):
                        hp, pr = h % 2, h // 2
                        if hp == 0:
                            for b in range(16):
                                P.tt("dve" if b % 2 == 0 else "pool", qdTm[:, b * 256:(b + 1) * 256].rearrange("p (a q) -> p a q", a=2),
                                     qz[:, 512 + pr * 256:512 + (pr + 1) * 256].rearrange("p (a q) -> p a q", a=2),
                                     bc_mid(cs["colmask"][:, b * 128:(b + 1) * 128], 2), ALU.mult)
                        P.mm(psC[:, h * 128:(h + 1) * 128], scm[:, h * 128:(h + 1) * 128], vb16[:, h * 128:(h + 1) * 128], start=True, stop=False)
                        for b in range(16):
                            P.mm(psC[:, h * 128:(h + 1) * 128], qdTm[:, b * 256 + hp * 128:b * 256 + (hp + 1) * 128],
                                 S0b[:, (b * 2 + pr) * 128:(b * 2 + pr + 1) * 128], start=False, stop=(b == 15))
                if stage <= 3.4:
                    return
                P.cp("act", osb[:], psC[:])
                if stage <= 3.5:
                    return
                if isp:
                    for h in range(4):
                        hp, pr = h % 2, h // 2
                        P.mm(psR[:, h * 128:(h + 1) * 128], r16[:, 768 + pr * 128:768 + (pr + 1) * 128], vb16[:, h * 128:(h + 1) * 128])
                    for h in range(4):
                        hp, pr = h % 2, h // 2
                        rows = slice(hp * 64, (hp + 1) * 64)
                        P.stt("dve", S32[rows, pr * 128:(pr + 1) * 128], S32[rows, pr * 128:(pr + 1) * 128], cs["cdec_p"][rows, pr:pr + 1],
                              psR[rows, h * 128:(h + 1) * 128], ALU.mult, ALU.add)
                    P.cp("act", Sb[:], S32[:])
                    if t == n_ptiles - 1:
                        for h in range(4):
                            hp, pr = h % 2, h // 2
                            P.dma(o_retp[h, :, :], S32[hp * 64:(hp + 1) * 64, pr * 128:(pr + 1) * 128])
                else:
                    S0 = caches["S0"]
                    vblk = caches["vblk"]
                    Sn = caches["Sn"]
                    for h in range(4):
                        hp, pr = h % 2, h // 2
                        rows = slice(hp * 64, (hp + 1) * 64)
                        P.tt("dve", vblk[:].rearrange("p (b e) -> p b e", b=16), bc_mid(vb16[:, h * 128:(h + 1) * 128], 16),
                             bc_last(cs["blkmask"][:, 0:16], 128), ALU.mult)
                        for q4 in range(4):
                            pa = psA[cnt["pa"] % 2]; cnt["pa"] += 1
                            P.mm(pa[:, :], r16[:, 768 + pr * 128:768 + (pr + 1) * 128], vblk[:, q4 * 512:(q4 + 1) * 512])
                            s0v = S0[rows, :].rearrange("p (b a e) -> p b a e", b=16, a=2)[:, q4 * 4:(q4 + 1) * 4, pr, :]
                            P.stt("dve", Sn[rows, q4 * 512:(q4 + 1) * 512].rearrange("p (b e) -> p b e", b=4), s0v, cs["cdec_s"][rows, pr:pr + 1],
                                  pa[rows, :].rearrange("p (b e) -> p b e", b=4), ALU.mult, ALU.add)
                        P.dma(o_rets[:, h, :, :].rearrange("b d e -> d b e"), Sn[rows, :].rearrange("p (b e) -> p b e", b=16))
                if stage <= 4:
                    return
                P.op("dve", lambda e: e.reduce_sum(stt_[:, 24:28], v3(osb[:], 4), AX.X), [osb], [stt_])
                P.ts("dve", stt_[:, 24:28], stt_[:, 24:28], -1.0 / 128, None, ALU.mult)
                P.tt("dve", v3(ocb[:], 4), v3(osb[:], 4), bc_last(stt_[:, 24:28], 128), ALU.add)
                P.tt("dve", osb[:], ocb[:], ocb[:], ALU.mult)
                P.op("dve", lambda e: e.reduce_sum(stt_[:, 28:32], v3(osb[:], 4), AX.X), [osb], [stt_])
                rsqrt_small(stt_[:, 28:32], stt_[:, 28:32], 1.0 / 128, EPS)
                P.tt("dve", v3(ocb[:], 4), v3(ocb[:], 4), bc_last(stt_[:, 28:32], 128), ALU.mult)
                P.tt("dve", ocb[:], ocb[:], gnb[:], ALU.mult)
                P.actv(sz[:], proj[:, ZA:ZA + 512], AF.Silu)
                P.tt("dve", mix[:, 0:512], ocb[:], sz[:], ALU.mult)
                if stage <= 5:
                    return
                P.actv(gts[:], proj[:, GL:GL + 24], AF.Sigmoid)
                P.cp("act", n16[:, 0:896], rp[:, QN:QN + 896])
                P.cp("dve", n16[:, 896:1024], proj[:, VC:VC + 128])
                for i4 in range(4):
                    P.tr(psT[:, i4 * 128:(i4 + 1) * 128], n16[:, i4 * 128:(i4 + 1) * 128], ident[:])
                P.cp("act", qTz[0:64, 0:512], psT[0:64, 0:512])
                P.cp("dve", qTz[64:128, 512:1024], psT[64:128, 0:512])
                for i4 in range(4):
                    P.tr(psT[:, i4 * 128:(i4 + 1) * 128], n16[:, 512 + i4 * 128:512 + (i4 + 1) * 128], ident[:])
                if isp:
                    cT = caches["cT"]; vs1 = caches["vs1"]; vw1 = caches["vw1"]
                    P.cp("act", cT[:].rearrange("p (k n) -> p k n", k=4)[:, :, t * 128:(t + 1) * 128], psT[:, 0:512].rearrange("p (k n) -> p k n", k=4))
                    P.cp("dve", vs1[:].rearrange("p (c g w) -> p c g w", c=16, g=2)[:, t, :, 0:64], v3(proj[:, VS:VS + 128], 2))
                    P.cp("dve", vw1[:].rearrange("p (c g w) -> p c g w", c=16, g=2)[:, t, :, 0:64], v3(proj[:, VW:VW + 128], 2))
                    ckT = caches["ckT"]; cvx = caches["cvx"]
                    compress(cT, 0, cT, 3 * 2048, ckT, caches["cvT"], cvx, 97, max(0, 8 * t - 1), 8 * t + 6)
                    groups = []
                    for g in range(2):
                        qTg = qTz[:, g * 512:(g + 1) * 512].rearrange("p (j q) -> p j q", j=4)
                        selc = []
                        for c in range(t + 1):
                            bias = cs["CB"][:, :] if c == t else None
                            selc.append((cT[:, 2048 + c * 128:2048 + (c + 1) * 128], vs1[:, (c * 2 + g) * 65:(c * 2 + g + 1) * 65], 128,
                                         cs["E"][0:32, c * 128:(c + 1) * 128], bias))
                        winc = []
                        for c in range(max(0, t - 4), t + 1):
                            bias = cs["CB"][:, :] if c == t else (cs["AB"][:, :] if c == t - 4 else None)
                            winc.append((cT[:, 4096 + c * 128:4096 + (c + 1) * 128], vw1[:, (c * 2 + g) * 65:(c * 2 + g + 1) * 65], 128, None, bias))
                        groups.append(dict(qTg=qTg, cmp_k=ckT[:, 0:127], cmp_v=cvx[0:127, g * 97:(g + 1) * 97], sel_chunks=selc, win_chunks=winc,
                                           on_dst=lambda x, g=g: on[:, x * 512 + g * 256:x * 512 + (g + 1) * 256].rearrange("p (j d) -> p j d", j=4)))
                    cmpb = bc_mid(cs["CMPB"][0:127, t * 128:(t + 1) * 128], 4)
                    nsa_tile(128, groups, 32, cs["mulc_p"][:, t * 32:(t + 1) * 32], cs["addc_p"][:, t * 32:(t + 1) * 32], cmpb, False)
                else:
                    cTs = caches["cTs"]; vs1s = caches["vs1s"]
                    P.cp("act", cTs[:], psT[:, 0:512])
                    P.cp("dve", vs1s[:].rearrange("p (k g w) -> p k g w", k=2, g=2)[:, 0, :, 0:64], v3(proj[:, VS:VS + 128], 2))
                    P.cp("dve", vs1s[:].rearrange("p (k g w) -> p k g w", k=2, g=2)[:, 1, :, 0:64], v3(proj[:, VW:VW + 128], 2))
                    sample_nsa(caches)
                if stage <= 6:
                    return
                gv = gts[:].rearrange("p (h x) -> p h x", h=8)
                for x in range(3):
                    P.tt("dve", v3(on[:, x * 512:(x + 1) * 512], 8), v3(on[:, x * 512:(x + 1) * 512], 8), bc_last(gv[:, :, x], 64), ALU.mult)
                P.tt("dve", obt[:], on[:, 0:512], on[:, 512:1024], ALU.add)
                P.tt("dve", obt[:], obt[:], on[:, 1024:1536], ALU.add)
                P.actv(sz[:], proj[:, ZB:ZB + 512], AF.Silu)
                P.tt("dve", mix[:, 512:1024], obt[:], sz[:], ALU.mult)
                if stage <= 7:
                    return
                for kc in range(8):
                    P.tr(psT[:, kc * 128:(kc + 1) * 128], mix[:, kc * 128:(kc + 1) * 128], ident[:])
                P.cp("act", mixT[:], psT[:])
                for hf in range(2):
                    pa = psA[cnt["pa"] % 2]; cnt["pa"] += 1
                    for kc in range(8):
                        P.mm(pa[:, :], mixT[:, kc * 128:(kc + 1) * 128], Wob[:, kc * 1024 + hf * 512:kc * 1024 + (hf + 1) * 512], start=(kc == 0), stop=(kc == 7))
                    P.tt("dve", xtile[:, hf * 512:(hf + 1) * 512], xtile[:, hf * 512:(hf + 1) * 512], pa[:, :], ALU.add)
                ydst = yp[t * 128:(t + 1) * 128, :] if isp else ys
                P.dma(ydst, xtile[:], writes=[("y0", mode, t)])

            def compress(kc_t, kc_off, vc_t, vc_off, ckT, cvT, cvx, wv, n0, n1, rkeys=None):
                nn = n1 - n0 + 1
                rd = ([BDW] + rkeys) if rkeys else None
                for l in range(32):
                    a0 = kc_off + 16 * n0 + l
                    P.mm(psC[:, 0:nn], BDW[:, l * 128:(l + 1) * 128], kc_t[:, a0:a0 + 16 * (nn - 1) + 1:16], start=(l == 0), stop=(l == 31), reads=rd)
                for l in range(32):
                    a0 = vc_off + 16 * n0 + l
                    P.mm(psC[:, 128:128 + nn], BDW[:, (32 + l) * 128:(33 + l) * 128], vc_t[:, a0:a0 + 16 * (nn - 1) + 1:16], start=(l == 0), stop=(l == 31), reads=rd)
                P.ts("dve", ckT[:, n0:n1 + 1], psC[:, 0:nn], posk[:, 0:1], None, ALU.add)
                P.ts("dve", cvT[:, n0:n1 + 1], psC[:, 128:128 + nn], posk[:, 1:2], None, ALU.add)
                P.tr(psT[0:127, 0:128], cvT[:, 0:127], ident[:])
                P.cp("act", cvx[0:127, :].rearrange("p (g w) -> p g w", g=2)[:, :, 0:64], psT[0:127, 0:128].rearrange("p (g d) -> p g d", g=2))

            def sample_nsa(caches):
                C = caches
                cTq, kwTq, vs1q, vw1q, ckTq, cvxq = C["cTq"], C["kwTq"], C["vs1q"], C["vw1q"], C["ckTq"], C["cvxq"]
                cTs, vs1s, idx = C["cTs"], C["vs1s"], C["idx"]
                on8 = [osb, ocb, sz]
                wst32, w16 = C["wst32"], C["w16"]
                P.barrier()
                C["stR"].close()
                stPG = C["stPG"]
                pgb = [alloc(stPG, "pgb%d" % i, [128, 16 * 384], BF16) for i in range(2)]
                pg = [alloc(stPG, "pgf%d" % i, [128, 512]) for i in range(6)]
                for b in range(16):
                    pb = pgb[b % 2]
                    for i in range(16):
                        pgt = pg[(b * 16 + i) % 6]
                        col = b * 16 + i
                        P.dma(pgt[:], cache, reads=[cache, idx], q="pool",
                              fn=lambda e, pgt=pgt, col=col: e.indirect_dma_start(
                                  out=pgt[:, :], out_offset=None, in_=cache[:, :],
                                  in_offset=bass.IndirectOffsetOnAxis(ap=idx[:, col:col + 1], axis=0)))
                        P.cp("act", pb[:, i * 384:(i + 1) * 384], pgt[:, 0:384], writes=[("pgb", b % 2, i)])
                        P.cp("dve", vs1q[:].rearrange("p (c g w) -> p c g w", c=16, g=2)[:, i, :, 0:64], v3(pgt[:, 384:512], 2), writes=[("vs1q", i)])
                    psRb = psR[:].bitcast(BF16)
                    for i in range(16):
                        pdst = psT[:, 0:384] if i % 2 == 0 else psRb[:, 0:384]
                        pkey = psT if i % 2 == 0 else psR
                        for k in range(3):
                            P.tr(pdst[:, k * 128:(k + 1) * 128], pb[:, i * 384 + k * 128:i * 384 + (k + 1) * 128], ident[:],
                                 reads=[("pgb", b % 2, i), ident], writes=[pkey])
                        P.cp("act" if i % 2 == 0 else "dve", cTq[:].rearrange("p (k n) -> p k n", k=3)[:, :, i * 128:(i + 1) * 128],
                             pdst.rearrange("p (k n) -> p k n", k=3), reads=[pkey], writes=[("cTq", i)])
                    P.dma(wst32[:].rearrange("p (c w) -> p c w", c=4), cwin[b].rearrange("(c r) w -> r c w", r=128))
                    P.cp("dve", w16[:], wst32[:])
                    for c in range(4):
                        P.tr(psT[:, 512 + c * 128:512 + (c + 1) * 128], w16[:, c * 256:c * 256 + 128], ident[:])
                    P.cp("act", kwTq[:], psT[:, 512:1024])
                    for c in range(4):
                        P.cp("dve", vw1q[:, c * 130:(c + 1) * 130].rearrange("p (g w) -> p g w", g=2)[:, :, 0:64], v3(w16[:, c * 256 + 128:(c + 1) * 256], 2))
                    compress(cTq, 0, cTq, 2048, ckTq, C["cvTq"], cvxq, 98, 0, 126, rkeys=[("cTq", i) for i in range(16)])
                    groups = []
                    for g in range(2):
                        qTg = qTz[:, g * 512:(g + 1) * 512].rearrange("p (j q) -> p j q", j=4)[:, :, 8 * b:8 * b + 8]
                        sbias = cs["SB"][:, b * 8:(b + 1) * 8]
                        selc = []
                        for c in range(16):
                            selc.append((cTq[:, 4096 + c * 128:4096 + (c + 1) * 128], vs1q[:, (c * 2 + g) * 65:(c * 2 + g + 1) * 65], 128,
                                         cs["E"][0:33, c * 128:(c + 1) * 128], None, ("cTq", c), ("vs1q", c)))
                        selc.append((cTs[:, 128:256], vs1s[:, (0 * 2 + g) * 65:(0 * 2 + g + 1) * 65], 128, None, sbias))
                        winc = []
                        for c in range(4):
                            bias = cs["ABs"][:, :] if c == 0 else None
                            winc.append((kwTq[:, c * 128:(c + 1) * 128], vw1q[:, (c * 2 + g) * 65:(c * 2 + g + 1) * 65], 128, None, bias))
                        winc.append((cTs[:, 256:384], vs1s[:, (1 * 2 + g) * 65:(1 * 2 + g + 1) * 65], 128, None, sbias))
                        groups.append(dict(qTg=qTg, cmp_k=ckTq[:, 0:127], cmp_v=cvxq[0:127, g * 98:(g + 1) * 98], sel_chunks=selc, win_chunks=winc,
                                           on_dst=lambda x, g=g: on8[x][0:8, g * 256:(g + 1) * 256].rearrange("p (j d) -> p j d", j=4)))
                    qall = qTz[:, :].rearrange("p (g j q) -> p g j q", g=2, j=4)[:, :, :, 8 * b:8 * b + 8]
                    nsa_tile(8, groups, 33, cs["mulc_s"][:, :], cs["addc_s"][:, :], None, True, qall)
                    for x in range(3):
                        P.dma(on[b * 8:(b + 1) * 8, x * 512:(x + 1) * 512], on8[x][0:8, :])
                P.barrier()

            with ExitStack() as stP:
                Ap = lambda name, shape, dt=F32: alloc(stP, name, shape, dt)
                load_consts(stP, P_ONLY)
                cT = Ap("cT", [128, 4 * 2048], BF16)
                vs1 = Ap("vs1", [128, 16 * 2 * 65], BF16)
                vw1 = Ap("vw1", [128, 16 * 2 * 65], BF16)
                ckT = Ap("ckT", [128, 128], BF16)
                cvT = Ap("cvT", [128, 128], BF16)
                cvx = Ap("cvx", [128, 2 * 97], BF16)
                ov32 = Ap("ov32", [128, 32])
                P.memset("pool", cT[:], 0.0)
                P.memset("pool", vs1[:], 1.0)
                P.memset("pool", vw1[:], 1.0)
                P.memset("pool", cvx[:], 1.0)
                P.memset("pool", ckT[:], 0.0)
                P.memset("pool", cvT[:], 0.0)
                for g in range(2):
                    P.cp("dve", cvx[0:127, g * 97 + 65:(g + 1) * 97], cs["ovl_p"][0:127, :])
                caches = {"cT": cT, "vs1": vs1, "vw1": vw1, "ckT": ckT, "cvx": cvx, "cvT": cvT}
                for t in range(n_ptiles):
                    caches["next"] = ("p", t + 1) if t + 1 < n_ptiles else (("s", 0) if do_sample else None)
                    even_tile("p", t, caches)
                P.barrier()

            if do_sample:
                stS = ExitStack()
                stPG = ExitStack()
                scaches = {}

                def sample_hook():
                    P.barrier()
                    stWb.close()
                    As = lambda name, shape, dt=F32: alloc(stS, name, shape, dt)
                    load_consts(stS, S_ONLY)
                    C = scaches
                    C["cTq"] = As("cTq", [128, 3 * 2048], BF16)
                    C["kwTq"] = As("kwTq", [128, 512], BF16)
                    C["vs1q"] = As("vs1q", [128, 16 * 130], BF16)
                    C["vw1q"] = As("vw1q", [128, 4 * 130], BF16)
                    C["ckTq"] = As("ckTq", [128, 128], BF16)
                    C["cvTq"] = As("cvTq", [128, 128], BF16)
                    C["cvxq"] = As("cvxq", [128, 2 * 98], BF16)
                    C["cTs"] = As("cTs", [128, 512], BF16)
                    C["vs1s"] = As("vs1s", [128, 4 * 65], BF16)
                    C["wst32"] = As("wst32", [128, 1024])
                    C["w16"] = As("w16", [128, 1024], BF16)
                    C["idx"] = As("idx", [128, 256], I32)
                    pti = As("pti", [128, 256], I32)
                    stR = ExitStack()
                    C["stR"] = stR
                    C["stPG"] = stPG
                    Ar = lambda name, shape, dt=F32: alloc(stR, name, shape, dt)
                    C["S0"] = Ar("S0", [128, 4096])
                    C["S0b"] = Ar("S0b", [128, 4096], BF16)
                    C["qdTm"] = Ar("qdTm", [128, 16 * 256], BF16)
                    C["vblk"] = Ar("vblk", [128, 2048], BF16)
                    C["Sn"] = Ar("Sn", [128, 2048])
                    ptf = tmpa[:, 0:256]
                    P.memset("pool", C["vs1q"][:], 1.0)
                    P.memset("pool", C["vw1q"][:], 1.0)
                    P.memset("pool", C["vs1s"][:], 1.0)
                    P.memset("pool", C["cvxq"][:], 1.0)
                    for g in range(2):
                        P.cp("dve", C["cvxq"][0:127, g * 98 + 65:(g + 1) * 98], cs["ovl_s"][0:127, :])
                    for hp in range(2):
                        P.dma(C["S0"][hp * 64:(hp + 1) * 64, :].rearrange("p (b a e) -> p b a e", b=16, a=2),
                              sret[:, hp::2, :, :].rearrange("b a d e -> d b a e"))
                    P.cp("dve", C["S0b"][:], C["S0"][:])
                    P.dma(pti[:], ptab.partition_broadcast(128))
                    P.cp("dve", ptf, pti[:])
                    P.ts("dve", ptf, ptf, 128.0, cs["iota_p"][:, 0:1], ALU.mult, ALU.add)
                    P.cp("dve", C["idx"][:], ptf)

                scaches["hook"] = sample_hook
                even_tile("s", 0, scaches)
                P.barrier()
                stPG.close()
                stS.close()
            else:
                stWb.close()
            P.barrier()
        if phase_b:
            NT = 2176
            NCH = 272
            TWO_PI = 2.0 * math.pi
            with ExitStack() as stB:
                Bf = lambda name, shape, dt=F32: alloc(stB, name, shape, dt)
                load_consts(stB, ["ident", "JV", "KI", "PM4", "BM16"], pre="kb_")
                ident = cs["ident"]
                uT = Bf("uT", [128, 8 * NT], BF16)
                szT = Bf("szT", [128, 8 * NT], BF16)
                ngo = Bf("ngo", [128, 8])
                dsk = Bf("dsk", [128, 8])
                statb = Bf("statb", [128, 8])
                FSp = Bf("FSp", [128, 64])
                FSs = Bf("FSs", [128, 1024])
                P.dma(ngo[:], norm_o)
                P.dma(dsk[:], ssmd)
                cntb = {"x": 0, "pa": 0}

                def load_tile(dst, ti):
                    if ti < 16:
                        P.dma(dst, yp[ti * 128:(ti + 1) * 128, :], reads=[("y0", "p", ti)])
                    else:
                        P.dma(dst, ys, reads=[("y0", "s", 0)])

                with ExitStack() as st1:
                    B1 = lambda name, shape, dt=F32: alloc(st1, name, shape, dt)
                    Wodd = B1("Wodd", [128, 8 * 2048], BF16)
                    yt = [B1("yt%d" % i, [128, 1024]) for i in range(2)]
                    hb = B1("hb", [128, 1024], BF16)
                    hT = B1("hT", [128, 8 * 512], BF16)
                    with ExitStack() as stg_:
                        stg = [alloc(stg_, "stgb%d" % i, [128, 2048]) for i in range(4)]
                        for kc in range(8):
                            s_ = stg[kc % 4]
                            P.dma(s_[:], w_in_o[kc * 128:(kc + 1) * 128, :])
                            if kc % 2 == 0:
                                P.ts("dve", Wodd[:, kc * 2048:(kc + 1) * 2048], s_[:], ngo[:, kc:kc + 1], None, ALU.mult)
                            else:
                                P.op("act", lambda e, kc=kc, s_=s_: e.mul(Wodd[:, kc * 2048:(kc + 1) * 2048], s_[:], ngo[:, kc:kc + 1]), [s_, ngo], [Wodd])
                        P.barrier()
                    blocks = [(0, 4), (4, 8), (8, 12), (12, 16), (16, 17)]
                    for (t0, t1) in blocks:
                        nb = (t1 - t0) * 128
                        col0 = t0 * 128
                        for ti in range(t0, t1):
                            ytile = yt[cntb["x"] % 2]; cntb["x"] += 1
                            load_tile(ytile[:], ti)
                            P.memset("dve", statb[:, 0:1], 0.0)
                            P.actv(hb[:], ytile[:], AF.Square, accum_out=statb[:, 0:1])
                            P.ts("dve", statb[:, 1:2], statb[:, 0:1], 1.0 / 1024, EPS, ALU.mult, ALU.add)
                            P.actv(statb[:, 1:2], statb[:, 1:2], AF.Sqrt)
                            P.op("dve", lambda e: e.reciprocal(statb[:, 1:2], statb[:, 1:2]), [statb], [statb])
                            P.ts("dve", hb[:], ytile[:], statb[:, 1:2], None, ALU.mult)
                            for kc in range(8):
                                P.tr(psT[:, kc * 128:(kc + 1) * 128], hb[:, kc * 128:(kc + 1) * 128], ident[:])
                            lt = ti - t0
                            P.cp("act", hT[:].rearrange("p (k n) -> p k n", k=8)[:, :, lt * 128:(lt + 1) * 128], psT[:].rearrange("p (k n) -> p k n", k=8))
                        for oc in range(16):
                            pa = psA[cntb["pa"] % 2]; cntb["pa"] += 1
                            for kc in range(8):
                                P.mm(pa[:, 0:nb], Wodd[:, kc * 2048 + oc * 128:kc * 2048 + (oc + 1) * 128], hT[:, kc * 512:kc * 512 + nb], start=(kc == 0), stop=(kc == 7))
                            if oc < 8:
                                P.cp("dve", uT[:, oc * NT + col0:oc * NT + col0 + nb], pa[:, 0:nb])
                            else:
                                P.actv(szT[:, (oc - 8) * NT + col0:(oc - 8) * NT + col0 + nb], pa[:, 0:nb], AF.Silu)
                    P.barrier()

                with ExitStack() as st2:
                    B2 = lambda name, shape, dt=F32: alloc(st2, name, shape, dt)
                    lr = B2("lr", [128, 32]); li = B2("li", [128, 32]); ls = B2("ls", [128, 32])
                    bre = B2("bre", [128, 512]); bim = B2("bim", [128, 512])
                    cre = B2("cre", [128, 512]); cim = B2("cim", [128, 512])
                    x0r = B2("x0r", [128, 512]); x0i = B2("x0i", [128, 512])
                    for tl, src in ((lr, lamre_A), (li, lamim_A), (ls, lstep_A), (bre, bA_re), (bim, bA_im), (cre, cA_re), (cim, cA_im), (x0r, x0A_re), (x0i, x0A_im)):
                        P.dma(tl[:], src)
                    aa = B2("aa", [128, 32]); th = B2("th", [128, 32])
                    A9 = B2("A9", [128, 288]); T9 = B2("T9", [128, 288]); T9c = B2("T9c", [128, 288])
                    PR = B2("PR", [128, 288]); PI = B2("PI", [128, 288])
                    rri = B2("rri", [128, 1024], I32); rri2 = B2("rri2", [128, 1024], I32)
                    npi = B2("npi", [128, 1])

                    hpi = B2("hpi", [128, 1])
                    P.memset("dve", hpi[:], math.pi / 2)

                    def range_reduce(x, n):
                        P.ts("dve", rri[:, 0:n], x, 1.0 / TWO_PI, None, ALU.mult)
                        P.stt("dve", x, rri[:, 0:n], -TWO_PI, x, ALU.mult, ALU.add)

                    def sincos(x, n, sin_out, cos_out):
                        P.ts("dve", rri[:, 0:n], x, 1.0 / TWO_PI, None, ALU.mult)
                        P.stt("dve", sin_out, rri[:, 0:n], -TWO_PI, x, ALU.mult, ALU.add)
                        P.actv(sin_out, sin_out, AF.Sin)
                        P.ts("dve", rri2[:, 0:n], x, 1.0 / TWO_PI, 0.25, ALU.mult, ALU.add)
                        P.stt("dve", cos_out, rri2[:, 0:n], -TWO_PI, x, ALU.mult, ALU.add)
                        P.actv(cos_out, cos_out, AF.Sin, bias=hpi[:, 0:1])

                    P.actv(ls[:], ls[:], AF.Exp)
                    P.tt("dve", aa[:], lr[:], ls[:], ALU.mult)
                    P.tt("dve", th[:], li[:], ls[:], ALU.mult)
                    JV = cs["JV"]
                    P.tt("dve", A9[:].rearrange("p (j m) -> p j m", j=9), JV[:].rearrange("p (j m) -> p j m", j=9), bc_mid(aa[:, :], 9), ALU.mult)
                    P.actv(A9[:], A9[:], AF.Exp)
                    range_reduce(th[:, :], 32)
                    P.tt("dve", T9[:].rearrange("p (j m) -> p j m", j=9), JV[:].rearrange("p (j m) -> p j m", j=9), bc_mid(th[:, :], 9), ALU.mult)
                    sincos(T9[:, :], 288, PI[:, :], T9c[:, :])
                    P.tt("dve", PR[:], A9[:], T9c[:], ALU.mult)
                    P.tt("dve", PI[:], A9[:], PI[:], ALU.mult)
                    PRv = PR[:].rearrange("p (j m) -> p j m", j=9)
                    PIv = PI[:].rearrange("p (j m) -> p j m", j=9)
                    den = B2("den", [128, 32]); nr = B2("nr", [128, 32]); fre = B2("fre", [128, 32]); fim = B2("fim", [128, 32]); t32 = B2("t32", [128, 32])
                    P.tt("dve", den[:], lr[:], lr[:], ALU.mult)
                    P.tt("dve", t32[:], li[:], li[:], ALU.mult)
                    P.tt("dve", den[:], den[:], t32[:], ALU.add)
                    P.op("dve", lambda e: e.reciprocal(den[:], den[:]), [den], [den])
                    P.ts("dve", nr[:], PR[:, 32:64], -1.0, None, ALU.add)
                    P.tt("dve", fre[:], nr[:], lr[:], ALU.mult)
                    P.tt("dve", t32[:], PI[:, 32:64], li[:], ALU.mult)
                    P.tt("dve", fre[:], fre[:], t32[:], ALU.add)
                    P.tt("dve", fre[:], fre[:], den[:], ALU.mult)
                    P.tt("dve", fim[:], PI[:, 32:64], lr[:], ALU.mult)
                    P.tt("dve", t32[:], nr[:], li[:], ALU.mult)
                    P.tt("dve", fim[:], fim[:], t32[:], ALU.subtract)
                    P.tt("dve", fim[:], fim[:], den[:], ALU.mult)
                    Bre = B2("Bre", [128, 512]); Bim = B2("Bim", [128, 512]); t512 = B2("t512", [128, 512])
                    v16 = lambda ap: ap.rearrange("p (m c) -> p m c", c=16)
                    P.tt("dve", v16(Bre[:]), v16(bre[:]), bc_last(fre[:, :], 16), ALU.mult)
                    P.tt("dve", v16(t512[:]), v16(bim[:]), bc_last(fim[:, :], 16), ALU.mult)
                    P.tt("dve", Bre[:], Bre[:], t512[:], ALU.subtract)
                    P.tt("dve", v16(Bim[:]), v16(bim[:]), bc_last(fre[:, :], 16), ALU.mult)
                    P.tt("dve", v16(t512[:]), v16(bre[:]), bc_last(fim[:, :], 16), ALU.mult)
                    P.tt("dve", Bim[:], Bim[:], t512[:], ALU.add)
                    Cbd_re = B2("Cbd_re", [128, 32 * 32], BF16); Cbd_nim = B2("Cbd_nim", [128, 32 * 32], BF16)
                    P.memset("pool", Cbd_re[:], 0.0)
                    P.memset("pool", Cbd_nim[:], 0.0)
                    cbv = lambda t_: t_[:].rearrange("p (m c) -> p m c", c=32)
                    for hp in range(2):
                        rows = slice(hp * 64, (hp + 1) * 64)
                        P.cp("dve", cbv(Cbd_re)[rows, :, hp * 16:(hp + 1) * 16], v16(cre[:])[rows, :, :])
                        P.ts("dve", cbv(Cbd_nim)[rows, :, hp * 16:(hp + 1) * 16], v16(cim[:])[rows, :, :], -1.0, None, ALU.mult)
                    th8 = B2("th8", [128, 32])
                    P.ts("dve", th8[:], th[:], 8.0, None, ALU.mult)
                    range_reduce(th8[:, :], 32)
                    Xre = B2("Xre", [128, 512]); Xim = B2("Xim", [128, 512]); tX = B2("tX", [128, 512])
                    XBD = B2("XBD", [128, 2 * 8 * 4 * 32], BF16)
                    VZ = B2("VZ", [128, 8 * 4 * 2 * 128], BF16)
                    WS = B2("WS", [128, 8 * 2 * 128], BF16)
                    uzb = [B2("uz%d" % i, [128, NT], BF16) for i in range(2)]
                    BD = B2("BD", [128, 8 * 128], BF16)
                    Sin_r = B2("Sin_r", [128, 4 * NCH]); Sin_i = B2("Sin_i", [128, 4 * NCH])
                    cosT = B2("cosT", [128, 1024]); sinT = B2("sinT", [128, 1024])
                    c_r = B2("c_r", [128, 1024]); c_i = B2("c_i", [128, 1024]); t1k = B2("t1k", [128, 1024])
                    w_r = B2("w_r", [128, 1024]); w_i = B2("w_i", [128, 1024])
                    Sp_r = B2("Sp_r", [128, 4 * NCH], BF16); Sp_i = B2("Sp_i", [128, 4 * NCH], BF16)
                    yvb = [B2("yv%d" % i, [128, NCH]) for i in range(2)]; y2b = [B2("y2%d" % i, [128, NCH]) for i in range(2)]; y3b = [B2("y3%d" % i, [128, NCH]) for i in range(2)]
                    ygc = B2("ygc", [128, NT], BF16)
                    P.memset("pool", XBD[:], 0.0)
                    P.memset("pool", VZ[:], 0.0)
                    for c in range(8):
                        msl = slice(4 * c, 4 * c + 4)
                        x4 = lambda t_: t_[:].rearrange("p (t m c) -> p t m c", t=8, m=4)
                        prb = PRv[:, 0:8, msl].unsqueeze(3).to_broadcast([128, 8, 4, 16])
                        pib = PIv[:, 0:8, msl].unsqueeze(3).to_broadcast([128, 8, 4, 16])
                        brb = v16(Bre[:])[:, msl, :].unsqueeze(1).to_broadcast([128, 8, 4, 16])
                        bib = v16(Bim[:])[:, msl, :].unsqueeze(1).to_broadcast([128, 8, 4, 16])
                        P.tt("dve", x4(Xre), prb, brb, ALU.mult)
                        P.tt("dve", x4(tX), pib, bib, ALU.mult)
                        P.tt("dve", Xre[:], Xre[:], tX[:], ALU.subtract)
                        P.tt("dve", x4(Xim), prb, bib, ALU.mult)
                        P.tt("dve", x4(tX), pib, brb, ALU.mult)
                        P.tt("dve", Xim[:], Xim[:], tX[:], ALU.add)
                        xbv = XBD[:].rearrange("p (r t m c) -> p r t m c", r=2, t=8, m=4)
                        for hp in range(2):
                            rows = slice(hp * 64, (hp + 1) * 64)
                            P.cp("dve", xbv[rows, 0, :, :, hp * 16:(hp + 1) * 16], x4(Xre)[rows])
                            P.cp("pool", xbv[rows, 1, :, :, hp * 16:(hp + 1) * 16], x4(Xim)[rows])
                        prb = PRv[:, 1:9, msl].unsqueeze(3).to_broadcast([128, 8, 4, 16])
                        pib = PIv[:, 1:9, msl].unsqueeze(3).to_broadcast([128, 8, 4, 16])
                        crb = v16(cre[:])[:, msl, :].unsqueeze(1).to_broadcast([128, 8, 4, 16])
                        cib = v16(cim[:])[:, msl, :].unsqueeze(1).to_broadcast([128, 8, 4, 16])
                        P.tt("dve", x4(Xre), prb, crb, ALU.mult)
                        P.tt("dve", x4(tX), pib, cib, ALU.mult)
                        P.tt("dve", Xre[:], Xre[:], tX[:], ALU.subtract)
                        P.tt("dve", x4(Xim), pib, crb, ALU.mult)
                        P.tt("dve", x4(tX), prb, cib, ALU.mult)
                        P.stt("dve", Xim[:], Xim[:], -1.0, tX[:], ALU.mult, ALU.subtract)
                        vzv = VZ[:].rearrange("p (t m r n) -> p t m r n", t=8, m=4, r=2)
                        for hp in range(2):
                            rows = slice(hp * 64, (hp + 1) * 64)
                            for m4 in range(4):
                                c0_ = 32 * m4 + 16 * hp
                                P.cp("dve", vzv[rows, :, m4, 0, c0_:c0_ + 16], x4(Xre)[rows, :, m4, :])
                                P.cp("pool", vzv[rows, :, m4, 1, c0_:c0_ + 16], x4(Xim)[rows, :, m4, :])
                        for th2 in range(2):
                            for tl_ in range(4):
                                tau = th2 * 4 + tl_
                                for ri in range(2):
                                    P.tr(psT[:, (tl_ * 2 + ri) * 128:(tl_ * 2 + ri + 1) * 128], XBD[:, (ri * 8 + tau) * 128:(ri * 8 + tau + 1) * 128], ident[:])
                            P.cp("act", WS[:, th2 * 1024:(th2 + 1) * 1024], psT[:])
                        for th2 in range(2):
                            for tl_ in range(4):
                                tau = th2 * 4 + tl_
                                outp = psC[:, tl_ * 128:(tl_ + 1) * 128]
                                P.mm(outp, XBD[:, (0 * 8 + tau) * 128:(0 * 8 + tau + 1) * 128], Cbd_re[:, c * 128:(c + 1) * 128], start=True, stop=False)
                                P.mm(outp, XBD[:, (1 * 8 + tau) * 128:(1 * 8 + tau + 1) * 128], Cbd_nim[:, c * 128:(c + 1) * 128], start=False, stop=True)
                            P.tt("dve", BD[:, th2 * 512:(th2 + 1) * 512].rearrange("p (t n) -> p t n", t=4), psC[:, 0:512].rearrange("p (t n) -> p t n", t=4),
                                 bc_mid(cs["BM16"][:, :], 4), ALU.mult)
                        uc = uT[:, c * NT:(c + 1) * NT]
                        for m4 in range(4):
                            uz = uzb[m4 % 2]
                            P.op("act", lambda e, uz=uz, uc=uc, m4=m4: e.mul(uz[:], uc, cs["PM4"][:, m4:m4 + 1]), [uT, cs["PM4"]], [uz])
                            for ri, dst in ((0, Sin_r), (1, Sin_i)):
                                pa = psA[ri]
                                for s_ in range(8):
                                    tau = 7 - s_
                                    P.mm(pa[:, 0:NCH], WS[:, (tau * 2 + ri) * 128:(tau * 2 + ri + 1) * 128], uz[:, s_:NT:8], start=(s_ == 0), stop=(s_ == 7))
                                if ri == 0:
                                    P.cp("act", dst[:, m4 * NCH:(m4 + 1) * NCH], pa[:, 0:NCH])
                                else:
                                    P.cp("dve", dst[:, m4 * NCH:(m4 + 1) * NCH], pa[:, 0:NCH])
                        a3 = lambda t_: t_[:].rearrange("p (m k) -> p m k", m=4)
                        P.tt("dve", a3(t1k), bc_last(th8[:, msl], 256), bc_mid(cs["KI"][:, :], 4), ALU.mult)
                        sincos(t1k[:, :], 1024, sinT[:, :], cosT[:, :])
                        s3 = lambda t_: t_[:].rearrange("p (m k) -> p m k", m=4)[:, :, 0:256]
                        P.tt("dve", a3(c_r), a3(cosT), s3(Sin_r), ALU.mult)
                        P.tt("dve", a3(t1k), a3(sinT), s3(Sin_i), ALU.mult)
                        P.tt("dve", c_r[:], c_r[:], t1k[:], ALU.add)
                        P.tt("dve", a3(c_i), a3(cosT), s3(Sin_i), ALU.mult)
                        P.tt("dve", a3(t1k), a3(sinT), s3(Sin_r), ALU.mult)
                        P.tt("dve", c_i[:], c_i[:], t1k[:], ALU.subtract)
                        for m4 in range(4):
                            m = 4 * c + m4
                            r8b = A9[:, 8 * 32 + m:8 * 32 + m + 1].to_broadcast([128, 256])
                            for tl_, wo_ in ((c_r, w_r), (c_i, w_i)):
                                seg = tl_[:, m4 * 256:(m4 + 1) * 256]
                                oseg = wo_[:, m4 * 256:(m4 + 1) * 256]
                                P.op("dve", lambda e, seg=seg, oseg=oseg, r8b=r8b: e.tensor_tensor_scan(oseg, r8b, seg, 0.0, ALU.mult, ALU.add), [tl_, A9], [wo_])
                        P.tt("dve", t1k[:], cosT[:], w_r[:], ALU.mult)
                        P.tt("pool", c_r[:], sinT[:], w_i[:], ALU.mult)
                        P.tt("dve", t1k[:], t1k[:], c_r[:], ALU.subtract)
                        P.tt("pool", c_i[:], cosT[:], w_i[:], ALU.mult)
                        P.tt("dve", c_r[:], sinT[:], w_r[:], ALU.mult)
                        P.tt("dve", c_i[:], c_i[:], c_r[:], ALU.add)
                        spr = Sp_r[:].rearrange("p (m k) -> p m k", m=4)
                        spi = Sp_i[:].rearrange("p (m k) -> p m k", m=4)
                        P.memset("pool", spr[:, :, 0:1], 0.0)
                        P.memset("pool", spi[:, :, 0:1], 0.0)
                        P.cp("dve", spr[:, :, 1:256], a3(t1k)[:, :, 0:255])
                        P.cp("pool", spi[:, :, 1:256], a3(c_i)[:, :, 0:255])
                        x0rv = x0r[:].rearrange("p (m b) -> p m b", b=16)[:, msl, :]
                        x0iv = x0i[:].rearrange("p (m b) -> p m b", b=16)[:, msl, :]
                        P.cp("dve", spr[:, :, 256:272], x0rv)
                        P.cp("pool", spi[:, :, 256:272], x0iv)
                        P.cp("dve", FSp[:, 4 * c:4 * c + 4], a3(t1k)[:, :, 255])
                        P.cp("dve", FSp[:, 32 + 4 * c:32 + 4 * c + 4], a3(c_i)[:, :, 255])
                        fsr = FSs[:, 0:512].rearrange("p (m b) -> p m b", b=16)[:, msl, :]
                        fsi = FSs[:, 512:1024].rearrange("p (m b) -> p m b", b=16)[:, msl, :]
                        p8 = bc_last(PR[:, 8 * 32 + 4 * c:8 * 32 + 4 * c + 4], 16)
                        i8_ = bc_last(PI[:, 8 * 32 + 4 * c:8 * 32 + 4 * c + 4], 16)
                        sir = Sin_r[:].rearrange("p (m k) -> p m k", m=4)[:, :, 256:272]
                        sii = Sin_i[:].rearrange("p (m k) -> p m k", m=4)[:, :, 256:272]
                        tq = tX[:, 0:64].rearrange("p (m b) -> p m b", b=16)
                        P.tt("dve", fsr, p8, x0rv, ALU.mult)
                        P.tt("dve", tq, i8_, x0iv, ALU.mult)
                        P.tt("dve", fsr, fsr, tq, ALU.subtract)
                        P.tt("dve", fsr, fsr, sir, ALU.add)
                        P.tt("dve", fsi, p8, x0iv, ALU.mult)
                        P.tt("dve", tq, i8_, x0rv, ALU.mult)
                        P.tt("dve", fsi, fsi, tq, ALU.add)
                        P.tt("dve", fsi, fsi, sii, ALU.add)
                        for j in range(8):
                            acc = psS[j % 2]
                            for s_ in range(j + 1):
                                P.mm(acc[:, 0:NCH], BD[:, (j - s_) * 128:(j - s_ + 1) * 128], uc[:, s_:NT:8], start=(s_ == 0), stop=False)
                            for m4 in range(4):
                                for ri, spt in ((0, Sp_r), (1, Sp_i)):
                                    last = (m4 == 3 and ri == 1)
                                    P.mm(acc[:, 0:NCH], VZ[:, ((j * 4 + m4) * 2 + ri) * 128:((j * 4 + m4) * 2 + ri + 1) * 128],
                                         spt[:, m4 * NCH:(m4 + 1) * NCH], start=False, stop=last)
                            yv, y2, y3 = yvb[j % 2], y2b[j % 2], y3b[j % 2]
                            P.stt("dve", yv[:], uc[:, j:NT:8], dsk[:, c:c + 1], acc[:, 0:NCH], ALU.mult, ALU.add)
                            P.actv(y2[:], yv[:], AF.Square)
                            P.ts("dve", y2[:], y2[:], 0.044715, 1.0, ALU.mult, ALU.add)
                            P.tt("dve", y2[:], y2[:], yv[:], ALU.mult)
                            P.actv(y3[:], y2[:], AF.Sigmoid, scale=1.5957691216057308)
                            P.tt("dve", ygc[:, j:NT:8], yv[:], y3[:], ALU.mult)
                        P.cp("pool", uT[:, c * NT:(c + 1) * NT], ygc[:])
                    P.dma(o_ssp.rearrange("r p m -> p r m"), FSp[:].rearrange("p (r m) -> p r m", r=2))
                    P.dma(o_sss.rearrange("r p x -> p r x"), FSs[:].rearrange("p (r x) -> p r x", r=2))
                    P.barrier()

                with ExitStack() as st3:
                    B3 = lambda name, shape, dt=F32: alloc(st3, name, shape, dt)
                    W1 = B3("W1", [128, 8 * 1024], BF16)
                    W2 = B3("W2", [128, 8 * 1024], BF16)
                    Wo2 = B3("Wo2", [128, 8 * 1024], BF16)
                    yt = [B3("ytc%d" % i, [128, 1024]) for i in range(2)]
                    oT = B3("oT", [128, 8 * 512], BF16)
                    sg = B3("sg", [128, 512]); tg = B3("tg", [128, 512])
                    with ExitStack() as stg_:
                        stg = [alloc(stg_, "stgc%d" % i, [128, 1024]) for i in range(6)]
                        k_ = 0
                        for (Wd, src) in ((W1, glu1), (W2, glu2), (Wo2, w_out_o)):
                            for kc in range(8):
                                s_ = stg[k_ % 6]; k_ += 1
                                P.dma(s_[:], src[kc * 128:(kc + 1) * 128, :])
                                P.cp("dve" if kc % 2 == 0 else "act", Wd[:, kc * 1024:(kc + 1) * 1024], s_[:])
                        P.barrier()
                    for (t0, t1) in blocks:
                        nb = (t1 - t0) * 128
                        col0 = t0 * 128
                        for fc in range(8):
                            p1 = psA[0]; p2 = psA[1]
                            for kc in range(8):
                                P.mm(p1[:, 0:nb], W1[:, kc * 1024 + fc * 128:kc * 1024 + (fc + 1) * 128], uT[:, kc * NT + col0:kc * NT + col0 + nb], start=(kc == 0), stop=(kc == 7))
                            for kc in range(8):
                                P.mm(p2[:, 0:nb], W2[:, kc * 1024 + fc * 128:kc * 1024 + (fc + 1) * 128], uT[:, kc * NT + col0:kc * NT + col0 + nb], start=(kc == 0), stop=(kc == 7))
                            P.actv(sg[:, 0:nb], p2[:, 0:nb], AF.Sigmoid)
                            P.tt("dve", tg[:, 0:nb], p1[:, 0:nb], sg[:, 0:nb], ALU.mult)
                            P.tt("pool", oT[:, fc * 512:fc * 512 + nb], tg[:, 0:nb], szT[:, fc * NT + col0:fc * NT + col0 + nb], ALU.mult)
                        for ti in range(t0, t1):
                            lt = ti - t0
                            ytile = yt[cntb["x"] % 2]; cntb["x"] += 1
                            load_tile(ytile[:], ti)
                            for hf in range(2):
                                pa = psS[hf]
                                for fc in range(8):
                                    P.mm(pa[:, :], oT[:, fc * 512 + lt * 128:fc * 512 + (lt + 1) * 128], Wo2[:, fc * 1024 + hf * 512:fc * 1024 + (hf + 1) * 512], start=(fc == 0), stop=(fc == 7))
                                P.tt("dve", ytile[:, hf * 512:(hf + 1) * 512], ytile[:, hf * 512:(hf + 1) * 512], pa[:, :], ALU.add)
                            if ti < 16:
                                P.dma(yp[ti * 128:(ti + 1) * 128, :], ytile[:], reads=[ytile, ("y0", "p", ti)], writes=[("y0", "p", ti)])
                            else:
                                P.dma(ys, ytile[:], reads=[ytile, ("y0", "s", 0)], writes=[("y0", "s", 0)])
                    P.barrier()
        P.barrier()
        P.flush()
    return nc, in_names


_PROG_CACHE = {}


def _shared_inputs(inputs, consts):
    perm = _perm_even()
    sh = {}
    sh["w_in_e"] = np.ascontiguousarray(inputs["w_in_even"][0][:, perm])
    sh["w_out_e"] = np.ascontiguousarray(inputs["w_out_even"][0])
    sh["norm_e"] = np.ascontiguousarray(inputs["norm_even"][0].reshape(8, 128).T)
    sh["gn_gain"] = np.ascontiguousarray(inputs["ret_gn_gain"][0].reshape(1, 512))
    qn = inputs["nsa_q_norm"][0]
    kn = inputs["nsa_k_norm"][0]
    sh["qk_gain"] = np.concatenate([np.tile(qn, 8), np.tile(kn[0], 2), np.tile(kn[1], 2), np.tile(kn[2], 2)]).reshape(1, 896).astype(np.float32)
    sh["cmp_posT"] = np.ascontiguousarray(inputs["nsa_cmp_pos"][0].transpose(2, 0, 1))
    sh["cmp_w"] = np.ascontiguousarray(inputs["nsa_cmp_w"][0])
    for k, v in consts.items():
        sh["c_" + k] = v
    ca = np.ascontiguousarray
    sh["w_in_o"] = ca(inputs["w_in_odd"][0])
    sh["glu1"] = ca(inputs["glu_w1"][0])
    sh["glu2"] = ca(inputs["glu_w2"][0])
    sh["w_out_o"] = ca(inputs["w_out_odd"][0])
    sh["norm_o"] = ca(inputs["norm_odd"][0].reshape(8, 128).T)
    sh["ssmd"] = ca(inputs["ssm_d"][0].reshape(8, 128).T)
    sh["lamre_A"] = ca(inputs["ssm_lambda_re"][0].reshape(32, 2, 64).transpose(1, 2, 0).reshape(128, 32))
    sh["lamim_A"] = ca(inputs["ssm_lambda_im"][0].reshape(32, 2, 64).transpose(1, 2, 0).reshape(128, 32))
    ls = np.broadcast_to(inputs["ssm_log_step"][0].reshape(32, 2, 1), (32, 2, 64))
    sh["lstep_A"] = ca(ls.transpose(1, 2, 0).reshape(128, 32)).astype(np.float32)
    sh["bA_re"] = ca(inputs["ssm_b_re"][0].reshape(32, 2, 64, 16).transpose(1, 2, 0, 3).reshape(128, 512))
    sh["bA_im"] = ca(inputs["ssm_b_im"][0].reshape(32, 2, 64, 16).transpose(1, 2, 0, 3).reshape(128, 512))
    sh["cA_re"] = ca(inputs["ssm_c_re"][0].reshape(32, 2, 16, 64).transpose(1, 3, 0, 2).reshape(128, 512))
    sh["cA_im"] = ca(inputs["ssm_c_im"][0].reshape(32, 2, 16, 64).transpose(1, 3, 0, 2).reshape(128, 512))
    return sh


def _core_inputs(inputs, c, sh, cache_rows):
    im = dict(sh)
    im["xp"] = np.ascontiguousarray(inputs["x_prompt"][c])
    im["xs"] = np.ascontiguousarray(inputs["x_sample"][16 * c:16 * c + 16].reshape(128, 1024))
    im["cache"] = inputs["cache_nsa_kv"][0].reshape(2560 * 128, 512)[0:cache_rows]
    im["cwin"] = np.ascontiguousarray(inputs["cache_nsa_win"][0, 16 * c:16 * c + 16].reshape(16, 512, 256))
    im["sret"] = np.ascontiguousarray(inputs["state_ret"][0, 16 * c:16 * c + 16])
    im["x0A_re"] = np.ascontiguousarray(inputs["state_ssm_re"][0, 16 * c:16 * c + 16].reshape(16, 32, 2, 64).transpose(2, 3, 1, 0).reshape(128, 512))
    im["x0A_im"] = np.ascontiguousarray(inputs["state_ssm_im"][0, 16 * c:16 * c + 16].reshape(16, 32, 2, 64).transpose(2, 3, 1, 0).reshape(128, 512))
    im["ptab"] = np.ascontiguousarray(inputs["page_table"][16 * c:16 * c + 16].reshape(1, 256)).astype(np.int32)
    return im


def run_cores(inputs, cores, trace=False, **opts):
    consts = make_consts()
    key = tuple(sorted(opts.items()))
    nc, in_names = build_program(consts, **opts)
    sh = _shared_inputs(inputs, consts)
    cache_rows = 2560 * 128 if opts.get("do_sample", True) else 128
    in_maps = [_core_inputs(inputs, c, sh, cache_rows) for c in cores]
    in_maps = [{k: m[k] for k in in_names} for m in in_maps]
    if trace:
        res = run_bass_kernel_spmd(nc, in_maps, core_ids=list(range(len(cores))), trace=True)
        print("EXEC_TIME_NS", res.exec_time_ns)
        return res.results
    res = run_bass_kernel_spmd(nc, in_maps, core_ids=list(range(len(cores))))
    return res.results


def kernel(**inputs):
    inputs = {k: np.asarray(v) for k, v in inputs.items()}
    res = run_cores(inputs, list(range(NCORES)))
    f32 = np.float32
    y_p = np.zeros((8, 2048, 1024), f32)
    y_s = np.zeros((128, 8, 1024), f32)
    ret_p = np.zeros((1, 8, 4, 64, 128), f32)
    ret_s = np.zeros((1, 128, 4, 64, 128), f32)
    kv_p = np.zeros((1, 8, 2048, 4, 2, 64), f32)
    kv_s = np.zeros((1, 128, 8, 4, 2, 64), f32)
    win_p = np.zeros((1, 8, 512, 2, 2, 64), f32)
    win_s = np.zeros((1, 128, 512, 2, 2, 64), f32)
    sre_p = np.zeros((1, 8, 64, 64), f32)
    sim_p = np.zeros((1, 8, 64, 64), f32)
    sre_s = np.zeros((1, 128, 64, 64), f32)
    sim_s = np.zeros((1, 128, 64, 64), f32)
    for c in range(NCORES):
        r = res[c]
        sl = slice(16 * c, 16 * c + 16)
        y_p[c] = r["yp"]
        y_s[sl] = r["ys"].reshape(16, 8, 1024)
        ret_p[0, c] = r["o_retp"]
        ret_s[0, sl] = r["o_rets"]
        kv_p[0, c] = r["o_kvp"].reshape(2048, 4, 2, 64)
        kv_s[0, sl] = r["o_kvs"].reshape(16, 8, 4, 2, 64)
        win_p[0, c] = r["o_winp"].reshape(512, 2, 2, 64)
        win_s[0, sl] = r["o_wins"].reshape(16, 512, 2, 2, 64)
        if "o_ssp" in r:
            sp = r["o_ssp"].reshape(2, 2, 64, 32).transpose(0, 3, 1, 2).reshape(2, 64, 64)
            sre_p[0, c] = sp[0]
            sim_p[0, c] = sp[1]
            ss = r["o_sss"].reshape(2, 2, 64, 32, 16).transpose(0, 4, 3, 1, 2).reshape(2, 16, 64, 64)
            sre_s[0, sl] = ss[0]
            sim_s[0, sl] = ss[1]
    return (y_p, y_s, ret_p, ret_s, kv_p, kv_s, win_p, win_s, sre_p, sim_p, sre_s, sim_s)
```
